# Optimizing a Trainium2 kernel written in Bass

```python
import jax, jax.numpy as jnp
from jax import lax
import numpy as np

D_MODEL = 1024
BATCH = 8
SEQ = 2048
DEPTH = 2
DEC_BATCH = 32
DEC_SEQ = 4
PAST_LEN = 16384
PAGE_SIZE = 128

CONV_CH = 512
CONV_WIDTH = 31
N_SLOTS = 8
HEAD_DIM = 64
ATT_WIDTH = N_SLOTS * HEAD_DIM
WINDOWS = (128, 512, 2048)
DILATIONS = (1, 4, 16)
N_GROUPS = 3
SPAN = 128
BLOCK = 128
D_FF = -(-8 * D_MODEL // (3 * 256)) * 256
DN_ALPHA = (2 * DEPTH) ** 0.25
DN_BETA = (8 * DEPTH) ** -0.25
LN_EPS = 1e-5
QKV_OFF = 2 * CONV_CH
GATE_OFF = QKV_OFF + 3 * N_GROUPS * ATT_WIDTH
IN_WIDTH = GATE_OFF + 2 * D_MODEL

kernel_name = 'dilated_conformer_hybrid_step'


def layer_norm(x, g, b):
    xf = x.astype(jnp.float32)
    xc = xf - jnp.mean(xf, axis=-1, keepdims=True)
    var = jnp.mean(xc * xc, axis=-1, keepdims=True)
    return (xc * lax.rsqrt(var + LN_EPS) * g + b).astype(x.dtype)


def alibi_slopes():
    return jnp.asarray(2.0 ** (-8.0 * (np.arange(N_SLOTS) + 1) / N_SLOTS), dtype=jnp.float32)


def conv_branch(conv_in, w_dw, b_dw, ln_g, ln_b, w_pc, b_pc):
    y = lax.conv_general_dilated(conv_in, w_dw[:, None, :], window_strides=(1,), padding='VALID',
                                 dimension_numbers=('NWC', 'WIO', 'NWC'),
                                 feature_group_count=CONV_CH) + b_dw
    y = jax.nn.silu(layer_norm(y, ln_g, ln_b))
    return y @ w_pc + b_pc


def dilated_attn_prompt(q, k, v, dil, slopes):
    B, S, H, Dh = q.shape
    n = S // dil
    nb = -(-n // BLOCK)
    n_pad = nb * BLOCK

    def to_sub(t):
        t = t.reshape(B, n, dil, H, Dh).transpose(0, 2, 1, 3, 4).reshape(B * dil, n, H, Dh)
        return jnp.pad(t, ((0, 0), (0, n_pad - n), (0, 0), (0, 0)))

    def band(t):
        tp = jnp.pad(t, ((0, 0), (BLOCK, 0), (0, 0), (0, 0))).reshape(B * dil, nb + 1, BLOCK, H, Dh)
        return jnp.concatenate([tp[:, :-1], tp[:, 1:]], axis=2)

    qb = to_sub(q).reshape(B * dil, nb, BLOCK, H, Dh)
    kb = band(to_sub(k))
    vb = band(to_sub(v))
    s = jnp.einsum('xcqhd,xckhd->xchqk', qb, kb, preferred_element_type=jnp.float32) * (HEAD_DIM ** -0.5)
    rel = BLOCK + jnp.arange(BLOCK)[:, None] - jnp.arange(2 * BLOCK)[None, :]
    key_sub = (jnp.arange(nb)[:, None] - 1) * BLOCK + jnp.arange(2 * BLOCK)[None, :]
    valid = ((rel >= 0) & (rel <= SPAN))[None] & (key_sub >= 0)[:, None, :]
    bias = -slopes[:, None, None] * (rel * dil).astype(jnp.float32)[None]
    s = jnp.where(valid[None, :, None], s + bias, -jnp.inf)
    lse = jax.nn.logsumexp(s, axis=-1)
    p = jnp.exp(s - lse[..., None]).astype(v.dtype)
    o = jnp.einsum('xchqk,xckhd->xcqhd', p, vb, preferred_element_type=jnp.float32)
    o = o.reshape(B, dil, n_pad, H, Dh)[:, :, :n].transpose(0, 2, 1, 3, 4).reshape(B, S, H, Dh)
    lse = lse.transpose(0, 1, 3, 2).reshape(B, dil, n_pad, H)[:, :, :n]
    lse = lse.transpose(0, 2, 1, 3).reshape(B, S, H)
    return o, lse


def dilated_attn_sample(q, kc, vc, dil, slopes):
    T = q.shape[1]
    L = kc.shape[1] - T
    j = jnp.arange(SPAN + 1)
    idx = L + jnp.arange(T)[:, None] - j[None, :] * dil
    valid = idx >= 0
    idx = jnp.maximum(idx, 0)
    kg = kc[:, idx]
    vg = vc[:, idx]
    s = jnp.einsum('bthd,btjhd->bthj', q, kg, preferred_element_type=jnp.float32) * (HEAD_DIM ** -0.5)
    bias = -slopes[:, None] * (j * dil).astype(jnp.float32)[None, :]
    s = jnp.where(valid[None, :, None, :], s + bias, -jnp.inf)
    lse = jax.nn.logsumexp(s, axis=-1)
    p = jnp.exp(s - lse[..., None]).astype(vc.dtype)
    o = jnp.einsum('bthj,btjhd->bthd', p, vg, preferred_element_type=jnp.float32)
    return o, lse


def token_mixers(u, mw, slopes, conv_prev, kv_prev):
    (w_in, b_in, w_dw, b_dw, conv_ln_g, conv_ln_b, w_pc, b_pc, w_pa, b_pa, w_out, b_out) = mw
    B, T, _ = u.shape
    a = u @ w_in + b_in
    glu = a[..., :CONV_CH] * jax.nn.sigmoid(a[..., CONV_CH:QKV_OFF])
    qkv = a[..., QKV_OFF:GATE_OFF].reshape(B, T, 3, N_GROUPS, N_SLOTS, HEAD_DIM)
    gate_c = jax.nn.sigmoid(a[..., GATE_OFF:GATE_OFF + D_MODEL])
    gate_a = jax.nn.sigmoid(a[..., GATE_OFF + D_MODEL:])

    if conv_prev is None:
        conv_in = jnp.pad(glu, ((0, 0), (CONV_WIDTH - 1, 0), (0, 0)))
    else:
        conv_in = jnp.concatenate([conv_prev.astype(glu.dtype), glu], axis=1)
    y_conv = conv_branch(conv_in, w_dw, b_dw, conv_ln_g, conv_ln_b, w_pc, b_pc)
    new_conv = conv_in[:, -(CONV_WIDTH - 1):]

    outs, lses, new_kv = [], [], []
    for g in range(N_GROUPS):
        q, k, v = qkv[:, :, 0, g], qkv[:, :, 1, g], qkv[:, :, 2, g]
        kv_new = jnp.stack([k, v], axis=2)
        if kv_prev is None:
            o, lse = dilated_attn_prompt(q, k, v, DILATIONS[g], slopes)
            new_kv.append(kv_new[:, -min(WINDOWS[g], T):])
        else:
            cat = jnp.concatenate([kv_prev[g].astype(kv_new.dtype), kv_new], axis=1)
            o, lse = dilated_attn_sample(q, cat[:, :, 0], cat[:, :, 1], DILATIONS[g], slopes)
            new_kv.append(cat[:, T:])
        outs.append(o)
        lses.append(lse)
    w = jax.nn.softmax(jnp.stack(lses), axis=0)
    o = jnp.sum(w[..., None] * jnp.stack(outs), axis=0).astype(u.dtype)
    y_att = o.reshape(B, T, ATT_WIDTH) @ w_pa + b_pa

    y = (gate_c * y_conv + gate_a * y_att) @ w_out + b_out
    return y, new_kv, new_conv


def block(x, c, lw, slopes, conv_prev, kv_prev):
    (w_ada, b_ada, w_in, b_in, w_dw, b_dw, conv_ln_g, conv_ln_b, w_pc, b_pc, w_pa, b_pa,
     w_out, b_out, ln1_g, ln1_b, w_gate, w_up, w_down, ln2_g, ln2_b) = lw
    mod = (jax.nn.silu(c) @ w_ada + b_ada)[:, None, :]
    sh1, sc1, g1, sh2, sc2, g2 = jnp.split(mod, 6, axis=-1)
    u = x * (1 + sc1) + sh1
    y, new_kv, new_conv = token_mixers(
        u, (w_in, b_in, w_dw, b_dw, conv_ln_g, conv_ln_b, w_pc, b_pc, w_pa, b_pa, w_out, b_out),
        slopes, conv_prev, kv_prev)
    x = layer_norm(DN_ALPHA * x + (1 + g1) * y, ln1_g, ln1_b)
    u = x * (1 + sc2) + sh2
    h = (jax.nn.silu(u @ w_gate) * (u @ w_up)) @ w_down
    x = layer_norm(DN_ALPHA * x + (1 + g2) * h, ln2_g, ln2_b)
    return x, new_kv, new_conv


def setup_inputs(seed: int = 0) -> dict:
    key = jax.random.key(seed)
    keys = list(jax.random.split(key, 32))

    def nrm(shape, scale):
        return jax.random.normal(keys.pop(), shape, jnp.float32) * scale

    L, D = DEPTH, D_MODEL
    buf = [min(w, PAST_LEN) for w in WINDOWS]
    return {
        'x_prompt': nrm((BATCH, SEQ, D), 1.0),
        'x_sample': nrm((DEC_BATCH, DEC_SEQ, D), 1.0),
        'c_prompt': nrm((BATCH, D), 1.0),
        'c_sample': nrm((DEC_BATCH, D), 1.0),
        'cache_kv_g0': nrm((L, DEC_BATCH, buf[0], 2, N_SLOTS, HEAD_DIM), 1.0),
        'cache_kv_g1': nrm((L, DEC_BATCH, buf[1], 2, N_SLOTS, HEAD_DIM), 1.0),
        'cache_kv_g2': nrm((L, DEC_BATCH, buf[2], 2, N_SLOTS, HEAD_DIM), 1.0),
        'state_conv': nrm((L, DEC_BATCH, CONV_WIDTH - 1, CONV_CH), 0.5),
        'w_ada': nrm((L, D, 6 * D), 0.1 * D ** -0.5),
        'b_ada': nrm((L, 6 * D), 0.01),
        'w_in': nrm((L, D, IN_WIDTH), D ** -0.5),
        'b_in': nrm((L, IN_WIDTH), 0.01),
        'w_dw': nrm((L, CONV_WIDTH, CONV_CH), CONV_WIDTH ** -0.5),
        'b_dw': nrm((L, CONV_CH), 0.01),
        'conv_ln_g': 1.0 + nrm((L, CONV_CH), 0.01),
        'conv_ln_b': nrm((L, CONV_CH), 0.01),
        'w_pc': nrm((L, CONV_CH, D), CONV_CH ** -0.5),
        'b_pc': nrm((L, D), 0.01),
        'w_pa': nrm((L, ATT_WIDTH, D), ATT_WIDTH ** -0.5),
        'b_pa': nrm((L, D), 0.01),
        'w_out': nrm((L, D, D), DN_BETA * D ** -0.5),
        'b_out': nrm((L, D), 0.01),
        'ln1_g': 1.0 + nrm((L, D), 0.01),
        'ln1_b': nrm((L, D), 0.01),
        'w_gate': nrm((L, D, D_FF), D ** -0.5),
        'w_up': nrm((L, D, D_FF), D ** -0.5),
        'w_down': nrm((L, D_FF, D), DN_BETA * D_FF ** -0.5),
        'ln2_g': 1.0 + nrm((L, D), 0.01),
        'ln2_b': nrm((L, D), 0.01),
    }


def reference(x_prompt, x_sample, c_prompt, c_sample, cache_kv_g0, cache_kv_g1, cache_kv_g2,
              state_conv, w_ada, b_ada, w_in, b_in, w_dw, b_dw, conv_ln_g, conv_ln_b, w_pc, b_pc,
              w_pa, b_pa, w_out, b_out, ln1_g, ln1_b, w_gate, w_up, w_down, ln2_g, ln2_b):
    slopes = alibi_slopes()
    caches = (cache_kv_g0, cache_kv_g1, cache_kv_g2)
    xp, xs = x_prompt, x_sample
    kv_p = [[], [], []]
    kv_s = [[], [], []]
    conv_p, conv_s = [], []
    for l in range(DEPTH):
        lw = (w_ada[l], b_ada[l], w_in[l], b_in[l], w_dw[l], b_dw[l], conv_ln_g[l], conv_ln_b[l],
              w_pc[l], b_pc[l], w_pa[l], b_pa[l], w_out[l], b_out[l], ln1_g[l], ln1_b[l],
              w_gate[l], w_up[l], w_down[l], ln2_g[l], ln2_b[l])
        xp, nkv_p, nconv_p = block(xp, c_prompt, lw, slopes, None, None)
        xs, nkv_s, nconv_s = block(xs, c_sample, lw, slopes, state_conv[l],
                                   (caches[0][l], caches[1][l], caches[2][l]))
        for g in range(N_GROUPS):
            kv_p[g].append(nkv_p[g])
            kv_s[g].append(nkv_s[g])
        conv_p.append(nconv_p)
        conv_s.append(nconv_s)
    return (xp, xs,
            jnp.stack(kv_p[0]), jnp.stack(kv_p[1]), jnp.stack(kv_p[2]), jnp.stack(conv_p),
            jnp.stack(kv_s[0]), jnp.stack(kv_s[1]), jnp.stack(kv_s[2]), jnp.stack(conv_s))
```

```python
import contextlib
import numpy as np
import concourse.bass as bass
import concourse.mybir as mybir
from concourse.bass_utils import run_bass_kernel_spmd

F32 = mybir.dt.float32
BF16 = mybir.dt.bfloat16
AF = mybir.ActivationFunctionType
ALU = mybir.AluOpType

NCORES = 8
D = 1024
S = 2048
NS = 16
NT = S + NS
L = 2
DFF = 2816
NF = DFF // 128
ALPHA = float((2 * L) ** 0.25)
EPS = 1e-5
TT = [(0, 512), (512, 512), (1024, 512), (1536, 512), (2048, 16)]
DIL = (1, 4, 16)
WIN = (128, 512, 2048)
ENGS = ("sp", "act", "pool", "dve", "pe")
SAME_SYNC = True
NSLOT = 4
A0_END = 8 * NT * 2
RB_BYTES = 112 * 1024

V_BIN, V_BADA, V_BPC, V_BPA, V_BOUT, V_L1G, V_L1B, V_L2G, V_L2B, V_BDW, V_CLG, V_CLB, V_WDW = (
    0, 60, 108, 116, 124, 132, 140, 148, 156, 164, 168, 172, 176)
NVEC = 384


def segs(tt):
    if tt < 4:
        return [(slice(0, 512), 0)]
    return [(slice(4 * s, 4 * s + 4), 1 + s) for s in range(4)]


class DSem:
    def __init__(self, h):
        self.h = h
        self.count = 0


class Buf:
    __slots__ = ("name", "w", "r", "dsem", "rng", "phase", "psum")

    def __init__(self, name, rng=None, phase=None):
        self.name = name
        self.psum = False
        self.w = {}
        self.r = {}
        self.dsem = None
        self.rng = rng
        self.phase = phase


class Op:
    __slots__ = ("eng", "fn", "deps", "sig", "sigidx", "dsem", "sigval", "seq", "calls")


class Rec:
    def __init__(self):
        self.calls = []

    def __getattr__(self, name):
        def f(*a, **k):
            self.calls.append((name, a, k))
            return self
        return f


class Prog:
    def __init__(self, nc, stack):
        self.nc = nc
        self.stack = stack
        self.ops = {e: [] for e in ENGS}
        self.dsems = []
        self.seq = 0
        self.esem = {e: stack.enter_context(nc.semaphore("es_" + e)) for e in ENGS}
        self.region_bufs = []
        self.phase = 0

    def dsem_of(self, buf):
        if buf.dsem is None:
            h = self.stack.enter_context(self.nc.semaphore("ds%d" % len(self.dsems)))
            buf.dsem = DSem(h)
            self.dsems.append(buf.dsem)
        return buf.dsem

    @staticmethod
    def _key(o):
        return ("d", id(o.dsem)) if o.dsem is not None else ("e", o.eng)

    def op(self, eng, fn, r=(), w=(), dma=None):
        o = Op()
        o.eng = eng
        o.fn = fn
        rec = Rec()
        fn(rec)
        o.calls = rec.calls
        assert len(o.calls) > 0
        o.sig = False
        o.sigidx = 0
        o.dsem = None
        o.sigval = 0
        self.seq += 1
        o.seq = self.seq
        deps = {}
        strong = {}

        def add(p, st=True):
            k = self._key(p)
            q = deps.get(k)
            if q is None or q.seq < p.seq:
                deps[k] = p
            if st:
                q = strong.get(k)
                if q is None or q.seq < p.seq:
                    strong[k] = p

        for b in r:
            for p in b.w.values():
                add(p)
            if b.psum and eng in ("act", "dve"):
                for p in b.r.values():
                    if p.eng != eng and p.eng in ("act", "dve"):
                        add(p)
        for b in w:
            for p in b.w.values():
                add(p, p.eng != eng)
            for p in b.r.values():
                add(p, p.eng != eng)
        k_self = ("e", eng)
        if k_self in deps and dma is None:
            if k_self in strong:
                deps[k_self] = strong[k_self]
            else:
                del deps[k_self]
        o.deps = deps
        if dma is not None:
            o.dsem = self.dsem_of(dma)
            o.dsem.count += 1
            o.sigval = 16 * o.dsem.count
        k = self._key(o)
        for b in r:
            b.r[k] = o
        for b in w:
            b.w = {k: o}
            b.r = {}
        self.ops[eng].append(o)
        return o

    def rbuf(self, name, off, nbytes):
        b = Buf(name, (off, off + nbytes), self.phase)
        for ob in self.region_bufs:
            if ob.phase != self.phase and ob.rng[0] < b.rng[1] and b.rng[0] < ob.rng[1]:
                for p in list(ob.w.values()) + list(ob.r.values()):
                    k = self._key(p)
                    q = b.w.get(k)
                    if q is None or q.seq < p.seq:
                        b.w[k] = p
        self.region_bufs.append(b)
        return b

    def prune_region(self):
        pass

    def finalize(self):
        for e in ENGS:
            for o in self.ops[e]:
                for p in o.deps.values():
                    if p.dsem is None and not (p.eng == o.eng and (o.eng == "pe" or not SAME_SYNC)):
                        p.sig = True
        for e in ENGS:
            i = 0
            for o in self.ops[e]:
                if o.sig:
                    i += 1
                    o.sigidx = i

    def emit(self, eng, e):
        known = {}
        for o in self.ops[eng]:
            for p in sorted(o.deps.values(), key=lambda x: x.seq):
                if p.dsem is not None:
                    sem, val, kk = p.dsem.h, p.sigval, id(p.dsem)
                else:
                    if p.eng == eng and (eng == "pe" or not SAME_SYNC):
                        continue
                    sem, val, kk = self.esem[p.eng], p.sigidx, p.eng
                if known.get(kk, 0) >= val:
                    continue
                e.wait_ge(sem, val)
                known[kk] = val
            if o.fn is None:
                continue
            ins = None
            for (name, a, k) in o.calls:
                ins = getattr(e, name)(*a, **k)
            if o.dsem is not None:
                ins.then_inc(o.dsem.h, 16)
            elif o.sig:
                ins.then_inc(self.esem[eng], 1)


class StopBuild(Exception):
    pass


KSTAGE = 99


class Builder:
    def stage(self, k):
        if not hasattr(self, "marks"):
            self.marks = []
        self.marks.append((k, sum(len(o.calls) for o in self.P.ops["pe"])))
        if KSTAGE <= k:
            raise StopBuild()

    def __init__(self):
        self.nc = bass.Bass("TRN2", target_bir_lowering=False)
        self.stack = contextlib.ExitStack()

    def sb(self, name, shape, dt):
        return self.stack.enter_context(self.nc.sbuf_tensor("sb_" + name, list(shape), dt))

    def carve(self, name, shape, dt, nbufs=1, area=1):
        esz = 4 if dt == F32 else 2
        per = int(np.prod(shape[1:])) * esz
        per = (per + 63) // 64 * 64
        off = self.rb_offs[area]
        assert off % 4 == 0
        self.rb_offs[area] += per
        lim = self.rb_lims[area]
        assert self.rb_offs[area] <= lim, (name, area, self.rb_offs[area], lim)
        a = self.regB[:, off // 4:(off + per) // 4]
        if dt == BF16:
            a = a.bitcast(BF16)
        n = int(np.prod(shape[1:]))
        a = a[:, 0:n]
        if len(shape) > 2:
            names = ["d%d" % i for i in range(len(shape) - 1)]
            kw = {names[i]: shape[1 + i] for i in range(len(shape) - 2)}
            a = a.rearrange("p (%s) -> p %s" % (" ".join(names), " ".join(names)), **kw)
        bufs = [self.P.rbuf("%s_%d" % (name, i), off, per) for i in range(nbufs)]
        return a, bufs

    def new_phase(self, start1, start0=0, lim0=A0_END):
        self.P.phase += 1
        self.rb_offs = [start0, start1]
        self.rb_lims = [lim0, RB_BYTES]

    def wload(self, src2d, kch, ncols):
        i = self.ring_i % NSLOT
        self.ring_i += 1
        slot = self.ring[i]
        buf = self.ringb[i]
        view = slot[:, 0:kch * ncols].rearrange("p (k n) -> p k n", k=kch)
        src = src2d.rearrange("(k p) n -> p k n", p=128)
        self.P.op("pool", lambda e, view=view, src=src: e.dma_start(out=view, in_=src), w=[buf], dma=buf)
        return view, buf

    def bank(self, pool):
        lst = self.pools[pool]
        i = self.pool_i[pool] % len(lst)
        self.pool_i[pool] += 1
        b = lst[i]
        return self.ps[b], self.psb[b]

    def mm_acc(self, out, outbuf, pairs, rbufs):
        def fn(e, out=out, pairs=pairs):
            n = len(pairs)
            ins = None
            for i, (lt, rh) in enumerate(pairs):
                ins = e.matmul(out, lhsT=lt, rhs=rh, start=(i == 0), stop=(i == n - 1))
            return ins
        return self.P.op("pe", fn, r=rbufs, w=[outbuf])

    def build(self):
        nc = self.nc
        st = self.stack
        P = self.P = Prog(nc, st)

        def din(name, shape):
            return nc.dram_tensor(name, list(shape), F32, kind="ExternalInput").ap()

        def dout(name, shape):
            return nc.dram_tensor(name, list(shape), F32, kind="ExternalOutput").ap()

        xp = din("xp", [S, D])
        xs = din("xs", [NS, D])
        c5 = din("c5", [5, D])
        ck = [din("ck%d" % g, [L, 4, WIN[g], 2 * 512]) for g in range(3)]
        stc = din("stc", [L, 4, 30, 512])
        w_ada = din("w_ada", [L, D, 6 * D])
        w_in = din("w_in", [L, D, 7680])
        w_pc = din("w_pc", [L, 512, D])
        w_pa = din("w_pa", [L, 512, D])
        w_out = din("w_out", [L, D, D])
        w_gate = din("w_gate", [L, D, DFF])
        w_up = din("w_up", [L, D, DFF])
        w_down = din("w_down", [L, DFF, D])
        vecs = din("vecs", [L, NVEC, 128])
        ident_d = din("ident", [128, 128])
        masks_d = din("masks", [128, 12 * 512])
        smp_d = din("smp", [128, 12 * 32])
        smc_d = din("smc", [4, 12 * 32])

        yp = dout("yp", [S, D])
        ys = dout("ys", [NS, D])
        kvp = [dout("kvp%d" % g, [L, WIN[g], 2, 512]) for g in range(3)]
        cvp = dout("cvp", [L, 30, 512])
        kvs = [dout("kvs%d" % g, [L, 4, WIN[g], 2 * 512]) for g in range(3)]
        cvs = dout("cvs", [L, 4, 30, 512])
        rscr = nc.dram_tensor("rscr", [128, 8, NT], F32).ap()

        U = self.sb("U", [128, 8, NT], BF16)
        Ub = [[Buf("U%d_%d" % (c, t)) for t in range(5)] for c in range(8)]
        self.ring = [self.sb("ring%d" % i, [128, 4096], BF16) for i in range(NSLOT)]
        self.ringb = [Buf("ring%d" % i) for i in range(NSLOT)]
        self.ring_i = 0
        self.ln_ri = 0
        ident = self.sb("ident", [128, 128], F32)
        ones_bf = self.sb("ones_bf", [128, 128], BF16)
        onesD = self.sb("onesD", [128, 128], BF16)
        onesC = self.sb("onesC", [128, 128], BF16)
        ones_f = self.sb("ones_f", [128, 64], F32)
        eps_t = self.sb("eps_t", [128, 1], F32)
        alpha_t = self.sb("alpha_t", [128, 8, 5], F32)
        masks = self.sb("masks", [128, 12, 512], BF16)
        smp = self.sb("smp", [128, 12, 32], F32)
        smc = self.sb("smc", [4, 12, 32], F32)
        vecT = self.sb("vecT", [128, L, NVEC], F32)
        bq8 = self.sb("bq8", [128, L, 12], F32)
        modT = self.sb("modT", [128, L, 48, 5], F32)
        coef = self.sb("coef", [128, 5, 4, 8, 5], F32)
        scT = self.sb("scT", [128, 8, 5], BF16)
        ctmp = self.sb("ctmp", [128, 8, 5], F32)
        CONST = Buf("const")
        MASKS = Buf("masks")
        VEC = Buf("vec")
        MOD = Buf("mod")
        COEF = Buf("coef")
        SCT = Buf("sct")
        self.regB = self.sb("regB", [128, RB_BYTES // 4], F32)
        self.rb_offs = [0, 0]
        self.rb_lims = [0, RB_BYTES]

        self.ps = [st.enter_context(nc.psum_tensor("ps%d" % i, [128, 512], F32)) for i in range(8)]
        self.psb = [Buf("ps%d" % i) for i in range(8)]
        for b in self.psb:
            b.psum = True
        self.pools = {"acc": [0, 1, 2], "o": [3, 4], "z": [5, 6], "misc": [7], "sS": [0, 1, 2], "tr": [7, 0, 1, 2], "all": [0, 1, 2, 3, 4, 5, 6, 7]}
        self.pool_i = {k: 0 for k in self.pools}

        def vcol(l, row):
            return vecT[:, l, row:row + 1]

        self.new_phase(0, 0, 0)
        P.op("sp", lambda e: e.dma_start(out=ident[:], in_=ident_d), w=[CONST], dma=CONST)
        P.op("pool", lambda e: e.dma_start(out=masks[:], in_=masks_d.rearrange("p (a b) -> p a b", a=12)),
             w=[MASKS], dma=MASKS)
        SM = Buf("sm")
        P.op("sp", lambda e: e.dma_start(out=smp[:], in_=smp_d.rearrange("p (a b) -> p a b", a=12)), w=[SM], dma=SM)
        SM2 = Buf("sm2")
        P.op("sp", lambda e: e.dma_start(out=smc[:], in_=smc_d.rearrange("p (a b) -> p a b", a=12)), w=[SM2], dma=SM2)
        ONES = Buf("ones")
        P.op("dve", lambda e: e.memset(ones_bf[:], 1.0), w=[ONES])
        P.op("dve", lambda e: e.memset(onesD[:], 1.0 / 1024.0), w=[ONES])
        P.op("dve", lambda e: e.memset(onesC[:], 1.0 / 512.0), w=[ONES])
        P.op("dve", lambda e: e.memset(ones_f[:], 1.0), w=[ONES])
        P.op("dve", lambda e: e.memset(eps_t[:], EPS), w=[ONES])
        P.op("dve", lambda e: e.memset(alpha_t[:], ALPHA), w=[ONES])

        D2D = [Buf("d2d%d" % i) for i in range(4)]
        self.d2d_list = []
        self.d2d_i = 0
        for l in range(L):
            for s in range(4):
                self.d2d_list.append((stc[l, s, 4:30, :], cvs[l, s, 0:26, :]))
        nsmall = len(self.d2d_list)
        for l in range(L):
            for g in range(3):
                W = WIN[g]
                for s in range(4):
                    self.d2d_list.append((ck[g][l, s, 4:W, :], kvs[g][l, s, 0:W - 4, :]))

        def emit_d2d(n):
            for _ in range(n):
                if self.d2d_i >= len(self.d2d_list):
                    return
                src, dst = self.d2d_list[self.d2d_i]
                bq = D2D[self.d2d_i % 4]
                self.d2d_i += 1
                P.op("act", lambda e, src=src, dst=dst: e.dma_start(out=dst, in_=src), w=[bq], dma=bq)
        self.emit_d2d = emit_d2d
        emit_d2d(nsmall)

        vraw, vrawb = self.carve("vraw", [128, 2, 128], F32, 2)
        for l in range(L):
            for j in range(3):
                i = (l * 3 + j) % 2
                P.op("sp", lambda e, i=i, l=l, j=j: e.dma_start(out=vraw[:, i, :], in_=vecs[l, 128 * j:128 * j + 128, :]),
                     w=[vrawb[i]], dma=vrawb[i])
                pt, ptb = self.bank("misc")
                P.op("pe", lambda e, pt=pt, i=i: e.transpose(pt[:, 0:128], vraw[:, i, :], ident[:]),
                     r=[vrawb[i], CONST], w=[ptb])
                P.op("dve", lambda e, pt=pt, l=l, j=j: e.tensor_copy(out=vecT[:, l, 128 * j:128 * j + 128], in_=pt[:, 0:128]),
                     r=[ptb], w=[VEC])
        for l in range(L):
            P.op("dve", lambda e, l=l: e.tensor_scalar(out=bq8[:, l, :], in0=vecT[:, l, 8:20], scalar1=0.125,
                                                       scalar2=None, op0=ALU.mult), r=[VEC], w=[VEC])
        c5t, c5b = self.carve("c5t", [128, 1024], F32, 1)
        P.op("sp", lambda e: e.dma_start(out=c5t[0:5, :], in_=c5), w=c5b, dma=c5b[0])
        pt, ptb = self.bank("misc")

        def fn_ct(e, pt=pt):
            ins = None
            for k in range(8):
                ins = e.transpose(pt[:, 5 * k:5 * k + 5], c5t[0:5, 128 * k:128 * k + 128], ident[0:5, 0:5])
            return ins
        P.op("pe", fn_ct, r=[c5b[0], CONST], w=[ptb])
        P.op("act", lambda e, pt=pt: e.activation(out=scT[:], in_=pt[:, 0:40].rearrange("p (a b) -> p a b", a=8),
                                                  func=AF.Silu), r=[ptb], w=[SCT])
        for l in range(L):
            pm, pmb = self.bank("acc")
            for grp in range(12):
                Wt, Wb = self.wload(w_ada[l, :, 512 * grp:512 * grp + 512], 8, 512)

                def fn_mod(e, Wt=Wt, pm=pm, grp=grp):
                    ins = None
                    for jj in range(4):
                        j = 4 * grp + jj
                        for k in range(8):
                            ins = e.matmul(pm[:, 5 * j:5 * j + 5], lhsT=Wt[:, k, 128 * jj:128 * jj + 128],
                                           rhs=scT[:, k, :], start=(k == 0), stop=(k == 7))
                    return ins
                P.op("pe", fn_mod, r=[Wb, SCT], w=[pmb])
            P.op("dve", lambda e, pm=pm, l=l: e.tensor_tensor(
                out=modT[:, l], in0=pm[:, 0:240].rearrange("p (a b) -> p a b", a=48),
                in1=vecT[:, l, V_BADA:V_BADA + 48].unsqueeze(2).to_broadcast([128, 48, 5]), op=ALU.add),
                r=[pmb, VEC], w=[MOD])
            for (a, b) in ((8, 24), (32, 48)):
                P.op("dve", lambda e, l=l, a=a, b=b: e.tensor_scalar(
                    out=modT[:, l, a:b], in0=modT[:, l, a:b], scalar1=1.0, scalar2=None, op0=ALU.add),
                    r=[MOD], w=[MOD])

        RS = [[Buf("rs%d_%d" % (c, t)) for t in range(5)] for c in range(8)]

        def vb(l, row):
            return vecT[:, l, row:row + 8].unsqueeze(2).to_broadcast([128, 8, 5])

        def cop(fn):
            P.op("dve", fn, r=[MOD, VEC, COEF, ONES], w=[COEF])

        SH1, SC1P, G1P, SH2, SC2P, G2P = 0, 8, 16, 24, 32, 40
        cop(lambda e: e.tensor_copy(out=coef[:, 0, 0], in_=modT[:, 0, SC1P:SC1P + 8]))
        cop(lambda e: e.tensor_copy(out=coef[:, 0, 1], in_=modT[:, 0, SH1:SH1 + 8]))
        cop(lambda e: e.tensor_copy(out=coef[:, 0, 2], in_=alpha_t[:]))
        cop(lambda e: e.tensor_tensor(out=coef[:, 0, 3], in0=modT[:, 0, G1P:G1P + 8], in1=vb(0, V_BOUT), op=ALU.mult))
        for l in range(L):
            s1 = 1 + 2 * l
            cop(lambda e, l=l, s1=s1: e.tensor_tensor(out=coef[:, s1, 0], in0=modT[:, l, SC2P:SC2P + 8],
                                                      in1=vb(l, V_L1G), op=ALU.mult))
            cop(lambda e, l=l, s1=s1: e.tensor_tensor(out=coef[:, s1, 1], in0=modT[:, l, SC2P:SC2P + 8],
                                                      in1=vb(l, V_L1B), op=ALU.mult))
            cop(lambda e, l=l, s1=s1: e.tensor_tensor(out=coef[:, s1, 1], in0=coef[:, s1, 1],
                                                      in1=modT[:, l, SH2:SH2 + 8], op=ALU.add))
            cop(lambda e, l=l, s1=s1: e.tensor_tensor(out=coef[:, s1, 2], in0=alpha_t[:], in1=vb(l, V_L1G), op=ALU.mult))
            cop(lambda e, l=l, s1=s1: e.tensor_tensor(out=coef[:, s1, 3], in0=alpha_t[:], in1=vb(l, V_L1B), op=ALU.mult))
            s2 = 2 + 2 * l
            if l + 1 < L:
                n = l + 1
                cop(lambda e, l=l, s2=s2, n=n: e.tensor_tensor(out=coef[:, s2, 0], in0=modT[:, n, SC1P:SC1P + 8],
                                                               in1=vb(l, V_L2G), op=ALU.mult))
                cop(lambda e, l=l, s2=s2, n=n: e.tensor_tensor(out=coef[:, s2, 1], in0=modT[:, n, SC1P:SC1P + 8],
                                                               in1=vb(l, V_L2B), op=ALU.mult))
                cop(lambda e, l=l, s2=s2, n=n: e.tensor_tensor(out=coef[:, s2, 1], in0=coef[:, s2, 1],
                                                               in1=modT[:, n, SH1:SH1 + 8], op=ALU.add))
                cop(lambda e, l=l, s2=s2: e.tensor_tensor(out=coef[:, s2, 2], in0=alpha_t[:], in1=vb(l, V_L2G), op=ALU.mult))
                cop(lambda e, l=l, s2=s2: e.tensor_tensor(out=coef[:, s2, 3], in0=alpha_t[:], in1=vb(l, V_L2B), op=ALU.mult))
                cop(lambda e, n=n: e.tensor_tensor(out=ctmp[:], in0=modT[:, n, G1P:G1P + 8], in1=vb(n, V_BOUT), op=ALU.mult))
                cop(lambda e, s2=s2: e.tensor_tensor(out=coef[:, s2, 3], in0=coef[:, s2, 3], in1=ctmp[:], op=ALU.add))
            else:
                cop(lambda e, l=l, s2=s2: e.tensor_copy(out=coef[:, s2, 0], in_=vb(l, V_L2G)))
                cop(lambda e, l=l, s2=s2: e.tensor_copy(out=coef[:, s2, 1], in_=vb(l, V_L2B)))

        def produce(src, srcbufs, c, tt, stg, rst, rstb, src_psum=False):
            t0, n = TT[tt]
            for (cs, s) in segs(tt):
                P.op("act", lambda e, cs=cs, s=s: e.activation(
                    out=U[:, c, t0 + cs.start:t0 + cs.stop], in_=src[:, cs], func=AF.Identity,
                    scale=coef[:, stg, 0, c, s:s + 1], bias=coef[:, stg, 1, c, s:s + 1]),
                    r=srcbufs + [COEF], w=[Ub[c][tt]])
                if src_psum:
                    P.op("dve", lambda e, cs=cs, s=s: e.tensor_scalar(
                        out=rst[:, cs], in0=src[:, cs], scalar1=coef[:, stg, 2, c, s:s + 1],
                        scalar2=coef[:, stg, 3, c, s:s + 1], op0=ALU.mult, op1=ALU.add),
                        r=srcbufs + [COEF], w=[rstb])
                else:
                    P.op("act", lambda e, cs=cs, s=s: e.activation(
                        out=rst[:, cs], in_=src[:, cs], func=AF.Identity, scale=coef[:, stg, 2, c, s:s + 1],
                        bias=coef[:, stg, 3, c, s:s + 1]), r=srcbufs + [COEF], w=[rstb])
            P.op("sp", lambda e: e.dma_start(out=rscr[:, c, t0:t0 + n], in_=rst[:, 0:n]), r=[rstb], w=[RS[c][tt]],
                 dma=rstb)

        xst, xstb = self.carve("xst", [128, 4, 1024], F32, 4)
        XIN = KSTAGE > 0
        rst, rstb = self.carve("rst", [128, 4, 512], F32, 4)
        ri = 0
        for tt in (range(5) if XIN else []):
            t0, n = TT[tt]
            if tt < 4:
                for tb in range(4):
                    P.op("sp", lambda e, tb=tb, t0=t0: e.dma_start(out=xst[:, tb, :], in_=xp[t0 + 128 * tb:t0 + 128 * tb + 128, :]),
                         w=[xstb[tb]], dma=xstb[tb])
            else:
                P.op("sp", lambda e: e.dma_start(out=xst[0:16, 0, :], in_=xs), w=[xstb[0]], dma=xstb[0])
            for c in range(8):
                pt, ptb = self.bank("acc")
                if tt < 4:
                    def fn_xt(e, pt=pt, c=c):
                        ins = None
                        for tb in range(4):
                            ins = e.transpose(pt[:, 128 * tb:128 * tb + 128], xst[:, tb, 128 * c:128 * c + 128], ident[:])
                        return ins
                    P.op("pe", fn_xt, r=xstb + [CONST], w=[ptb])
                else:
                    P.op("pe", lambda e, pt=pt, c=c: e.transpose(pt[:, 0:16], xst[0:16, 0, 128 * c:128 * c + 128],
                                                                 ident[0:16, 0:16]), r=[xstb[0], CONST], w=[ptb])
                produce(pt, [ptb], c, tt, 0, rst[:, ri % 4, :], rstb[ri % 4], src_psum=True)
                ri += 1

        try:
            self.stage(1)
            for l in range(L):
                self.layer(l, locals())
        except StopBuild:
            pass

        self.emit_d2d(1000)
        last = []
        allops = [o for e in ENGS for o in P.ops[e] if o.dsem is not None]
        lastd = {}
        for o in allops:
            q = lastd.get(id(o.dsem))
            if q is None or q.sigval < o.sigval:
                lastd[id(o.dsem)] = o
        fin = Op()
        fin.eng = "sp"
        fin.fn = None
        fin.sig = False
        fin.sigidx = 0
        fin.dsem = None
        fin.sigval = 0
        fin.seq = P.seq + 1
        fin.deps = {("d", k): o for k, o in lastd.items()}
        P.ops["sp"].append(fin)

        P.finalize()
        with nc.Block() as block:
            @block.sync
            def _(e):
                P.emit("sp", e)

            @block.scalar
            def _(e):
                P.emit("act", e)

            @block.gpsimd
            def _(e):
                P.emit("pool", e)

            @block.vector
            def _(e):
                P.emit("dve", e)

            @block.tensor
            def _(e):
                P.emit("pe", e)
        self.stack.close()
        return nc

    def layer(self, l, env):
        P = self.P
        g = env
        U, Ub, ident, CONST, ONES = g["U"], g["Ub"], g["ident"], g["CONST"], g["ONES"]
        ones_bf, onesD, onesC, ones_f, eps_t = g["ones_bf"], g["onesD"], g["onesC"], g["ones_f"], g["eps_t"]
        masks, MASKS, smp, smc, SM, SM2 = g["masks"], g["MASKS"], g["smp"], g["smc"], g["SM"], g["SM2"]
        vecT, VEC, bq8, modT, MOD, coef, COEF = g["vecT"], g["VEC"], g["bq8"], g["modT"], g["MOD"], g["coef"], g["COEF"]
        w_in, w_pc, w_pa, w_out, w_gate, w_up, w_down = (g["w_in"], g["w_pc"], g["w_pa"], g["w_out"], g["w_gate"],
                                                         g["w_up"], g["w_down"])
        ck, stc, kvp, cvp, kvs, cvs, yp, ys, rscr, RS = (g["ck"], g["stc"], g["kvp"], g["cvp"], g["kvs"], g["cvs"],
                                                         g["yp"], g["ys"], g["rscr"], g["RS"])
        produce = g["produce"]
        G1P, G2P = 16, 40

        def vcol(row):
            return vecT[:, l, row:row + 1]

        def ubufs(tt):
            return [Ub[k][tt] for k in range(8)]

        def gemm_u(ps, psb, Wt, Wb, col0, tt, n):
            t0 = TT[tt][0]
            self.mm_acc(ps[:, 0:n], psb, [(Wt[:, k, col0:col0 + 128], U[:, k, t0:t0 + n]) for k in range(8)],
                        [Wb] + ubufs(tt))

        self.new_phase(A0_END)
        cact, cactb = self.carve("cact", [128, 4, NT], BF16, 5)
        off_c = self.rb_offs[1]

        glu, glub = self.carve("glu", [128, 4, 30 + S], BF16, 4)
        sglu, sglub = self.carve("sglu", [128, 4, 4, 34], BF16, 1)
        gluf, glufb = self.carve("gluf", [128, 4, 48], F32, 1)
        ycv, ycvb = self.carve("ycv", [128, 4, NT], F32, 5, area=0)
        dwd, dwdb = self.carve("dwd", [128, 1, 31 * 128], BF16, 1)
        sig, sigb = self.carve("sig", [128, 2, 512], F32, 2)
        ybq, ybqb = self.carve("ybq", [128, 4, 512], BF16, 4)
        stt, sttb = self.carve("stt", [128, 4, 512], F32, 4)
        tmpa, tmpab = self.carve("tmpa", [128, 2, 512], F32, 2)
        sst, sstb = self.carve("sst", [128, 4, 512], F32, 1)
        cvo, cvob = self.carve("cvo", [128, 2, 512], F32, 2)

        for c in range(4):
            P.op("dve", lambda e, c=c: e.memset(glu[:, c, 0:30], 0.0), w=[glub[c]])
        for s in range(4):
            P.op("sp", lambda e, s=s: e.dma_start(out=sst[0:30, s, :], in_=stc[l, s, :, :]), w=sstb, dma=sstb[0])
        for c in range(4):
            pt, ptb = self.bank("misc")

            def fn_st(e, pt=pt, c=c):
                ins = None
                for s in range(4):
                    ins = e.transpose(pt[:, 30 * s:30 * s + 30], sst[0:30, s, 128 * c:128 * c + 128], ident[0:30, 0:30])
                return ins
            P.op("pe", fn_st, r=sstb + [CONST], w=[ptb])
            P.op("act", lambda e, pt=pt, c=c: e.activation(out=sglu[:, c, :, 0:30],
                                                           in_=pt[:, 0:120].rearrange("p (s t) -> p s t", s=4),
                                                           func=AF.Copy), r=[ptb], w=sglub)
        WA, WAb = self.wload(w_in[l, :, 0:512], 8, 512)
        WG, WGb = self.wload(w_in[l, :, 512:1024], 8, 512)
        self.si = 0

        def glu_chunk(c):
            for tt in range(5):
                t0, n = TT[tt]
                pa, pab = self.bank("acc")
                gemm_u(pa, pab, WA, WAb, 128 * c, tt, n)
                pg, pgb = self.bank("acc")
                gemm_u(pg, pgb, WG, WGb, 128 * c, tt, n)
                sg = sig[:, self.si % 2, :]
                sgb = sigb[self.si % 2]
                self.si += 1
                P.op("act", lambda e: e.activation(out=sg[:, 0:n], in_=pg[:, 0:n], func=AF.Sigmoid,
                                                   bias=vcol(V_BIN + 4 + c), scale=1.0), r=[pgb, VEC], w=[sgb])
                if tt < 4:
                    P.op("dve", lambda e: e.scalar_tensor_tensor(
                        out=glu[:, c, 30 + t0:30 + t0 + 512], in0=pa[:, 0:512], scalar=vcol(V_BIN + c), in1=sg[:, 0:512],
                        op0=ALU.add, op1=ALU.mult), r=[pab, sgb, VEC], w=[glub[c]])
                    if tt == 3:
                        P.op("dve", lambda e: e.scalar_tensor_tensor(
                            out=gluf[:, c, 0:32], in0=pa[:, 480:512], scalar=vcol(V_BIN + c), in1=sg[:, 480:512],
                            op0=ALU.add, op1=ALU.mult), r=[pab, sgb, VEC], w=glufb)
                else:
                    P.op("dve", lambda e: e.scalar_tensor_tensor(
                        out=gluf[:, c, 32:48], in0=pa[:, 0:16], scalar=vcol(V_BIN + c), in1=sg[:, 0:16],
                        op0=ALU.add, op1=ALU.mult), r=[pab, sgb, VEC], w=glufb)
                    P.op("dve", lambda e: e.tensor_copy(out=sglu[:, c, :, 30:34],
                                                        in_=gluf[:, c, 32:48].rearrange("p (s t) -> p s t", s=4)),
                         r=glufb, w=sglub)

        def conv_state_outputs():
            pt, ptb = self.bank("misc")

            def fn_cs(e, pt=pt):
                ins = None
                for c in range(4):
                    ins = e.transpose(pt[0:30, 128 * c:128 * c + 128], gluf[:, c, 2:32], ident[:])
                return ins
            P.op("pe", fn_cs, r=glufb + [CONST], w=[ptb])
            P.op("act", lambda e, pt=pt: e.activation(out=cvo[0:30, 0, :], in_=pt[0:30, :], func=AF.Copy), r=[ptb], w=[cvob[0]])
            P.op("sp", lambda e: e.dma_start(out=cvp[l, :, :], in_=cvo[0:30, 0, :]), r=[cvob[0]], w=[Buf("x")], dma=cvob[0])
            pt, ptb = self.bank("misc")

            def fn_cs2(e, pt=pt):
                ins = None
                for c in range(4):
                    ins = e.transpose(pt[0:16, 128 * c:128 * c + 128], gluf[:, c, 32:48], ident[:])
                return ins
            P.op("pe", fn_cs2, r=glufb + [CONST], w=[ptb])
            P.op("act", lambda e, pt=pt: e.activation(out=cvo[0:16, 1, :], in_=pt[0:16, :], func=AF.Copy), r=[ptb], w=[cvob[1]])
            for s in range(4):
                P.op("sp", lambda e, s=s: e.dma_start(out=cvs[l, s, 26:30, :], in_=cvo[4 * s:4 * s + 4, 1, :]),
                     r=[cvob[1]], w=[Buf("x")], dma=cvob[1])

        glu_chunk(0)
        glu_chunk(1)
        self.yi = 0

        def ln_tile(tt):
            t0, n = TT[tt]
            pm, pmb = self.bank("o")
            pq, pqb = self.bank("z")
            for c in range(4):
                yb_ = ybq[:, self.yi % 4, :]
                ybb = ybqb[self.yi % 4]
                self.yi += 1
                ys_ = ybq[:, self.yi % 4, :]
                ysb = ybqb[self.yi % 4]
                self.yi += 1
                P.op("dve", lambda e, yb_=yb_, c=c: e.tensor_copy(out=yb_[:, 0:n], in_=ycv[:, c, t0:t0 + n]),
                     r=[ycvb[tt]], w=[ybb])
                P.op("act", lambda e, ys_=ys_, c=c: e.activation(out=ys_[:, 0:n], in_=ycv[:, c, t0:t0 + n],
                                                                func=AF.Square), r=[ycvb[tt]], w=[ysb])
                P.op("pe", lambda e, yb_=yb_, c=c: e.matmul(pm[:, 0:n], lhsT=onesC[:], rhs=yb_[:, 0:n],
                                                           start=(c == 0), stop=(c == 3)), r=[ybb, ONES], w=[pmb])
                P.op("pe", lambda e, ys_=ys_, c=c: e.matmul(pq[:, 0:n], lhsT=onesC[:], rhs=ys_[:, 0:n],
                                                           start=(c == 0), stop=(c == 3)), r=[ysb, ONES], w=[pqb])
            mean, rstd = self.ln_stats(pm, pmb, pq, pqb, stt, sttb, n, eps_t, ONES)
            for c in range(4):
                ta = tmpa[:, c % 2, :]
                tab = tmpab[c % 2]
                P.op("dve", lambda e, ta=ta, c=c: e.tensor_tensor(out=ta[:, 0:n], in0=ycv[:, c, t0:t0 + n],
                                                                 in1=mean[:, 0:n], op=ALU.subtract),
                     r=[ycvb[tt], sttb[0]], w=[tab])
                P.op("dve", lambda e, ta=ta: e.tensor_tensor(out=ta[:, 0:n], in0=ta[:, 0:n], in1=rstd[:, 0:n],
                                                            op=ALU.mult), r=[tab, sttb[2]], w=[tab])
                P.op("act", lambda e, ta=ta, c=c: e.activation(out=cact[:, c, t0:t0 + n], in_=ta[:, 0:n],
                                                              func=AF.Silu, scale=vcol(V_CLG + c), bias=vcol(V_CLB + c)),
                     r=[tab, VEC], w=[cactb[tt]])

        def conv_chunk(c, with_ln):
            dw = dwd[:, 0, :].rearrange("p (k n) -> p k n", k=31)
            dwb = dwdb[0]
            P.op("dve", lambda e: e.tensor_tensor(
                out=dw, in0=ident[:].unsqueeze(1).to_broadcast([128, 31, 128]),
                in1=vecT[:, l, V_WDW + c:V_WDW + 124:4].unsqueeze(2).to_broadcast([128, 31, 128]), op=ALU.mult),
                r=[CONST, VEC], w=[dwb])
            for tt in range(5):
                t0, n = TT[tt]
                pc, pcb = self.bank("acc")
                if tt < 4:
                    pairs = [(dw[:, k, :], glu[:, c, t0 + k:t0 + k + 512]) for k in range(31)]
                    self.mm_acc(pc[:, 0:512], pcb, pairs, [dwb, glub[c]])
                else:
                    pairs = [(dw[:, k, :], sglu[:, c, :, k:k + 4]) for k in range(31)]
                    self.mm_acc(pc[:, 0:16], pcb, pairs, [dwb] + sglub)
                P.op("act", lambda e, pc=pc, t0=t0, n=n: e.activation(out=ycv[:, c, t0:t0 + n], in_=pc[:, 0:n],
                                                                     func=AF.Identity, bias=vcol(V_BDW + c), scale=1.0),
                     r=[pcb, VEC], w=[ycvb[tt]])
                if with_ln:
                    ln_tile(tt)

        conv_chunk(0, False)
        glu_chunk(2)
        conv_chunk(1, False)
        glu_chunk(3)
        conv_state_outputs()
        conv_chunk(2, False)
        conv_chunk(3, True)

        self.stage(2 + 10 * l)
        self.new_phase(off_c)
        oatt, oattb = self.carve("oatt", [128, 4, NT], BF16, 1)
        off_co = self.rb_offs[1]
        Qt, Qtb = self.carve("Qt", [128, NT], BF16, 5)
        Kt, Ktb = self.carve("Kt", [128, NT], BF16, 5)
        Ktf, Ktfb = self.carve("Ktf", [128, NT], F32, 5, area=0)
        Vtf, Vtfb = self.carve("Vtf", [128, NT], F32, 5, area=0)
        Vb, Vbb = self.carve("Vb", [128, 16, 128], BF16, 4)
        Oacc, Oaccb = self.carve("Oacc", [128, NT], F32, 1, area=0)
        Zacc, Zaccb = self.carve("Zacc", [128, NT], F32, 1, area=0)
        Eb, Ebb = self.carve("Eb", [128, 2, 512], BF16, 2)
        Pt, Ptb = self.carve("Pt", [128, 4, 512], BF16, 4)
        kvst, kvstb = self.carve("kvst", [128, 4, 512], F32, 4)
        sKc, sKcb = self.carve("sKc", [128, 2, 4, 2, 128], F32, 2)
        sKt, sKtb = self.carve("sKt", [128, 2, 512], BF16, 2)
        sNew, sNewb = self.carve("sNew", [128, 4, 2, 128], F32, 1)
        sE, sEb = self.carve("sE", [128, 2, 32], F32, 2)
        sPp, sPpb = self.carve("sPp", [128, 32], BF16, 1)
        sPc, sPcb = self.carve("sPc", [128, 32], BF16, 1)
        sVb, sVbb = self.carve("sVb", [128, 2, 4, 128], BF16, 2)
        sNb, sNbb = self.carve("sNb", [128, 4, 128], BF16, 1)
        Qf, Qfb = self.carve("Qf", [128, 16], F32, 1)
        Ktfs, Ktfsb = self.carve("Ktfs", [128, 16], F32, 1)

        self.ei = 0
        self.pi = 0
        ki = 0
        import os
        for hp in range(int(os.environ.get('HPN', '4'))):
            for gi in range(int(os.environ.get('GIN', '3'))):
                d = DIL[gi]
                nblk = S // d // 128
                Wn = WIN[gi]
                qcol = 1024 + (0 + gi) * 512 + hp * 128
                kcol = 1024 + (3 + gi) * 512 + hp * 128
                vcol_ = 1024 + (6 + gi) * 512 + hp * 128
                self.emit_d2d(1)
                WQ, WQb = self.wload(w_in[l, :, qcol:qcol + 128], 8, 128)
                WK, WKb = self.wload(w_in[l, :, kcol:kcol + 128], 8, 128)
                WV, WVb = self.wload(w_in[l, :, vcol_:vcol_ + 128], 8, 128)
                for tt in range(5):
                    t0, n = TT[tt]
                    pq, pqb = self.bank("acc")
                    gemm_u(pq, pqb, WQ, WQb, 0, tt, n)
                    P.op("act", lambda e, pq=pq, t0=t0, n=n, gi=gi, hp=hp: e.activation(
                        out=Qt[:, t0:t0 + n], in_=pq[:, 0:n], func=AF.Identity, scale=0.125,
                        bias=bq8[:, l, gi * 4 + hp:gi * 4 + hp + 1]), r=[pqb, VEC], w=[Qtb[tt]])
                    pk, pkb = self.bank("acc")
                    gemm_u(pk, pkb, WK, WKb, 0, tt, n)
                    P.op("act", lambda e, pk=pk, t0=t0, n=n, kcol=kcol: e.activation(
                        out=Ktf[:, t0:t0 + n], in_=pk[:, 0:n], func=AF.Identity, scale=1.0, bias=vcol(kcol // 128)),
                        r=[pkb, VEC], w=[Ktfb[tt]])
                    P.op("dve", lambda e, pk=pk, t0=t0, n=n, kcol=kcol: e.tensor_scalar(
                        out=Kt[:, t0:t0 + n], in0=pk[:, 0:n], scalar1=vcol(kcol // 128), scalar2=None, op0=ALU.add),
                        r=[pkb, VEC], w=[Ktb[tt]])
                    pv, pvb = self.bank("acc")
                    gemm_u(pv, pvb, WV, WVb, 0, tt, n)
                    P.op("act", lambda e, pv=pv, t0=t0, n=n, vc=vcol_: e.activation(
                        out=Vtf[:, t0:t0 + n], in_=pv[:, 0:n], func=AF.Identity, scale=1.0, bias=vcol(vc // 128)),
                        r=[pvb, VEC], w=[Vtfb[tt]])

                def blk_cols(r, c):
                    a = r + d * 128 * c
                    return a, a + d * 127 + 1, d

                def blk_rc(b):
                    return b // nblk, b % nblk

                for bg in range(4):
                    for kv, src, srcb in ((1, Vtf, Vtfb), (0, Ktf, Ktfb)):
                        blks = [4 * bg + j for j in range(4)]
                        inwin = [(blk_rc(b)[0] + d * 128 * blk_rc(b)[1]) >= S - Wn for b in blks]
                        if kv == 0 and not any(inwin):
                            continue
                        pt, ptb = self.bank("tr")

                        def fn_tr(e, pt=pt, src=src, blks=blks):
                            ins = None
                            for j, b in enumerate(blks):
                                r_, c_ = blk_rc(b)
                                a, z, stp = blk_cols(r_, c_)
                                ins = e.transpose(pt[:, 128 * j:128 * j + 128], src[:, a:z:stp], ident[:])
                            return ins
                        P.op("pe", fn_tr, r=list(srcb[0:4]) + [CONST], w=[ptb])
                        if kv == 1:
                            P.op("dve", lambda e, pt=pt, bg=bg: e.tensor_copy(
                                out=Vb[:, 4 * bg:4 * bg + 4, :], in_=pt[:, :].rearrange("p (a b) -> p a b", a=4)),
                                r=[ptb], w=[Vbb[bg]])
                        if any(inwin):
                            kst = kvst[:, ki % 4, :]
                            kstb = kvstb[ki % 4]
                            ki += 1
                            P.op("act", lambda e, pt=pt, kst=kst: e.activation(out=kst, in_=pt[:, :], func=AF.Copy),
                                 r=[ptb], w=[kstb])
                            for j, b in enumerate(blks):
                                if not inwin[j]:
                                    continue
                                r_, c_ = blk_rc(b)
                                a = r_ + d * 128 * c_ - (S - Wn)
                                dst = kvp[gi][l, a:a + 127 * d + 1:d, kv, hp * 128:hp * 128 + 128]
                                P.op("sp", lambda e, dst=dst, kst=kst, j=j: e.dma_start(out=dst, in_=kst[:, 128 * j:128 * j + 128]),
                                     r=[kstb], w=[Buf("x")], dma=kstb)
                for half in range(2):
                    pt, ptb = self.bank("tr")

                    def fn_sn(e, pt=pt, half=half):
                        ins = None
                        for s in (2 * half, 2 * half + 1):
                            for kv, src in ((0, Ktf), (1, Vtf)):
                                o = ((s % 2) * 2 + kv) * 128
                                ins = e.transpose(pt[0:4, o:o + 128], src[:, S + 4 * s:S + 4 * s + 4], ident[:])
                        return ins
                    P.op("pe", fn_sn, r=[Ktfb[4], Vtfb[4], CONST], w=[ptb])
                    P.op("act", lambda e, pt=pt, half=half: e.activation(
                        out=sNew[0:4, 2 * half:2 * half + 2, :, :],
                        in_=pt[0:4, :].rearrange("p (s k f) -> p s k f", s=2, k=2), func=AF.Copy), r=[ptb], w=sNewb)
                P.op("act", lambda e: e.activation(out=sNb[0:4, :, :], in_=sNew[0:4, :, 1, :], func=AF.Copy), r=sNewb, w=sNbb)
                for s in range(4):
                    dst = kvs[gi][l, s, Wn - 4:Wn, :].rearrange("t (k f) -> t k f", k=2)[:, :, hp * 128:hp * 128 + 128]
                    P.op("sp", lambda e, dst=dst, s=s: e.dma_start(out=dst, in_=sNew[0:4, s, :, :]), r=sNewb, w=[Buf("x")],
                         dma=sNewb[0])

                mk = masks[:, gi * 4 + hp, :]
                first = (gi == 0)
                gidx = gi * 4 + hp
                nt_ = 1 if gi == 0 else 4

                def q_group(qg):
                    if gi == 0:
                        return [(0, 4 * qg + j) for j in range(4)], (lambda T: T[:, 512 * qg:512 * qg + 512])
                    if gi == 1:
                        return [(qg, j) for j in range(4)], (lambda T: T[:, qg:S:4])
                    return ([(4 * qg + j, 0) for j in range(4)],
                            (lambda T: T[:, 0:S].rearrange("p (i r) -> p r i", r=16)[:, 4 * qg:4 * qg + 4, :]))

                def emit_S(qg, j, r_, c_, po, pob, pz, pzb):
                    hasp = c_ > 0
                    a, z, stp = blk_cols(r_, c_)
                    psA, psAb = self.bank("sS")
                    psB, psBb = self.bank("sS")

                    def fn_s(e):
                        ins = None
                        for hh, ps_ in ((0, psA), (1, psB)):
                            ph = slice(64 * hh, 64 * hh + 64)
                            ins = e.matmul(ps_[:, 0:128], lhsT=Kt[ph, a:z:stp], rhs=Qt[ph, a:z:stp], start=True, stop=True)
                            if hasp:
                                a2, z2, _ = blk_cols(r_, c_ - 1)
                                ins = e.matmul(ps_[:, 128:256], lhsT=Kt[ph, a2:z2:stp], rhs=Qt[ph, a:z:stp],
                                               start=True, stop=True)
                        return ins
                    P.op("pe", fn_s, r=list(Qtb[0:4]) + list(Ktb[0:4]), w=[psAb, psBb])
                    E_ = Eb[:, self.ei % 2, :]
                    Eb_ = Ebb[self.ei % 2]
                    self.ei += 1
                    P_ = Pt[:, self.pi % 4, :]
                    Pb_ = Ptb[self.pi % 4]
                    self.pi += 1
                    nv = 256 if hasp else 128
                    P.op("act", lambda e: e.activation(out=E_[:, 0:nv], in_=psA[:, 0:nv], func=AF.Exp), r=[psAb], w=[Eb_])
                    P.op("act", lambda e: e.activation(out=E_[:, 256:256 + nv], in_=psB[:, 0:nv], func=AF.Exp),
                         r=[psBb], w=[Eb_])
                    if hasp:
                        P.op("dve", lambda e: e.tensor_tensor(out=P_, in0=E_, in1=mk, op=ALU.mult), r=[Eb_, MASKS], w=[Pb_])
                    else:
                        v3 = lambda T: T.rearrange("p (a b) -> p a b", a=2)[:, :, 0:128]
                        P.op("dve", lambda e: e.tensor_tensor(out=v3(P_), in0=v3(E_), in1=v3(mk), op=ALU.mult),
                             r=[Eb_, MASKS], w=[Pb_])
                    return (qg, j, hasp, r_ * nblk + c_, P_, Pb_, po, pob, pz, pzb)

                def emit_PV(st):
                    qg, j, hasp, bcur, P_, Pb_, po, pob, pz, pzb = st

                    def fn_pv(e):
                        ins = None
                        for hh in range(2):
                            ph = slice(64 * hh, 64 * hh + 64)
                            oc = slice(128 * j, 128 * j + 128)
                            ins = e.matmul(po[ph, oc], lhsT=Vb[:, bcur, ph], rhs=P_[:, 256 * hh:256 * hh + 128],
                                           start=True, stop=not hasp)
                            if hasp:
                                ins = e.matmul(po[ph, oc], lhsT=Vb[:, bcur - 1, ph],
                                               rhs=P_[:, 256 * hh + 128:256 * hh + 256], start=False, stop=True)
                            ins = e.matmul(pz[ph, oc], lhsT=ones_bf[:, 0:64], rhs=P_[:, 256 * hh:256 * hh + 128],
                                           start=True, stop=not hasp)
                            if hasp:
                                ins = e.matmul(pz[ph, oc], lhsT=ones_bf[:, 0:64],
                                               rhs=P_[:, 256 * hh + 128:256 * hh + 256], start=False, stop=True)
                        return ins
                    P.op("pe", fn_pv, r=[Pb_, ONES] + Vbb, w=[pob, pzb])
                    if j == 3:
                        _, acc_sl = q_group(qg)
                        pv3 = (lambda T: T[:, :]) if gi < 2 else (lambda T: T[:, :].rearrange("p (r i) -> p r i", r=4))
                        if first:
                            P.op("dve", lambda e: e.tensor_copy(out=acc_sl(Oacc), in_=pv3(po)), r=[pob], w=Oaccb)
                            P.op("act", lambda e: e.activation(out=acc_sl(Zacc), in_=pv3(pz), func=AF.Copy), r=[pzb], w=Zaccb)
                        else:
                            P.op("dve", lambda e: e.tensor_tensor(out=acc_sl(Oacc), in0=pv3(po), in1=acc_sl(Oacc), op=ALU.add),
                                 r=[pob] + Oaccb, w=Oaccb)
                            P.op("dve", lambda e: e.tensor_tensor(out=acc_sl(Zacc), in0=pv3(pz), in1=acc_sl(Zacc), op=ALU.add),
                                 r=[pzb] + Zaccb, w=Zaccb)

                pso_full, psob = self.ps[7], self.psb[7]
                sstate = {}

                def s_st1(s):
                    sl = (s % 2)
                    kc_ = sKc[:, sl, :, :, :]
                    kcb = sKcb[sl]
                    for kv in range(2):
                        if gi == 0:
                            src = ck[gi][l, s, :, :].rearrange("j (k f) -> j k f", k=2)[:, kv, hp * 128:hp * 128 + 128]
                            P.op("sp", lambda e, src=src, kv=kv: e.dma_start(out=kc_[:, 0, kv, :], in_=src), w=[kcb], dma=kcb)
                        else:
                            src = ck[gi][l, s, :, :].rearrange("(j t) (k f) -> j t k f", t=d, k=2)[:, 0:4, kv,
                                                                                                 hp * 128:hp * 128 + 128]
                            P.op("sp", lambda e, src=src, kv=kv: e.dma_start(out=kc_[:, :, kv, :], in_=src), w=[kcb], dma=kcb)
                    pt, ptb = self.bank("sS")

                    def fn_kt(e):
                        ins = None
                        for t in range(nt_):
                            ins = e.transpose(pt[:, 128 * t:128 * t + 128], kc_[:, t, 0, :], ident[:])
                        return ins
                    P.op("pe", fn_kt, r=[kcb, CONST], w=[ptb])
                    kt_ = sKt[:, sl, :]
                    ktb_ = sKtb[sl]
                    P.op("dve", lambda e: e.tensor_copy(out=kt_[:, 0:128 * nt_], in_=pt[:, 0:128 * nt_]), r=[ptb], w=[ktb_])
                    vb_ = sVb[:, sl, :, :]
                    vbb_ = sVbb[sl]
                    P.op("act", lambda e: e.activation(out=vb_[:, 0:nt_, :], in_=kc_[:, 0:nt_, 1, :], func=AF.Copy), r=[kcb], w=[vbb_])
                    sstate[s] = (kc_, kcb, kt_, ktb_, vb_, vbb_)

                def s_st2(s):
                    kc_, kcb, kt_, ktb_, vb_, vbb_ = sstate[s]
                    pss0, pss0b = self.bank("sS")
                    pss1, pss1b = self.bank("sS")

                    def fn_ss(e):
                        ins = None
                        for hh, pss in ((0, pss0), (1, pss1)):
                            ph = slice(64 * hh, 64 * hh + 64)
                            for t in range(4):
                                tk = 0 if nt_ == 1 else t
                                ins = e.matmul(pss[:, t:t + 1], lhsT=kt_[ph, 128 * tk:128 * tk + 128],
                                               rhs=Qt[ph, S + 4 * s + t:S + 4 * s + t + 1], start=True, stop=True)
                            ins = e.matmul(pss[0:4, 16:20], lhsT=Kt[ph, S + 4 * s:S + 4 * s + 4],
                                           rhs=Qt[ph, S + 4 * s:S + 4 * s + 4], start=True, stop=True)
                        return ins
                    P.op("pe", fn_ss, r=[ktb_, Qtb[4], Ktb[4]], w=[pss0b, pss1b])
                    cs_ = slice(8 * s, 8 * s + 8)
                    for hh, pss, pssb in ((0, pss0, pss0b), (1, pss1, pss1b)):
                        c4 = slice(8 * s + 4 * hh, 8 * s + 4 * hh + 4)
                        P.op("act", lambda e, pss=pss, c4=c4: e.activation(out=sE[:, 0, c4], in_=pss[:, 0:4], func=AF.Exp),
                             r=[pssb], w=[sEb[0]])
                        P.op("act", lambda e, pss=pss, c4=c4: e.activation(out=sE[0:4, 1, c4], in_=pss[0:4, 16:20], func=AF.Exp),
                             r=[pssb], w=[sEb[1]])
                    P.op("dve", lambda e: e.tensor_tensor(out=sPp[:, cs_], in0=sE[:, 0, cs_], in1=smp[:, gidx, cs_], op=ALU.mult),
                         r=[sEb[0], SM], w=sPpb)
                    P.op("dve", lambda e: e.tensor_tensor(out=sPc[0:4, cs_], in0=sE[0:4, 1, cs_], in1=smc[0:4, gidx, cs_],
                                                          op=ALU.mult), r=[sEb[1], SM2], w=sPcb)

                def s_st3(s):
                    kc_, kcb, kt_, ktb_, vb_, vbb_ = sstate[s]

                    def fn_so(e):
                        ins = None
                        for hh in range(2):
                            ph = slice(64 * hh, 64 * hh + 64)
                            c0_ = s * 8 + hh * 4
                            o4 = slice(4 * s, 4 * s + 4)
                            z4 = slice(16 + 4 * s, 16 + 4 * s + 4)
                            if nt_ == 1:
                                ins = e.matmul(pso_full[ph, o4], lhsT=vb_[:, 0, ph], rhs=sPp[:, c0_:c0_ + 4], start=True, stop=False)
                            else:
                                for t in range(4):
                                    ins = e.matmul(pso_full[ph, 4 * s + t:4 * s + t + 1], lhsT=vb_[:, t, ph],
                                                   rhs=sPp[:, c0_ + t:c0_ + t + 1], start=(t == 0), stop=False,
                                                   skip_group_check=True)
                            ins = e.matmul(pso_full[ph, o4], lhsT=sNb[0:4, s, ph], rhs=sPc[0:4, c0_:c0_ + 4], start=False, stop=True,
                                           skip_group_check=True)
                            ins = e.matmul(pso_full[ph, z4], lhsT=ones_bf[:, 0:64], rhs=sPp[:, c0_:c0_ + 4], start=True, stop=False)
                            ins = e.matmul(pso_full[ph, z4], lhsT=ones_bf[0:4, 0:64], rhs=sPc[0:4, c0_:c0_ + 4], start=False, stop=True)
                        return ins
                    P.op("pe", fn_so, r=[vbb_, ONES] + sPpb + sPcb + sNbb, w=[psob])

                def s_acc(_):
                    if first:
                        P.op("dve", lambda e: e.tensor_copy(out=Oacc[:, S:NT], in_=pso_full[:, 0:16]), r=[psob], w=Oaccb)
                        P.op("dve", lambda e: e.tensor_copy(out=Zacc[:, S:NT], in_=pso_full[:, 16:32]), r=[psob], w=Zaccb)
                    else:
                        P.op("dve", lambda e: e.tensor_tensor(out=Oacc[:, S:NT], in0=pso_full[:, 0:16], in1=Oacc[:, S:NT],
                                                              op=ALU.add), r=[psob] + Oaccb, w=Oaccb)
                        P.op("dve", lambda e: e.tensor_tensor(out=Zacc[:, S:NT], in0=pso_full[:, 16:32], in1=Zacc[:, S:NT],
                                                              op=ALU.add), r=[psob] + Zaccb, w=Zaccb)

                ssteps = [(s_st1, 0), (s_st1, 1), (s_st2, 0), (s_st3, 0), (s_st1, 2), (s_st2, 1), (s_st3, 1), (s_st1, 3),
                          (s_st2, 2), (s_st3, 2), (s_st2, 3), (s_st3, 3), (s_acc, 0)]
                PIPE = 2
                pending = []
                qi = 0
                for qg in range(4):
                    qbl, _ = q_group(qg)
                    po, pob = self.bank("o")
                    pz, pzb = self.bank("z")
                    for j, (r_, c_) in enumerate(qbl):
                        pending.append(emit_S(qg, j, r_, c_, po, pob, pz, pzb))
                        if len(pending) > PIPE:
                            emit_PV(pending.pop(0))
                        if qi < len(ssteps):
                            f_, a_ = ssteps[qi]
                            f_(a_)
                        qi += 1
                while pending:
                    emit_PV(pending.pop(0))
                for f_, a_ in ssteps[qi:]:
                    f_(a_)
            P.op("dve", lambda e: e.reciprocal(out=Zacc[:, :], in_=Zacc[:, :]), r=Zaccb, w=Zaccb)
            P.op("dve", lambda e, hp=hp: e.tensor_tensor(out=oatt[:, hp, :], in0=Oacc[:, :], in1=Zacc[:, :], op=ALU.mult),
                 r=Zaccb + Oaccb, w=oattb)

        self.stage(3 + 10 * l)
        self.new_phase(off_co)
        m, mb = self.carve("m", [128, 8, NT], BF16, 5, area=0)
        sgc, sgcb = self.carve("sgc", [128, 2, 512], F32, 2)
        sga, sgab = self.carve("sga", [128, 2, 512], F32, 2)
        t1, t1b = self.carve("t1", [128, 2, 512], F32, 2)
        t2, t2b = self.carve("t2", [128, 2, 512], F32, 2)
        mi = 0
        for og in range(2):
            for half in range(2):
                Wpc, Wpcb = self.wload(w_pc[l, :, 512 * og:512 * og + 512], 4, 512)
                Wpa, Wpab = self.wload(w_pa[l, :, 512 * og:512 * og + 512], 4, 512)
                cbase = 4 * og + 2 * half
                c0 = 1024 + 4608 + 128 * cbase
                Wgc, Wgcb = self.wload(w_in[l, :, c0:c0 + 256], 8, 256)
                Wga, Wgab = self.wload(w_in[l, :, c0 + 1024:c0 + 1024 + 256], 8, 256)
                for cc in range(2):
                    c = cbase + cc
                    for tt in range(5):
                        t0, n = TT[tt]
                        pyc, pycb = self.bank("all")
                        self.mm_acc(pyc[:, 0:n], pycb, [(Wpc[:, k, 128 * (c % 4):128 * (c % 4) + 128], cact[:, k, t0:t0 + n])
                                                         for k in range(4)], [Wpcb, cactb[tt]])
                        pgc, pgcb = self.bank("all")
                        gemm_u(pgc, pgcb, Wgc, Wgcb, 128 * cc, tt, n)
                        pya, pyab = self.bank("all")
                        self.mm_acc(pya[:, 0:n], pyab, [(Wpa[:, k, 128 * (c % 4):128 * (c % 4) + 128], oatt[:, k, t0:t0 + n])
                                                         for k in range(4)], [Wpab] + oattb)
                        pga, pgab = self.bank("all")
                        gemm_u(pga, pgab, Wga, Wgab, 128 * cc, tt, n)
                        i2 = mi % 2
                        mi += 1
                        P.op("act", lambda e, pgc=pgc, i2=i2, c=c, n=n: e.activation(
                            out=sgc[:, i2, 0:n], in_=pgc[:, 0:n], func=AF.Sigmoid, bias=vcol(V_BIN + 44 + c), scale=1.0),
                            r=[pgcb, VEC], w=[sgcb[i2]])
                        P.op("act", lambda e, pga=pga, i2=i2, c=c, n=n: e.activation(
                            out=sga[:, i2, 0:n], in_=pga[:, 0:n], func=AF.Sigmoid, bias=vcol(V_BIN + 52 + c), scale=1.0),
                            r=[pgab, VEC], w=[sgab[i2]])
                        P.op("dve", lambda e, pyc=pyc, i2=i2, c=c, n=n: e.scalar_tensor_tensor(
                            out=t1[:, i2, 0:n], in0=pyc[:, 0:n], scalar=vcol(V_BPC + c), in1=sgc[:, i2, 0:n],
                            op0=ALU.add, op1=ALU.mult), r=[pycb, sgcb[i2], VEC], w=[t1b[i2]])
                        P.op("dve", lambda e, pya=pya, i2=i2, c=c, n=n: e.scalar_tensor_tensor(
                            out=t2[:, i2, 0:n], in0=pya[:, 0:n], scalar=vcol(V_BPA + c), in1=sga[:, i2, 0:n],
                            op0=ALU.add, op1=ALU.mult), r=[pyab, sgab[i2], VEC], w=[t2b[i2]])
                        P.op("dve", lambda e, i2=i2, c=c, t0=t0, n=n: e.tensor_tensor(
                            out=m[:, c, t0:t0 + n], in0=t1[:, i2, 0:n], in1=t2[:, i2, 0:n], op=ALU.add),
                            r=[t1b[i2], t2b[i2]], w=[mb[tt]])
        self.stage(4 + 10 * l)
        self.new_phase(A0_END, A0_END, A0_END)
        self.ln_block(l, 1 + 2 * l, lambda c: w_out[l, :, 128 * c:128 * c + 128], 8, m, mb, G1P, final=False, env=g,
                      tiles=range(5))

        self.stage(5 + 10 * l)
        for half, tiles in enumerate(([0, 1], [2, 3, 4])):
            self.new_phase(0, 0, 0)
            c0 = TT[tiles[0]][0]
            ncol = sum(TT[t][1] for t in tiles)
            h, hb = self.carve("h", [128, NF, ncol], BF16, len(tiles))
            sgt, sgtb = self.carve("sgt", [128, 2, 512], F32, 2)
            hi = 0
            for fg in range(11):
                Wg_, Wgb_ = self.wload(w_gate[l, :, 256 * fg:256 * fg + 256], 8, 256)
                Wu_, Wub_ = self.wload(w_up[l, :, 256 * fg:256 * fg + 256], 8, 256)
                for ff in range(2):
                    f = 2 * fg + ff
                    for ti, tt in enumerate(tiles):
                        t0, n = TT[tt]
                        pgt, pgtb = self.bank("all")
                        gemm_u(pgt, pgtb, Wg_, Wgb_, 128 * ff, tt, n)
                        pup, pupb = self.bank("all")
                        gemm_u(pup, pupb, Wu_, Wub_, 128 * ff, tt, n)
                        i2 = hi % 2
                        hi += 1
                        P.op("act", lambda e, pgt=pgt, i2=i2, n=n: e.activation(out=sgt[:, i2, 0:n], in_=pgt[:, 0:n],
                                                                               func=AF.Silu), r=[pgtb], w=[sgtb[i2]])
                        P.op("dve", lambda e, pup=pup, i2=i2, f=f, t0=t0, n=n: e.tensor_tensor(
                            out=h[:, f, t0 - c0:t0 - c0 + n], in0=pup[:, 0:n], in1=sgt[:, i2, 0:n], op=ALU.mult),
                            r=[pupb, sgtb[i2]], w=[hb[ti]])
            hview = lambda k, t0, n, h=h, c0=c0: h[:, k, t0 - c0:t0 - c0 + n]
            self.ln_block(l, 2 + 2 * l, lambda c: w_down[l, :, 128 * c:128 * c + 128], NF, None, hb, G2P,
                          final=(l == L - 1), env=g, tiles=tiles, hview=hview)

    def ln_stats(self, pm, pmb, pq, pqb, stt, sttb, n, eps_t, ONES):
        P = self.P
        mean = stt[:, 0, :]
        m2 = stt[:, 1, :]
        rstd = stt[:, 2, :]
        P.op("dve", lambda e: e.tensor_copy(out=mean[:, 0:n], in_=pm[:, 0:n]), r=[pmb], w=[sttb[0]])
        P.op("dve", lambda e: e.tensor_tensor(out=m2[:, 0:n], in0=mean[:, 0:n], in1=mean[:, 0:n], op=ALU.mult),
             r=[sttb[0]], w=[sttb[1]])
        P.op("dve", lambda e: e.tensor_tensor(out=m2[:, 0:n], in0=pq[:, 0:n], in1=m2[:, 0:n], op=ALU.subtract),
             r=[pqb, sttb[1]], w=[sttb[1]])
        P.op("act", lambda e: e.activation(out=rstd[:, 0:n], in_=m2[:, 0:n], func=AF.Sqrt, bias=eps_t[:, 0:1], scale=1.0),
             r=[sttb[1], ONES], w=[sttb[2]])
        P.op("dve", lambda e: e.reciprocal(out=rstd[:, 0:n], in_=rstd[:, 0:n]), r=[sttb[2]], w=[sttb[2]])
        return mean, rstd

    def ln_block(self, l, stg, wsrc, kch, xin, xinb, gofs, final, env, tiles, hview=None):
        P = self.P
        g = env
        U, Ub, ident, CONST, ONES = g["U"], g["Ub"], g["ident"], g["CONST"], g["ONES"]
        onesD, eps_t, modT, MOD, coef, COEF = g["onesD"], g["eps_t"], g["modT"], g["MOD"], g["coef"], g["COEF"]
        rscr, RS, yp, ys = g["rscr"], g["RS"], g["yp"], g["ys"]
        produce = g["produce"]
        tiles = list(tiles)
        groups = [[t] for t in tiles]
        zt, ztb = self.carve("zt", [128, 8, 1024], F32, 16)
        rt, rtb = self.carve("rt", [128, 4, 512], F32, 4)
        zbq, zbqb = self.carve("zbq", [128, 4, 512], BF16, 4)
        stt, sttb = self.carve("stt", [128, 3, 512], F32, 3)
        if final:
            yst, ystb = self.carve("yst", [128, 2, 1024], F32, 2)
            rst, rstb = None, None
        else:
            rst, rstb = self.carve("rst", [128, 4, 512], F32, 4)
        bufs = (zt, ztb, rt, rtb, zbq, zbqb, stt, sttb, rst, rstb, (yst, ystb) if final else None)
        for gi_, grp in enumerate(groups):
            self.ln_group(l, stg, wsrc, kch, xin, xinb, gofs, final, env, tiles, grp, hview, bufs, gi_ % 2, "proj")
            if gi_ > 0:
                self.ln_group(l, stg, wsrc, kch, xin, xinb, gofs, final, env, tiles, groups[gi_ - 1], hview, bufs,
                              (gi_ - 1) % 2, "norm")
        self.ln_group(l, stg, wsrc, kch, xin, xinb, gofs, final, env, tiles, groups[-1], hview, bufs,
                      (len(groups) - 1) % 2, "norm")

    def ln_group(self, l, stg, wsrc, kch, xin, xinb, gofs, final, env, alltiles, tiles, hview, bufs, slot, part):
        P = self.P
        g = env
        U, Ub, ident, CONST, ONES = g["U"], g["Ub"], g["ident"], g["CONST"], g["ONES"]
        onesD, eps_t, modT, MOD, coef, COEF = g["onesD"], g["eps_t"], g["modT"], g["MOD"], g["coef"], g["COEF"]
        rscr, RS, yp, ys = g["rscr"], g["RS"], g["yp"], g["ys"]
        produce = g["produce"]
        zt, ztb, rt, rtb, zbq, zbqb, stt, sttb, rst, rstb, ysts = bufs
        if final:
            yst, ystb = ysts
        zoff = slot * 512
        ri = self.ln_ri
        for c in (range(8) if part == "proj" else []):
            Wt, Wb = self.wload(wsrc(c), kch, 128)
            for ti, tt in enumerate(tiles):
                t0, n = TT[tt]
                pz_, pzb_ = self.bank("acc")
                if hview is None:
                    pairs = [(Wt[:, k, :], xin[:, k, t0:t0 + n]) for k in range(kch)]
                    rb = [Wb, xinb[tt]]
                else:
                    pairs = [(Wt[:, k, :], hview(k, t0, n)) for k in range(kch)]
                    rb = [Wb, xinb[alltiles.index(tt)]]
                self.mm_acc(pz_[:, 0:n], pzb_, pairs, rb)
                r_ = rt[:, ri % 4, :]
                rb_ = rtb[ri % 4]
                ri += 1
                P.op("sp", lambda e, r_=r_, c=c, t0=t0, n=n: e.dma_start(out=r_[:, 0:n], in_=rscr[:, c, t0:t0 + n]),
                     r=[RS[c][tt]], w=[rb_], dma=rb_)
                for (cs, s) in segs(tt):
                    P.op("dve", lambda e, pz_=pz_, r_=r_, c=c, cs=cs, s=s, t0=t0: e.scalar_tensor_tensor(
                        out=zt[:, c, zoff + cs.start:zoff + cs.stop], in0=pz_[:, cs], scalar=modT[:, l, gofs + c, s:s + 1],
                        in1=r_[:, cs], op0=ALU.mult, op1=ALU.add), r=[pzb_, rb_, MOD], w=[ztb[slot * 8 + c]])
        self.ln_ri = ri
        zi = 0
        for ti, tt in enumerate(tiles if part == "norm" else []):
            t0, n = TT[tt]
            o0 = zoff
            pm, pmb = self.bank("o")
            pq, pqb = self.bank("z")
            for c in range(8):
                zb_ = zbq[:, zi % 4, :]
                zbb = zbqb[zi % 4]
                zi += 1
                zs_ = zbq[:, zi % 4, :]
                zsb = zbqb[zi % 4]
                zi += 1
                P.op("act", lambda e, zb_=zb_, c=c: e.activation(out=zb_[:, 0:n], in_=zt[:, c, o0:o0 + n], func=AF.Copy),
                     r=[ztb[slot * 8 + c]], w=[zbb])
                P.op("act", lambda e, zs_=zs_, c=c: e.activation(out=zs_[:, 0:n], in_=zt[:, c, o0:o0 + n], func=AF.Square),
                     r=[ztb[slot * 8 + c]], w=[zsb])
                P.op("pe", lambda e, pm=pm, zb_=zb_, c=c: e.matmul(pm[:, 0:n], lhsT=onesD[:], rhs=zb_[:, 0:n],
                                                                  start=(c == 0), stop=(c == 7)), r=[zbb, ONES], w=[pmb])
                P.op("pe", lambda e, pq=pq, zs_=zs_, c=c: e.matmul(pq[:, 0:n], lhsT=onesD[:], rhs=zs_[:, 0:n],
                                                                  start=(c == 0), stop=(c == 7)), r=[zsb, ONES], w=[pqb])
            mean, rstd = self.ln_stats(pm, pmb, pq, pqb, stt, sttb, n, eps_t, ONES)
            for c in range(8):
                zc = zt[:, c, o0:o0 + n]
                P.op("dve", lambda e, zc=zc: e.tensor_tensor(out=zc, in0=zc, in1=mean[:, 0:n], op=ALU.subtract),
                     r=[sttb[0], ztb[slot * 8 + c]], w=[ztb[slot * 8 + c]])
                P.op("dve", lambda e, zc=zc: e.tensor_tensor(out=zc, in0=zc, in1=rstd[:, 0:n], op=ALU.mult),
                     r=[sttb[2], ztb[slot * 8 + c]], w=[ztb[slot * 8 + c]])
                if not final:
                    produce(zc, [ztb[slot * 8 + c]], c, tt, stg, rst[:, ri % 4, :], rstb[ri % 4])
                    ri += 1
                    self.ln_ri = ri
                else:
                    P.op("act", lambda e, zc=zc, c=c: e.activation(out=zc, in_=zc, func=AF.Identity,
                                                                   scale=coef[:, stg, 0, c, 0:1], bias=coef[:, stg, 1, c, 0:1]),
                         r=[COEF, ztb[slot * 8 + c]], w=[ztb[slot * 8 + c]])
            if final:
                nb = (n + 127) // 128
                for tb in range(nb):
                    rows = min(128, n - 128 * tb)
                    ysl = yst[:, tb % 2, :]
                    yslb = ystb[tb % 2]
                    for hf in range(2):
                        pt, ptb = self.bank("acc")

                        def fn_yt(e, pt=pt, hf=hf, tb=tb, rows=rows):
                            ins = None
                            for cc in range(4):
                                c = 4 * hf + cc
                                ins = e.transpose(pt[0:rows, 128 * cc:128 * cc + 128],
                                                  zt[:, c, o0 + 128 * tb:o0 + 128 * tb + rows], ident[:])
                            return ins
                        P.op("pe", fn_yt, r=ztb[slot * 8:slot * 8 + 8] + [CONST], w=[ptb])
                        P.op("act", lambda e, pt=pt, hf=hf, ysl=ysl, rows=rows: e.activation(
                            out=ysl[0:rows, 512 * hf:512 * hf + 512], in_=pt[0:rows, :], func=AF.Copy), r=[ptb], w=[yslb])
                    if tt < 4:
                        dst = yp[t0 + 128 * tb:t0 + 128 * tb + 128, :]
                    else:
                        dst = ys[:, :]
                    P.op("sp", lambda e, dst=dst, ysl=ysl, rows=rows: e.dma_start(out=dst, in_=ysl[0:rows, :]),
                         r=[yslb], w=[Buf("x")], dma=yslb)


_CACHE = {}


def _consts():
    slopes = 2.0 ** (-8.0 * (np.arange(8) + 1) / 8.0)
    k = np.arange(128)[:, None].astype(np.float64)
    q = np.arange(128)[None, :].astype(np.float64)
    masks = np.zeros((128, 12, 512), np.float64)
    smp = np.zeros((128, 12, 32), np.float64)
    smc = np.zeros((4, 12, 32), np.float64)
    for g, d in enumerate(DIL):
        for hp in range(4):
            for hh in range(2):
                h = 2 * hp + hh
                cur = np.where(k <= q, np.exp(-slopes[h] * d * np.maximum(q - k, 0.0)), 0.0)
                prev = np.where(k >= q, np.exp(-slopes[h] * d * np.maximum(128 + q - k, 0.0)), 0.0)
                masks[:, g * 4 + hp, (2 * hh) * 128:(2 * hh + 1) * 128] = cur
                masks[:, g * 4 + hp, (2 * hh + 1) * 128:(2 * hh + 2) * 128] = prev
                for s in range(4):
                    for t in range(4):
                        col = s * 8 + hh * 4 + t
                        smp[:, g * 4 + hp, col] = prev[:, t if g == 0 else 0]
                        if g == 0:
                            smc[:, g * 4 + hp, col] = cur[0:4, t]
                        else:
                            smc[:, g * 4 + hp, col] = (np.arange(4) == t).astype(np.float64)
    return (np.eye(128, dtype=np.float32), masks.reshape(128, -1).astype(np.float32),
            smp.reshape(128, -1).astype(np.float32), smc.reshape(4, -1).astype(np.float32))


def _pack_vecs(inp):
    out = np.zeros((L, NVEC, 128), np.float32)
    for l in range(L):
        rows = [inp["b_in"][l].reshape(60, 128), inp["b_ada"][l].reshape(48, 128), inp["b_pc"][l].reshape(8, 128),
                inp["b_pa"][l].reshape(8, 128), inp["b_out"][l].reshape(8, 128), inp["ln1_g"][l].reshape(8, 128),
                inp["ln1_b"][l].reshape(8, 128), inp["ln2_g"][l].reshape(8, 128), inp["ln2_b"][l].reshape(8, 128),
                inp["b_dw"][l].reshape(4, 128), inp["conv_ln_g"][l].reshape(4, 128), inp["conv_ln_b"][l].reshape(4, 128),
                inp["w_dw"][l].reshape(124, 128)]
        cat = np.concatenate(rows, axis=0)
        out[l, :cat.shape[0]] = cat
    return out


def get_nc():
    global KSTAGE
    import os
    KSTAGE = int(os.environ.get("KSTAGE", "99"))
    if "nc" not in _CACHE:
        _CACHE["nc"] = Builder().build()
    return _CACHE["nc"]


def make_in_maps(inp, cores):
    f = lambda a: np.ascontiguousarray(np.asarray(a, dtype=np.float32))
    ident, masks, smp, smc = _consts()
    vecs = _pack_vecs({k: np.asarray(v) for k, v in inp.items()})
    shared = {k: f(inp[k]) for k in ("w_ada", "w_in", "w_pc", "w_pa", "w_out", "w_gate", "w_up", "w_down")}
    shared.update(vecs=vecs, ident=ident, masks=masks, smp=smp, smc=smc)
    maps = []
    for i in cores:
        m = dict(shared)
        m["xp"] = f(inp["x_prompt"][i])
        m["xs"] = f(np.asarray(inp["x_sample"][4 * i:4 * i + 4]).reshape(NS, D))
        m["c5"] = f(np.concatenate([np.asarray(inp["c_prompt"][i:i + 1]), np.asarray(inp["c_sample"][4 * i:4 * i + 4])], 0))
        caches = (inp["cache_kv_g0"], inp["cache_kv_g1"], inp["cache_kv_g2"])
        for g in range(3):
            m["ck%d" % g] = f(np.asarray(caches[g][:, 4 * i:4 * i + 4]).reshape(L, 4, WIN[g], 1024))
        m["stc"] = f(inp["state_conv"][:, 4 * i:4 * i + 4])
        maps.append(m)
    return maps


def kernel(**inputs):
    nc = get_nc()
    cores = list(range(NCORES))
    maps = make_in_maps(inputs, cores)
    res = run_bass_kernel_spmd(nc, maps, core_ids=cores)
    R = res.results
    y_p = np.stack([R[i]["yp"] for i in cores]).astype(np.float32)
    y_s = np.concatenate([R[i]["ys"].reshape(4, 4, D) for i in cores], 0).astype(np.float32)
    outs = [y_p, y_s]
    for g in range(3):
        outs.append(np.stack([R[i]["kvp%d" % g] for i in cores], 1).reshape(L, NCORES, WIN[g], 2, 8, 64).astype(np.float32))
    outs.append(np.stack([R[i]["cvp"] for i in cores], 1).astype(np.float32))
    for g in range(3):
        outs.append(np.concatenate([R[i]["kvs%d" % g] for i in cores], 1).reshape(L, 32, WIN[g], 2, 8, 64).astype(np.float32))
    outs.append(np.concatenate([R[i]["cvs"] for i in cores], 1).astype(np.float32))
    return tuple(outs)
```

```python
import contextlib
import numpy as np
import concourse.bass as bass
import concourse.mybir as mybir
from concourse.bass_utils import run_bass_kernel_spmd

F32 = mybir.dt.float32
BF16 = mybir.dt.bfloat16
AF = mybir.ActivationFunctionType
ALU = mybir.AluOpType

NCORES = 8
D = 1024
S = 2048
NS = 16
NT = S + NS
L = 2
DFF = 2816
NF = DFF // 128
ALPHA = float((2 * L) ** 0.25)
EPS = 1e-5
TT = [(0, 512), (512, 512), (1024, 512), (1536, 512), (2048, 16)]
DIL = (1, 4, 16)
WIN = (128, 512, 2048)
ENGS = ("sp", "act", "pool", "dve", "pe")
SAME_SYNC = True
NSLOT = 4
A0_END = 8 * NT * 2
RB_BYTES = 112 * 1024

V_BIN, V_BADA, V_BPC, V_BPA, V_BOUT, V_L1G, V_L1B, V_L2G, V_L2B, V_BDW, V_CLG, V_CLB, V_WDW = (
    0, 60, 108, 116, 124, 132, 140, 148, 156, 164, 168, 172, 176)
NVEC = 384


def segs(tt):
    if tt < 4:
        return [(slice(0, 512), 0)]
    return [(slice(4 * s, 4 * s + 4), 1 + s) for s in range(4)]


class DSem:
    def __init__(self, h):
        self.h = h
        self.count = 0


class Buf:
    __slots__ = ("name", "w", "r", "dsem", "rng", "phase", "psum")

    def __init__(self, name, rng=None, phase=None):
        self.name = name
        self.psum = False
        self.w = {}
        self.r = {}
        self.dsem = None
        self.rng = rng
        self.phase = phase


class Op:
    __slots__ = ("eng", "fn", "deps", "sig", "sigidx", "dsem", "sigval", "seq", "calls")


class Rec:
    def __init__(self):
        self.calls = []

    def __getattr__(self, name):
        def f(*a, **k):
            self.calls.append((name, a, k))
            return self
        return f


class Prog:
    def __init__(self, nc, stack):
        self.nc = nc
        self.stack = stack
        self.ops = {e: [] for e in ENGS}
        self.dsems = []
        self.seq = 0
        self.esem = {e: stack.enter_context(nc.semaphore("es_" + e)) for e in ENGS}
        self.region_bufs = []
        self.phase = 0

    def dsem_of(self, buf):
        if buf.dsem is None:
            h = self.stack.enter_context(self.nc.semaphore("ds%d" % len(self.dsems)))
            buf.dsem = DSem(h)
            self.dsems.append(buf.dsem)
        return buf.dsem

    @staticmethod
    def _key(o):
        return ("d", id(o.dsem)) if o.dsem is not None else ("e", o.eng)

    def op(self, eng, fn, r=(), w=(), dma=None):
        o = Op()
        o.eng = eng
        o.fn = fn
        rec = Rec()
        fn(rec)
        o.calls = rec.calls
        assert len(o.calls) > 0
        o.sig = False
        o.sigidx = 0
        o.dsem = None
        o.sigval = 0
        self.seq += 1
        o.seq = self.seq
        deps = {}
        strong = {}

        def add(p, st=True):
            k = self._key(p)
            q = deps.get(k)
            if q is None or q.seq < p.seq:
                deps[k] = p
            if st:
                q = strong.get(k)
                if q is None or q.seq < p.seq:
                    strong[k] = p

        for b in r:
            for p in b.w.values():
                add(p)
            if b.psum and eng in ("act", "dve"):
                for p in b.r.values():
                    if p.eng != eng and p.eng in ("act", "dve"):
                        add(p)
        for b in w:
            for p in b.w.values():
                add(p, p.eng != eng)
            for p in b.r.values():
                add(p, p.eng != eng)
        k_self = ("e", eng)
        if k_self in deps and dma is None:
            if k_self in strong:
                deps[k_self] = strong[k_self]
            else:
                del deps[k_self]
        o.deps = deps
        if dma is not None:
            o.dsem = self.dsem_of(dma)
            o.dsem.count += 1
            o.sigval = 16 * o.dsem.count
        k = self._key(o)
        for b in r:
            b.r[k] = o
        for b in w:
            b.w = {k: o}
            b.r = {}
        self.ops[eng].append(o)
        return o

    def rbuf(self, name, off, nbytes):
        b = Buf(name, (off, off + nbytes), self.phase)
        for ob in self.region_bufs:
            if ob.phase != self.phase and ob.rng[0] < b.rng[1] and b.rng[0] < ob.rng[1]:
                for p in list(ob.w.values()) + list(ob.r.values()):
                    k = self._key(p)
                    q = b.w.get(k)
                    if q is None or q.seq < p.seq:
                        b.w[k] = p
        self.region_bufs.append(b)
        return b

    def prune_region(self):
        pass

    def finalize(self):
        for e in ENGS:
            for o in self.ops[e]:
                for p in o.deps.values():
                    if p.dsem is None and not (p.eng == o.eng and (o.eng == "pe" or not SAME_SYNC)):
                        p.sig = True
        for e in ENGS:
            i = 0
            for o in self.ops[e]:
                if o.sig:
                    i += 1
                    o.sigidx = i

    def emit(self, eng, e):
        known = {}
        for o in self.ops[eng]:
            for p in sorted(o.deps.values(), key=lambda x: x.seq):
                if p.dsem is not None:
                    sem, val, kk = p.dsem.h, p.sigval, id(p.dsem)
                else:
                    if p.eng == eng and (eng == "pe" or not SAME_SYNC):
                        continue
                    sem, val, kk = self.esem[p.eng], p.sigidx, p.eng
                if known.get(kk, 0) >= val:
                    continue
                e.wait_ge(sem, val)
                known[kk] = val
            if o.fn is None:
                continue
            ins = None
            for (name, a, k) in o.calls:
                ins = getattr(e, name)(*a, **k)
            if o.dsem is not None:
                ins.then_inc(o.dsem.h, 16)
            elif o.sig:
                ins.then_inc(self.esem[eng], 1)


class StopBuild(Exception):
    pass


KSTAGE = 99


class Builder:
    def stage(self, k):
        if not hasattr(self, "marks"):
            self.marks = []
        self.marks.append((k, sum(len(o.calls) for o in self.P.ops["pe"])))
        if KSTAGE <= k:
            raise StopBuild()

    def __init__(self):
        self.nc = bass.Bass("TRN2", target_bir_lowering=False)
        self.stack = contextlib.ExitStack()

    def sb(self, name, shape, dt):
        return self.stack.enter_context(self.nc.sbuf_tensor("sb_" + name, list(shape), dt))

    def carve(self, name, shape, dt, nbufs=1, area=1):
        esz = 4 if dt == F32 else 2
        per = int(np.prod(shape[1:])) * esz
        per = (per + 63) // 64 * 64
        off = self.rb_offs[area]
        assert off % 4 == 0
        self.rb_offs[area] += per
        lim = self.rb_lims[area]
        assert self.rb_offs[area] <= lim, (name, area, self.rb_offs[area], lim)
        a = self.regB[:, off // 4:(off + per) // 4]
        if dt == BF16:
            a = a.bitcast(BF16)
        n = int(np.prod(shape[1:]))
        a = a[:, 0:n]
        if len(shape) > 2:
            names = ["d%d" % i for i in range(len(shape) - 1)]
            kw = {names[i]: shape[1 + i] for i in range(len(shape) - 2)}
            a = a.rearrange("p (%s) -> p %s" % (" ".join(names), " ".join(names)), **kw)
        bufs = [self.P.rbuf("%s_%d" % (name, i), off, per) for i in range(nbufs)]
        return a, bufs

    def new_phase(self, start1, start0=0, lim0=A0_END):
        self.P.phase += 1
        self.rb_offs = [start0, start1]
        self.rb_lims = [lim0, RB_BYTES]

    def wload(self, src2d, kch, ncols):
        i = self.ring_i % NSLOT
        self.ring_i += 1
        slot = self.ring[i]
        buf = self.ringb[i]
        view = slot[:, 0:kch * ncols].rearrange("p (k n) -> p k n", k=kch)
        src = src2d.rearrange("(k p) n -> p k n", p=128)
        self.P.op("pool", lambda e, view=view, src=src: e.dma_start(out=view, in_=src), w=[buf], dma=buf)
        return view, buf

    def bank(self, pool):
        lst = self.pools[pool]
        i = self.pool_i[pool] % len(lst)
        self.pool_i[pool] += 1
        b = lst[i]
        return self.ps[b], self.psb[b]

    def mm_acc(self, out, outbuf, pairs, rbufs):
        def fn(e, out=out, pairs=pairs):
            n = len(pairs)
            ins = None
            for i, (lt, rh) in enumerate(pairs):
                ins = e.matmul(out, lhsT=lt, rhs=rh, start=(i == 0), stop=(i == n - 1))
            return ins
        return self.P.op("pe", fn, r=rbufs, w=[outbuf])

    def build(self):
        nc = self.nc
        st = self.stack
        P = self.P = Prog(nc, st)

        def din(name, shape):
            return nc.dram_tensor(name, list(shape), F32, kind="ExternalInput").ap()

        def dout(name, shape):
            return nc.dram_tensor(name, list(shape), F32, kind="ExternalOutput").ap()

        xp = din("xp", [S, D])
        xs = din("xs", [NS, D])
        c5 = din("c5", [5, D])
        ck = [din("ck%d" % g, [L, 4, WIN[g], 2 * 512]) for g in range(3)]
        stc = din("stc", [L, 4, 30, 512])
        w_ada = din("w_ada", [L, D, 6 * D])
        w_in = din("w_in", [L, D, 7680])
        w_pc = din("w_pc", [L, 512, D])
        w_pa = din("w_pa", [L, 512, D])
        w_out = din("w_out", [L, D, D])
        w_gate = din("w_gate", [L, D, DFF])
        w_up = din("w_up", [L, D, DFF])
        w_down = din("w_down", [L, DFF, D])
        vecs = din("vecs", [L, NVEC, 128])
        ident_d = din("ident", [128, 128])
        masks_d = din("masks", [128, 12 * 512])
        smp_d = din("smp", [128, 12 * 32])
        smc_d = din("smc", [4, 12 * 32])

        yp = dout("yp", [S, D])
        ys = dout("ys", [NS, D])
        kvp = [dout("kvp%d" % g, [L, WIN[g], 2, 512]) for g in range(3)]
        cvp = dout("cvp", [L, 30, 512])
        kvs = [dout("kvs%d" % g, [L, 4, WIN[g], 2 * 512]) for g in range(3)]
        cvs = dout("cvs", [L, 4, 30, 512])
        rscr = nc.dram_tensor("rscr", [128, 8, NT], F32).ap()

        U = self.sb("U", [128, 8, NT], BF16)
        Ub = [[Buf("U%d_%d" % (c, t)) for t in range(5)] for c in range(8)]
        self.ring = [self.sb("ring%d" % i, [128, 4096], BF16) for i in range(NSLOT)]
        self.ringb = [Buf("ring%d" % i) for i in range(NSLOT)]
        self.ring_i = 0
        self.ln_ri = 0
        ident = self.sb("ident", [128, 128], F32)
        ones_bf = self.sb("ones_bf", [128, 128], BF16)
        onesD = self.sb("onesD", [128, 128], BF16)
        onesC = self.sb("onesC", [128, 128], BF16)
        ones_f = self.sb("ones_f", [128, 64], F32)
        eps_t = self.sb("eps_t", [128, 1], F32)
        alpha_t = self.sb("alpha_t", [128, 8, 5], F32)
        masks = self.sb("masks", [128, 12, 512], BF16)
        smp = self.sb("smp", [128, 12, 32], F32)
        smc = self.sb("smc", [4, 12, 32], F32)
        vecT = self.sb("vecT", [128, L, NVEC], F32)
        bq8 = self.sb("bq8", [128, L, 12], F32)
        modT = self.sb("modT", [128, L, 48, 5], F32)
        coef = self.sb("coef", [128, 5, 4, 8, 5], F32)
        scT = self.sb("scT", [128, 8, 5], BF16)
        ctmp = self.sb("ctmp", [128, 8, 5], F32)
        CONST = Buf("const")
        MASKS = Buf("masks")
        VEC = Buf("vec")
        MOD = Buf("mod")
        COEF = Buf("coef")
        SCT = Buf("sct")
        self.regB = self.sb("regB", [128, RB_BYTES // 4], F32)
        self.rb_offs = [0, 0]
        self.rb_lims = [0, RB_BYTES]

        self.ps = [st.enter_context(nc.psum_tensor("ps%d" % i, [128, 512], F32)) for i in range(8)]
        self.psb = [Buf("ps%d" % i) for i in range(8)]
        for b in self.psb:
            b.psum = True
        self.pools = {"acc": [0, 1, 2], "o": [3, 4], "z": [5, 6], "misc": [7], "sS": [0, 1, 2], "tr": [7, 0, 1, 2], "all": [0, 1, 2, 3, 4, 5, 6, 7]}
        self.pool_i = {k: 0 for k in self.pools}

        def vcol(l, row):
            return vecT[:, l, row:row + 1]

        self.new_phase(0, 0, 0)
        P.op("sp", lambda e: e.dma_start(out=ident[:], in_=ident_d), w=[CONST], dma=CONST)
        P.op("pool", lambda e: e.dma_start(out=masks[:], in_=masks_d.rearrange("p (a b) -> p a b", a=12)),
             w=[MASKS], dma=MASKS)
        SM = Buf("sm")
        P.op("sp", lambda e: e.dma_start(out=smp[:], in_=smp_d.rearrange("p (a b) -> p a b", a=12)), w=[SM], dma=SM)
        SM2 = Buf("sm2")
        P.op("sp", lambda e: e.dma_start(out=smc[:], in_=smc_d.rearrange("p (a b) -> p a b", a=12)), w=[SM2], dma=SM2)
        ONES = Buf("ones")
        P.op("dve", lambda e: e.memset(ones_bf[:], 1.0), w=[ONES])
        P.op("dve", lambda e: e.memset(onesD[:], 1.0 / 1024.0), w=[ONES])
        P.op("dve", lambda e: e.memset(onesC[:], 1.0 / 512.0), w=[ONES])
        P.op("dve", lambda e: e.memset(ones_f[:], 1.0), w=[ONES])
        P.op("dve", lambda e: e.memset(eps_t[:], EPS), w=[ONES])
        P.op("dve", lambda e: e.memset(alpha_t[:], ALPHA), w=[ONES])

        D2D = [Buf("d2d%d" % i) for i in range(4)]
        self.d2d_list = []
        self.d2d_i = 0
        for l in range(L):
            for s in range(4):
                self.d2d_list.append((stc[l, s, 4:30, :], cvs[l, s, 0:26, :]))
        nsmall = len(self.d2d_list)
        for l in range(L):
            for g in range(3):
                W = WIN[g]
                for s in range(4):
                    self.d2d_list.append((ck[g][l, s, 4:W, :], kvs[g][l, s, 0:W - 4, :]))

        def emit_d2d(n):
            for _ in range(n):
                if self.d2d_i >= len(self.d2d_list):
                    return
                src, dst = self.d2d_list[self.d2d_i]
                bq = D2D[self.d2d_i % 4]
                self.d2d_i += 1
                P.op("act", lambda e, src=src, dst=dst: e.dma_start(out=dst, in_=src), w=[bq], dma=bq)
        self.emit_d2d = emit_d2d
        emit_d2d(nsmall)

        vraw, vrawb = self.carve("vraw", [128, 2, 128], F32, 2)
        for l in range(L):
            for j in range(3):
                i = (l * 3 + j) % 2
                P.op("sp", lambda e, i=i, l=l, j=j: e.dma_start(out=vraw[:, i, :], in_=vecs[l, 128 * j:128 * j + 128, :]),
                     w=[vrawb[i]], dma=vrawb[i])
                pt, ptb = self.bank("misc")
                P.op("pe", lambda e, pt=pt, i=i: e.transpose(pt[:, 0:128], vraw[:, i, :], ident[:]),
                     r=[vrawb[i], CONST], w=[ptb])
                P.op("dve", lambda e, pt=pt, l=l, j=j: e.tensor_copy(out=vecT[:, l, 128 * j:128 * j + 128], in_=pt[:, 0:128]),
                     r=[ptb], w=[VEC])
        for l in range(L):
            P.op("dve", lambda e, l=l: e.tensor_scalar(out=bq8[:, l, :], in0=vecT[:, l, 8:20], scalar1=0.125,
                                                       scalar2=None, op0=ALU.mult), r=[VEC], w=[VEC])
        c5t, c5b = self.carve("c5t", [128, 1024], F32, 1)
        P.op("sp", lambda e: e.dma_start(out=c5t[0:5, :], in_=c5), w=c5b, dma=c5b[0])
        pt, ptb = self.bank("misc")

        def fn_ct(e, pt=pt):
            ins = None
            for k in range(8):
                ins = e.transpose(pt[:, 5 * k:5 * k + 5], c5t[0:5, 128 * k:128 * k + 128], ident[0:5, 0:5])
            return ins
        P.op("pe", fn_ct, r=[c5b[0], CONST], w=[ptb])
        P.op("act", lambda e, pt=pt: e.activation(out=scT[:], in_=pt[:, 0:40].rearrange("p (a b) -> p a b", a=8),
                                                  func=AF.Silu), r=[ptb], w=[SCT])
        for l in range(L):
            pm, pmb = self.bank("acc")
            for grp in range(12):
                Wt, Wb = self.wload(w_ada[l, :, 512 * grp:512 * grp + 512], 8, 512)

                def fn_mod(e, Wt=Wt, pm=pm, grp=grp):
                    ins = None
                    for jj in range(4):
                        j = 4 * grp + jj
                        for k in range(8):
                            ins = e.matmul(pm[:, 5 * j:5 * j + 5], lhsT=Wt[:, k, 128 * jj:128 * jj + 128],
                                           rhs=scT[:, k, :], start=(k == 0), stop=(k == 7))
                    return ins
                P.op("pe", fn_mod, r=[Wb, SCT], w=[pmb])
            P.op("dve", lambda e, pm=pm, l=l: e.tensor_tensor(
                out=modT[:, l], in0=pm[:, 0:240].rearrange("p (a b) -> p a b", a=48),
                in1=vecT[:, l, V_BADA:V_BADA + 48].unsqueeze(2).to_broadcast([128, 48, 5]), op=ALU.add),
                r=[pmb, VEC], w=[MOD])
            for (a, b) in ((8, 24), (32, 48)):
                P.op("dve", lambda e, l=l, a=a, b=b: e.tensor_scalar(
                    out=modT[:, l, a:b], in0=modT[:, l, a:b], scalar1=1.0, scalar2=None, op0=ALU.add),
                    r=[MOD], w=[MOD])

        RS = [[Buf("rs%d_%d" % (c, t)) for t in range(5)] for c in range(8)]

        def vb(l, row):
            return vecT[:, l, row:row + 8].unsqueeze(2).to_broadcast([128, 8, 5])

        def cop(fn):
            P.op("dve", fn, r=[MOD, VEC, COEF, ONES], w=[COEF])

        SH1, SC1P, G1P, SH2, SC2P, G2P = 0, 8, 16, 24, 32, 40
        cop(lambda e: e.tensor_copy(out=coef[:, 0, 0], in_=modT[:, 0, SC1P:SC1P + 8]))
        cop(lambda e: e.tensor_copy(out=coef[:, 0, 1], in_=modT[:, 0, SH1:SH1 + 8]))
        cop(lambda e: e.tensor_copy(out=coef[:, 0, 2], in_=alpha_t[:]))
        cop(lambda e: e.tensor_tensor(out=coef[:, 0, 3], in0=modT[:, 0, G1P:G1P + 8], in1=vb(0, V_BOUT), op=ALU.mult))
        for l in range(L):
            s1 = 1 + 2 * l
            cop(lambda e, l=l, s1=s1: e.tensor_tensor(out=coef[:, s1, 0], in0=modT[:, l, SC2P:SC2P + 8],
                                                      in1=vb(l, V_L1G), op=ALU.mult))
            cop(lambda e, l=l, s1=s1: e.tensor_tensor(out=coef[:, s1, 1], in0=modT[:, l, SC2P:SC2P + 8],
                                                      in1=vb(l, V_L1B), op=ALU.mult))
            cop(lambda e, l=l, s1=s1: e.tensor_tensor(out=coef[:, s1, 1], in0=coef[:, s1, 1],
                                                      in1=modT[:, l, SH2:SH2 + 8], op=ALU.add))
            cop(lambda e, l=l, s1=s1: e.tensor_tensor(out=coef[:, s1, 2], in0=alpha_t[:], in1=vb(l, V_L1G), op=ALU.mult))
            cop(lambda e, l=l, s1=s1: e.tensor_tensor(out=coef[:, s1, 3], in0=alpha_t[:], in1=vb(l, V_L1B), op=ALU.mult))
            s2 = 2 + 2 * l
            if l + 1 < L:
                n = l + 1
                cop(lambda e, l=l, s2=s2, n=n: e.tensor_tensor(out=coef[:, s2, 0], in0=modT[:, n, SC1P:SC1P + 8],
                                                               in1=vb(l, V_L2G), op=ALU.mult))
                cop(lambda e, l=l, s2=s2, n=n: e.tensor_tensor(out=coef[:, s2, 1], in0=modT[:, n, SC1P:SC1P + 8],
                                                               in1=vb(l, V_L2B), op=ALU.mult))
                cop(lambda e, l=l, s2=s2, n=n: e.tensor_tensor(out=coef[:, s2, 1], in0=coef[:, s2, 1],
                                                               in1=modT[:, n, SH1:SH1 + 8], op=ALU.add))
                cop(lambda e, l=l, s2=s2: e.tensor_tensor(out=coef[:, s2, 2], in0=alpha_t[:], in1=vb(l, V_L2G), op=ALU.mult))
                cop(lambda e, l=l, s2=s2: e.tensor_tensor(out=coef[:, s2, 3], in0=alpha_t[:], in1=vb(l, V_L2B), op=ALU.mult))
                cop(lambda e, n=n: e.tensor_tensor(out=ctmp[:], in0=modT[:, n, G1P:G1P + 8], in1=vb(n, V_BOUT), op=ALU.mult))
                cop(lambda e, s2=s2: e.tensor_tensor(out=coef[:, s2, 3], in0=coef[:, s2, 3], in1=ctmp[:], op=ALU.add))
            else:
                cop(lambda e, l=l, s2=s2: e.tensor_copy(out=coef[:, s2, 0], in_=vb(l, V_L2G)))
                cop(lambda e, l=l, s2=s2: e.tensor_copy(out=coef[:, s2, 1], in_=vb(l, V_L2B)))

        def produce(src, srcbufs, c, tt, stg, rst, rstb, src_psum=False):
            t0, n = TT[tt]
            for (cs, s) in segs(tt):
                P.op("act", lambda e, cs=cs, s=s: e.activation(
                    out=U[:, c, t0 + cs.start:t0 + cs.stop], in_=src[:, cs], func=AF.Identity,
                    scale=coef[:, stg, 0, c, s:s + 1], bias=coef[:, stg, 1, c, s:s + 1]),
                    r=srcbufs + [COEF], w=[Ub[c][tt]])
                if src_psum:
                    P.op("dve", lambda e, cs=cs, s=s: e.tensor_scalar(
                        out=rst[:, cs], in0=src[:, cs], scalar1=coef[:, stg, 2, c, s:s + 1],
                        scalar2=coef[:, stg, 3, c, s:s + 1], op0=ALU.mult, op1=ALU.add),
                        r=srcbufs + [COEF], w=[rstb])
                else:
                    P.op("act", lambda e, cs=cs, s=s: e.activation(
                        out=rst[:, cs], in_=src[:, cs], func=AF.Identity, scale=coef[:, stg, 2, c, s:s + 1],
                        bias=coef[:, stg, 3, c, s:s + 1]), r=srcbufs + [COEF], w=[rstb])
            P.op("sp", lambda e: e.dma_start(out=rscr[:, c, t0:t0 + n], in_=rst[:, 0:n]), r=[rstb], w=[RS[c][tt]],
                 dma=rstb)

        xst, xstb = self.carve("xst", [128, 4, 1024], F32, 4)
        XIN = KSTAGE > 0
        rst, rstb = self.carve("rst", [128, 4, 512], F32, 4)
        ri = 0
        for tt in (range(5) if XIN else []):
            t0, n = TT[tt]
            if tt < 4:
                for tb in range(4):
                    P.op("sp", lambda e, tb=tb, t0=t0: e.dma_start(out=xst[:, tb, :], in_=xp[t0 + 128 * tb:t0 + 128 * tb + 128, :]),
                         w=[xstb[tb]], dma=xstb[tb])
            else:
                P.op("sp", lambda e: e.dma_start(out=xst[0:16, 0, :], in_=xs), w=[xstb[0]], dma=xstb[0])
            for c in range(8):
                pt, ptb = self.bank("acc")
                if tt < 4:
                    def fn_xt(e, pt=pt, c=c):
                        ins = None
                        for tb in range(4):
                            ins = e.transpose(pt[:, 128 * tb:128 * tb + 128], xst[:, tb, 128 * c:128 * c + 128], ident[:])
                        return ins
                    P.op("pe", fn_xt, r=xstb + [CONST], w=[ptb])
                else:
                    P.op("pe", lambda e, pt=pt, c=c: e.transpose(pt[:, 0:16], xst[0:16, 0, 128 * c:128 * c + 128],
                                                                 ident[0:16, 0:16]), r=[xstb[0], CONST], w=[ptb])
                produce(pt, [ptb], c, tt, 0, rst[:, ri % 4, :], rstb[ri % 4], src_psum=True)
                ri += 1

        try:
            self.stage(1)
            for l in range(L):
                self.layer(l, locals())
        except StopBuild:
            pass

        self.emit_d2d(1000)
        last = []
        allops = [o for e in ENGS for o in P.ops[e] if o.dsem is not None]
        lastd = {}
        for o in allops:
            q = lastd.get(id(o.dsem))
            if q is None or q.sigval < o.sigval:
                lastd[id(o.dsem)] = o
        fin = Op()
        fin.eng = "sp"
        fin.fn = None
        fin.sig = False
        fin.sigidx = 0
        fin.dsem = None
        fin.sigval = 0
        fin.seq = P.seq + 1
        fin.deps = {("d", k): o for k, o in lastd.items()}
        P.ops["sp"].append(fin)

        P.finalize()
        with nc.Block() as block:
            @block.sync
            def _(e):
                P.emit("sp", e)

            @block.scalar
            def _(e):
                P.emit("act", e)

            @block.gpsimd
            def _(e):
                P.emit("pool", e)

            @block.vector
            def _(e):
                P.emit("dve", e)

            @block.tensor
            def _(e):
                P.emit("pe", e)
        self.stack.close()
        return nc

    def layer(self, l, env):
        P = self.P
        g = env
        U, Ub, ident, CONST, ONES = g["U"], g["Ub"], g["ident"], g["CONST"], g["ONES"]
        ones_bf, onesD, onesC, ones_f, eps_t = g["ones_bf"], g["onesD"], g["onesC"], g["ones_f"], g["eps_t"]
        masks, MASKS, smp, smc, SM, SM2 = g["masks"], g["MASKS"], g["smp"], g["smc"], g["SM"], g["SM2"]
        vecT, VEC, bq8, modT, MOD, coef, COEF = g["vecT"], g["VEC"], g["bq8"], g["modT"], g["MOD"], g["coef"], g["COEF"]
        w_in, w_pc, w_pa, w_out, w_gate, w_up, w_down = (g["w_in"], g["w_pc"], g["w_pa"], g["w_out"], g["w_gate"],
                                                         g["w_up"], g["w_down"])
        ck, stc, kvp, cvp, kvs, cvs, yp, ys, rscr, RS = (g["ck"], g["stc"], g["kvp"], g["cvp"], g["kvs"], g["cvs"],
                                                         g["yp"], g["ys"], g["rscr"], g["RS"])
        produce = g["produce"]
        G1P, G2P = 16, 40

        def vcol(row):
            return vecT[:, l, row:row + 1]

        def ubufs(tt):
            return [Ub[k][tt] for k in range(8)]

        def gemm_u(ps, psb, Wt, Wb, col0, tt, n):
            t0 = TT[tt][0]
            self.mm_acc(ps[:, 0:n], psb, [(Wt[:, k, col0:col0 + 128], U[:, k, t0:t0 + n]) for k in range(8)],
                        [Wb] + ubufs(tt))

        self.new_phase(A0_END)
        cact, cactb = self.carve("cact", [128, 4, NT], BF16, 5)
        off_c = self.rb_offs[1]

        glu, glub = self.carve("glu", [128, 4, 30 + S], BF16, 4)
        sglu, sglub = self.carve("sglu", [128, 4, 4, 34], BF16, 1)
        gluf, glufb = self.carve("gluf", [128, 4, 48], F32, 1)
        ycv, ycvb = self.carve("ycv", [128, 4, NT], F32, 5, area=0)
        dwd, dwdb = self.carve("dwd", [128, 1, 31 * 128], BF16, 1)
        sig, sigb = self.carve("sig", [128, 2, 512], F32, 2)
        ybq, ybqb = self.carve("ybq", [128, 4, 512], BF16, 4)
        stt, sttb = self.carve("stt", [128, 4, 512], F32, 4)
        tmpa, tmpab = self.carve("tmpa", [128, 2, 512], F32, 2)
        sst, sstb = self.carve("sst", [128, 4, 512], F32, 1)
        cvo, cvob = self.carve("cvo", [128, 2, 512], F32, 2)

        for c in range(4):
            P.op("dve", lambda e, c=c: e.memset(glu[:, c, 0:30], 0.0), w=[glub[c]])
        for s in range(4):
            P.op("sp", lambda e, s=s: e.dma_start(out=sst[0:30, s, :], in_=stc[l, s, :, :]), w=sstb, dma=sstb[0])
        for c in range(4):
            pt, ptb = self.bank("misc")

            def fn_st(e, pt=pt, c=c):
                ins = None
                for s in range(4):
                    ins = e.transpose(pt[:, 30 * s:30 * s + 30], sst[0:30, s, 128 * c:128 * c + 128], ident[0:30, 0:30])
                return ins
            P.op("pe", fn_st, r=sstb + [CONST], w=[ptb])
            P.op("act", lambda e, pt=pt, c=c: e.activation(out=sglu[:, c, :, 0:30],
                                                           in_=pt[:, 0:120].rearrange("p (s t) -> p s t", s=4),
                                                           func=AF.Copy), r=[ptb], w=sglub)
        WA, WAb = self.wload(w_in[l, :, 0:512], 8, 512)
        WG, WGb = self.wload(w_in[l, :, 512:1024], 8, 512)
        self.si = 0

        def glu_chunk(c):
            for tt in range(5):
                t0, n = TT[tt]
                pa, pab = self.bank("acc")
                gemm_u(pa, pab, WA, WAb, 128 * c, tt, n)
                pg, pgb = self.bank("acc")
                gemm_u(pg, pgb, WG, WGb, 128 * c, tt, n)
                sg = sig[:, self.si % 2, :]
                sgb = sigb[self.si % 2]
                self.si += 1
                P.op("act", lambda e: e.activation(out=sg[:, 0:n], in_=pg[:, 0:n], func=AF.Sigmoid,
                                                   bias=vcol(V_BIN + 4 + c), scale=1.0), r=[pgb, VEC], w=[sgb])
                if tt < 4:
                    P.op("dve", lambda e: e.scalar_tensor_tensor(
                        out=glu[:, c, 30 + t0:30 + t0 + 512], in0=pa[:, 0:512], scalar=vcol(V_BIN + c), in1=sg[:, 0:512],
                        op0=ALU.add, op1=ALU.mult), r=[pab, sgb, VEC], w=[glub[c]])
                    if tt == 3:
                        P.op("dve", lambda e: e.scalar_tensor_tensor(
                            out=gluf[:, c, 0:32], in0=pa[:, 480:512], scalar=vcol(V_BIN + c), in1=sg[:, 480:512],
                            op0=ALU.add, op1=ALU.mult), r=[pab, sgb, VEC], w=glufb)
                else:
                    P.op("dve", lambda e: e.scalar_tensor_tensor(
                        out=gluf[:, c, 32:48], in0=pa[:, 0:16], scalar=vcol(V_BIN + c), in1=sg[:, 0:16],
                        op0=ALU.add, op1=ALU.mult), r=[pab, sgb, VEC], w=glufb)
                    P.op("dve", lambda e: e.tensor_copy(out=sglu[:, c, :, 30:34],
                                                        in_=gluf[:, c, 32:48].rearrange("p (s t) -> p s t", s=4)),
                         r=glufb, w=sglub)

        def conv_state_outputs():
            pt, ptb = self.bank("misc")

            def fn_cs(e, pt=pt):
                ins = None
                for c in range(4):
                    ins = e.transpose(pt[0:30, 128 * c:128 * c + 128], gluf[:, c, 2:32], ident[:])
                return ins
            P.op("pe", fn_cs, r=glufb + [CONST], w=[ptb])
            P.op("act", lambda e, pt=pt: e.activation(out=cvo[0:30, 0, :], in_=pt[0:30, :], func=AF.Copy), r=[ptb], w=[cvob[0]])
            P.op("sp", lambda e: e.dma_start(out=cvp[l, :, :], in_=cvo[0:30, 0, :]), r=[cvob[0]], w=[Buf("x")], dma=cvob[0])
            pt, ptb = self.bank("misc")

            def fn_cs2(e, pt=pt):
                ins = None
                for c in range(4):
                    ins = e.transpose(pt[0:16, 128 * c:128 * c + 128], gluf[:, c, 32:48], ident[:])
                return ins
            P.op("pe", fn_cs2, r=glufb + [CONST], w=[ptb])
            P.op("act", lambda e, pt=pt: e.activation(out=cvo[0:16, 1, :], in_=pt[0:16, :], func=AF.Copy), r=[ptb], w=[cvob[1]])
            for s in range(4):
                P.op("sp", lambda e, s=s: e.dma_start(out=cvs[l, s, 26:30, :], in_=cvo[4 * s:4 * s + 4, 1, :]),
                     r=[cvob[1]], w=[Buf("x")], dma=cvob[1])

        glu_chunk(0)
        glu_chunk(1)
        self.yi = 0

        def ln_tile(tt):
            t0, n = TT[tt]
            pm, pmb = self.bank("o")
            pq, pqb = self.bank("z")
            for c in range(4):
                yb_ = ybq[:, self.yi % 4, :]
                ybb = ybqb[self.yi % 4]
                self.yi += 1
                ys_ = ybq[:, self.yi % 4, :]
                ysb = ybqb[self.yi % 4]
                self.yi += 1
                P.op("dve", lambda e, yb_=yb_, c=c: e.tensor_copy(out=yb_[:, 0:n], in_=ycv[:, c, t0:t0 + n]),
                     r=[ycvb[tt]], w=[ybb])
                P.op("act", lambda e, ys_=ys_, c=c: e.activation(out=ys_[:, 0:n], in_=ycv[:, c, t0:t0 + n],
                                                                func=AF.Square), r=[ycvb[tt]], w=[ysb])
                P.op("pe", lambda e, yb_=yb_, c=c: e.matmul(pm[:, 0:n], lhsT=onesC[:], rhs=yb_[:, 0:n],
                                                           start=(c == 0), stop=(c == 3)), r=[ybb, ONES], w=[pmb])
                P.op("pe", lambda e, ys_=ys_, c=c: e.matmul(pq[:, 0:n], lhsT=onesC[:], rhs=ys_[:, 0:n],
                                                           start=(c == 0), stop=(c == 3)), r=[ysb, ONES], w=[pqb])
            mean, rstd = self.ln_stats(pm, pmb, pq, pqb, stt, sttb, n, eps_t, ONES)
            for c in range(4):
                ta = tmpa[:, c % 2, :]
                tab = tmpab[c % 2]
                P.op("dve", lambda e, ta=ta, c=c: e.tensor_tensor(out=ta[:, 0:n], in0=ycv[:, c, t0:t0 + n],
                                                                 in1=mean[:, 0:n], op=ALU.subtract),
                     r=[ycvb[tt], sttb[0]], w=[tab])
                P.op("dve", lambda e, ta=ta: e.tensor_tensor(out=ta[:, 0:n], in0=ta[:, 0:n], in1=rstd[:, 0:n],
                                                            op=ALU.mult), r=[tab, sttb[2]], w=[tab])
                P.op("act", lambda e, ta=ta, c=c: e.activation(out=cact[:, c, t0:t0 + n], in_=ta[:, 0:n],
                                                              func=AF.Silu, scale=vcol(V_CLG + c), bias=vcol(V_CLB + c)),
                     r=[tab, VEC], w=[cactb[tt]])

        def conv_chunk(c, with_ln):
            dw = dwd[:, 0, :].rearrange("p (k n) -> p k n", k=31)
            dwb = dwdb[0]
            P.op("dve", lambda e: e.tensor_tensor(
                out=dw, in0=ident[:].unsqueeze(1).to_broadcast([128, 31, 128]),
                in1=vecT[:, l, V_WDW + c:V_WDW + 124:4].unsqueeze(2).to_broadcast([128, 31, 128]), op=ALU.mult),
                r=[CONST, VEC], w=[dwb])
            for tt in range(5):
                t0, n = TT[tt]
                pc, pcb = self.bank("acc")
                if tt < 4:
                    pairs = [(dw[:, k, :], glu[:, c, t0 + k:t0 + k + 512]) for k in range(31)]
                    self.mm_acc(pc[:, 0:512], pcb, pairs, [dwb, glub[c]])
                else:
                    pairs = [(dw[:, k, :], sglu[:, c, :, k:k + 4]) for k in range(31)]
                    self.mm_acc(pc[:, 0:16], pcb, pairs, [dwb] + sglub)
                P.op("act", lambda e, pc=pc, t0=t0, n=n: e.activation(out=ycv[:, c, t0:t0 + n], in_=pc[:, 0:n],
                                                                     func=AF.Identity, bias=vcol(V_BDW + c), scale=1.0),
                     r=[pcb, VEC], w=[ycvb[tt]])
                if with_ln:
                    ln_tile(tt)

        conv_chunk(0, False)
        glu_chunk(2)
        conv_chunk(1, False)
        glu_chunk(3)
        conv_state_outputs()
        conv_chunk(2, False)
        conv_chunk(3, True)

        self.stage(2 + 10 * l)
        self.new_phase(off_c)
        oatt, oattb = self.carve("oatt", [128, 4, NT], BF16, 1)
        off_co = self.rb_offs[1]
        Qt, Qtb = self.carve("Qt", [128, NT], BF16, 5)
        Kt, Ktb = self.carve("Kt", [128, NT], BF16, 5)
        Ktf, Ktfb = self.carve("Ktf", [128, NT], F32, 5, area=0)
        Vtf, Vtfb = self.carve("Vtf", [128, NT], F32, 5, area=0)
        Vb, Vbb = self.carve("Vb", [128, 16, 128], BF16, 4)
        Oacc, Oaccb = self.carve("Oacc", [128, NT], F32, 1, area=0)
        Zacc, Zaccb = self.carve("Zacc", [128, NT], F32, 1, area=0)
        Eb, Ebb = self.carve("Eb", [128, 2, 512], BF16, 2)
        Pt, Ptb = self.carve("Pt", [128, 4, 512], BF16, 4)
        kvst, kvstb = self.carve("kvst", [128, 4, 512], F32, 4)
        sKc, sKcb = self.carve("sKc", [128, 2, 4, 2, 128], F32, 2)
        sKt, sKtb = self.carve("sKt", [128, 2, 512], BF16, 2)
        sNew, sNewb = self.carve("sNew", [128, 4, 2, 128], F32, 1)
        sE, sEb = self.carve("sE", [128, 2, 32], F32, 2)
        sPp, sPpb = self.carve("sPp", [128, 32], BF16, 1)
        sPc, sPcb = self.carve("sPc", [128, 32], BF16, 1)
        sVb, sVbb = self.carve("sVb", [128, 2, 4, 128], BF16, 2)
        sNb, sNbb = self.carve("sNb", [128, 4, 128], BF16, 1)
        Qf, Qfb = self.carve("Qf", [128, 16], F32, 1)
        Ktfs, Ktfsb = self.carve("Ktfs", [128, 16], F32, 1)

        self.ei = 0
        self.pi = 0
        ki = 0
        import os
        for hp in range(int(os.environ.get('HPN', '4'))):
            for gi in range(int(os.environ.get('GIN', '3'))):
                d = DIL[gi]
                nblk = S // d // 128
                Wn = WIN[gi]
                qcol = 1024 + (0 + gi) * 512 + hp * 128
                kcol = 1024 + (3 + gi) * 512 + hp * 128
                vcol_ = 1024 + (6 + gi) * 512 + hp * 128
                self.emit_d2d(1)
                WQ, WQb = self.wload(w_in[l, :, qcol:qcol + 128], 8, 128)
                WK, WKb = self.wload(w_in[l, :, kcol:kcol + 128], 8, 128)
                WV, WVb = self.wload(w_in[l, :, vcol_:vcol_ + 128], 8, 128)
                for tt in range(5):
                    t0, n = TT[tt]
                    pq, pqb = self.bank("acc")
                    gemm_u(pq, pqb, WQ, WQb, 0, tt, n)
                    P.op("act", lambda e, pq=pq, t0=t0, n=n, gi=gi, hp=hp: e.activation(
                        out=Qt[:, t0:t0 + n], in_=pq[:, 0:n], func=AF.Identity, scale=0.125,
                        bias=bq8[:, l, gi * 4 + hp:gi * 4 + hp + 1]), r=[pqb, VEC], w=[Qtb[tt]])
                    pk, pkb = self.bank("acc")
                    gemm_u(pk, pkb, WK, WKb, 0, tt, n)
                    P.op("act", lambda e, pk=pk, t0=t0, n=n, kcol=kcol: e.activation(
                        out=Ktf[:, t0:t0 + n], in_=pk[:, 0:n], func=AF.Identity, scale=1.0, bias=vcol(kcol // 128)),
                        r=[pkb, VEC], w=[Ktfb[tt]])
                    P.op("dve", lambda e, pk=pk, t0=t0, n=n, kcol=kcol: e.tensor_scalar(
                        out=Kt[:, t0:t0 + n], in0=pk[:, 0:n], scalar1=vcol(kcol // 128), scalar2=None, op0=ALU.add),
                        r=[pkb, VEC], w=[Ktb[tt]])
                    pv, pvb = self.bank("acc")
                    gemm_u(pv, pvb, WV, WVb, 0, tt, n)
                    P.op("act", lambda e, pv=pv, t0=t0, n=n, vc=vcol_: e.activation(
                        out=Vtf[:, t0:t0 + n], in_=pv[:, 0:n], func=AF.Identity, scale=1.0, bias=vcol(vc // 128)),
                        r=[pvb, VEC], w=[Vtfb[tt]])

                def blk_cols(r, c):
                    a = r + d * 128 * c
                    return a, a + d * 127 + 1, d

                def blk_rc(b):
                    return b // nblk, b % nblk

                for bg in range(4):
                    for kv, src, srcb in ((1, Vtf, Vtfb), (0, Ktf, Ktfb)):
                        blks = [4 * bg + j for j in range(4)]
                        inwin = [(blk_rc(b)[0] + d * 128 * blk_rc(b)[1]) >= S - Wn for b in blks]
                        if kv == 0 and not any(inwin):
                            continue
                        pt, ptb = self.bank("tr")

                        def fn_tr(e, pt=pt, src=src, blks=blks):
                            ins = None
                            for j, b in enumerate(blks):
                                r_, c_ = blk_rc(b)
                                a, z, stp = blk_cols(r_, c_)
                                ins = e.transpose(pt[:, 128 * j:128 * j + 128], src[:, a:z:stp], ident[:])
                            return ins
                        P.op("pe", fn_tr, r=list(srcb[0:4]) + [CONST], w=[ptb])
                        if kv == 1:
                            P.op("dve", lambda e, pt=pt, bg=bg: e.tensor_copy(
                                out=Vb[:, 4 * bg:4 * bg + 4, :], in_=pt[:, :].rearrange("p (a b) -> p a b", a=4)),
                                r=[ptb], w=[Vbb[bg]])
                        if any(inwin):
                            kst = kvst[:, ki % 4, :]
                            kstb = kvstb[ki % 4]
                            ki += 1
                            P.op("act", lambda e, pt=pt, kst=kst: e.activation(out=kst, in_=pt[:, :], func=AF.Copy),
                                 r=[ptb], w=[kstb])
                            for j, b in enumerate(blks):
                                if not inwin[j]:
                                    continue
                                r_, c_ = blk_rc(b)
                                a = r_ + d * 128 * c_ - (S - Wn)
                                dst = kvp[gi][l, a:a + 127 * d + 1:d, kv, hp * 128:hp * 128 + 128]
                                P.op("sp", lambda e, dst=dst, kst=kst, j=j: e.dma_start(out=dst, in_=kst[:, 128 * j:128 * j + 128]),
                                     r=[kstb], w=[Buf("x")], dma=kstb)
                for half in range(2):
                    pt, ptb = self.bank("tr")

                    def fn_sn(e, pt=pt, half=half):
                        ins = None
                        for s in (2 * half, 2 * half + 1):
                            for kv, src in ((0, Ktf), (1, Vtf)):
                                o = ((s % 2) * 2 + kv) * 128
                                ins = e.transpose(pt[0:4, o:o + 128], src[:, S + 4 * s:S + 4 * s + 4], ident[:])
                        return ins
                    P.op("pe", fn_sn, r=[Ktfb[4], Vtfb[4], CONST], w=[ptb])
                    P.op("act", lambda e, pt=pt, half=half: e.activation(
                        out=sNew[0:4, 2 * half:2 * half + 2, :, :],
                        in_=pt[0:4, :].rearrange("p (s k f) -> p s k f", s=2, k=2), func=AF.Copy), r=[ptb], w=sNewb)
                P.op("act", lambda e: e.activation(out=sNb[0:4, :, :], in_=sNew[0:4, :, 1, :], func=AF.Copy), r=sNewb, w=sNbb)
                for s in range(4):
                    dst = kvs[gi][l, s, Wn - 4:Wn, :].rearrange("t (k f) -> t k f", k=2)[:, :, hp * 128:hp * 128 + 128]
                    P.op("sp", lambda e, dst=dst, s=s: e.dma_start(out=dst, in_=sNew[0:4, s, :, :]), r=sNewb, w=[Buf("x")],
                         dma=sNewb[0])

                mk = masks[:, gi * 4 + hp, :]
                first = (gi == 0)
                gidx = gi * 4 + hp
                nt_ = 1 if gi == 0 else 4

                def q_group(qg):
                    if gi == 0:
                        return [(0, 4 * qg + j) for j in range(4)], (lambda T: T[:, 512 * qg:512 * qg + 512])
                    if gi == 1:
                        return [(qg, j) for j in range(4)], (lambda T: T[:, qg:S:4])
                    return ([(4 * qg + j, 0) for j in range(4)],
                            (lambda T: T[:, 0:S].rearrange("p (i r) -> p r i", r=16)[:, 4 * qg:4 * qg + 4, :]))

                def emit_S(qg, j, r_, c_, po, pob, pz, pzb):
                    hasp = c_ > 0
                    a, z, stp = blk_cols(r_, c_)
                    psA, psAb = self.bank("sS")
                    psB, psBb = self.bank("sS")

                    def fn_s(e):
                        ins = None
                        for hh, ps_ in ((0, psA), (1, psB)):
                            ph = slice(64 * hh, 64 * hh + 64)
                            ins = e.matmul(ps_[:, 0:128], lhsT=Kt[ph, a:z:stp], rhs=Qt[ph, a:z:stp], start=True, stop=True)
                            if hasp:
                                a2, z2, _ = blk_cols(r_, c_ - 1)
                                ins = e.matmul(ps_[:, 128:256], lhsT=Kt[ph, a2:z2:stp], rhs=Qt[ph, a:z:stp],
                                               start=True, stop=True)
                        return ins
                    P.op("pe", fn_s, r=list(Qtb[0:4]) + list(Ktb[0:4]), w=[psAb, psBb])
                    E_ = Eb[:, self.ei % 2, :]
                    Eb_ = Ebb[self.ei % 2]
                    self.ei += 1
                    P_ = Pt[:, self.pi % 4, :]
                    Pb_ = Ptb[self.pi % 4]
                    self.pi += 1
                    nv = 256 if hasp else 128
                    P.op("act", lambda e: e.activation(out=E_[:, 0:nv], in_=psA[:, 0:nv], func=AF.Exp), r=[psAb], w=[Eb_])
                    P.op("act", lambda e: e.activation(out=E_[:, 256:256 + nv], in_=psB[:, 0:nv], func=AF.Exp),
                         r=[psBb], w=[Eb_])
                    if hasp:
                        P.op("dve", lambda e: e.tensor_tensor(out=P_, in0=E_, in1=mk, op=ALU.mult), r=[Eb_, MASKS], w=[Pb_])
                    else:
                        v3 = lambda T: T.rearrange("p (a b) -> p a b", a=2)[:, :, 0:128]
                        P.op("dve", lambda e: e.tensor_tensor(out=v3(P_), in0=v3(E_), in1=v3(mk), op=ALU.mult),
                             r=[Eb_, MASKS], w=[Pb_])
                    return (qg, j, hasp, r_ * nblk + c_, P_, Pb_, po, pob, pz, pzb)

                def emit_PV(st):
                    qg, j, hasp, bcur, P_, Pb_, po, pob, pz, pzb = st

                    def fn_pv(e):
                        ins = None
                        for hh in range(2):
                            ph = slice(64 * hh, 64 * hh + 64)
                            oc = slice(128 * j, 128 * j + 128)
                            ins = e.matmul(po[ph, oc], lhsT=Vb[:, bcur, ph], rhs=P_[:, 256 * hh:256 * hh + 128],
                                           start=True, stop=not hasp)
                            if hasp:
                                ins = e.matmul(po[ph, oc], lhsT=Vb[:, bcur - 1, ph],
                                               rhs=P_[:, 256 * hh + 128:256 * hh + 256], start=False, stop=True)
                            ins = e.matmul(pz[ph, oc], lhsT=ones_bf[:, 0:64], rhs=P_[:, 256 * hh:256 * hh + 128],
                                           start=True, stop=not hasp)
                            if hasp:
                                ins = e.matmul(pz[ph, oc], lhsT=ones_bf[:, 0:64],
                                               rhs=P_[:, 256 * hh + 128:256 * hh + 256], start=False, stop=True)
                        return ins
                    P.op("pe", fn_pv, r=[Pb_, ONES] + Vbb, w=[pob, pzb])
                    if j == 3:
                        _, acc_sl = q_group(qg)
                        pv3 = (lambda T: T[:, :]) if gi < 2 else (lambda T: T[:, :].rearrange("p (r i) -> p r i", r=4))
                        if first:
                            P.op("dve", lambda e: e.tensor_copy(out=acc_sl(Oacc), in_=pv3(po)), r=[pob], w=Oaccb)
                            P.op("act", lambda e: e.activation(out=acc_sl(Zacc), in_=pv3(pz), func=AF.Copy), r=[pzb], w=Zaccb)
                        else:
                            P.op("dve", lambda e: e.tensor_tensor(out=acc_sl(Oacc), in0=pv3(po), in1=acc_sl(Oacc), op=ALU.add),
                                 r=[pob] + Oaccb, w=Oaccb)
                            P.op("dve", lambda e: e.tensor_tensor(out=acc_sl(Zacc), in0=pv3(pz), in1=acc_sl(Zacc), op=ALU.add),
                                 r=[pzb] + Zaccb, w=Zaccb)

                pso_full, psob = self.ps[7], self.psb[7]
                sstate = {}

                def s_st1(s):
                    sl = (s % 2)
                    kc_ = sKc[:, sl, :, :, :]
                    kcb = sKcb[sl]
                    for kv in range(2):
                        if gi == 0:
                            src = ck[gi][l, s, :, :].rearrange("j (k f) -> j k f", k=2)[:, kv, hp * 128:hp * 128 + 128]
                            P.op("sp", lambda e, src=src, kv=kv: e.dma_start(out=kc_[:, 0, kv, :], in_=src), w=[kcb], dma=kcb)
                        else:
                            src = ck[gi][l, s, :, :].rearrange("(j t) (k f) -> j t k f", t=d, k=2)[:, 0:4, kv,
                                                                                                 hp * 128:hp * 128 + 128]
                            P.op("sp", lambda e, src=src, kv=kv: e.dma_start(out=kc_[:, :, kv, :], in_=src), w=[kcb], dma=kcb)
                    pt, ptb = self.bank("sS")

                    def fn_kt(e):
                        ins = None
                        for t in range(nt_):
                            ins = e.transpose(pt[:, 128 * t:128 * t + 128], kc_[:, t, 0, :], ident[:])
                        return ins
                    P.op("pe", fn_kt, r=[kcb, CONST], w=[ptb])
                    kt_ = sKt[:, sl, :]
                    ktb_ = sKtb[sl]
                    P.op("dve", lambda e: e.tensor_copy(out=kt_[:, 0:128 * nt_], in_=pt[:, 0:128 * nt_]), r=[ptb], w=[ktb_])
                    vb_ = sVb[:, sl, :, :]
                    vbb_ = sVbb[sl]
                    P.op("act", lambda e: e.activation(out=vb_[:, 0:nt_, :], in_=kc_[:, 0:nt_, 1, :], func=AF.Copy), r=[kcb], w=[vbb_])
                    sstate[s] = (kc_, kcb, kt_, ktb_, vb_, vbb_)

                def s_st2(s):
                    kc_, kcb, kt_, ktb_, vb_, vbb_ = sstate[s]
                    pss0, pss0b = self.bank("sS")
                    pss1, pss1b = self.bank("sS")

                    def fn_ss(e):
                        ins = None
                        for hh, pss in ((0, pss0), (1, pss1)):
                            ph = slice(64 * hh, 64 * hh + 64)
                            for t in range(4):
                                tk = 0 if nt_ == 1 else t
                                ins = e.matmul(pss[:, t:t + 1], lhsT=kt_[ph, 128 * tk:128 * tk + 128],
                                               rhs=Qt[ph, S + 4 * s + t:S + 4 * s + t + 1], start=True, stop=True)
                            ins = e.matmul(pss[0:4, 16:20], lhsT=Kt[ph, S + 4 * s:S + 4 * s + 4],
                                           rhs=Qt[ph, S + 4 * s:S + 4 * s + 4], start=True, stop=True)
                        return ins
                    P.op("pe", fn_ss, r=[ktb_, Qtb[4], Ktb[4]], w=[pss0b, pss1b])
                    cs_ = slice(8 * s, 8 * s + 8)
                    for hh, pss, pssb in ((0, pss0, pss0b), (1, pss1, pss1b)):
                        c4 = slice(8 * s + 4 * hh, 8 * s + 4 * hh + 4)
                        P.op("act", lambda e, pss=pss, c4=c4: e.activation(out=sE[:, 0, c4], in_=pss[:, 0:4], func=AF.Exp),
                             r=[pssb], w=[sEb[0]])
                        P.op("act", lambda e, pss=pss, c4=c4: e.activation(out=sE[0:4, 1, c4], in_=pss[0:4, 16:20], func=AF.Exp),
                             r=[pssb], w=[sEb[1]])
                    P.op("dve", lambda e: e.tensor_tensor(out=sPp[:, cs_], in0=sE[:, 0, cs_], in1=smp[:, gidx, cs_], op=ALU.mult),
                         r=[sEb[0], SM], w=sPpb)
                    P.op("dve", lambda e: e.tensor_tensor(out=sPc[0:4, cs_], in0=sE[0:4, 1, cs_], in1=smc[0:4, gidx, cs_],
                                                          op=ALU.mult), r=[sEb[1], SM2], w=sPcb)

                def s_st3(s):
                    kc_, kcb, kt_, ktb_, vb_, vbb_ = sstate[s]

                    def fn_so(e):
                        ins = None
                        for hh in range(2):
                            ph = slice(64 * hh, 64 * hh + 64)
                            c0_ = s * 8 + hh * 4
                            o4 = slice(4 * s, 4 * s + 4)
                            z4 = slice(16 + 4 * s, 16 + 4 * s + 4)
                            if nt_ == 1:
                                ins = e.matmul(pso_full[ph, o4], lhsT=vb_[:, 0, ph], rhs=sPp[:, c0_:c0_ + 4], start=True, stop=False)
                            else:
                                for t in range(4):
                                    ins = e.matmul(pso_full[ph, 4 * s + t:4 * s + t + 1], lhsT=vb_[:, t, ph],
                                                   rhs=sPp[:, c0_ + t:c0_ + t + 1], start=(t == 0), stop=False,
                                                   skip_group_check=True)
                            ins = e.matmul(pso_full[ph, o4], lhsT=sNb[0:4, s, ph], rhs=sPc[0:4, c0_:c0_ + 4], start=False, stop=True,
                                           skip_group_check=True)
                            ins = e.matmul(pso_full[ph, z4], lhsT=ones_bf[:, 0:64], rhs=sPp[:, c0_:c0_ + 4], start=True, stop=False)
                            ins = e.matmul(pso_full[ph, z4], lhsT=ones_bf[0:4, 0:64], rhs=sPc[0:4, c0_:c0_ + 4], start=False, stop=True)
                        return ins
                    P.op("pe", fn_so, r=[vbb_, ONES] + sPpb + sPcb + sNbb, w=[psob])

                def s_acc(_):
                    if first:
                        P.op("dve", lambda e: e.tensor_copy(out=Oacc[:, S:NT], in_=pso_full[:, 0:16]), r=[psob], w=Oaccb)
                        P.op("dve", lambda e: e.tensor_copy(out=Zacc[:, S:NT], in_=pso_full[:, 16:32]), r=[psob], w=Zaccb)
                    else:
                        P.op("dve", lambda e: e.tensor_tensor(out=Oacc[:, S:NT], in0=pso_full[:, 0:16], in1=Oacc[:, S:NT],
                                                              op=ALU.add), r=[psob] + Oaccb, w=Oaccb)
                        P.op("dve", lambda e: e.tensor_tensor(out=Zacc[:, S:NT], in0=pso_full[:, 16:32], in1=Zacc[:, S:NT],
                                                              op=ALU.add), r=[psob] + Zaccb, w=Zaccb)

                ssteps = [(s_st1, 0), (s_st1, 1), (s_st2, 0), (s_st3, 0), (s_st1, 2), (s_st2, 1), (s_st3, 1), (s_st1, 3),
                          (s_st2, 2), (s_st3, 2), (s_st2, 3), (s_st3, 3), (s_acc, 0)]
                PIPE = 2
                pending = []
                qi = 0
                for qg in range(4):
                    qbl, _ = q_group(qg)
                    po, pob = self.bank("o")
                    pz, pzb = self.bank("z")
                    for j, (r_, c_) in enumerate(qbl):
                        pending.append(emit_S(qg, j, r_, c_, po, pob, pz, pzb))
                        if len(pending) > PIPE:
                            emit_PV(pending.pop(0))
                        if qi < len(ssteps):
                            f_, a_ = ssteps[qi]
                            f_(a_)
                        qi += 1
                while pending:
                    emit_PV(pending.pop(0))
                for f_, a_ in ssteps[qi:]:
                    f_(a_)
            P.op("dve", lambda e: e.reciprocal(out=Zacc[:, :], in_=Zacc[:, :]), r=Zaccb, w=Zaccb)
            P.op("dve", lambda e, hp=hp: e.tensor_tensor(out=oatt[:, hp, :], in0=Oacc[:, :], in1=Zacc[:, :], op=ALU.mult),
                 r=Zaccb + Oaccb, w=oattb)

        self.stage(3 + 10 * l)
        self.new_phase(off_co)
        m, mb = self.carve("m", [128, 8, NT], BF16, 5, area=0)
        sgc, sgcb = self.carve("sgc", [128, 2, 512], F32, 2)
        sga, sgab = self.carve("sga", [128, 2, 512], F32, 2)
        t1, t1b = self.carve("t1", [128, 2, 512], F32, 2)
        t2, t2b = self.carve("t2", [128, 2, 512], F32, 2)
        mi = 0
        for og in range(2):
            for half in range(2):
                Wpc, Wpcb = self.wload(w_pc[l, :, 512 * og:512 * og + 512], 4, 512)
                Wpa, Wpab = self.wload(w_pa[l, :, 512 * og:512 * og + 512], 4, 512)
                cbase = 4 * og + 2 * half
                c0 = 1024 + 4608 + 128 * cbase
                Wgc, Wgcb = self.wload(w_in[l, :, c0:c0 + 256], 8, 256)
                Wga, Wgab = self.wload(w_in[l, :, c0 + 1024:c0 + 1024 + 256], 8, 256)
                for cc in range(2):
                    c = cbase + cc
                    for tt in range(5):
                        t0, n = TT[tt]
                        pyc, pycb = self.bank("all")
                        self.mm_acc(pyc[:, 0:n], pycb, [(Wpc[:, k, 128 * (c % 4):128 * (c % 4) + 128], cact[:, k, t0:t0 + n])
                                                         for k in range(4)], [Wpcb, cactb[tt]])
                        pgc, pgcb = self.bank("all")
                        gemm_u(pgc, pgcb, Wgc, Wgcb, 128 * cc, tt, n)
                        pya, pyab = self.bank("all")
                        self.mm_acc(pya[:, 0:n], pyab, [(Wpa[:, k, 128 * (c % 4):128 * (c % 4) + 128], oatt[:, k, t0:t0 + n])
                                                         for k in range(4)], [Wpab] + oattb)
                        pga, pgab = self.bank("all")
                        gemm_u(pga, pgab, Wga, Wgab, 128 * cc, tt, n)
                        i2 = mi % 2
                        mi += 1
                        P.op("act", lambda e, pgc=pgc, i2=i2, c=c, n=n: e.activation(
                            out=sgc[:, i2, 0:n], in_=pgc[:, 0:n], func=AF.Sigmoid, bias=vcol(V_BIN + 44 + c), scale=1.0),
                            r=[pgcb, VEC], w=[sgcb[i2]])
                        P.op("act", lambda e, pga=pga, i2=i2, c=c, n=n: e.activation(
                            out=sga[:, i2, 0:n], in_=pga[:, 0:n], func=AF.Sigmoid, bias=vcol(V_BIN + 52 + c), scale=1.0),
                            r=[pgab, VEC], w=[sgab[i2]])
                        P.op("dve", lambda e, pyc=pyc, i2=i2, c=c, n=n: e.scalar_tensor_tensor(
                            out=t1[:, i2, 0:n], in0=pyc[:, 0:n], scalar=vcol(V_BPC + c), in1=sgc[:, i2, 0:n],
                            op0=ALU.add, op1=ALU.mult), r=[pycb, sgcb[i2], VEC], w=[t1b[i2]])
                        P.op("dve", lambda e, pya=pya, i2=i2, c=c, n=n: e.scalar_tensor_tensor(
                            out=t2[:, i2, 0:n], in0=pya[:, 0:n], scalar=vcol(V_BPA + c), in1=sga[:, i2, 0:n],
                            op0=ALU.add, op1=ALU.mult), r=[pyab, sgab[i2], VEC], w=[t2b[i2]])
                        P.op("dve", lambda e, i2=i2, c=c, t0=t0, n=n: e.tensor_tensor(
                            out=m[:, c, t0:t0 + n], in0=t1[:, i2, 0:n], in1=t2[:, i2, 0:n], op=ALU.add),
                            r=[t1b[i2], t2b[i2]], w=[mb[tt]])
        self.stage(4 + 10 * l)
        self.new_phase(A0_END, A0_END, A0_END)
        self.ln_wfull = w_out[l]
        self.ln_block(l, 1 + 2 * l, lambda c: w_out[l, :, 128 * c:128 * c + 128], 8, m, mb, G1P, final=False, env=g,
                      tiles=range(5))

        self.stage(5 + 10 * l)
        for half, tiles in enumerate(([0, 1], [2, 3, 4])):
            self.new_phase(0, 0, 0)
            c0 = TT[tiles[0]][0]
            ncol = sum(TT[t][1] for t in tiles)
            h, hb = self.carve("h", [128, NF, ncol], BF16, len(tiles))
            sgt, sgtb = self.carve("sgt", [128, 2, 512], F32, 2)
            hi = 0
            for fg in range(11):
                Wg_, Wgb_ = self.wload(w_gate[l, :, 256 * fg:256 * fg + 256], 8, 256)
                Wu_, Wub_ = self.wload(w_up[l, :, 256 * fg:256 * fg + 256], 8, 256)
                for ff in range(2):
                    f = 2 * fg + ff
                    for ti, tt in enumerate(tiles):
                        t0, n = TT[tt]
                        pgt, pgtb = self.bank("all")
                        gemm_u(pgt, pgtb, Wg_, Wgb_, 128 * ff, tt, n)
                        pup, pupb = self.bank("all")
                        gemm_u(pup, pupb, Wu_, Wub_, 128 * ff, tt, n)
                        i2 = hi % 2
                        hi += 1
                        P.op("act", lambda e, pgt=pgt, i2=i2, n=n: e.activation(out=sgt[:, i2, 0:n], in_=pgt[:, 0:n],
                                                                               func=AF.Silu), r=[pgtb], w=[sgtb[i2]])
                        P.op("dve", lambda e, pup=pup, i2=i2, f=f, t0=t0, n=n: e.tensor_tensor(
                            out=h[:, f, t0 - c0:t0 - c0 + n], in0=pup[:, 0:n], in1=sgt[:, i2, 0:n], op=ALU.mult),
                            r=[pupb, sgtb[i2]], w=[hb[ti]])
            hview = lambda k, t0, n, h=h, c0=c0: h[:, k, t0 - c0:t0 - c0 + n]
            self.ln_block(l, 2 + 2 * l, lambda c: w_down[l, :, 128 * c:128 * c + 128], NF, None, hb, G2P,
                          final=(l == L - 1), env=g, tiles=tiles, hview=hview)

    def ln_stats(self, pm, pmb, pq, pqb, stt, sttb, n, eps_t, ONES):
        P = self.P
        mean = stt[:, 0, :]
        m2 = stt[:, 1, :]
        rstd = stt[:, 2, :]
        P.op("dve", lambda e: e.tensor_copy(out=mean[:, 0:n], in_=pm[:, 0:n]), r=[pmb], w=[sttb[0]])
        P.op("dve", lambda e: e.tensor_tensor(out=m2[:, 0:n], in0=mean[:, 0:n], in1=mean[:, 0:n], op=ALU.mult),
             r=[sttb[0]], w=[sttb[1]])
        P.op("dve", lambda e: e.tensor_tensor(out=m2[:, 0:n], in0=pq[:, 0:n], in1=m2[:, 0:n], op=ALU.subtract),
             r=[pqb, sttb[1]], w=[sttb[1]])
        P.op("act", lambda e: e.activation(out=rstd[:, 0:n], in_=m2[:, 0:n], func=AF.Sqrt, bias=eps_t[:, 0:1], scale=1.0),
             r=[sttb[1], ONES], w=[sttb[2]])
        P.op("dve", lambda e: e.reciprocal(out=rstd[:, 0:n], in_=rstd[:, 0:n]), r=[sttb[2]], w=[sttb[2]])
        return mean, rstd

    def ln_block(self, l, stg, wsrc, kch, xin, xinb, gofs, final, env, tiles, hview=None):
        P = self.P
        g = env
        U, Ub, ident, CONST, ONES = g["U"], g["Ub"], g["ident"], g["CONST"], g["ONES"]
        onesD, eps_t, modT, MOD, coef, COEF = g["onesD"], g["eps_t"], g["modT"], g["MOD"], g["coef"], g["COEF"]
        rscr, RS, yp, ys = g["rscr"], g["RS"], g["yp"], g["ys"]
        produce = g["produce"]
        tiles = list(tiles)
        pipelined = (kch <= 8)
        groups = [[t] for t in tiles] if pipelined else [tiles[i:i + 2] for i in range(0, len(tiles), 2)]
        zt, ztb = self.carve("zt", [128, 8, 1024], F32, 16)
        rt, rtb = self.carve("rt", [128, 4, 512], F32, 4)
        zbq, zbqb = self.carve("zbq", [128, 4, 512], BF16, 4)
        stt, sttb = self.carve("stt", [128, 3, 512], F32, 3)
        if final:
            yst, ystb = self.carve("yst", [128, 2, 1024], F32, 2)
            rst, rstb = None, None
        else:
            rst, rstb = self.carve("rst", [128, 4, 512], F32, 4)
        bufs = (zt, ztb, rt, rtb, zbq, zbqb, stt, sttb, rst, rstb, (yst, ystb) if final else None)
        self.ln_wpre = None
        if pipelined:
            self.ln_wpre = [self.wload(self.ln_wfull[:, 512 * i:512 * i + 512], 8, 512) for i in range(2)]
            for gi_, grp in enumerate(groups):
                self.ln_group(l, stg, wsrc, kch, xin, xinb, gofs, final, env, tiles, grp, hview, bufs, gi_ % 2, "proj")
                if gi_ > 0:
                    self.ln_group(l, stg, wsrc, kch, xin, xinb, gofs, final, env, tiles, groups[gi_ - 1], hview, bufs,
                                  (gi_ - 1) % 2, "norm")
            self.ln_group(l, stg, wsrc, kch, xin, xinb, gofs, final, env, tiles, groups[-1], hview, bufs,
                          (len(groups) - 1) % 2, "norm")
        else:
            for grp in groups:
                self.ln_group(l, stg, wsrc, kch, xin, xinb, gofs, final, env, tiles, grp, hview, bufs, 0, "proj")
                self.ln_group(l, stg, wsrc, kch, xin, xinb, gofs, final, env, tiles, grp, hview, bufs, 0, "norm")

    def ln_group(self, l, stg, wsrc, kch, xin, xinb, gofs, final, env, alltiles, tiles, hview, bufs, slot, part):
        P = self.P
        g = env
        U, Ub, ident, CONST, ONES = g["U"], g["Ub"], g["ident"], g["CONST"], g["ONES"]
        onesD, eps_t, modT, MOD, coef, COEF = g["onesD"], g["eps_t"], g["modT"], g["MOD"], g["coef"], g["COEF"]
        rscr, RS, yp, ys = g["rscr"], g["RS"], g["yp"], g["ys"]
        produce = g["produce"]
        zt, ztb, rt, rtb, zbq, zbqb, stt, sttb, rst, rstb, ysts = bufs
        if final:
            yst, ystb = ysts
        ri = self.ln_ri
        for c in (range(8) if part == "proj" else []):
            if self.ln_wpre is not None:
                Wfull, Wb = self.ln_wpre[c // 4]
                Wt = Wfull[:, :, 128 * (c % 4):128 * (c % 4) + 128]
            else:
                Wt, Wb = self.wload(wsrc(c), kch, 128)
            for ti, tt in enumerate(tiles):
                t0, n = TT[tt]
                zoff = (slot + ti) * 512
                pz_, pzb_ = self.bank("acc")
                if hview is None:
                    pairs = [(Wt[:, k, :], xin[:, k, t0:t0 + n]) for k in range(kch)]
                    rb = [Wb, xinb[tt]]
                else:
                    pairs = [(Wt[:, k, :], hview(k, t0, n)) for k in range(kch)]
                    rb = [Wb, xinb[alltiles.index(tt)]]
                self.mm_acc(pz_[:, 0:n], pzb_, pairs, rb)
                r_ = rt[:, ri % 4, :]
                rb_ = rtb[ri % 4]
                ri += 1
                P.op("sp", lambda e, r_=r_, c=c, t0=t0, n=n: e.dma_start(out=r_[:, 0:n], in_=rscr[:, c, t0:t0 + n]),
                     r=[RS[c][tt]], w=[rb_], dma=rb_)
                for (cs, s) in segs(tt):
                    P.op("dve", lambda e, pz_=pz_, r_=r_, c=c, cs=cs, s=s, t0=t0: e.scalar_tensor_tensor(
                        out=zt[:, c, zoff + cs.start:zoff + cs.stop], in0=pz_[:, cs], scalar=modT[:, l, gofs + c, s:s + 1],
                        in1=r_[:, cs], op0=ALU.mult, op1=ALU.add), r=[pzb_, rb_, MOD], w=[ztb[(slot + ti) * 8 + c]])
        self.ln_ri = ri
        zi = 0
        for ti, tt in enumerate(tiles if part == "norm" else []):
            t0, n = TT[tt]
            o0 = (slot + ti) * 512
            pm, pmb = self.bank("o")
            pq, pqb = self.bank("z")
            for c in range(8):
                zb_ = zbq[:, zi % 4, :]
                zbb = zbqb[zi % 4]
                zi += 1
                zs_ = zbq[:, zi % 4, :]
                zsb = zbqb[zi % 4]
                zi += 1
                P.op("act", lambda e, zb_=zb_, c=c: e.activation(out=zb_[:, 0:n], in_=zt[:, c, o0:o0 + n], func=AF.Copy),
                     r=[ztb[(slot + ti) * 8 + c]], w=[zbb])
                P.op("act", lambda e, zs_=zs_, c=c: e.activation(out=zs_[:, 0:n], in_=zt[:, c, o0:o0 + n], func=AF.Square),
                     r=[ztb[(slot + ti) * 8 + c]], w=[zsb])
                P.op("pe", lambda e, pm=pm, zb_=zb_, c=c: e.matmul(pm[:, 0:n], lhsT=onesD[:], rhs=zb_[:, 0:n],
                                                                  start=(c == 0), stop=(c == 7)), r=[zbb, ONES], w=[pmb])
                P.op("pe", lambda e, pq=pq, zs_=zs_, c=c: e.matmul(pq[:, 0:n], lhsT=onesD[:], rhs=zs_[:, 0:n],
                                                                  start=(c == 0), stop=(c == 7)), r=[zsb, ONES], w=[pqb])
            mean, rstd = self.ln_stats(pm, pmb, pq, pqb, stt, sttb, n, eps_t, ONES)
            for c in range(8):
                zc = zt[:, c, o0:o0 + n]
                P.op("dve", lambda e, zc=zc: e.tensor_tensor(out=zc, in0=zc, in1=mean[:, 0:n], op=ALU.subtract),
                     r=[sttb[0], ztb[(slot + ti) * 8 + c]], w=[ztb[(slot + ti) * 8 + c]])
                P.op("dve", lambda e, zc=zc: e.tensor_tensor(out=zc, in0=zc, in1=rstd[:, 0:n], op=ALU.mult),
                     r=[sttb[2], ztb[(slot + ti) * 8 + c]], w=[ztb[(slot + ti) * 8 + c]])
                if not final:
                    produce(zc, [ztb[(slot + ti) * 8 + c]], c, tt, stg, rst[:, ri % 4, :], rstb[ri % 4])
                    ri += 1
                    self.ln_ri = ri
                else:
                    P.op("act", lambda e, zc=zc, c=c: e.activation(out=zc, in_=zc, func=AF.Identity,
                                                                   scale=coef[:, stg, 0, c, 0:1], bias=coef[:, stg, 1, c, 0:1]),
                         r=[COEF, ztb[(slot + ti) * 8 + c]], w=[ztb[(slot + ti) * 8 + c]])
            if final:
                nb = (n + 127) // 128
                for tb in range(nb):
                    rows = min(128, n - 128 * tb)
                    ysl = yst[:, tb % 2, :]
                    yslb = ystb[tb % 2]
                    for hf in range(2):
                        pt, ptb = self.bank("acc")

                        def fn_yt(e, pt=pt, hf=hf, tb=tb, rows=rows):
                            ins = None
                            for cc in range(4):
                                c = 4 * hf + cc
                                ins = e.transpose(pt[0:rows, 128 * cc:128 * cc + 128],
                                                  zt[:, c, o0 + 128 * tb:o0 + 128 * tb + rows], ident[:])
                            return ins
                        P.op("pe", fn_yt, r=ztb[(slot + ti) * 8:(slot + ti) * 8 + 8] + [CONST], w=[ptb])
                        P.op("act", lambda e, pt=pt, hf=hf, ysl=ysl, rows=rows: e.activation(
                            out=ysl[0:rows, 512 * hf:512 * hf + 512], in_=pt[0:rows, :], func=AF.Copy), r=[ptb], w=[yslb])
                    if tt < 4:
                        dst = yp[t0 + 128 * tb:t0 + 128 * tb + 128, :]
                    else:
                        dst = ys[:, :]
                    P.op("sp", lambda e, dst=dst, ysl=ysl, rows=rows: e.dma_start(out=dst, in_=ysl[0:rows, :]),
                         r=[yslb], w=[Buf("x")], dma=yslb)


_CACHE = {}


def _consts():
    slopes = 2.0 ** (-8.0 * (np.arange(8) + 1) / 8.0)
    k = np.arange(128)[:, None].astype(np.float64)
    q = np.arange(128)[None, :].astype(np.float64)
    masks = np.zeros((128, 12, 512), np.float64)
    smp = np.zeros((128, 12, 32), np.float64)
    smc = np.zeros((4, 12, 32), np.float64)
    for g, d in enumerate(DIL):
        for hp in range(4):
            for hh in range(2):
                h = 2 * hp + hh
                cur = np.where(k <= q, np.exp(-slopes[h] * d * np.maximum(q - k, 0.0)), 0.0)
                prev = np.where(k >= q, np.exp(-slopes[h] * d * np.maximum(128 + q - k, 0.0)), 0.0)
                masks[:, g * 4 + hp, (2 * hh) * 128:(2 * hh + 1) * 128] = cur
                masks[:, g * 4 + hp, (2 * hh + 1) * 128:(2 * hh + 2) * 128] = prev
                for s in range(4):
                    for t in range(4):
                        col = s * 8 + hh * 4 + t
                        smp[:, g * 4 + hp, col] = prev[:, t if g == 0 else 0]
                        if g == 0:
                            smc[:, g * 4 + hp, col] = cur[0:4, t]
                        else:
                            smc[:, g * 4 + hp, col] = (np.arange(4) == t).astype(np.float64)
    return (np.eye(128, dtype=np.float32), masks.reshape(128, -1).astype(np.float32),
            smp.reshape(128, -1).astype(np.float32), smc.reshape(4, -1).astype(np.float32))


def _pack_vecs(inp):
    out = np.zeros((L, NVEC, 128), np.float32)
    for l in range(L):
        rows = [inp["b_in"][l].reshape(60, 128), inp["b_ada"][l].reshape(48, 128), inp["b_pc"][l].reshape(8, 128),
                inp["b_pa"][l].reshape(8, 128), inp["b_out"][l].reshape(8, 128), inp["ln1_g"][l].reshape(8, 128),
                inp["ln1_b"][l].reshape(8, 128), inp["ln2_g"][l].reshape(8, 128), inp["ln2_b"][l].reshape(8, 128),
                inp["b_dw"][l].reshape(4, 128), inp["conv_ln_g"][l].reshape(4, 128), inp["conv_ln_b"][l].reshape(4, 128),
                inp["w_dw"][l].reshape(124, 128)]
        cat = np.concatenate(rows, axis=0)
        out[l, :cat.shape[0]] = cat
    return out


def get_nc():
    global KSTAGE
    import os
    KSTAGE = int(os.environ.get("KSTAGE", "99"))
    if "nc" not in _CACHE:
        _CACHE["nc"] = Builder().build()
    return _CACHE["nc"]


def make_in_maps(inp, cores):
    f = lambda a: np.ascontiguousarray(np.asarray(a, dtype=np.float32))
    ident, masks, smp, smc = _consts()
    vecs = _pack_vecs({k: np.asarray(v) for k, v in inp.items()})
    shared = {k: f(inp[k]) for k in ("w_ada", "w_in", "w_pc", "w_pa", "w_out", "w_gate", "w_up", "w_down")}
    shared.update(vecs=vecs, ident=ident, masks=masks, smp=smp, smc=smc)
    maps = []
    for i in cores:
        m = dict(shared)
        m["xp"] = f(inp["x_prompt"][i])
        m["xs"] = f(np.asarray(inp["x_sample"][4 * i:4 * i + 4]).reshape(NS, D))
        m["c5"] = f(np.concatenate([np.asarray(inp["c_prompt"][i:i + 1]), np.asarray(inp["c_sample"][4 * i:4 * i + 4])], 0))
        caches = (inp["cache_kv_g0"], inp["cache_kv_g1"], inp["cache_kv_g2"])
        for g in range(3):
            m["ck%d" % g] = f(np.asarray(caches[g][:, 4 * i:4 * i + 4]).reshape(L, 4, WIN[g], 1024))
        m["stc"] = f(inp["state_conv"][:, 4 * i:4 * i + 4])
        maps.append(m)
    return maps


def kernel(**inputs):
    nc = get_nc()
    cores = list(range(NCORES))
    maps = make_in_maps(inputs, cores)
    res = run_bass_kernel_spmd(nc, maps, core_ids=cores)
    R = res.results
    y_p = np.stack([R[i]["yp"] for i in cores]).astype(np.float32)
    y_s = np.concatenate([R[i]["ys"].reshape(4, 4, D) for i in cores], 0).astype(np.float32)
    outs = [y_p, y_s]
    for g in range(3):
        outs.append(np.stack([R[i]["kvp%d" % g] for i in cores], 1).reshape(L, NCORES, WIN[g], 2, 8, 64).astype(np.float32))
    outs.append(np.stack([R[i]["cvp"] for i in cores], 1).astype(np.float32))
    for g in range(3):
        outs.append(np.concatenate([R[i]["kvs%d" % g] for i in cores], 1).reshape(L, 32, WIN[g], 2, 8, 64).astype(np.float32))
    outs.append(np.concatenate([R[i]["cvs"] for i in cores], 1).astype(np.float32))
    return tuple(outs)
```

```python
import contextlib
import numpy as np
import concourse.bass as bass
import concourse.mybir as mybir
from concourse.bass_utils import run_bass_kernel_spmd

F32 = mybir.dt.float32
BF16 = mybir.dt.bfloat16
AF = mybir.ActivationFunctionType
ALU = mybir.AluOpType

NCORES = 8
D = 1024
S = 2048
NS = 16
NT = S + NS
L = 2
DFF = 2816
NF = DFF // 128
ALPHA = float((2 * L) ** 0.25)
EPS = 1e-5
TT = [(0, 512), (512, 512), (1024, 512), (1536, 512), (2048, 16)]
DIL = (1, 4, 16)
WIN = (128, 512, 2048)
ENGS = ("sp", "act", "pool", "dve", "pe")
SAME_SYNC = True
NSLOT = 4
A0_END = 8 * NT * 2
RB_BYTES = 112 * 1024

V_BIN, V_BADA, V_BPC, V_BPA, V_BOUT, V_L1G, V_L1B, V_L2G, V_L2B, V_BDW, V_CLG, V_CLB, V_WDW = (
    0, 60, 108, 116, 124, 132, 140, 148, 156, 164, 168, 172, 176)
NVEC = 384


def segs(tt):
    if tt < 4:
        return [(slice(0, 512), 0)]
    return [(slice(4 * s, 4 * s + 4), 1 + s) for s in range(4)]


class DSem:
    def __init__(self, h):
        self.h = h
        self.count = 0
        self.last_buf = None
        self.last_op = None


class Buf:
    __slots__ = ("name", "w", "r", "dsem", "rng", "phase", "psum")

    def __init__(self, name, rng=None, phase=None):
        self.name = name
        self.psum = False
        self.w = {}
        self.r = {}
        self.dsem = None
        self.rng = rng
        self.phase = phase


class Op:
    __slots__ = ("eng", "fn", "deps", "sig", "sigidx", "dsem", "sigval", "seq", "calls")


class Rec:
    def __init__(self):
        self.calls = []

    def __getattr__(self, name):
        def f(*a, **k):
            self.calls.append((name, a, k))
            return self
        return f


class Prog:
    def __init__(self, nc, stack):
        self.nc = nc
        self.stack = stack
        self.ops = {e: [] for e in ENGS}
        self.dsems = []
        self.seq = 0
        self.esem = {e: stack.enter_context(nc.semaphore("es_" + e)) for e in ENGS}
        self.region_bufs = []
        self.phase = 0
        self.dsem_by_name = {}

    def dsem_of(self, buf):
        if buf.dsem is None:
            d = self.dsem_by_name.get(buf.name)
            if d is None:
                h = self.stack.enter_context(self.nc.semaphore("ds%d" % len(self.dsems)))
                d = DSem(h)
                self.dsems.append(d)
                self.dsem_by_name[buf.name] = d
            buf.dsem = d
        return buf.dsem

    @staticmethod
    def _key(o):
        return ("d", id(o.dsem)) if o.dsem is not None else ("e", o.eng)

    def op(self, eng, fn, r=(), w=(), dma=None):
        o = Op()
        o.eng = eng
        o.fn = fn
        rec = Rec()
        fn(rec)
        o.calls = rec.calls
        assert len(o.calls) > 0
        o.sig = False
        o.sigidx = 0
        o.dsem = None
        o.sigval = 0
        self.seq += 1
        o.seq = self.seq
        deps = {}
        strong = {}

        def add(p, st=True):
            k = self._key(p)
            q = deps.get(k)
            if q is None or q.seq < p.seq:
                deps[k] = p
            if st:
                q = strong.get(k)
                if q is None or q.seq < p.seq:
                    strong[k] = p

        for b in r:
            for p in b.w.values():
                add(p)
            if b.psum and eng in ("act", "dve"):
                for p in b.r.values():
                    if p.eng != eng and p.eng in ("act", "dve"):
                        add(p)
        for b in w:
            for p in b.w.values():
                add(p, p.eng != eng)
            for p in b.r.values():
                add(p, p.eng != eng)
        k_self = ("e", eng)
        if k_self in deps and dma is None:
            if k_self in strong:
                deps[k_self] = strong[k_self]
            else:
                del deps[k_self]
        if dma is not None:
            dsm = self.dsem_of(dma)
            if dsm.last_buf is not dma and dsm.last_op is not None:
                deps[("d", id(dsm))] = dsm.last_op
        o.deps = deps
        if dma is not None:
            o.dsem = self.dsem_of(dma)
            o.dsem.count += 1
            o.sigval = 16 * o.dsem.count
            o.dsem.last_buf = dma
            o.dsem.last_op = o
        k = self._key(o)
        for b in r:
            b.r[k] = o
        for b in w:
            b.w = {k: o}
            b.r = {}
        self.ops[eng].append(o)
        return o

    def rbuf(self, name, off, nbytes):
        b = Buf(name, (off, off + nbytes), self.phase)
        for ob in self.region_bufs:
            if ob.phase != self.phase and ob.rng[0] < b.rng[1] and b.rng[0] < ob.rng[1]:
                for p in list(ob.w.values()) + list(ob.r.values()):
                    k = self._key(p)
                    q = b.w.get(k)
                    if q is None or q.seq < p.seq:
                        b.w[k] = p
        self.region_bufs.append(b)
        return b

    def prune_region(self):
        pass

    def finalize(self):
        for e in ENGS:
            for o in self.ops[e]:
                for p in o.deps.values():
                    if p.dsem is None and not (p.eng == o.eng and (o.eng == "pe" or not SAME_SYNC)):
                        p.sig = True
        for e in ENGS:
            i = 0
            for o in self.ops[e]:
                if o.sig:
                    i += 1
                    o.sigidx = i

    def emit(self, eng, e):
        known = {}
        for o in self.ops[eng]:
            for p in sorted(o.deps.values(), key=lambda x: x.seq):
                if p.dsem is not None:
                    sem, val, kk = p.dsem.h, p.sigval, id(p.dsem)
                else:
                    if p.eng == eng and (eng == "pe" or not SAME_SYNC):
                        continue
                    sem, val, kk = self.esem[p.eng], p.sigidx, p.eng
                if known.get(kk, 0) >= val:
                    continue
                e.wait_ge(sem, val)
                known[kk] = val
            if o.fn is None:
                continue
            ins = None
            for (name, a, k) in o.calls:
                ins = getattr(e, name)(*a, **k)
            if o.dsem is not None:
                ins.then_inc(o.dsem.h, 16)
            elif o.sig:
                ins.then_inc(self.esem[eng], 1)


class StopBuild(Exception):
    pass


KSTAGE = 99


class Builder:
    def stage(self, k):
        if not hasattr(self, "marks"):
            self.marks = []
        self.marks.append((k, sum(len(o.calls) for o in self.P.ops["pe"])))
        if KSTAGE <= k:
            raise StopBuild()

    def __init__(self):
        self.nc = bass.Bass("TRN2", target_bir_lowering=False)
        self.stack = contextlib.ExitStack()

    def sb(self, name, shape, dt):
        return self.stack.enter_context(self.nc.sbuf_tensor("sb_" + name, list(shape), dt))

    def carve(self, name, shape, dt, nbufs=1, area=1):
        esz = 4 if dt == F32 else 2
        per = int(np.prod(shape[1:])) * esz
        per = (per + 63) // 64 * 64
        off = self.rb_offs[area]
        assert off % 4 == 0
        self.rb_offs[area] += per
        lim = self.rb_lims[area]
        assert self.rb_offs[area] <= lim, (name, area, self.rb_offs[area], lim)
        a = self.regB[:, off // 4:(off + per) // 4]
        if dt == BF16:
            a = a.bitcast(BF16)
        n = int(np.prod(shape[1:]))
        a = a[:, 0:n]
        if len(shape) > 2:
            names = ["d%d" % i for i in range(len(shape) - 1)]
            kw = {names[i]: shape[1 + i] for i in range(len(shape) - 2)}
            a = a.rearrange("p (%s) -> p %s" % (" ".join(names), " ".join(names)), **kw)
        bufs = [self.P.rbuf("%s_%d" % (name, i), off, per) for i in range(nbufs)]
        return a, bufs

    def new_phase(self, start1, start0=0, lim0=A0_END):
        self.P.phase += 1
        self.rb_offs = [start0, start1]
        self.rb_lims = [lim0, RB_BYTES]

    def wload(self, src2d, kch, ncols):
        i = self.ring_i % NSLOT
        self.ring_i += 1
        slot = self.ring[i]
        buf = self.ringb[i]
        view = slot[:, 0:kch * ncols].rearrange("p (k n) -> p k n", k=kch)
        src = src2d.rearrange("(k p) n -> p k n", p=128)
        self.P.op("pool", lambda e, view=view, src=src: e.dma_start(out=view, in_=src), w=[buf], dma=buf)
        return view, buf

    def bank(self, pool):
        lst = self.pools[pool]
        i = self.pool_i[pool] % len(lst)
        self.pool_i[pool] += 1
        b = lst[i]
        return self.ps[b], self.psb[b]

    def mm_acc(self, out, outbuf, pairs, rbufs):
        def fn(e, out=out, pairs=pairs):
            n = len(pairs)
            ins = None
            for i, (lt, rh) in enumerate(pairs):
                ins = e.matmul(out, lhsT=lt, rhs=rh, start=(i == 0), stop=(i == n - 1))
            return ins
        return self.P.op("pe", fn, r=rbufs, w=[outbuf])

    def build(self):
        nc = self.nc
        st = self.stack
        P = self.P = Prog(nc, st)

        def din(name, shape):
            return nc.dram_tensor(name, list(shape), F32, kind="ExternalInput").ap()

        def dout(name, shape):
            return nc.dram_tensor(name, list(shape), F32, kind="ExternalOutput").ap()

        xp = din("xp", [S, D])
        xs = din("xs", [NS, D])
        c5 = din("c5", [5, D])
        ck = [din("ck%d" % g, [L, 4, WIN[g], 2 * 512]) for g in range(3)]
        stc = din("stc", [L, 4, 30, 512])
        w_ada = din("w_ada", [L, D, 6 * D])
        w_in = din("w_in", [L, D, 7680])
        w_pc = din("w_pc", [L, 512, D])
        w_pa = din("w_pa", [L, 512, D])
        w_out = din("w_out", [L, D, D])
        w_gate = din("w_gate", [L, D, DFF])
        w_up = din("w_up", [L, D, DFF])
        w_down = din("w_down", [L, DFF, D])
        vecs = din("vecs", [L, NVEC, 128])
        ident_d = din("ident", [128, 128])
        masks_d = din("masks", [128, 12 * 512])
        smp_d = din("smp", [128, 12 * 32])
        smc_d = din("smc", [4, 12 * 32])

        yp = dout("yp", [S, D])
        ys = dout("ys", [NS, D])
        kvp = [dout("kvp%d" % g, [L, WIN[g], 2, 512]) for g in range(3)]
        cvp = dout("cvp", [L, 30, 512])
        kvs = [dout("kvs%d" % g, [L, 4, WIN[g], 2 * 512]) for g in range(3)]
        cvs = dout("cvs", [L, 4, 30, 512])
        rscr = nc.dram_tensor("rscr", [128, 8, NT], F32).ap()

        U = self.sb("U", [128, 8, NT], BF16)
        Ub = [[Buf("U%d_%d" % (c, t)) for t in range(5)] for c in range(8)]
        self.ring = [self.sb("ring%d" % i, [128, 4096], BF16) for i in range(NSLOT)]
        self.ringb = [Buf("ring%d" % i) for i in range(NSLOT)]
        self.ring_i = 0
        self.ln_ri = 0
        ident = self.sb("ident", [128, 128], F32)
        ones_bf = self.sb("ones_bf", [128, 128], BF16)
        onesD = self.sb("onesD", [128, 128], BF16)
        onesC = self.sb("onesC", [128, 128], BF16)
        ones_f = self.sb("ones_f", [128, 64], F32)
        eps_t = self.sb("eps_t", [128, 1], F32)
        alpha_t = self.sb("alpha_t", [128, 8, 5], F32)
        masks = self.sb("masks", [128, 12, 512], BF16)
        smp = self.sb("smp", [128, 12, 32], F32)
        smc = self.sb("smc", [4, 12, 32], F32)
        vecT = self.sb("vecT", [128, L, NVEC], F32)
        bq8 = self.sb("bq8", [128, L, 12], F32)
        modT = self.sb("modT", [128, L, 48, 5], F32)
        coef = self.sb("coef", [128, 5, 4, 8, 5], F32)
        scT = self.sb("scT", [128, 8, 5], BF16)
        ctmp = self.sb("ctmp", [128, 8, 5], F32)
        CONST = Buf("const")
        MASKS = Buf("masks")
        VEC = Buf("vec")
        MOD = Buf("mod")
        COEF = Buf("coef")
        SCT = Buf("sct")
        self.regB = self.sb("regB", [128, RB_BYTES // 4], F32)
        self.rb_offs = [0, 0]
        self.rb_lims = [0, RB_BYTES]

        self.ps = [st.enter_context(nc.psum_tensor("ps%d" % i, [128, 512], F32)) for i in range(8)]
        self.psb = [Buf("ps%d" % i) for i in range(8)]
        for b in self.psb:
            b.psum = True
        self.pools = {"acc": [0, 1, 2], "o": [3, 4], "z": [5, 6], "misc": [7], "sS": [0, 1, 2], "tr": [7, 0, 1, 2], "all": [0, 1, 2, 3, 4, 5, 6, 7]}
        self.pool_i = {k: 0 for k in self.pools}

        def vcol(l, row):
            return vecT[:, l, row:row + 1]

        self.new_phase(0, 0, 0)
        P.op("sp", lambda e: e.dma_start(out=ident[:], in_=ident_d), w=[CONST], dma=CONST)
        P.op("pool", lambda e: e.dma_start(out=masks[:], in_=masks_d.rearrange("p (a b) -> p a b", a=12)),
             w=[MASKS], dma=MASKS)
        SM = Buf("sm")
        P.op("sp", lambda e: e.dma_start(out=smp[:], in_=smp_d.rearrange("p (a b) -> p a b", a=12)), w=[SM], dma=SM)
        SM2 = Buf("sm2")
        P.op("sp", lambda e: e.dma_start(out=smc[:], in_=smc_d.rearrange("p (a b) -> p a b", a=12)), w=[SM2], dma=SM2)
        ONES = Buf("ones")
        P.op("dve", lambda e: e.memset(ones_bf[:], 1.0), w=[ONES])
        P.op("dve", lambda e: e.memset(onesD[:], 1.0 / 1024.0), w=[ONES])
        P.op("dve", lambda e: e.memset(onesC[:], 1.0 / 512.0), w=[ONES])
        P.op("dve", lambda e: e.memset(ones_f[:], 1.0), w=[ONES])
        P.op("dve", lambda e: e.memset(eps_t[:], EPS), w=[ONES])
        P.op("dve", lambda e: e.memset(alpha_t[:], ALPHA), w=[ONES])

        D2D = [Buf("d2d%d" % i) for i in range(4)]
        self.d2d_list = []
        self.d2d_i = 0
        for l in range(L):
            for s in range(4):
                self.d2d_list.append((stc[l, s, 4:30, :], cvs[l, s, 0:26, :]))
        nsmall = len(self.d2d_list)
        for l in range(L):
            for g in range(3):
                W = WIN[g]
                for s in range(4):
                    self.d2d_list.append((ck[g][l, s, 4:W, :], kvs[g][l, s, 0:W - 4, :]))

        def emit_d2d(n):
            for _ in range(n):
                if self.d2d_i >= len(self.d2d_list):
                    return
                src, dst = self.d2d_list[self.d2d_i]
                bq = D2D[self.d2d_i % 4]
                self.d2d_i += 1
                P.op("act", lambda e, src=src, dst=dst: e.dma_start(out=dst, in_=src), w=[bq], dma=bq)
        self.emit_d2d = emit_d2d
        emit_d2d(nsmall)

        vraw, vrawb = self.carve("vraw", [128, 2, 128], F32, 2)
        for l in range(L):
            for j in range(3):
                i = (l * 3 + j) % 2
                P.op("sp", lambda e, i=i, l=l, j=j: e.dma_start(out=vraw[:, i, :], in_=vecs[l, 128 * j:128 * j + 128, :]),
                     w=[vrawb[i]], dma=vrawb[i])
                pt, ptb = self.bank("misc")
                P.op("pe", lambda e, pt=pt, i=i: e.transpose(pt[:, 0:128], vraw[:, i, :], ident[:]),
                     r=[vrawb[i], CONST], w=[ptb])
                P.op("dve", lambda e, pt=pt, l=l, j=j: e.tensor_copy(out=vecT[:, l, 128 * j:128 * j + 128], in_=pt[:, 0:128]),
                     r=[ptb], w=[VEC])
        for l in range(L):
            P.op("dve", lambda e, l=l: e.tensor_scalar(out=bq8[:, l, :], in0=vecT[:, l, 8:20], scalar1=0.125,
                                                       scalar2=None, op0=ALU.mult), r=[VEC], w=[VEC])
        c5t, c5b = self.carve("c5t", [128, 1024], F32, 1)
        P.op("sp", lambda e: e.dma_start(out=c5t[0:5, :], in_=c5), w=c5b, dma=c5b[0])
        pt, ptb = self.bank("misc")

        def fn_ct(e, pt=pt):
            ins = None
            for k in range(8):
                ins = e.transpose(pt[:, 5 * k:5 * k + 5], c5t[0:5, 128 * k:128 * k + 128], ident[0:5, 0:5])
            return ins
        P.op("pe", fn_ct, r=[c5b[0], CONST], w=[ptb])
        P.op("act", lambda e, pt=pt: e.activation(out=scT[:], in_=pt[:, 0:40].rearrange("p (a b) -> p a b", a=8),
                                                  func=AF.Silu), r=[ptb], w=[SCT])
        for l in range(L):
            pm, pmb = self.bank("acc")
            for grp in range(12):
                Wt, Wb = self.wload(w_ada[l, :, 512 * grp:512 * grp + 512], 8, 512)

                def fn_mod(e, Wt=Wt, pm=pm, grp=grp):
                    ins = None
                    for jj in range(4):
                        j = 4 * grp + jj
                        for k in range(8):
                            ins = e.matmul(pm[:, 5 * j:5 * j + 5], lhsT=Wt[:, k, 128 * jj:128 * jj + 128],
                                           rhs=scT[:, k, :], start=(k == 0), stop=(k == 7))
                    return ins
                P.op("pe", fn_mod, r=[Wb, SCT], w=[pmb])
            P.op("dve", lambda e, pm=pm, l=l: e.tensor_tensor(
                out=modT[:, l], in0=pm[:, 0:240].rearrange("p (a b) -> p a b", a=48),
                in1=vecT[:, l, V_BADA:V_BADA + 48].unsqueeze(2).to_broadcast([128, 48, 5]), op=ALU.add),
                r=[pmb, VEC], w=[MOD])
            for (a, b) in ((8, 24), (32, 48)):
                P.op("dve", lambda e, l=l, a=a, b=b: e.tensor_scalar(
                    out=modT[:, l, a:b], in0=modT[:, l, a:b], scalar1=1.0, scalar2=None, op0=ALU.add),
                    r=[MOD], w=[MOD])

        RS = [[Buf("rs%d_%d" % (c, t)) for t in range(5)] for c in range(8)]

        def vb(l, row):
            return vecT[:, l, row:row + 8].unsqueeze(2).to_broadcast([128, 8, 5])

        def cop(fn):
            P.op("dve", fn, r=[MOD, VEC, COEF, ONES], w=[COEF])

        SH1, SC1P, G1P, SH2, SC2P, G2P = 0, 8, 16, 24, 32, 40
        cop(lambda e: e.tensor_copy(out=coef[:, 0, 0], in_=modT[:, 0, SC1P:SC1P + 8]))
        cop(lambda e: e.tensor_copy(out=coef[:, 0, 1], in_=modT[:, 0, SH1:SH1 + 8]))
        cop(lambda e: e.tensor_copy(out=coef[:, 0, 2], in_=alpha_t[:]))
        cop(lambda e: e.tensor_tensor(out=coef[:, 0, 3], in0=modT[:, 0, G1P:G1P + 8], in1=vb(0, V_BOUT), op=ALU.mult))
        for l in range(L):
            s1 = 1 + 2 * l
            cop(lambda e, l=l, s1=s1: e.tensor_tensor(out=coef[:, s1, 0], in0=modT[:, l, SC2P:SC2P + 8],
                                                      in1=vb(l, V_L1G), op=ALU.mult))
            cop(lambda e, l=l, s1=s1: e.tensor_tensor(out=coef[:, s1, 1], in0=modT[:, l, SC2P:SC2P + 8],
                                                      in1=vb(l, V_L1B), op=ALU.mult))
            cop(lambda e, l=l, s1=s1: e.tensor_tensor(out=coef[:, s1, 1], in0=coef[:, s1, 1],
                                                      in1=modT[:, l, SH2:SH2 + 8], op=ALU.add))
            cop(lambda e, l=l, s1=s1: e.tensor_tensor(out=coef[:, s1, 2], in0=alpha_t[:], in1=vb(l, V_L1G), op=ALU.mult))
            cop(lambda e, l=l, s1=s1: e.tensor_tensor(out=coef[:, s1, 3], in0=alpha_t[:], in1=vb(l, V_L1B), op=ALU.mult))
            s2 = 2 + 2 * l
            if l + 1 < L:
                n = l + 1
                cop(lambda e, l=l, s2=s2, n=n: e.tensor_tensor(out=coef[:, s2, 0], in0=modT[:, n, SC1P:SC1P + 8],
                                                               in1=vb(l, V_L2G), op=ALU.mult))
                cop(lambda e, l=l, s2=s2, n=n: e.tensor_tensor(out=coef[:, s2, 1], in0=modT[:, n, SC1P:SC1P + 8],
                                                               in1=vb(l, V_L2B), op=ALU.mult))
                cop(lambda e, l=l, s2=s2, n=n: e.tensor_tensor(out=coef[:, s2, 1], in0=coef[:, s2, 1],
                                                               in1=modT[:, n, SH1:SH1 + 8], op=ALU.add))
                cop(lambda e, l=l, s2=s2: e.tensor_tensor(out=coef[:, s2, 2], in0=alpha_t[:], in1=vb(l, V_L2G), op=ALU.mult))
                cop(lambda e, l=l, s2=s2: e.tensor_tensor(out=coef[:, s2, 3], in0=alpha_t[:], in1=vb(l, V_L2B), op=ALU.mult))
                cop(lambda e, n=n: e.tensor_tensor(out=ctmp[:], in0=modT[:, n, G1P:G1P + 8], in1=vb(n, V_BOUT), op=ALU.mult))
                cop(lambda e, s2=s2: e.tensor_tensor(out=coef[:, s2, 3], in0=coef[:, s2, 3], in1=ctmp[:], op=ALU.add))
            else:
                cop(lambda e, l=l, s2=s2: e.tensor_copy(out=coef[:, s2, 0], in_=vb(l, V_L2G)))
                cop(lambda e, l=l, s2=s2: e.tensor_copy(out=coef[:, s2, 1], in_=vb(l, V_L2B)))

        def produce(src, srcbufs, c, tt, stg, rst, rstb, src_psum=False):
            t0, n = TT[tt]
            for (cs, s) in segs(tt):
                P.op("act", lambda e, cs=cs, s=s: e.activation(
                    out=U[:, c, t0 + cs.start:t0 + cs.stop], in_=src[:, cs], func=AF.Identity,
                    scale=coef[:, stg, 0, c, s:s + 1], bias=coef[:, stg, 1, c, s:s + 1]),
                    r=srcbufs + [COEF], w=[Ub[c][tt]])
                if src_psum:
                    P.op("dve", lambda e, cs=cs, s=s: e.tensor_scalar(
                        out=rst[:, cs], in0=src[:, cs], scalar1=coef[:, stg, 2, c, s:s + 1],
                        scalar2=coef[:, stg, 3, c, s:s + 1], op0=ALU.mult, op1=ALU.add),
                        r=srcbufs + [COEF], w=[rstb])
                else:
                    P.op("act", lambda e, cs=cs, s=s: e.activation(
                        out=rst[:, cs], in_=src[:, cs], func=AF.Identity, scale=coef[:, stg, 2, c, s:s + 1],
                        bias=coef[:, stg, 3, c, s:s + 1]), r=srcbufs + [COEF], w=[rstb])
            P.op("sp", lambda e: e.dma_start(out=rscr[:, c, t0:t0 + n], in_=rst[:, 0:n]), r=[rstb], w=[RS[c][tt]],
                 dma=rstb)

        xst, xstb = self.carve("xst", [128, 4, 1024], F32, 4)
        XIN = KSTAGE > 0
        rst, rstb = self.carve("rst", [128, 4, 512], F32, 4)
        ri = 0
        for tt in (range(5) if XIN else []):
            t0, n = TT[tt]
            if tt < 4:
                for tb in range(4):
                    P.op("sp", lambda e, tb=tb, t0=t0: e.dma_start(out=xst[:, tb, :], in_=xp[t0 + 128 * tb:t0 + 128 * tb + 128, :]),
                         w=[xstb[tb]], dma=xstb[tb])
            else:
                P.op("sp", lambda e: e.dma_start(out=xst[0:16, 0, :], in_=xs), w=[xstb[0]], dma=xstb[0])
            for c in range(8):
                pt, ptb = self.bank("acc")
                if tt < 4:
                    def fn_xt(e, pt=pt, c=c):
                        ins = None
                        for tb in range(4):
                            ins = e.transpose(pt[:, 128 * tb:128 * tb + 128], xst[:, tb, 128 * c:128 * c + 128], ident[:])
                        return ins
                    P.op("pe", fn_xt, r=xstb + [CONST], w=[ptb])
                else:
                    P.op("pe", lambda e, pt=pt, c=c: e.transpose(pt[:, 0:16], xst[0:16, 0, 128 * c:128 * c + 128],
                                                                 ident[0:16, 0:16]), r=[xstb[0], CONST], w=[ptb])
                produce(pt, [ptb], c, tt, 0, rst[:, ri % 4, :], rstb[ri % 4], src_psum=True)
                ri += 1

        try:
            self.stage(1)
            for l in range(L):
                self.layer(l, locals())
        except StopBuild:
            pass

        self.emit_d2d(1000)
        last = []
        allops = [o for e in ENGS for o in P.ops[e] if o.dsem is not None]
        lastd = {}
        for o in allops:
            q = lastd.get(id(o.dsem))
            if q is None or q.sigval < o.sigval:
                lastd[id(o.dsem)] = o
        fin = Op()
        fin.eng = "sp"
        fin.fn = None
        fin.sig = False
        fin.sigidx = 0
        fin.dsem = None
        fin.sigval = 0
        fin.seq = P.seq + 1
        fin.deps = {("d", k): o for k, o in lastd.items()}
        P.ops["sp"].append(fin)

        P.finalize()
        with nc.Block() as block:
            @block.sync
            def _(e):
                P.emit("sp", e)

            @block.scalar
            def _(e):
                P.emit("act", e)

            @block.gpsimd
            def _(e):
                P.emit("pool", e)

            @block.vector
            def _(e):
                P.emit("dve", e)

            @block.tensor
            def _(e):
                P.emit("pe", e)
        self.stack.close()
        return nc

    def layer(self, l, env):
        P = self.P
        g = env
        U, Ub, ident, CONST, ONES = g["U"], g["Ub"], g["ident"], g["CONST"], g["ONES"]
        ones_bf, onesD, onesC, ones_f, eps_t = g["ones_bf"], g["onesD"], g["onesC"], g["ones_f"], g["eps_t"]
        masks, MASKS, smp, smc, SM, SM2 = g["masks"], g["MASKS"], g["smp"], g["smc"], g["SM"], g["SM2"]
        vecT, VEC, bq8, modT, MOD, coef, COEF = g["vecT"], g["VEC"], g["bq8"], g["modT"], g["MOD"], g["coef"], g["COEF"]
        w_in, w_pc, w_pa, w_out, w_gate, w_up, w_down = (g["w_in"], g["w_pc"], g["w_pa"], g["w_out"], g["w_gate"],
                                                         g["w_up"], g["w_down"])
        ck, stc, kvp, cvp, kvs, cvs, yp, ys, rscr, RS = (g["ck"], g["stc"], g["kvp"], g["cvp"], g["kvs"], g["cvs"],
                                                         g["yp"], g["ys"], g["rscr"], g["RS"])
        produce = g["produce"]
        G1P, G2P = 16, 40

        def vcol(row):
            return vecT[:, l, row:row + 1]

        def ubufs(tt):
            return [Ub[k][tt] for k in range(8)]

        def gemm_u(ps, psb, Wt, Wb, col0, tt, n):
            t0 = TT[tt][0]
            self.mm_acc(ps[:, 0:n], psb, [(Wt[:, k, col0:col0 + 128], U[:, k, t0:t0 + n]) for k in range(8)],
                        [Wb] + ubufs(tt))

        self.new_phase(A0_END)
        cact, cactb = self.carve("cact", [128, 4, NT], BF16, 5)
        off_c = self.rb_offs[1]

        glu, glub = self.carve("glu", [128, 4, 30 + S], BF16, 4)
        sglu, sglub = self.carve("sglu", [128, 4, 4, 34], BF16, 1)
        gluf, glufb = self.carve("gluf", [128, 4, 48], F32, 1)
        ycv, ycvb = self.carve("ycv", [128, 4, NT], F32, 5, area=0)
        dwd, dwdb = self.carve("dwd", [128, 1, 31 * 128], BF16, 1)
        sig, sigb = self.carve("sig", [128, 2, 512], F32, 2)
        ybq, ybqb = self.carve("ybq", [128, 4, 512], BF16, 4)
        stt, sttb = self.carve("stt", [128, 4, 512], F32, 4)
        tmpa, tmpab = self.carve("tmpa", [128, 2, 512], F32, 2)
        sst, sstb = self.carve("sst", [128, 4, 512], F32, 1)
        cvo, cvob = self.carve("cvo", [128, 2, 512], F32, 2)

        for c in range(4):
            P.op("dve", lambda e, c=c: e.memset(glu[:, c, 0:30], 0.0), w=[glub[c]])
        for s in range(4):
            P.op("sp", lambda e, s=s: e.dma_start(out=sst[0:30, s, :], in_=stc[l, s, :, :]), w=sstb, dma=sstb[0])
        for c in range(4):
            pt, ptb = self.bank("misc")

            def fn_st(e, pt=pt, c=c):
                ins = None
                for s in range(4):
                    ins = e.transpose(pt[:, 30 * s:30 * s + 30], sst[0:30, s, 128 * c:128 * c + 128], ident[0:30, 0:30])
                return ins
            P.op("pe", fn_st, r=sstb + [CONST], w=[ptb])
            P.op("act", lambda e, pt=pt, c=c: e.activation(out=sglu[:, c, :, 0:30],
                                                           in_=pt[:, 0:120].rearrange("p (s t) -> p s t", s=4),
                                                           func=AF.Copy), r=[ptb], w=sglub)
        WA, WAb = self.wload(w_in[l, :, 0:512], 8, 512)
        WG, WGb = self.wload(w_in[l, :, 512:1024], 8, 512)
        self.si = 0

        def glu_chunk(c):
            for tt in range(5):
                t0, n = TT[tt]
                pa, pab = self.bank("acc")
                gemm_u(pa, pab, WA, WAb, 128 * c, tt, n)
                pg, pgb = self.bank("acc")
                gemm_u(pg, pgb, WG, WGb, 128 * c, tt, n)
                sg = sig[:, self.si % 2, :]
                sgb = sigb[self.si % 2]
                self.si += 1
                P.op("act", lambda e: e.activation(out=sg[:, 0:n], in_=pg[:, 0:n], func=AF.Sigmoid,
                                                   bias=vcol(V_BIN + 4 + c), scale=1.0), r=[pgb, VEC], w=[sgb])
                if tt < 4:
                    P.op("dve", lambda e: e.scalar_tensor_tensor(
                        out=glu[:, c, 30 + t0:30 + t0 + 512], in0=pa[:, 0:512], scalar=vcol(V_BIN + c), in1=sg[:, 0:512],
                        op0=ALU.add, op1=ALU.mult), r=[pab, sgb, VEC], w=[glub[c]])
                    if tt == 3:
                        P.op("dve", lambda e: e.scalar_tensor_tensor(
                            out=gluf[:, c, 0:32], in0=pa[:, 480:512], scalar=vcol(V_BIN + c), in1=sg[:, 480:512],
                            op0=ALU.add, op1=ALU.mult), r=[pab, sgb, VEC], w=glufb)
                else:
                    P.op("dve", lambda e: e.scalar_tensor_tensor(
                        out=gluf[:, c, 32:48], in0=pa[:, 0:16], scalar=vcol(V_BIN + c), in1=sg[:, 0:16],
                        op0=ALU.add, op1=ALU.mult), r=[pab, sgb, VEC], w=glufb)
                    P.op("dve", lambda e: e.tensor_copy(out=sglu[:, c, :, 30:34],
                                                        in_=gluf[:, c, 32:48].rearrange("p (s t) -> p s t", s=4)),
                         r=glufb, w=sglub)

        def conv_state_outputs():
            pt, ptb = self.bank("misc")

            def fn_cs(e, pt=pt):
                ins = None
                for c in range(4):
                    ins = e.transpose(pt[0:30, 128 * c:128 * c + 128], gluf[:, c, 2:32], ident[:])
                return ins
            P.op("pe", fn_cs, r=glufb + [CONST], w=[ptb])
            P.op("act", lambda e, pt=pt: e.activation(out=cvo[0:30, 0, :], in_=pt[0:30, :], func=AF.Copy), r=[ptb], w=[cvob[0]])
            P.op("sp", lambda e: e.dma_start(out=cvp[l, :, :], in_=cvo[0:30, 0, :]), r=[cvob[0]], w=[Buf("x")], dma=cvob[0])
            pt, ptb = self.bank("misc")

            def fn_cs2(e, pt=pt):
                ins = None
                for c in range(4):
                    ins = e.transpose(pt[0:16, 128 * c:128 * c + 128], gluf[:, c, 32:48], ident[:])
                return ins
            P.op("pe", fn_cs2, r=glufb + [CONST], w=[ptb])
            P.op("act", lambda e, pt=pt: e.activation(out=cvo[0:16, 1, :], in_=pt[0:16, :], func=AF.Copy), r=[ptb], w=[cvob[1]])
            for s in range(4):
                P.op("sp", lambda e, s=s: e.dma_start(out=cvs[l, s, 26:30, :], in_=cvo[4 * s:4 * s + 4, 1, :]),
                     r=[cvob[1]], w=[Buf("x")], dma=cvob[1])

        glu_chunk(0)
        glu_chunk(1)
        self.yi = 0

        def ln_tile(tt):
            t0, n = TT[tt]
            pm, pmb = self.bank("o")
            pq, pqb = self.bank("z")
            for c in range(4):
                yb_ = ybq[:, self.yi % 4, :]
                ybb = ybqb[self.yi % 4]
                self.yi += 1
                ys_ = ybq[:, self.yi % 4, :]
                ysb = ybqb[self.yi % 4]
                self.yi += 1
                P.op("dve", lambda e, yb_=yb_, c=c: e.tensor_copy(out=yb_[:, 0:n], in_=ycv[:, c, t0:t0 + n]),
                     r=[ycvb[tt]], w=[ybb])
                P.op("act", lambda e, ys_=ys_, c=c: e.activation(out=ys_[:, 0:n], in_=ycv[:, c, t0:t0 + n],
                                                                func=AF.Square), r=[ycvb[tt]], w=[ysb])
                P.op("pe", lambda e, yb_=yb_, c=c: e.matmul(pm[:, 0:n], lhsT=onesC[:], rhs=yb_[:, 0:n],
                                                           start=(c == 0), stop=(c == 3)), r=[ybb, ONES], w=[pmb])
                P.op("pe", lambda e, ys_=ys_, c=c: e.matmul(pq[:, 0:n], lhsT=onesC[:], rhs=ys_[:, 0:n],
                                                           start=(c == 0), stop=(c == 3)), r=[ysb, ONES], w=[pqb])
            mean, rstd = self.ln_stats(pm, pmb, pq, pqb, stt, sttb, n, eps_t, ONES)
            for c in range(4):
                ta = tmpa[:, c % 2, :]
                tab = tmpab[c % 2]
                P.op("dve", lambda e, ta=ta, c=c: e.tensor_tensor(out=ta[:, 0:n], in0=ycv[:, c, t0:t0 + n],
                                                                 in1=mean[:, 0:n], op=ALU.subtract),
                     r=[ycvb[tt], sttb[0]], w=[tab])
                P.op("dve", lambda e, ta=ta: e.tensor_tensor(out=ta[:, 0:n], in0=ta[:, 0:n], in1=rstd[:, 0:n],
                                                            op=ALU.mult), r=[tab, sttb[2]], w=[tab])
                P.op("act", lambda e, ta=ta, c=c: e.activation(out=cact[:, c, t0:t0 + n], in_=ta[:, 0:n],
                                                              func=AF.Silu, scale=vcol(V_CLG + c), bias=vcol(V_CLB + c)),
                     r=[tab, VEC], w=[cactb[tt]])

        def conv_chunk(c, with_ln):
            dw = dwd[:, 0, :].rearrange("p (k n) -> p k n", k=31)
            dwb = dwdb[0]
            P.op("dve", lambda e: e.tensor_tensor(
                out=dw, in0=ident[:].unsqueeze(1).to_broadcast([128, 31, 128]),
                in1=vecT[:, l, V_WDW + c:V_WDW + 124:4].unsqueeze(2).to_broadcast([128, 31, 128]), op=ALU.mult),
                r=[CONST, VEC], w=[dwb])
            for tt in range(5):
                t0, n = TT[tt]
                pc, pcb = self.bank("acc")
                if tt < 4:
                    pairs = [(dw[:, k, :], glu[:, c, t0 + k:t0 + k + 512]) for k in range(31)]
                    self.mm_acc(pc[:, 0:512], pcb, pairs, [dwb, glub[c]])
                else:
                    pairs = [(dw[:, k, :], sglu[:, c, :, k:k + 4]) for k in range(31)]
                    self.mm_acc(pc[:, 0:16], pcb, pairs, [dwb] + sglub)
                P.op("act", lambda e, pc=pc, t0=t0, n=n: e.activation(out=ycv[:, c, t0:t0 + n], in_=pc[:, 0:n],
                                                                     func=AF.Identity, bias=vcol(V_BDW + c), scale=1.0),
                     r=[pcb, VEC], w=[ycvb[tt]])
                if with_ln:
                    ln_tile(tt)

        conv_chunk(0, False)
        glu_chunk(2)
        conv_chunk(1, False)
        glu_chunk(3)
        conv_state_outputs()
        conv_chunk(2, False)
        conv_chunk(3, True)

        self.stage(2 + 10 * l)
        self.new_phase(off_c)
        oatt, oattb = self.carve("oatt", [128, 4, NT], BF16, 1)
        off_co = self.rb_offs[1]
        Qt, Qtb = self.carve("Qt", [128, NT], BF16, 5)
        Kt, Ktb = self.carve("Kt", [128, NT], BF16, 5)
        Ktf, Ktfb = self.carve("Ktf", [128, NT], F32, 5, area=0)
        Vtf, Vtfb = self.carve("Vtf", [128, NT], F32, 5, area=0)
        Vb, Vbb = self.carve("Vb", [128, 16, 128], BF16, 4)
        Oacc, Oaccb = self.carve("Oacc", [128, NT], F32, 1, area=0)
        Zacc, Zaccb = self.carve("Zacc", [128, NT], F32, 1, area=0)
        Eb, Ebb = self.carve("Eb", [128, 2, 512], BF16, 2)
        Pt, Ptb = self.carve("Pt", [128, 4, 512], BF16, 4)
        kvst, kvstb = self.carve("kvst", [128, 4, 512], F32, 4)
        sK32, sK32b = self.carve("sK32", [128, 4, 4, 128], F32, 4)
        sKt, sKtb = self.carve("sKt", [128, 2, 512], BF16, 2)
        sNew, sNewb = self.carve("sNew", [128, 4, 2, 128], F32, 1)
        sE, sEb = self.carve("sE", [128, 2, 32], F32, 2)
        sPp, sPpb = self.carve("sPp", [128, 32], BF16, 1)
        sPc, sPcb = self.carve("sPc", [128, 32], BF16, 1)
        sVb, sVbb = self.carve("sVb", [128, 4, 4, 128], BF16, 4)
        sNb, sNbb = self.carve("sNb", [128, 4, 128], BF16, 1)
        Qf, Qfb = self.carve("Qf", [128, 16], F32, 1)
        Ktfs, Ktfsb = self.carve("Ktfs", [128, 16], F32, 1)

        self.ei = 0
        self.pi = 0
        ki = 0
        import os
        for hp in range(int(os.environ.get('HPN', '4'))):
            for gi in range(int(os.environ.get('GIN', '3'))):
                d = DIL[gi]
                nblk = S // d // 128
                Wn = WIN[gi]
                qcol = 1024 + (0 + gi) * 512 + hp * 128
                kcol = 1024 + (3 + gi) * 512 + hp * 128
                vcol_ = 1024 + (6 + gi) * 512 + hp * 128
                self.emit_d2d(1)
                WQ, WQb = self.wload(w_in[l, :, qcol:qcol + 128], 8, 128)
                WK, WKb = self.wload(w_in[l, :, kcol:kcol + 128], 8, 128)
                WV, WVb = self.wload(w_in[l, :, vcol_:vcol_ + 128], 8, 128)
                nt_ = 1 if gi == 0 else 4
                for s_ in range(4):
                    if gi == 0:
                        srck = ck[gi][l, s_, :, :].rearrange("j (k f) -> j k f", k=2)[:, 0, hp * 128:hp * 128 + 128]
                        srcv = ck[gi][l, s_, :, :].rearrange("j (k f) -> j k f", k=2)[:, 1, hp * 128:hp * 128 + 128]
                        P.op("sp", lambda e, srck=srck, s_=s_: e.dma_start(out=sK32[:, s_, 0, :], in_=srck), w=[sK32b[s_]],
                             dma=sK32b[s_])
                        P.op("pool", lambda e, srcv=srcv, s_=s_: e.dma_start(out=sVb[:, s_, 0, :], in_=srcv), w=[sVbb[s_]],
                             dma=sVbb[s_])
                    else:
                        v4 = ck[gi][l, s_, :, :].rearrange("(j t) (k f) -> j t k f", t=d, k=2)
                        srck = v4[:, 0:4, 0, hp * 128:hp * 128 + 128]
                        srcv = v4[:, 0:4, 1, hp * 128:hp * 128 + 128]
                        P.op("sp", lambda e, srck=srck, s_=s_: e.dma_start(out=sK32[:, s_, :, :], in_=srck), w=[sK32b[s_]],
                             dma=sK32b[s_])
                        P.op("pool", lambda e, srcv=srcv, s_=s_: e.dma_start(out=sVb[:, s_, :, :], in_=srcv), w=[sVbb[s_]],
                             dma=sVbb[s_])
                for tt in range(5):
                    t0, n = TT[tt]
                    pq, pqb = self.bank("acc")
                    gemm_u(pq, pqb, WQ, WQb, 0, tt, n)
                    P.op("act", lambda e, pq=pq, t0=t0, n=n, gi=gi, hp=hp: e.activation(
                        out=Qt[:, t0:t0 + n], in_=pq[:, 0:n], func=AF.Identity, scale=0.125,
                        bias=bq8[:, l, gi * 4 + hp:gi * 4 + hp + 1]), r=[pqb, VEC], w=[Qtb[tt]])
                    pk, pkb = self.bank("acc")
                    gemm_u(pk, pkb, WK, WKb, 0, tt, n)
                    P.op("act", lambda e, pk=pk, t0=t0, n=n, kcol=kcol: e.activation(
                        out=Ktf[:, t0:t0 + n], in_=pk[:, 0:n], func=AF.Identity, scale=1.0, bias=vcol(kcol // 128)),
                        r=[pkb, VEC], w=[Ktfb[tt]])
                    P.op("dve", lambda e, pk=pk, t0=t0, n=n, kcol=kcol: e.tensor_scalar(
                        out=Kt[:, t0:t0 + n], in0=pk[:, 0:n], scalar1=vcol(kcol // 128), scalar2=None, op0=ALU.add),
                        r=[pkb, VEC], w=[Ktb[tt]])
                    pv, pvb = self.bank("acc")
                    gemm_u(pv, pvb, WV, WVb, 0, tt, n)
                    P.op("act", lambda e, pv=pv, t0=t0, n=n, vc=vcol_: e.activation(
                        out=Vtf[:, t0:t0 + n], in_=pv[:, 0:n], func=AF.Identity, scale=1.0, bias=vcol(vc // 128)),
                        r=[pvb, VEC], w=[Vtfb[tt]])

                def blk_cols(r, c):
                    a = r + d * 128 * c
                    return a, a + d * 127 + 1, d

                def blk_rc(b):
                    return b // nblk, b % nblk

                for bg in range(4):
                    for kv, src, srcb in ((1, Vtf, Vtfb), (0, Ktf, Ktfb)):
                        blks = [4 * bg + j for j in range(4)]
                        inwin = [(blk_rc(b)[0] + d * 128 * blk_rc(b)[1]) >= S - Wn for b in blks]
                        if kv == 0 and not any(inwin):
                            continue
                        pt, ptb = self.bank("tr")

                        def fn_tr(e, pt=pt, src=src, blks=blks):
                            ins = None
                            for j, b in enumerate(blks):
                                r_, c_ = blk_rc(b)
                                a, z, stp = blk_cols(r_, c_)
                                ins = e.transpose(pt[:, 128 * j:128 * j + 128], src[:, a:z:stp], ident[:])
                            return ins
                        P.op("pe", fn_tr, r=list(srcb[0:4]) + [CONST], w=[ptb])
                        if kv == 1:
                            P.op("dve", lambda e, pt=pt, bg=bg: e.tensor_copy(
                                out=Vb[:, 4 * bg:4 * bg + 4, :], in_=pt[:, :].rearrange("p (a b) -> p a b", a=4)),
                                r=[ptb], w=[Vbb[bg]])
                        if any(inwin):
                            kst = kvst[:, ki % 4, :]
                            kstb = kvstb[ki % 4]
                            ki += 1
                            P.op("act", lambda e, pt=pt, kst=kst: e.activation(out=kst, in_=pt[:, :], func=AF.Copy),
                                 r=[ptb], w=[kstb])
                            for j, b in enumerate(blks):
                                if not inwin[j]:
                                    continue
                                r_, c_ = blk_rc(b)
                                a = r_ + d * 128 * c_ - (S - Wn)
                                dst = kvp[gi][l, a:a + 127 * d + 1:d, kv, hp * 128:hp * 128 + 128]
                                P.op("sp", lambda e, dst=dst, kst=kst, j=j: e.dma_start(out=dst, in_=kst[:, 128 * j:128 * j + 128]),
                                     r=[kstb], w=[Buf("x")], dma=kstb)
                for half in range(2):
                    pt, ptb = self.bank("tr")

                    def fn_sn(e, pt=pt, half=half):
                        ins = None
                        for s in (2 * half, 2 * half + 1):
                            for kv, src in ((0, Ktf), (1, Vtf)):
                                o = ((s % 2) * 2 + kv) * 128
                                ins = e.transpose(pt[0:4, o:o + 128], src[:, S + 4 * s:S + 4 * s + 4], ident[:])
                        return ins
                    P.op("pe", fn_sn, r=[Ktfb[4], Vtfb[4], CONST], w=[ptb])
                    P.op("act", lambda e, pt=pt, half=half: e.activation(
                        out=sNew[0:4, 2 * half:2 * half + 2, :, :],
                        in_=pt[0:4, :].rearrange("p (s k f) -> p s k f", s=2, k=2), func=AF.Copy), r=[ptb], w=sNewb)
                P.op("act", lambda e: e.activation(out=sNb[0:4, :, :], in_=sNew[0:4, :, 1, :], func=AF.Copy), r=sNewb, w=sNbb)
                for s in range(4):
                    dst = kvs[gi][l, s, Wn - 4:Wn, :].rearrange("t (k f) -> t k f", k=2)[:, :, hp * 128:hp * 128 + 128]
                    P.op("sp", lambda e, dst=dst, s=s: e.dma_start(out=dst, in_=sNew[0:4, s, :, :]), r=sNewb, w=[Buf("x")],
                         dma=sNewb[0])

                mk = masks[:, gi * 4 + hp, :]
                first = (gi == 0)
                gidx = gi * 4 + hp
                nt_ = 1 if gi == 0 else 4

                def q_group(qg):
                    if gi == 0:
                        return [(0, 4 * qg + j) for j in range(4)], (lambda T: T[:, 512 * qg:512 * qg + 512])
                    if gi == 1:
                        return [(qg, j) for j in range(4)], (lambda T: T[:, qg:S:4])
                    return ([(4 * qg + j, 0) for j in range(4)],
                            (lambda T: T[:, 0:S].rearrange("p (i r) -> p r i", r=16)[:, 4 * qg:4 * qg + 4, :]))

                def emit_S(qg, j, r_, c_, po, pob, pz, pzb):
                    hasp = c_ > 0
                    a, z, stp = blk_cols(r_, c_)
                    psA, psAb = self.bank("sS")
                    psB, psBb = self.bank("sS")

                    def fn_s(e):
                        ins = None
                        for hh, ps_ in ((0, psA), (1, psB)):
                            ph = slice(64 * hh, 64 * hh + 64)
                            ins = e.matmul(ps_[:, 0:128], lhsT=Kt[ph, a:z:stp], rhs=Qt[ph, a:z:stp], start=True, stop=True)
                            if hasp:
                                a2, z2, _ = blk_cols(r_, c_ - 1)
                                ins = e.matmul(ps_[:, 128:256], lhsT=Kt[ph, a2:z2:stp], rhs=Qt[ph, a:z:stp],
                                               start=True, stop=True)
                        return ins
                    P.op("pe", fn_s, r=list(Qtb[0:4]) + list(Ktb[0:4]), w=[psAb, psBb])
                    E_ = Eb[:, self.ei % 2, :]
                    Eb_ = Ebb[self.ei % 2]
                    self.ei += 1
                    P_ = Pt[:, self.pi % 4, :]
                    Pb_ = Ptb[self.pi % 4]
                    self.pi += 1
                    nv = 256 if hasp else 128
                    P.op("act", lambda e: e.activation(out=E_[:, 0:nv], in_=psA[:, 0:nv], func=AF.Exp), r=[psAb], w=[Eb_])
                    P.op("act", lambda e: e.activation(out=E_[:, 256:256 + nv], in_=psB[:, 0:nv], func=AF.Exp),
                         r=[psBb], w=[Eb_])
                    if hasp:
                        P.op("dve", lambda e: e.tensor_tensor(out=P_, in0=E_, in1=mk, op=ALU.mult), r=[Eb_, MASKS], w=[Pb_])
                    else:
                        v3 = lambda T: T.rearrange("p (a b) -> p a b", a=2)[:, :, 0:128]
                        P.op("dve", lambda e: e.tensor_tensor(out=v3(P_), in0=v3(E_), in1=v3(mk), op=ALU.mult),
                             r=[Eb_, MASKS], w=[Pb_])
                    return (qg, j, hasp, r_ * nblk + c_, P_, Pb_, po, pob, pz, pzb)

                def emit_PV(st):
                    qg, j, hasp, bcur, P_, Pb_, po, pob, pz, pzb = st

                    def fn_pv(e):
                        ins = None
                        for hh in range(2):
                            ph = slice(64 * hh, 64 * hh + 64)
                            oc = slice(128 * j, 128 * j + 128)
                            ins = e.matmul(po[ph, oc], lhsT=Vb[:, bcur, ph], rhs=P_[:, 256 * hh:256 * hh + 128],
                                           start=True, stop=not hasp)
                            if hasp:
                                ins = e.matmul(po[ph, oc], lhsT=Vb[:, bcur - 1, ph],
                                               rhs=P_[:, 256 * hh + 128:256 * hh + 256], start=False, stop=True)
                            ins = e.matmul(pz[ph, oc], lhsT=ones_bf[:, 0:64], rhs=P_[:, 256 * hh:256 * hh + 128],
                                           start=True, stop=not hasp)
                            if hasp:
                                ins = e.matmul(pz[ph, oc], lhsT=ones_bf[:, 0:64],
                                               rhs=P_[:, 256 * hh + 128:256 * hh + 256], start=False, stop=True)
                        return ins
                    P.op("pe", fn_pv, r=[Pb_, ONES] + Vbb, w=[pob, pzb])
                    if j == 3:
                        _, acc_sl = q_group(qg)
                        pv3 = (lambda T: T[:, :]) if gi < 2 else (lambda T: T[:, :].rearrange("p (r i) -> p r i", r=4))
                        if first:
                            P.op("dve", lambda e: e.tensor_copy(out=acc_sl(Oacc), in_=pv3(po)), r=[pob], w=Oaccb)
                            P.op("act", lambda e: e.activation(out=acc_sl(Zacc), in_=pv3(pz), func=AF.Copy), r=[pzb], w=Zaccb)
                        else:
                            P.op("dve", lambda e: e.tensor_tensor(out=acc_sl(Oacc), in0=pv3(po), in1=acc_sl(Oacc), op=ALU.add),
                                 r=[pob] + Oaccb, w=Oaccb)
                            P.op("dve", lambda e: e.tensor_tensor(out=acc_sl(Zacc), in0=pv3(pz), in1=acc_sl(Zacc), op=ALU.add),
                                 r=[pzb] + Zaccb, w=Zaccb)

                pso_full, psob = self.ps[7], self.psb[7]
                sstate = {}

                def s_T(s):
                    sl = (s % 2)
                    pt, ptb = self.bank("sS")

                    def fn_kt(e):
                        ins = None
                        for t in range(nt_):
                            ins = e.transpose(pt[:, 128 * t:128 * t + 128], sK32[:, s, t, :], ident[:])
                        return ins
                    P.op("pe", fn_kt, r=[sK32b[s], CONST], w=[ptb])
                    kt_ = sKt[:, sl, :]
                    ktb_ = sKtb[sl]
                    P.op("dve", lambda e: e.tensor_copy(out=kt_[:, 0:128 * nt_], in_=pt[:, 0:128 * nt_]), r=[ptb], w=[ktb_])
                    sstate[s] = (None, None, kt_, ktb_, sVb[:, s, :, :], sVbb[s])

                def s_st2(s):
                    kc_, kcb, kt_, ktb_, vb_, vbb_ = sstate[s]
                    pss0, pss0b = self.bank("sS")
                    pss1, pss1b = self.bank("sS")

                    def fn_ss(e):
                        ins = None
                        for hh, pss in ((0, pss0), (1, pss1)):
                            ph = slice(64 * hh, 64 * hh + 64)
                            for t in range(4):
                                tk = 0 if nt_ == 1 else t
                                ins = e.matmul(pss[:, t:t + 1], lhsT=kt_[ph, 128 * tk:128 * tk + 128],
                                               rhs=Qt[ph, S + 4 * s + t:S + 4 * s + t + 1], start=True, stop=True)
                            ins = e.matmul(pss[0:4, 16:20], lhsT=Kt[ph, S + 4 * s:S + 4 * s + 4],
                                           rhs=Qt[ph, S + 4 * s:S + 4 * s + 4], start=True, stop=True)
                        return ins
                    P.op("pe", fn_ss, r=[ktb_, Qtb[4], Ktb[4]], w=[pss0b, pss1b])
                    cs_ = slice(8 * s, 8 * s + 8)
                    for hh, pss, pssb in ((0, pss0, pss0b), (1, pss1, pss1b)):
                        c4 = slice(8 * s + 4 * hh, 8 * s + 4 * hh + 4)
                        P.op("act", lambda e, pss=pss, c4=c4: e.activation(out=sE[:, 0, c4], in_=pss[:, 0:4], func=AF.Exp),
                             r=[pssb], w=[sEb[0]])
                        P.op("act", lambda e, pss=pss, c4=c4: e.activation(out=sE[0:4, 1, c4], in_=pss[0:4, 16:20], func=AF.Exp),
                             r=[pssb], w=[sEb[1]])
                    P.op("dve", lambda e: e.tensor_tensor(out=sPp[:, cs_], in0=sE[:, 0, cs_], in1=smp[:, gidx, cs_], op=ALU.mult),
                         r=[sEb[0], SM], w=sPpb)
                    P.op("dve", lambda e: e.tensor_tensor(out=sPc[0:4, cs_], in0=sE[0:4, 1, cs_], in1=smc[0:4, gidx, cs_],
                                                          op=ALU.mult), r=[sEb[1], SM2], w=sPcb)

                def s_st3(s):
                    kc_, kcb, kt_, ktb_, vb_, vbb_ = sstate[s]

                    def fn_so(e):
                        ins = None
                        for hh in range(2):
                            ph = slice(64 * hh, 64 * hh + 64)
                            c0_ = s * 8 + hh * 4
                            o4 = slice(4 * s, 4 * s + 4)
                            z4 = slice(16 + 4 * s, 16 + 4 * s + 4)
                            if nt_ == 1:
                                ins = e.matmul(pso_full[ph, o4], lhsT=vb_[:, 0, ph], rhs=sPp[:, c0_:c0_ + 4], start=True, stop=False)
                            else:
                                for t in range(4):
                                    ins = e.matmul(pso_full[ph, 4 * s + t:4 * s + t + 1], lhsT=vb_[:, t, ph],
                                                   rhs=sPp[:, c0_ + t:c0_ + t + 1], start=(t == 0), stop=False,
                                                   skip_group_check=True)
                            ins = e.matmul(pso_full[ph, o4], lhsT=sNb[0:4, s, ph], rhs=sPc[0:4, c0_:c0_ + 4], start=False, stop=True,
                                           skip_group_check=True)
                            ins = e.matmul(pso_full[ph, z4], lhsT=ones_bf[:, 0:64], rhs=sPp[:, c0_:c0_ + 4], start=True, stop=False)
                            ins = e.matmul(pso_full[ph, z4], lhsT=ones_bf[0:4, 0:64], rhs=sPc[0:4, c0_:c0_ + 4], start=False, stop=True)
                        return ins
                    P.op("pe", fn_so, r=[vbb_, ONES] + sPpb + sPcb + sNbb, w=[psob])

                def s_acc(_):
                    if first:
                        P.op("dve", lambda e: e.tensor_copy(out=Oacc[:, S:NT], in_=pso_full[:, 0:16]), r=[psob], w=Oaccb)
                        P.op("dve", lambda e: e.tensor_copy(out=Zacc[:, S:NT], in_=pso_full[:, 16:32]), r=[psob], w=Zaccb)
                    else:
                        P.op("dve", lambda e: e.tensor_tensor(out=Oacc[:, S:NT], in0=pso_full[:, 0:16], in1=Oacc[:, S:NT],
                                                              op=ALU.add), r=[psob] + Oaccb, w=Oaccb)
                        P.op("dve", lambda e: e.tensor_tensor(out=Zacc[:, S:NT], in0=pso_full[:, 16:32], in1=Zacc[:, S:NT],
                                                              op=ALU.add), r=[psob] + Zaccb, w=Zaccb)

                ssched = {1: [(s_T, 0)], 2: [(s_T, 1)], 3: [(s_st2, 0)], 5: [(s_st3, 0), (s_st2, 1)], 6: [(s_T, 2)],
                          8: [(s_st3, 1), (s_T, 3)], 9: [(s_st2, 2)], 10: [(s_st2, 3)], 12: [(s_st3, 2)], 13: [(s_st3, 3)],
                          15: [(s_acc, 0)]}
                PIPE = 2
                pending = []
                qi = 0
                for qg in range(4):
                    qbl, _ = q_group(qg)
                    po, pob = self.bank("o")
                    pz, pzb = self.bank("z")
                    for j, (r_, c_) in enumerate(qbl):
                        pending.append(emit_S(qg, j, r_, c_, po, pob, pz, pzb))
                        if len(pending) > PIPE:
                            emit_PV(pending.pop(0))
                        for f_, a_ in ssched.get(qi, []):
                            f_(a_)
                        qi += 1
                while pending:
                    emit_PV(pending.pop(0))
            P.op("dve", lambda e: e.reciprocal(out=Zacc[:, :], in_=Zacc[:, :]), r=Zaccb, w=Zaccb)
            P.op("dve", lambda e, hp=hp: e.tensor_tensor(out=oatt[:, hp, :], in0=Oacc[:, :], in1=Zacc[:, :], op=ALU.mult),
                 r=Zaccb + Oaccb, w=oattb)

        self.stage(3 + 10 * l)
        self.new_phase(off_co)
        m, mb = self.carve("m", [128, 8, NT], BF16, 5, area=0)
        sgc, sgcb = self.carve("sgc", [128, 2, 512], F32, 2)
        sga, sgab = self.carve("sga", [128, 2, 512], F32, 2)
        t1, t1b = self.carve("t1", [128, 2, 512], F32, 2)
        t2, t2b = self.carve("t2", [128, 2, 512], F32, 2)
        mi = 0
        for og in range(2):
            for half in range(2):
                Wpc, Wpcb = self.wload(w_pc[l, :, 512 * og:512 * og + 512], 4, 512)
                Wpa, Wpab = self.wload(w_pa[l, :, 512 * og:512 * og + 512], 4, 512)
                cbase = 4 * og + 2 * half
                c0 = 1024 + 4608 + 128 * cbase
                Wgc, Wgcb = self.wload(w_in[l, :, c0:c0 + 256], 8, 256)
                Wga, Wgab = self.wload(w_in[l, :, c0 + 1024:c0 + 1024 + 256], 8, 256)
                for cc in range(2):
                    c = cbase + cc
                    for tt in range(5):
                        t0, n = TT[tt]
                        pyc, pycb = self.bank("all")
                        self.mm_acc(pyc[:, 0:n], pycb, [(Wpc[:, k, 128 * (c % 4):128 * (c % 4) + 128], cact[:, k, t0:t0 + n])
                                                         for k in range(4)], [Wpcb, cactb[tt]])
                        pgc, pgcb = self.bank("all")
                        gemm_u(pgc, pgcb, Wgc, Wgcb, 128 * cc, tt, n)
                        pya, pyab = self.bank("all")
                        self.mm_acc(pya[:, 0:n], pyab, [(Wpa[:, k, 128 * (c % 4):128 * (c % 4) + 128], oatt[:, k, t0:t0 + n])
                                                         for k in range(4)], [Wpab] + oattb)
                        pga, pgab = self.bank("all")
                        gemm_u(pga, pgab, Wga, Wgab, 128 * cc, tt, n)
                        i2 = mi % 2
                        mi += 1
                        P.op("act", lambda e, pgc=pgc, i2=i2, c=c, n=n: e.activation(
                            out=sgc[:, i2, 0:n], in_=pgc[:, 0:n], func=AF.Sigmoid, bias=vcol(V_BIN + 44 + c), scale=1.0),
                            r=[pgcb, VEC], w=[sgcb[i2]])
                        P.op("act", lambda e, pga=pga, i2=i2, c=c, n=n: e.activation(
                            out=sga[:, i2, 0:n], in_=pga[:, 0:n], func=AF.Sigmoid, bias=vcol(V_BIN + 52 + c), scale=1.0),
                            r=[pgab, VEC], w=[sgab[i2]])
                        P.op("dve", lambda e, pyc=pyc, i2=i2, c=c, n=n: e.scalar_tensor_tensor(
                            out=t1[:, i2, 0:n], in0=pyc[:, 0:n], scalar=vcol(V_BPC + c), in1=sgc[:, i2, 0:n],
                            op0=ALU.add, op1=ALU.mult), r=[pycb, sgcb[i2], VEC], w=[t1b[i2]])
                        P.op("dve", lambda e, pya=pya, i2=i2, c=c, n=n: e.scalar_tensor_tensor(
                            out=t2[:, i2, 0:n], in0=pya[:, 0:n], scalar=vcol(V_BPA + c), in1=sga[:, i2, 0:n],
                            op0=ALU.add, op1=ALU.mult), r=[pyab, sgab[i2], VEC], w=[t2b[i2]])
                        P.op("dve", lambda e, i2=i2, c=c, t0=t0, n=n: e.tensor_tensor(
                            out=m[:, c, t0:t0 + n], in0=t1[:, i2, 0:n], in1=t2[:, i2, 0:n], op=ALU.add),
                            r=[t1b[i2], t2b[i2]], w=[mb[tt]])
        self.stage(4 + 10 * l)
        self.new_phase(A0_END, A0_END, A0_END)
        self.ln_wfull = w_out[l]
        self.ln_block(l, 1 + 2 * l, lambda c: w_out[l, :, 128 * c:128 * c + 128], 8, m, mb, G1P, final=False, env=g,
                      tiles=range(5))

        self.stage(5 + 10 * l)
        for half, tiles in enumerate(([0, 1], [2, 3, 4])):
            self.new_phase(0, 0, 0)
            c0 = TT[tiles[0]][0]
            ncol = sum(TT[t][1] for t in tiles)
            h, hb = self.carve("h", [128, NF, ncol], BF16, len(tiles))
            sgt, sgtb = self.carve("sgt", [128, 2, 512], F32, 2)
            hi = 0
            for fg in range(11):
                Wg_, Wgb_ = self.wload(w_gate[l, :, 256 * fg:256 * fg + 256], 8, 256)
                Wu_, Wub_ = self.wload(w_up[l, :, 256 * fg:256 * fg + 256], 8, 256)
                for ff in range(2):
                    f = 2 * fg + ff
                    for ti, tt in enumerate(tiles):
                        t0, n = TT[tt]
                        pgt, pgtb = self.bank("all")
                        gemm_u(pgt, pgtb, Wg_, Wgb_, 128 * ff, tt, n)
                        pup, pupb = self.bank("all")
                        gemm_u(pup, pupb, Wu_, Wub_, 128 * ff, tt, n)
                        i2 = hi % 2
                        hi += 1
                        P.op("act", lambda e, pgt=pgt, i2=i2, n=n: e.activation(out=sgt[:, i2, 0:n], in_=pgt[:, 0:n],
                                                                               func=AF.Silu), r=[pgtb], w=[sgtb[i2]])
                        P.op("dve", lambda e, pup=pup, i2=i2, f=f, t0=t0, n=n: e.tensor_tensor(
                            out=h[:, f, t0 - c0:t0 - c0 + n], in0=pup[:, 0:n], in1=sgt[:, i2, 0:n], op=ALU.mult),
                            r=[pupb, sgtb[i2]], w=[hb[ti]])
            hview = lambda k, t0, n, h=h, c0=c0: h[:, k, t0 - c0:t0 - c0 + n]
            self.ln_block(l, 2 + 2 * l, lambda c: w_down[l, :, 128 * c:128 * c + 128], NF, None, hb, G2P,
                          final=(l == L - 1), env=g, tiles=tiles, hview=hview)

    def ln_stats(self, pm, pmb, pq, pqb, stt, sttb, n, eps_t, ONES):
        P = self.P
        mean = stt[:, 0, :]
        m2 = stt[:, 1, :]
        rstd = stt[:, 2, :]
        P.op("dve", lambda e: e.tensor_copy(out=mean[:, 0:n], in_=pm[:, 0:n]), r=[pmb], w=[sttb[0]])
        P.op("dve", lambda e: e.tensor_tensor(out=m2[:, 0:n], in0=mean[:, 0:n], in1=mean[:, 0:n], op=ALU.mult),
             r=[sttb[0]], w=[sttb[1]])
        P.op("dve", lambda e: e.tensor_tensor(out=m2[:, 0:n], in0=pq[:, 0:n], in1=m2[:, 0:n], op=ALU.subtract),
             r=[pqb, sttb[1]], w=[sttb[1]])
        P.op("act", lambda e: e.activation(out=rstd[:, 0:n], in_=m2[:, 0:n], func=AF.Sqrt, bias=eps_t[:, 0:1], scale=1.0),
             r=[sttb[1], ONES], w=[sttb[2]])
        P.op("dve", lambda e: e.reciprocal(out=rstd[:, 0:n], in_=rstd[:, 0:n]), r=[sttb[2]], w=[sttb[2]])
        return mean, rstd

    def ln_block(self, l, stg, wsrc, kch, xin, xinb, gofs, final, env, tiles, hview=None):
        P = self.P
        g = env
        U, Ub, ident, CONST, ONES = g["U"], g["Ub"], g["ident"], g["CONST"], g["ONES"]
        onesD, eps_t, modT, MOD, coef, COEF = g["onesD"], g["eps_t"], g["modT"], g["MOD"], g["coef"], g["COEF"]
        rscr, RS, yp, ys = g["rscr"], g["RS"], g["yp"], g["ys"]
        produce = g["produce"]
        tiles = list(tiles)
        pipelined = (kch <= 8)
        groups = [[t] for t in tiles] if pipelined else [tiles[i:i + 2] for i in range(0, len(tiles), 2)]
        zt, ztb = self.carve("zt", [128, 8, 1024], F32, 16)
        rt, rtb = self.carve("rt", [128, 4, 512], F32, 4)
        zbq, zbqb = self.carve("zbq", [128, 4, 512], BF16, 4)
        stt, sttb = self.carve("stt", [128, 3, 512], F32, 3)
        if final:
            yst, ystb = self.carve("yst", [128, 2, 1024], F32, 2)
            rst, rstb = None, None
        else:
            rst, rstb = self.carve("rst", [128, 4, 512], F32, 4)
        bufs = (zt, ztb, rt, rtb, zbq, zbqb, stt, sttb, rst, rstb, (yst, ystb) if final else None)
        self.ln_wpre = None
        if pipelined:
            self.ln_wpre = [self.wload(self.ln_wfull[:, 512 * i:512 * i + 512], 8, 512) for i in range(2)]
            for gi_, grp in enumerate(groups):
                self.ln_group(l, stg, wsrc, kch, xin, xinb, gofs, final, env, tiles, grp, hview, bufs, gi_ % 2, "proj")
                if gi_ > 0:
                    self.ln_group(l, stg, wsrc, kch, xin, xinb, gofs, final, env, tiles, groups[gi_ - 1], hview, bufs,
                                  (gi_ - 1) % 2, "norm")
            self.ln_group(l, stg, wsrc, kch, xin, xinb, gofs, final, env, tiles, groups[-1], hview, bufs,
                          (len(groups) - 1) % 2, "norm")
        else:
            for grp in groups:
                self.ln_group(l, stg, wsrc, kch, xin, xinb, gofs, final, env, tiles, grp, hview, bufs, 0, "proj")
                self.ln_group(l, stg, wsrc, kch, xin, xinb, gofs, final, env, tiles, grp, hview, bufs, 0, "norm")

    def ln_group(self, l, stg, wsrc, kch, xin, xinb, gofs, final, env, alltiles, tiles, hview, bufs, slot, part):
        P = self.P
        g = env
        U, Ub, ident, CONST, ONES = g["U"], g["Ub"], g["ident"], g["CONST"], g["ONES"]
        onesD, eps_t, modT, MOD, coef, COEF = g["onesD"], g["eps_t"], g["modT"], g["MOD"], g["coef"], g["COEF"]
        rscr, RS, yp, ys = g["rscr"], g["RS"], g["yp"], g["ys"]
        produce = g["produce"]
        zt, ztb, rt, rtb, zbq, zbqb, stt, sttb, rst, rstb, ysts = bufs
        if final:
            yst, ystb = ysts
        ri = self.ln_ri
        for c in (range(8) if part == "proj" else []):
            if self.ln_wpre is not None:
                Wfull, Wb = self.ln_wpre[c // 4]
                Wt = Wfull[:, :, 128 * (c % 4):128 * (c % 4) + 128]
            else:
                Wt, Wb = self.wload(wsrc(c), kch, 128)
            for ti, tt in enumerate(tiles):
                t0, n = TT[tt]
                zoff = (slot + ti) * 512
                pz_, pzb_ = self.bank("acc")
                if hview is None:
                    pairs = [(Wt[:, k, :], xin[:, k, t0:t0 + n]) for k in range(kch)]
                    rb = [Wb, xinb[tt]]
                else:
                    pairs = [(Wt[:, k, :], hview(k, t0, n)) for k in range(kch)]
                    rb = [Wb, xinb[alltiles.index(tt)]]
                self.mm_acc(pz_[:, 0:n], pzb_, pairs, rb)
                r_ = rt[:, ri % 4, :]
                rb_ = rtb[ri % 4]
                ri += 1
                P.op("sp", lambda e, r_=r_, c=c, t0=t0, n=n: e.dma_start(out=r_[:, 0:n], in_=rscr[:, c, t0:t0 + n]),
                     r=[RS[c][tt]], w=[rb_], dma=rb_)
                for (cs, s) in segs(tt):
                    P.op("dve", lambda e, pz_=pz_, r_=r_, c=c, cs=cs, s=s, t0=t0: e.scalar_tensor_tensor(
                        out=zt[:, c, zoff + cs.start:zoff + cs.stop], in0=pz_[:, cs], scalar=modT[:, l, gofs + c, s:s + 1],
                        in1=r_[:, cs], op0=ALU.mult, op1=ALU.add), r=[pzb_, rb_, MOD], w=[ztb[(slot + ti) * 8 + c]])
        self.ln_ri = ri
        zi = 0
        for ti, tt in enumerate(tiles if part == "norm" else []):
            t0, n = TT[tt]
            o0 = (slot + ti) * 512
            pm, pmb = self.bank("o")
            pq, pqb = self.bank("z")
            for c in range(8):
                zb_ = zbq[:, zi % 4, :]
                zbb = zbqb[zi % 4]
                zi += 1
                zs_ = zbq[:, zi % 4, :]
                zsb = zbqb[zi % 4]
                zi += 1
                P.op("act", lambda e, zb_=zb_, c=c: e.activation(out=zb_[:, 0:n], in_=zt[:, c, o0:o0 + n], func=AF.Copy),
                     r=[ztb[(slot + ti) * 8 + c]], w=[zbb])
                P.op("act", lambda e, zs_=zs_, c=c: e.activation(out=zs_[:, 0:n], in_=zt[:, c, o0:o0 + n], func=AF.Square),
                     r=[ztb[(slot + ti) * 8 + c]], w=[zsb])
                P.op("pe", lambda e, pm=pm, zb_=zb_, c=c: e.matmul(pm[:, 0:n], lhsT=onesD[:], rhs=zb_[:, 0:n],
                                                                  start=(c == 0), stop=(c == 7)), r=[zbb, ONES], w=[pmb])
                P.op("pe", lambda e, pq=pq, zs_=zs_, c=c: e.matmul(pq[:, 0:n], lhsT=onesD[:], rhs=zs_[:, 0:n],
                                                                  start=(c == 0), stop=(c == 7)), r=[zsb, ONES], w=[pqb])
            mean, rstd = self.ln_stats(pm, pmb, pq, pqb, stt, sttb, n, eps_t, ONES)
            for c in range(8):
                zc = zt[:, c, o0:o0 + n]
                P.op("dve", lambda e, zc=zc: e.tensor_tensor(out=zc, in0=zc, in1=mean[:, 0:n], op=ALU.subtract),
                     r=[sttb[0], ztb[(slot + ti) * 8 + c]], w=[ztb[(slot + ti) * 8 + c]])
                P.op("dve", lambda e, zc=zc: e.tensor_tensor(out=zc, in0=zc, in1=rstd[:, 0:n], op=ALU.mult),
                     r=[sttb[2], ztb[(slot + ti) * 8 + c]], w=[ztb[(slot + ti) * 8 + c]])
                if not final:
                    produce(zc, [ztb[(slot + ti) * 8 + c]], c, tt, stg, rst[:, ri % 4, :], rstb[ri % 4])
                    ri += 1
                    self.ln_ri = ri
                else:
                    P.op("act", lambda e, zc=zc, c=c: e.activation(out=zc, in_=zc, func=AF.Identity,
                                                                   scale=coef[:, stg, 0, c, 0:1], bias=coef[:, stg, 1, c, 0:1]),
                         r=[COEF, ztb[(slot + ti) * 8 + c]], w=[ztb[(slot + ti) * 8 + c]])
            if final:
                nb = (n + 127) // 128
                for tb in range(nb):
                    rows = min(128, n - 128 * tb)
                    ysl = yst[:, tb % 2, :]
                    yslb = ystb[tb % 2]
                    for hf in range(2):
                        pt, ptb = self.bank("acc")

                        def fn_yt(e, pt=pt, hf=hf, tb=tb, rows=rows):
                            ins = None
                            for cc in range(4):
                                c = 4 * hf + cc
                                ins = e.transpose(pt[0:rows, 128 * cc:128 * cc + 128],
                                                  zt[:, c, o0 + 128 * tb:o0 + 128 * tb + rows], ident[:])
                            return ins
                        P.op("pe", fn_yt, r=ztb[(slot + ti) * 8:(slot + ti) * 8 + 8] + [CONST], w=[ptb])
                        P.op("act", lambda e, pt=pt, hf=hf, ysl=ysl, rows=rows: e.activation(
                            out=ysl[0:rows, 512 * hf:512 * hf + 512], in_=pt[0:rows, :], func=AF.Copy), r=[ptb], w=[yslb])
                    if tt < 4:
                        dst = yp[t0 + 128 * tb:t0 + 128 * tb + 128, :]
                    else:
                        dst = ys[:, :]
                    P.op("sp", lambda e, dst=dst, ysl=ysl, rows=rows: e.dma_start(out=dst, in_=ysl[0:rows, :]),
                         r=[yslb], w=[Buf("x")], dma=yslb)


_CACHE = {}


def _consts():
    slopes = 2.0 ** (-8.0 * (np.arange(8) + 1) / 8.0)
    k = np.arange(128)[:, None].astype(np.float64)
    q = np.arange(128)[None, :].astype(np.float64)
    masks = np.zeros((128, 12, 512), np.float64)
    smp = np.zeros((128, 12, 32), np.float64)
    smc = np.zeros((4, 12, 32), np.float64)
    for g, d in enumerate(DIL):
        for hp in range(4):
            for hh in range(2):
                h = 2 * hp + hh
                cur = np.where(k <= q, np.exp(-slopes[h] * d * np.maximum(q - k, 0.0)), 0.0)
                prev = np.where(k >= q, np.exp(-slopes[h] * d * np.maximum(128 + q - k, 0.0)), 0.0)
                masks[:, g * 4 + hp, (2 * hh) * 128:(2 * hh + 1) * 128] = cur
                masks[:, g * 4 + hp, (2 * hh + 1) * 128:(2 * hh + 2) * 128] = prev
                for s in range(4):
                    for t in range(4):
                        col = s * 8 + hh * 4 + t
                        smp[:, g * 4 + hp, col] = prev[:, t if g == 0 else 0]
                        if g == 0:
                            smc[:, g * 4 + hp, col] = cur[0:4, t]
                        else:
                            smc[:, g * 4 + hp, col] = (np.arange(4) == t).astype(np.float64)
    return (np.eye(128, dtype=np.float32), masks.reshape(128, -1).astype(np.float32),
            smp.reshape(128, -1).astype(np.float32), smc.reshape(4, -1).astype(np.float32))


def _pack_vecs(inp):
    out = np.zeros((L, NVEC, 128), np.float32)
    for l in range(L):
        rows = [inp["b_in"][l].reshape(60, 128), inp["b_ada"][l].reshape(48, 128), inp["b_pc"][l].reshape(8, 128),
                inp["b_pa"][l].reshape(8, 128), inp["b_out"][l].reshape(8, 128), inp["ln1_g"][l].reshape(8, 128),
                inp["ln1_b"][l].reshape(8, 128), inp["ln2_g"][l].reshape(8, 128), inp["ln2_b"][l].reshape(8, 128),
                inp["b_dw"][l].reshape(4, 128), inp["conv_ln_g"][l].reshape(4, 128), inp["conv_ln_b"][l].reshape(4, 128),
                inp["w_dw"][l].reshape(124, 128)]
        cat = np.concatenate(rows, axis=0)
        out[l, :cat.shape[0]] = cat
    return out


def get_nc():
    global KSTAGE
    import os
    KSTAGE = int(os.environ.get("KSTAGE", "99"))
    if "nc" not in _CACHE:
        _CACHE["nc"] = Builder().build()
    return _CACHE["nc"]


def make_in_maps(inp, cores):
    f = lambda a: np.ascontiguousarray(np.asarray(a, dtype=np.float32))
    ident, masks, smp, smc = _consts()
    vecs = _pack_vecs({k: np.asarray(v) for k, v in inp.items()})
    shared = {k: f(inp[k]) for k in ("w_ada", "w_in", "w_pc", "w_pa", "w_out", "w_gate", "w_up", "w_down")}
    shared.update(vecs=vecs, ident=ident, masks=masks, smp=smp, smc=smc)
    maps = []
    for i in cores:
        m = dict(shared)
        m["xp"] = f(inp["x_prompt"][i])
        m["xs"] = f(np.asarray(inp["x_sample"][4 * i:4 * i + 4]).reshape(NS, D))
        m["c5"] = f(np.concatenate([np.asarray(inp["c_prompt"][i:i + 1]), np.asarray(inp["c_sample"][4 * i:4 * i + 4])], 0))
        caches = (inp["cache_kv_g0"], inp["cache_kv_g1"], inp["cache_kv_g2"])
        for g in range(3):
            m["ck%d" % g] = f(np.asarray(caches[g][:, 4 * i:4 * i + 4]).reshape(L, 4, WIN[g], 1024))
        m["stc"] = f(inp["state_conv"][:, 4 * i:4 * i + 4])
        maps.append(m)
    return maps


def kernel(**inputs):
    nc = get_nc()
    cores = list(range(NCORES))
    maps = make_in_maps(inputs, cores)
    res = run_bass_kernel_spmd(nc, maps, core_ids=cores)
    R = res.results
    y_p = np.stack([R[i]["yp"] for i in cores]).astype(np.float32)
    y_s = np.concatenate([R[i]["ys"].reshape(4, 4, D) for i in cores], 0).astype(np.float32)
    outs = [y_p, y_s]
    for g in range(3):
        outs.append(np.stack([R[i]["kvp%d" % g] for i in cores], 1).reshape(L, NCORES, WIN[g], 2, 8, 64).astype(np.float32))
    outs.append(np.stack([R[i]["cvp"] for i in cores], 1).astype(np.float32))
    for g in range(3):
        outs.append(np.concatenate([R[i]["kvs%d" % g] for i in cores], 1).reshape(L, 32, WIN[g], 2, 8, 64).astype(np.float32))
    outs.append(np.concatenate([R[i]["cvs"] for i in cores], 1).astype(np.float32))
    return tuple(outs)
```

```python
import contextlib
import numpy as np
import concourse.bass as bass
import concourse.mybir as mybir
from concourse.bass_utils import run_bass_kernel_spmd

F32 = mybir.dt.float32
BF16 = mybir.dt.bfloat16
AF = mybir.ActivationFunctionType
ALU = mybir.AluOpType

NCORES = 8
D = 1024
S = 2048
NS = 16
NT = S + NS
L = 2
DFF = 2816
NF = DFF // 128
ALPHA = float((2 * L) ** 0.25)
EPS = 1e-5
TT = [(0, 512), (512, 512), (1024, 512), (1536, 512), (2048, 16)]
DIL = (1, 4, 16)
WIN = (128, 512, 2048)
ENGS = ("sp", "act", "pool", "dve", "pe")
SAME_SYNC = True
NSLOT = 4
A0_END = 8 * NT * 2
RB_BYTES = 112 * 1024

V_BIN, V_BADA, V_BPC, V_BPA, V_BOUT, V_L1G, V_L1B, V_L2G, V_L2B, V_BDW, V_CLG, V_CLB, V_WDW = (
    0, 60, 108, 116, 124, 132, 140, 148, 156, 164, 168, 172, 176)
NVEC = 384


def segs(tt):
    if tt < 4:
        return [(slice(0, 512), 0)]
    return [(slice(4 * s, 4 * s + 4), 1 + s) for s in range(4)]


class DSem:
    def __init__(self, h):
        self.h = h
        self.count = 0
        self.last_buf = None
        self.last_op = None


class Buf:
    __slots__ = ("name", "w", "r", "dsem", "rng", "phase", "psum")

    def __init__(self, name, rng=None, phase=None):
        self.name = name
        self.psum = False
        self.w = {}
        self.r = {}
        self.dsem = None
        self.rng = rng
        self.phase = phase


class Op:
    __slots__ = ("eng", "fn", "deps", "sig", "sigidx", "dsem", "sigval", "seq", "calls")


class Rec:
    def __init__(self):
        self.calls = []

    def __getattr__(self, name):
        def f(*a, **k):
            self.calls.append((name, a, k))
            return self
        return f


class Prog:
    def __init__(self, nc, stack):
        self.nc = nc
        self.stack = stack
        self.ops = {e: [] for e in ENGS}
        self.dsems = []
        self.seq = 0
        self.esem = {e: stack.enter_context(nc.semaphore("es_" + e)) for e in ENGS}
        self.region_bufs = []
        self.phase = 0
        self.dsem_by_name = {}

    def dsem_of(self, buf):
        if buf.dsem is None:
            d = self.dsem_by_name.get(buf.name)
            if d is None:
                h = self.stack.enter_context(self.nc.semaphore("ds%d" % len(self.dsems)))
                d = DSem(h)
                self.dsems.append(d)
                self.dsem_by_name[buf.name] = d
            buf.dsem = d
        return buf.dsem

    @staticmethod
    def _key(o):
        return ("d", id(o.dsem)) if o.dsem is not None else ("e", o.eng)

    def op(self, eng, fn, r=(), w=(), dma=None):
        o = Op()
        o.eng = eng
        o.fn = fn
        rec = Rec()
        fn(rec)
        o.calls = rec.calls
        assert len(o.calls) > 0
        o.sig = False
        o.sigidx = 0
        o.dsem = None
        o.sigval = 0
        self.seq += 1
        o.seq = self.seq
        deps = {}
        strong = {}

        def add(p, st=True):
            k = self._key(p)
            q = deps.get(k)
            if q is None or q.seq < p.seq:
                deps[k] = p
            if st:
                q = strong.get(k)
                if q is None or q.seq < p.seq:
                    strong[k] = p

        for b in r:
            for p in b.w.values():
                add(p)
            if b.psum and eng in ("act", "dve"):
                for p in b.r.values():
                    if p.eng != eng and p.eng in ("act", "dve"):
                        add(p)
        for b in w:
            for p in b.w.values():
                add(p, p.eng != eng)
            for p in b.r.values():
                add(p, p.eng != eng)
        k_self = ("e", eng)
        if k_self in deps and dma is None:
            if k_self in strong:
                deps[k_self] = strong[k_self]
            else:
                del deps[k_self]
        if dma is not None:
            dsm = self.dsem_of(dma)
            if dsm.last_buf is not dma and dsm.last_op is not None:
                deps[("d", id(dsm))] = dsm.last_op
        o.deps = deps
        if dma is not None:
            o.dsem = self.dsem_of(dma)
            o.dsem.count += 1
            o.sigval = 16 * o.dsem.count
            o.dsem.last_buf = dma
            o.dsem.last_op = o
        k = self._key(o)
        for b in r:
            b.r[k] = o
        for b in w:
            b.w = {k: o}
            b.r = {}
        self.ops[eng].append(o)
        return o

    def rbuf(self, name, off, nbytes):
        b = Buf(name, (off, off + nbytes), self.phase)
        for ob in self.region_bufs:
            if ob.phase != self.phase and ob.rng[0] < b.rng[1] and b.rng[0] < ob.rng[1]:
                for p in list(ob.w.values()) + list(ob.r.values()):
                    k = self._key(p)
                    q = b.w.get(k)
                    if q is None or q.seq < p.seq:
                        b.w[k] = p
        self.region_bufs.append(b)
        return b

    def prune_region(self):
        pass

    def finalize(self):
        for e in ENGS:
            for o in self.ops[e]:
                for p in o.deps.values():
                    if p.dsem is None and not (p.eng == o.eng and (o.eng == "pe" or not SAME_SYNC)):
                        p.sig = True
        for e in ENGS:
            i = 0
            for o in self.ops[e]:
                if o.sig:
                    i += 1
                    o.sigidx = i

    def emit(self, eng, e):
        known = {}
        for o in self.ops[eng]:
            for p in sorted(o.deps.values(), key=lambda x: x.seq):
                if p.dsem is not None:
                    sem, val, kk = p.dsem.h, p.sigval, id(p.dsem)
                else:
                    if p.eng == eng and (eng == "pe" or not SAME_SYNC):
                        continue
                    sem, val, kk = self.esem[p.eng], p.sigidx, p.eng
                if known.get(kk, 0) >= val:
                    continue
                e.wait_ge(sem, val)
                known[kk] = val
            if o.fn is None:
                continue
            ins = None
            for (name, a, k) in o.calls:
                ins = getattr(e, name)(*a, **k)
            if o.dsem is not None:
                ins.then_inc(o.dsem.h, 16)
            elif o.sig:
                ins.then_inc(self.esem[eng], 1)


class StopBuild(Exception):
    pass


KSTAGE = 99


class Builder:
    def stage(self, k):
        if not hasattr(self, "marks"):
            self.marks = []
        self.marks.append((k, sum(len(o.calls) for o in self.P.ops["pe"])))
        if KSTAGE <= k:
            raise StopBuild()

    def __init__(self):
        self.nc = bass.Bass("TRN2", target_bir_lowering=False)
        self.stack = contextlib.ExitStack()

    def sb(self, name, shape, dt):
        return self.stack.enter_context(self.nc.sbuf_tensor("sb_" + name, list(shape), dt))

    def carve(self, name, shape, dt, nbufs=1, area=1):
        esz = 4 if dt == F32 else 2
        per = int(np.prod(shape[1:])) * esz
        per = (per + 63) // 64 * 64
        off = self.rb_offs[area]
        assert off % 4 == 0
        self.rb_offs[area] += per
        lim = self.rb_lims[area]
        assert self.rb_offs[area] <= lim, (name, area, self.rb_offs[area], lim)
        a = self.regB[:, off // 4:(off + per) // 4]
        if dt == BF16:
            a = a.bitcast(BF16)
        n = int(np.prod(shape[1:]))
        a = a[:, 0:n]
        if len(shape) > 2:
            names = ["d%d" % i for i in range(len(shape) - 1)]
            kw = {names[i]: shape[1 + i] for i in range(len(shape) - 2)}
            a = a.rearrange("p (%s) -> p %s" % (" ".join(names), " ".join(names)), **kw)
        bufs = [self.P.rbuf("%s_%d" % (name, i), off, per) for i in range(nbufs)]
        return a, bufs

    def new_phase(self, start1, start0=0, lim0=A0_END):
        self.P.phase += 1
        self.rb_offs = [start0, start1]
        self.rb_lims = [lim0, RB_BYTES]

    def wload(self, src2d, kch, ncols, slot=None):
        if slot == "free2":
            cand = [j for j in range(NSLOT) if j not in self.pinned]
            i = cand[self.free2_i % len(cand)]
            self.free2_i += 1
        else:
            i = self.ring_i % NSLOT
            self.ring_i += 1
        slot = self.ring[i]
        buf = self.ringb[i]
        view = slot[:, 0:kch * ncols].rearrange("p (k n) -> p k n", k=kch)
        src = src2d.rearrange("(k p) n -> p k n", p=128)
        self.P.op("pool", lambda e, view=view, src=src: e.dma_start(out=view, in_=src), w=[buf], dma=buf)
        return view, buf

    def bank(self, pool):
        lst = self.pools[pool]
        i = self.pool_i[pool] % len(lst)
        self.pool_i[pool] += 1
        b = lst[i]
        return self.ps[b], self.psb[b]

    def mm_acc(self, out, outbuf, pairs, rbufs):
        def fn(e, out=out, pairs=pairs):
            n = len(pairs)
            ins = None
            for i, (lt, rh) in enumerate(pairs):
                ins = e.matmul(out, lhsT=lt, rhs=rh, start=(i == 0), stop=(i == n - 1))
            return ins
        return self.P.op("pe", fn, r=rbufs, w=[outbuf])

    def build(self):
        nc = self.nc
        st = self.stack
        P = self.P = Prog(nc, st)

        def din(name, shape):
            return nc.dram_tensor(name, list(shape), F32, kind="ExternalInput").ap()

        def dout(name, shape):
            return nc.dram_tensor(name, list(shape), F32, kind="ExternalOutput").ap()

        xp = din("xp", [S, D])
        xs = din("xs", [NS, D])
        c5 = din("c5", [5, D])
        ck = [din("ck%d" % g, [L, 4, WIN[g], 2 * 512]) for g in range(3)]
        stc = din("stc", [L, 4, 30, 512])
        w_ada = din("w_ada", [L, D, 6 * D])
        w_in = din("w_in", [L, D, 7680])
        w_pc = din("w_pc", [L, 512, D])
        w_pa = din("w_pa", [L, 512, D])
        w_out = din("w_out", [L, D, D])
        w_gate = din("w_gate", [L, D, DFF])
        w_up = din("w_up", [L, D, DFF])
        w_down = din("w_down", [L, DFF, D])
        vecs = din("vecs", [L, NVEC, 128])
        ident_d = din("ident", [128, 128])
        masks_d = din("masks", [128, 12 * 512])
        smp_d = din("smp", [128, 12 * 32])
        smc_d = din("smc", [4, 12 * 32])

        yp = dout("yp", [S, D])
        ys = dout("ys", [NS, D])
        kvp = [dout("kvp%d" % g, [L, WIN[g], 2, 512]) for g in range(3)]
        cvp = dout("cvp", [L, 30, 512])
        kvs = [dout("kvs%d" % g, [L, 4, WIN[g], 2 * 512]) for g in range(3)]
        cvs = dout("cvs", [L, 4, 30, 512])
        rscr = nc.dram_tensor("rscr", [128, 8, NT], F32).ap()

        U = self.sb("U", [128, 8, NT], BF16)
        Ub = [[Buf("U%d_%d" % (c, t)) for t in range(5)] for c in range(8)]
        self.ring = [self.sb("ring%d" % i, [128, 4096], BF16) for i in range(NSLOT)]
        self.ringb = [Buf("ring%d" % i) for i in range(NSLOT)]
        self.ring_i = 0
        self.pinned = []
        self.free2_i = 0
        self.ln_ri = 0
        ident = self.sb("ident", [128, 128], F32)
        ones_bf = self.sb("ones_bf", [128, 128], BF16)
        onesD = self.sb("onesD", [128, 128], BF16)
        onesC = self.sb("onesC", [128, 128], BF16)
        ones_f = self.sb("ones_f", [128, 64], F32)
        eps_t = self.sb("eps_t", [128, 1], F32)
        alpha_t = self.sb("alpha_t", [128, 8, 5], F32)
        masks = self.sb("masks", [128, 12, 512], BF16)
        smp = self.sb("smp", [128, 12, 32], F32)
        smc = self.sb("smc", [4, 12, 32], F32)
        vecT = self.sb("vecT", [128, L, NVEC], F32)
        bq8 = self.sb("bq8", [128, L, 12], F32)
        modT = self.sb("modT", [128, L, 48, 5], F32)
        coef = self.sb("coef", [128, 5, 4, 8, 5], F32)
        scT = self.sb("scT", [128, 8, 5], BF16)
        ctmp = self.sb("ctmp", [128, 8, 5], F32)
        CONST = Buf("const")
        MASKS = Buf("masks")
        VEC = Buf("vec")
        MOD = Buf("mod")
        COEF = Buf("coef")
        SCT = Buf("sct")
        self.regB = self.sb("regB", [128, RB_BYTES // 4], F32)
        self.rb_offs = [0, 0]
        self.rb_lims = [0, RB_BYTES]

        self.ps = [st.enter_context(nc.psum_tensor("ps%d" % i, [128, 512], F32)) for i in range(8)]
        self.psb = [Buf("ps%d" % i) for i in range(8)]
        for b in self.psb:
            b.psum = True
        self.pools = {"acc": [0, 1, 2], "o": [3, 4], "z": [5, 6], "misc": [7], "sS": [0, 1, 2], "tr": [7, 0, 1, 2], "all": [0, 1, 2, 3, 4, 5, 6, 7]}
        self.pool_i = {k: 0 for k in self.pools}

        def vcol(l, row):
            return vecT[:, l, row:row + 1]

        self.new_phase(0, 0, 0)
        P.op("sp", lambda e: e.dma_start(out=ident[:], in_=ident_d), w=[CONST], dma=CONST)
        P.op("pool", lambda e: e.dma_start(out=masks[:], in_=masks_d.rearrange("p (a b) -> p a b", a=12)),
             w=[MASKS], dma=MASKS)
        SM = Buf("sm")
        P.op("sp", lambda e: e.dma_start(out=smp[:], in_=smp_d.rearrange("p (a b) -> p a b", a=12)), w=[SM], dma=SM)
        SM2 = Buf("sm2")
        P.op("sp", lambda e: e.dma_start(out=smc[:], in_=smc_d.rearrange("p (a b) -> p a b", a=12)), w=[SM2], dma=SM2)
        ONES = Buf("ones")
        P.op("dve", lambda e: e.memset(ones_bf[:], 1.0), w=[ONES])
        P.op("dve", lambda e: e.memset(onesD[:], 1.0 / 1024.0), w=[ONES])
        P.op("dve", lambda e: e.memset(onesC[:], 1.0 / 512.0), w=[ONES])
        P.op("dve", lambda e: e.memset(ones_f[:], 1.0), w=[ONES])
        P.op("dve", lambda e: e.memset(eps_t[:], EPS), w=[ONES])
        P.op("dve", lambda e: e.memset(alpha_t[:], ALPHA), w=[ONES])

        D2D = [Buf("d2d%d" % i) for i in range(4)]
        self.d2d_list = []
        self.d2d_i = 0
        for l in range(L):
            for s in range(4):
                self.d2d_list.append((stc[l, s, 4:30, :], cvs[l, s, 0:26, :]))
        nsmall = len(self.d2d_list)
        for l in range(L):
            for g in range(3):
                W = WIN[g]
                for s in range(4):
                    self.d2d_list.append((ck[g][l, s, 4:W, :], kvs[g][l, s, 0:W - 4, :]))

        def emit_d2d(n):
            for _ in range(n):
                if self.d2d_i >= len(self.d2d_list):
                    return
                src, dst = self.d2d_list[self.d2d_i]
                bq = D2D[self.d2d_i % 4]
                self.d2d_i += 1
                P.op("act", lambda e, src=src, dst=dst: e.dma_start(out=dst, in_=src), w=[bq], dma=bq)
        self.emit_d2d = emit_d2d
        emit_d2d(nsmall)

        vraw, vrawb = self.carve("vraw", [128, 2, 128], F32, 2)
        for l in range(L):
            for j in range(3):
                i = (l * 3 + j) % 2
                P.op("sp", lambda e, i=i, l=l, j=j: e.dma_start(out=vraw[:, i, :], in_=vecs[l, 128 * j:128 * j + 128, :]),
                     w=[vrawb[i]], dma=vrawb[i])
                pt, ptb = self.bank("misc")
                P.op("pe", lambda e, pt=pt, i=i: e.transpose(pt[:, 0:128], vraw[:, i, :], ident[:]),
                     r=[vrawb[i], CONST], w=[ptb])
                P.op("dve", lambda e, pt=pt, l=l, j=j: e.tensor_copy(out=vecT[:, l, 128 * j:128 * j + 128], in_=pt[:, 0:128]),
                     r=[ptb], w=[VEC])
        for l in range(L):
            P.op("dve", lambda e, l=l: e.tensor_scalar(out=bq8[:, l, :], in0=vecT[:, l, 8:20], scalar1=0.125,
                                                       scalar2=None, op0=ALU.mult), r=[VEC], w=[VEC])
        c5t, c5b = self.carve("c5t", [128, 1024], F32, 1)
        P.op("sp", lambda e: e.dma_start(out=c5t[0:5, :], in_=c5), w=c5b, dma=c5b[0])
        pt, ptb = self.bank("misc")

        def fn_ct(e, pt=pt):
            ins = None
            for k in range(8):
                ins = e.transpose(pt[:, 5 * k:5 * k + 5], c5t[0:5, 128 * k:128 * k + 128], ident[0:5, 0:5])
            return ins
        P.op("pe", fn_ct, r=[c5b[0], CONST], w=[ptb])
        P.op("act", lambda e, pt=pt: e.activation(out=scT[:], in_=pt[:, 0:40].rearrange("p (a b) -> p a b", a=8),
                                                  func=AF.Silu), r=[ptb], w=[SCT])
        def mod_group(l, grp, pm, pmb, slot=None):
            Wt, Wb = self.wload(w_ada[l, :, 512 * grp:512 * grp + 512], 8, 512, slot=slot)

            def fn_mod(e):
                ins = None
                for jj in range(4):
                    j = 4 * grp + jj
                    for k in range(8):
                        ins = e.matmul(pm[:, 5 * j:5 * j + 5], lhsT=Wt[:, k, 128 * jj:128 * jj + 128],
                                       rhs=scT[:, k, :], start=(k == 0), stop=(k == 7))
                return ins
            P.op("pe", fn_mod, r=[Wb, SCT], w=[pmb])

        def mod_finish(l, pm, pmb):
            P.op("dve", lambda e: e.tensor_tensor(
                out=modT[:, l], in0=pm[:, 0:240].rearrange("p (a b) -> p a b", a=48),
                in1=vecT[:, l, V_BADA:V_BADA + 48].unsqueeze(2).to_broadcast([128, 48, 5]), op=ALU.add),
                r=[pmb, VEC], w=[MOD])
            for (a, b) in ((8, 24), (32, 48)):
                P.op("dve", lambda e, a=a, b=b: e.tensor_scalar(
                    out=modT[:, l, a:b], in0=modT[:, l, a:b], scalar1=1.0, scalar2=None, op0=ALU.add),
                    r=[MOD], w=[MOD])

        pm0, pm0b = self.bank("acc")
        for grp in range(12):
            mod_group(0, grp, pm0, pm0b)
        mod_finish(0, pm0, pm0b)

        RS = [[Buf("rs%d_%d" % (c, t)) for t in range(5)] for c in range(8)]

        def vb(l, row):
            return vecT[:, l, row:row + 8].unsqueeze(2).to_broadcast([128, 8, 5])

        def cop(fn):
            P.op("dve", fn, r=[MOD, VEC, COEF, ONES], w=[COEF])

        SH1, SC1P, G1P, SH2, SC2P, G2P = 0, 8, 16, 24, 32, 40
        cop(lambda e: e.tensor_copy(out=coef[:, 0, 0], in_=modT[:, 0, SC1P:SC1P + 8]))
        cop(lambda e: e.tensor_copy(out=coef[:, 0, 1], in_=modT[:, 0, SH1:SH1 + 8]))
        cop(lambda e: e.tensor_copy(out=coef[:, 0, 2], in_=alpha_t[:]))
        cop(lambda e: e.tensor_tensor(out=coef[:, 0, 3], in0=modT[:, 0, G1P:G1P + 8], in1=vb(0, V_BOUT), op=ALU.mult))
        def coef_s1(l):
            s1 = 1 + 2 * l
            cop(lambda e, l=l, s1=s1: e.tensor_tensor(out=coef[:, s1, 0], in0=modT[:, l, SC2P:SC2P + 8],
                                                      in1=vb(l, V_L1G), op=ALU.mult))
            cop(lambda e, l=l, s1=s1: e.tensor_tensor(out=coef[:, s1, 1], in0=modT[:, l, SC2P:SC2P + 8],
                                                      in1=vb(l, V_L1B), op=ALU.mult))
            cop(lambda e, l=l, s1=s1: e.tensor_tensor(out=coef[:, s1, 1], in0=coef[:, s1, 1],
                                                      in1=modT[:, l, SH2:SH2 + 8], op=ALU.add))
            cop(lambda e, l=l, s1=s1: e.tensor_tensor(out=coef[:, s1, 2], in0=alpha_t[:], in1=vb(l, V_L1G), op=ALU.mult))
            cop(lambda e, l=l, s1=s1: e.tensor_tensor(out=coef[:, s1, 3], in0=alpha_t[:], in1=vb(l, V_L1B), op=ALU.mult))

        def coef_s2(l):
            s2 = 2 + 2 * l
            if l + 1 < L:
                n = l + 1
                cop(lambda e, l=l, s2=s2, n=n: e.tensor_tensor(out=coef[:, s2, 0], in0=modT[:, n, SC1P:SC1P + 8],
                                                               in1=vb(l, V_L2G), op=ALU.mult))
                cop(lambda e, l=l, s2=s2, n=n: e.tensor_tensor(out=coef[:, s2, 1], in0=modT[:, n, SC1P:SC1P + 8],
                                                               in1=vb(l, V_L2B), op=ALU.mult))
                cop(lambda e, l=l, s2=s2, n=n: e.tensor_tensor(out=coef[:, s2, 1], in0=coef[:, s2, 1],
                                                               in1=modT[:, n, SH1:SH1 + 8], op=ALU.add))
                cop(lambda e, l=l, s2=s2: e.tensor_tensor(out=coef[:, s2, 2], in0=alpha_t[:], in1=vb(l, V_L2G), op=ALU.mult))
                cop(lambda e, l=l, s2=s2: e.tensor_tensor(out=coef[:, s2, 3], in0=alpha_t[:], in1=vb(l, V_L2B), op=ALU.mult))
                cop(lambda e, n=n: e.tensor_tensor(out=ctmp[:], in0=modT[:, n, G1P:G1P + 8], in1=vb(n, V_BOUT), op=ALU.mult))
                cop(lambda e, s2=s2: e.tensor_tensor(out=coef[:, s2, 3], in0=coef[:, s2, 3], in1=ctmp[:], op=ALU.add))
            else:
                cop(lambda e, l=l, s2=s2: e.tensor_copy(out=coef[:, s2, 0], in_=vb(l, V_L2G)))
                cop(lambda e, l=l, s2=s2: e.tensor_copy(out=coef[:, s2, 1], in_=vb(l, V_L2B)))

        coef_s1(0)
        pm1, pm1b = self.ps[6], self.psb[6]
        self.deferred_mod = [(lambda grp=grp: mod_group(1, grp, pm1, pm1b, slot="free2")) for grp in range(12)]

        def mod1_finish():
            mod_finish(1, pm1, pm1b)
            coef_s2(0)
            coef_s1(1)
            coef_s2(1)
        self.deferred_mod.append(mod1_finish)

        def produce(src, srcbufs, c, tt, stg, rst, rstb, src_psum=False):
            t0, n = TT[tt]
            for (cs, s) in segs(tt):
                P.op("act", lambda e, cs=cs, s=s: e.activation(
                    out=U[:, c, t0 + cs.start:t0 + cs.stop], in_=src[:, cs], func=AF.Identity,
                    scale=coef[:, stg, 0, c, s:s + 1], bias=coef[:, stg, 1, c, s:s + 1]),
                    r=srcbufs + [COEF], w=[Ub[c][tt]])
                if src_psum:
                    P.op("dve", lambda e, cs=cs, s=s: e.tensor_scalar(
                        out=rst[:, cs], in0=src[:, cs], scalar1=coef[:, stg, 2, c, s:s + 1],
                        scalar2=coef[:, stg, 3, c, s:s + 1], op0=ALU.mult, op1=ALU.add),
                        r=srcbufs + [COEF], w=[rstb])
                else:
                    P.op("act", lambda e, cs=cs, s=s: e.activation(
                        out=rst[:, cs], in_=src[:, cs], func=AF.Identity, scale=coef[:, stg, 2, c, s:s + 1],
                        bias=coef[:, stg, 3, c, s:s + 1]), r=srcbufs + [COEF], w=[rstb])
            P.op("sp", lambda e: e.dma_start(out=rscr[:, c, t0:t0 + n], in_=rst[:, 0:n]), r=[rstb], w=[RS[c][tt]],
                 dma=rstb)

        xst, xstb = self.carve("xst", [128, 4, 1024], F32, 4)
        XIN = KSTAGE > 0
        rst, rstb = self.carve("rst", [128, 4, 512], F32, 4)
        ri = 0
        for tt in (range(5) if XIN else []):
            t0, n = TT[tt]
            if tt < 4:
                for tb in range(4):
                    P.op("sp", lambda e, tb=tb, t0=t0: e.dma_start(out=xst[:, tb, :], in_=xp[t0 + 128 * tb:t0 + 128 * tb + 128, :]),
                         w=[xstb[tb]], dma=xstb[tb])
            else:
                P.op("sp", lambda e: e.dma_start(out=xst[0:16, 0, :], in_=xs), w=[xstb[0]], dma=xstb[0])
            for c in range(8):
                pt, ptb = self.bank("acc")
                if tt < 4:
                    def fn_xt(e, pt=pt, c=c):
                        ins = None
                        for tb in range(4):
                            ins = e.transpose(pt[:, 128 * tb:128 * tb + 128], xst[:, tb, 128 * c:128 * c + 128], ident[:])
                        return ins
                    P.op("pe", fn_xt, r=xstb + [CONST], w=[ptb])
                else:
                    P.op("pe", lambda e, pt=pt, c=c: e.transpose(pt[:, 0:16], xst[0:16, 0, 128 * c:128 * c + 128],
                                                                 ident[0:16, 0:16]), r=[xstb[0], CONST], w=[ptb])
                produce(pt, [ptb], c, tt, 0, rst[:, ri % 4, :], rstb[ri % 4], src_psum=True)
                ri += 1

        try:
            self.stage(1)
            for l in range(L):
                self.layer(l, locals())
        except StopBuild:
            pass

        self.emit_d2d(1000)
        last = []
        allops = [o for e in ENGS for o in P.ops[e] if o.dsem is not None]
        lastd = {}
        for o in allops:
            q = lastd.get(id(o.dsem))
            if q is None or q.sigval < o.sigval:
                lastd[id(o.dsem)] = o
        fin = Op()
        fin.eng = "sp"
        fin.fn = None
        fin.sig = False
        fin.sigidx = 0
        fin.dsem = None
        fin.sigval = 0
        fin.seq = P.seq + 1
        fin.deps = {("d", k): o for k, o in lastd.items()}
        P.ops["sp"].append(fin)

        P.finalize()
        with nc.Block() as block:
            @block.sync
            def _(e):
                P.emit("sp", e)

            @block.scalar
            def _(e):
                P.emit("act", e)

            @block.gpsimd
            def _(e):
                P.emit("pool", e)

            @block.vector
            def _(e):
                P.emit("dve", e)

            @block.tensor
            def _(e):
                P.emit("pe", e)
        self.stack.close()
        return nc

    def layer(self, l, env):
        P = self.P
        g = env
        U, Ub, ident, CONST, ONES = g["U"], g["Ub"], g["ident"], g["CONST"], g["ONES"]
        ones_bf, onesD, onesC, ones_f, eps_t = g["ones_bf"], g["onesD"], g["onesC"], g["ones_f"], g["eps_t"]
        masks, MASKS, smp, smc, SM, SM2 = g["masks"], g["MASKS"], g["smp"], g["smc"], g["SM"], g["SM2"]
        vecT, VEC, bq8, modT, MOD, coef, COEF = g["vecT"], g["VEC"], g["bq8"], g["modT"], g["MOD"], g["coef"], g["COEF"]
        w_in, w_pc, w_pa, w_out, w_gate, w_up, w_down = (g["w_in"], g["w_pc"], g["w_pa"], g["w_out"], g["w_gate"],
                                                         g["w_up"], g["w_down"])
        ck, stc, kvp, cvp, kvs, cvs, yp, ys, rscr, RS = (g["ck"], g["stc"], g["kvp"], g["cvp"], g["kvs"], g["cvs"],
                                                         g["yp"], g["ys"], g["rscr"], g["RS"])
        produce = g["produce"]
        G1P, G2P = 16, 40

        def vcol(row):
            return vecT[:, l, row:row + 1]

        def ubufs(tt):
            return [Ub[k][tt] for k in range(8)]

        def gemm_u(ps, psb, Wt, Wb, col0, tt, n):
            t0 = TT[tt][0]
            self.mm_acc(ps[:, 0:n], psb, [(Wt[:, k, col0:col0 + 128], U[:, k, t0:t0 + n]) for k in range(8)],
                        [Wb] + ubufs(tt))

        self.new_phase(A0_END)
        cact, cactb = self.carve("cact", [128, 4, NT], BF16, 5)
        off_c = self.rb_offs[1]

        glu, glub = self.carve("glu", [128, 4, 30 + S], BF16, 4)
        sglu, sglub = self.carve("sglu", [128, 4, 4, 34], BF16, 1)
        gluf, glufb = self.carve("gluf", [128, 4, 48], F32, 1)
        ycv, ycvb = self.carve("ycv", [128, 4, NT], F32, 5, area=0)
        dwd, dwdb = self.carve("dwd", [128, 1, 31 * 128], BF16, 1)
        sig, sigb = self.carve("sig", [128, 2, 512], F32, 2)
        ybq, ybqb = self.carve("ybq", [128, 4, 512], BF16, 4)
        stt, sttb = self.carve("stt", [128, 4, 512], F32, 4)
        tmpa, tmpab = self.carve("tmpa", [128, 2, 512], F32, 2)
        sst, sstb = self.carve("sst", [128, 4, 512], F32, 1)
        cvo, cvob = self.carve("cvo", [128, 2, 512], F32, 2)

        for c in range(4):
            P.op("dve", lambda e, c=c: e.memset(glu[:, c, 0:30], 0.0), w=[glub[c]])
        for s in range(4):
            P.op("sp", lambda e, s=s: e.dma_start(out=sst[0:30, s, :], in_=stc[l, s, :, :]), w=sstb, dma=sstb[0])
        for c in range(4):
            pt, ptb = self.bank("misc")

            def fn_st(e, pt=pt, c=c):
                ins = None
                for s in range(4):
                    ins = e.transpose(pt[:, 30 * s:30 * s + 30], sst[0:30, s, 128 * c:128 * c + 128], ident[0:30, 0:30])
                return ins
            P.op("pe", fn_st, r=sstb + [CONST], w=[ptb])
            P.op("act", lambda e, pt=pt, c=c: e.activation(out=sglu[:, c, :, 0:30],
                                                           in_=pt[:, 0:120].rearrange("p (s t) -> p s t", s=4),
                                                           func=AF.Copy), r=[ptb], w=sglub)
        WA, WAb = self.wload(w_in[l, :, 0:512], 8, 512)
        WG, WGb = self.wload(w_in[l, :, 512:1024], 8, 512)
        self.pinned = [(self.ring_i - 2) % NSLOT, (self.ring_i - 1) % NSLOT]
        self.si = 0

        def drain_mod(n):
            for _ in range(n):
                if l == 0 and self.deferred_mod:
                    self.deferred_mod.pop(0)()

        def glu_chunk(c):
            for tt in range(5):
                t0, n = TT[tt]
                pa, pab = self.bank("acc")
                gemm_u(pa, pab, WA, WAb, 128 * c, tt, n)
                pg, pgb = self.bank("acc")
                gemm_u(pg, pgb, WG, WGb, 128 * c, tt, n)
                sg = sig[:, self.si % 2, :]
                sgb = sigb[self.si % 2]
                self.si += 1
                P.op("act", lambda e: e.activation(out=sg[:, 0:n], in_=pg[:, 0:n], func=AF.Sigmoid,
                                                   bias=vcol(V_BIN + 4 + c), scale=1.0), r=[pgb, VEC], w=[sgb])
                if tt < 4:
                    P.op("dve", lambda e: e.scalar_tensor_tensor(
                        out=glu[:, c, 30 + t0:30 + t0 + 512], in0=pa[:, 0:512], scalar=vcol(V_BIN + c), in1=sg[:, 0:512],
                        op0=ALU.add, op1=ALU.mult), r=[pab, sgb, VEC], w=[glub[c]])
                    if tt == 3:
                        P.op("dve", lambda e: e.scalar_tensor_tensor(
                            out=gluf[:, c, 0:32], in0=pa[:, 480:512], scalar=vcol(V_BIN + c), in1=sg[:, 480:512],
                            op0=ALU.add, op1=ALU.mult), r=[pab, sgb, VEC], w=glufb)
                else:
                    P.op("dve", lambda e: e.scalar_tensor_tensor(
                        out=gluf[:, c, 32:48], in0=pa[:, 0:16], scalar=vcol(V_BIN + c), in1=sg[:, 0:16],
                        op0=ALU.add, op1=ALU.mult), r=[pab, sgb, VEC], w=glufb)
                    P.op("dve", lambda e: e.tensor_copy(out=sglu[:, c, :, 30:34],
                                                        in_=gluf[:, c, 32:48].rearrange("p (s t) -> p s t", s=4)),
                         r=glufb, w=sglub)

        def conv_state_outputs():
            pt, ptb = self.bank("misc")

            def fn_cs(e, pt=pt):
                ins = None
                for c in range(4):
                    ins = e.transpose(pt[0:30, 128 * c:128 * c + 128], gluf[:, c, 2:32], ident[:])
                return ins
            P.op("pe", fn_cs, r=glufb + [CONST], w=[ptb])
            P.op("act", lambda e, pt=pt: e.activation(out=cvo[0:30, 0, :], in_=pt[0:30, :], func=AF.Copy), r=[ptb], w=[cvob[0]])
            P.op("sp", lambda e: e.dma_start(out=cvp[l, :, :], in_=cvo[0:30, 0, :]), r=[cvob[0]], w=[Buf("x")], dma=cvob[0])
            pt, ptb = self.bank("misc")

            def fn_cs2(e, pt=pt):
                ins = None
                for c in range(4):
                    ins = e.transpose(pt[0:16, 128 * c:128 * c + 128], gluf[:, c, 32:48], ident[:])
                return ins
            P.op("pe", fn_cs2, r=glufb + [CONST], w=[ptb])
            P.op("act", lambda e, pt=pt: e.activation(out=cvo[0:16, 1, :], in_=pt[0:16, :], func=AF.Copy), r=[ptb], w=[cvob[1]])
            for s in range(4):
                P.op("sp", lambda e, s=s: e.dma_start(out=cvs[l, s, 26:30, :], in_=cvo[4 * s:4 * s + 4, 1, :]),
                     r=[cvob[1]], w=[Buf("x")], dma=cvob[1])

        glu_chunk(0)
        drain_mod(2)
        glu_chunk(1)
        drain_mod(2)
        self.yi = 0

        def ln_tile(tt):
            t0, n = TT[tt]
            pm, pmb = self.bank("o")
            pq, pqb = self.bank("z")
            for c in range(4):
                yb_ = ybq[:, self.yi % 4, :]
                ybb = ybqb[self.yi % 4]
                self.yi += 1
                ys_ = ybq[:, self.yi % 4, :]
                ysb = ybqb[self.yi % 4]
                self.yi += 1
                P.op("dve", lambda e, yb_=yb_, c=c: e.tensor_copy(out=yb_[:, 0:n], in_=ycv[:, c, t0:t0 + n]),
                     r=[ycvb[tt]], w=[ybb])
                P.op("act", lambda e, ys_=ys_, c=c: e.activation(out=ys_[:, 0:n], in_=ycv[:, c, t0:t0 + n],
                                                                func=AF.Square), r=[ycvb[tt]], w=[ysb])
                P.op("pe", lambda e, yb_=yb_, c=c: e.matmul(pm[:, 0:n], lhsT=onesC[:], rhs=yb_[:, 0:n],
                                                           start=(c == 0), stop=(c == 3)), r=[ybb, ONES], w=[pmb])
                P.op("pe", lambda e, ys_=ys_, c=c: e.matmul(pq[:, 0:n], lhsT=onesC[:], rhs=ys_[:, 0:n],
                                                           start=(c == 0), stop=(c == 3)), r=[ysb, ONES], w=[pqb])
            mean, rstd = self.ln_stats(pm, pmb, pq, pqb, stt, sttb, n, eps_t, ONES)
            for c in range(4):
                ta = tmpa[:, c % 2, :]
                tab = tmpab[c % 2]
                P.op("dve", lambda e, ta=ta, c=c: e.tensor_tensor(out=ta[:, 0:n], in0=ycv[:, c, t0:t0 + n],
                                                                 in1=mean[:, 0:n], op=ALU.subtract),
                     r=[ycvb[tt], sttb[0]], w=[tab])
                P.op("dve", lambda e, ta=ta: e.tensor_tensor(out=ta[:, 0:n], in0=ta[:, 0:n], in1=rstd[:, 0:n],
                                                            op=ALU.mult), r=[tab, sttb[2]], w=[tab])
                P.op("act", lambda e, ta=ta, c=c: e.activation(out=cact[:, c, t0:t0 + n], in_=ta[:, 0:n],
                                                              func=AF.Silu, scale=vcol(V_CLG + c), bias=vcol(V_CLB + c)),
                     r=[tab, VEC], w=[cactb[tt]])

        def conv_chunk(c, with_ln):
            dw = dwd[:, 0, :].rearrange("p (k n) -> p k n", k=31)
            dwb = dwdb[0]
            P.op("dve", lambda e: e.tensor_tensor(
                out=dw, in0=ident[:].unsqueeze(1).to_broadcast([128, 31, 128]),
                in1=vecT[:, l, V_WDW + c:V_WDW + 124:4].unsqueeze(2).to_broadcast([128, 31, 128]), op=ALU.mult),
                r=[CONST, VEC], w=[dwb])
            for tt in range(5):
                t0, n = TT[tt]
                pc, pcb = self.bank("acc")
                if tt < 4:
                    pairs = [(dw[:, k, :], glu[:, c, t0 + k:t0 + k + 512]) for k in range(31)]
                    self.mm_acc(pc[:, 0:512], pcb, pairs, [dwb, glub[c]])
                else:
                    pairs = [(dw[:, k, :], sglu[:, c, :, k:k + 4]) for k in range(31)]
                    self.mm_acc(pc[:, 0:16], pcb, pairs, [dwb] + sglub)
                P.op("act", lambda e, pc=pc, t0=t0, n=n: e.activation(out=ycv[:, c, t0:t0 + n], in_=pc[:, 0:n],
                                                                     func=AF.Identity, bias=vcol(V_BDW + c), scale=1.0),
                     r=[pcb, VEC], w=[ycvb[tt]])
                if with_ln:
                    ln_tile(tt)

        conv_chunk(0, False)
        drain_mod(2)
        glu_chunk(2)
        drain_mod(2)
        conv_chunk(1, False)
        drain_mod(2)
        glu_chunk(3)
        drain_mod(2)
        conv_state_outputs()
        conv_chunk(2, False)
        drain_mod(100)
        conv_chunk(3, True)

        self.stage(2 + 10 * l)
        self.new_phase(off_c)
        oatt, oattb = self.carve("oatt", [128, 4, NT], BF16, 1)
        off_co = self.rb_offs[1]
        Qt, Qtb = self.carve("Qt", [128, NT], BF16, 5)
        Kt, Ktb = self.carve("Kt", [128, NT], BF16, 5)
        Ktf, Ktfb = self.carve("Ktf", [128, NT], F32, 5, area=0)
        Vtf, Vtfb = self.carve("Vtf", [128, NT], F32, 5, area=0)
        Vb, Vbb = self.carve("Vb", [128, 16, 128], BF16, 4)
        Oacc, Oaccb = self.carve("Oacc", [128, NT], F32, 1, area=0)
        Zacc, Zaccb = self.carve("Zacc", [128, NT], F32, 1, area=0)
        Eb, Ebb = self.carve("Eb", [128, 2, 512], BF16, 2)
        Pt, Ptb = self.carve("Pt", [128, 4, 512], BF16, 4)
        kvst, kvstb = self.carve("kvst", [128, 4, 512], F32, 4)
        sK32, sK32b = self.carve("sK32", [128, 4, 4, 128], F32, 4)
        sKt, sKtb = self.carve("sKt", [128, 2, 512], BF16, 2)
        sNew, sNewb = self.carve("sNew", [128, 4, 2, 128], F32, 1)
        sE, sEb = self.carve("sE", [128, 2, 32], F32, 2)
        sPp, sPpb = self.carve("sPp", [128, 32], BF16, 1)
        sPc, sPcb = self.carve("sPc", [128, 32], BF16, 1)
        sVb, sVbb = self.carve("sVb", [128, 4, 4, 128], BF16, 4)
        sNb, sNbb = self.carve("sNb", [128, 4, 128], BF16, 1)
        Qf, Qfb = self.carve("Qf", [128, 16], F32, 1)
        Ktfs, Ktfsb = self.carve("Ktfs", [128, 16], F32, 1)

        self.ei = 0
        self.pi = 0
        ki = 0
        import os
        for hp in range(int(os.environ.get('HPN', '4'))):
            for gi in range(int(os.environ.get('GIN', '3'))):
                d = DIL[gi]
                nblk = S // d // 128
                Wn = WIN[gi]
                qcol = 1024 + (0 + gi) * 512 + hp * 128
                kcol = 1024 + (3 + gi) * 512 + hp * 128
                vcol_ = 1024 + (6 + gi) * 512 + hp * 128
                self.emit_d2d(1)
                WQ, WQb = self.wload(w_in[l, :, qcol:qcol + 128], 8, 128)
                WK, WKb = self.wload(w_in[l, :, kcol:kcol + 128], 8, 128)
                WV, WVb = self.wload(w_in[l, :, vcol_:vcol_ + 128], 8, 128)
                nt_ = 1 if gi == 0 else 4
                for s_ in range(4):
                    if gi == 0:
                        srck = ck[gi][l, s_, :, :].rearrange("j (k f) -> j k f", k=2)[:, 0, hp * 128:hp * 128 + 128]
                        srcv = ck[gi][l, s_, :, :].rearrange("j (k f) -> j k f", k=2)[:, 1, hp * 128:hp * 128 + 128]
                        P.op("sp", lambda e, srck=srck, s_=s_: e.dma_start(out=sK32[:, s_, 0, :], in_=srck), w=[sK32b[s_]],
                             dma=sK32b[s_])
                        P.op("pool", lambda e, srcv=srcv, s_=s_: e.dma_start(out=sVb[:, s_, 0, :], in_=srcv), w=[sVbb[s_]],
                             dma=sVbb[s_])
                    else:
                        v4 = ck[gi][l, s_, :, :].rearrange("(j t) (k f) -> j t k f", t=d, k=2)
                        srck = v4[:, 0:4, 0, hp * 128:hp * 128 + 128]
                        srcv = v4[:, 0:4, 1, hp * 128:hp * 128 + 128]
                        P.op("sp", lambda e, srck=srck, s_=s_: e.dma_start(out=sK32[:, s_, :, :], in_=srck), w=[sK32b[s_]],
                             dma=sK32b[s_])
                        P.op("pool", lambda e, srcv=srcv, s_=s_: e.dma_start(out=sVb[:, s_, :, :], in_=srcv), w=[sVbb[s_]],
                             dma=sVbb[s_])
                for tt in range(5):
                    t0, n = TT[tt]
                    pq, pqb = self.bank("acc")
                    gemm_u(pq, pqb, WQ, WQb, 0, tt, n)
                    P.op("act", lambda e, pq=pq, t0=t0, n=n, gi=gi, hp=hp: e.activation(
                        out=Qt[:, t0:t0 + n], in_=pq[:, 0:n], func=AF.Identity, scale=0.125,
                        bias=bq8[:, l, gi * 4 + hp:gi * 4 + hp + 1]), r=[pqb, VEC], w=[Qtb[tt]])
                    pk, pkb = self.bank("acc")
                    gemm_u(pk, pkb, WK, WKb, 0, tt, n)
                    P.op("act", lambda e, pk=pk, t0=t0, n=n, kcol=kcol: e.activation(
                        out=Ktf[:, t0:t0 + n], in_=pk[:, 0:n], func=AF.Identity, scale=1.0, bias=vcol(kcol // 128)),
                        r=[pkb, VEC], w=[Ktfb[tt]])
                    P.op("dve", lambda e, pk=pk, t0=t0, n=n, kcol=kcol: e.tensor_scalar(
                        out=Kt[:, t0:t0 + n], in0=pk[:, 0:n], scalar1=vcol(kcol // 128), scalar2=None, op0=ALU.add),
                        r=[pkb, VEC], w=[Ktb[tt]])
                    pv, pvb = self.bank("acc")
                    gemm_u(pv, pvb, WV, WVb, 0, tt, n)
                    P.op("act", lambda e, pv=pv, t0=t0, n=n, vc=vcol_: e.activation(
                        out=Vtf[:, t0:t0 + n], in_=pv[:, 0:n], func=AF.Identity, scale=1.0, bias=vcol(vc // 128)),
                        r=[pvb, VEC], w=[Vtfb[tt]])

                def blk_cols(r, c):
                    a = r + d * 128 * c
                    return a, a + d * 127 + 1, d

                def blk_rc(b):
                    return b // nblk, b % nblk

                for bg in range(4):
                    for kv, src, srcb in ((1, Vtf, Vtfb), (0, Ktf, Ktfb)):
                        blks = [4 * bg + j for j in range(4)]
                        inwin = [(blk_rc(b)[0] + d * 128 * blk_rc(b)[1]) >= S - Wn for b in blks]
                        if kv == 0 and not any(inwin):
                            continue
                        pt, ptb = self.bank("tr")

                        def fn_tr(e, pt=pt, src=src, blks=blks):
                            ins = None
                            for j, b in enumerate(blks):
                                r_, c_ = blk_rc(b)
                                a, z, stp = blk_cols(r_, c_)
                                ins = e.transpose(pt[:, 128 * j:128 * j + 128], src[:, a:z:stp], ident[:])
                            return ins
                        P.op("pe", fn_tr, r=list(srcb[0:4]) + [CONST], w=[ptb])
                        if kv == 1:
                            P.op("dve", lambda e, pt=pt, bg=bg: e.tensor_copy(
                                out=Vb[:, 4 * bg:4 * bg + 4, :], in_=pt[:, :].rearrange("p (a b) -> p a b", a=4)),
                                r=[ptb], w=[Vbb[bg]])
                        if any(inwin):
                            kst = kvst[:, ki % 4, :]
                            kstb = kvstb[ki % 4]
                            ki += 1
                            P.op("act", lambda e, pt=pt, kst=kst: e.activation(out=kst, in_=pt[:, :], func=AF.Copy),
                                 r=[ptb], w=[kstb])
                            for j, b in enumerate(blks):
                                if not inwin[j]:
                                    continue
                                r_, c_ = blk_rc(b)
                                a = r_ + d * 128 * c_ - (S - Wn)
                                dst = kvp[gi][l, a:a + 127 * d + 1:d, kv, hp * 128:hp * 128 + 128]
                                P.op("sp", lambda e, dst=dst, kst=kst, j=j: e.dma_start(out=dst, in_=kst[:, 128 * j:128 * j + 128]),
                                     r=[kstb], w=[Buf("x")], dma=kstb)
                for half in range(2):
                    pt, ptb = self.bank("tr")

                    def fn_sn(e, pt=pt, half=half):
                        ins = None
                        for s in (2 * half, 2 * half + 1):
                            for kv, src in ((0, Ktf), (1, Vtf)):
                                o = ((s % 2) * 2 + kv) * 128
                                ins = e.transpose(pt[0:4, o:o + 128], src[:, S + 4 * s:S + 4 * s + 4], ident[:])
                        return ins
                    P.op("pe", fn_sn, r=[Ktfb[4], Vtfb[4], CONST], w=[ptb])
                    P.op("act", lambda e, pt=pt, half=half: e.activation(
                        out=sNew[0:4, 2 * half:2 * half + 2, :, :],
                        in_=pt[0:4, :].rearrange("p (s k f) -> p s k f", s=2, k=2), func=AF.Copy), r=[ptb], w=sNewb)
                P.op("act", lambda e: e.activation(out=sNb[0:4, :, :], in_=sNew[0:4, :, 1, :], func=AF.Copy), r=sNewb, w=sNbb)
                for s in range(4):
                    dst = kvs[gi][l, s, Wn - 4:Wn, :].rearrange("t (k f) -> t k f", k=2)[:, :, hp * 128:hp * 128 + 128]
                    P.op("sp", lambda e, dst=dst, s=s: e.dma_start(out=dst, in_=sNew[0:4, s, :, :]), r=sNewb, w=[Buf("x")],
                         dma=sNewb[0])

                mk = masks[:, gi * 4 + hp, :]
                first = (gi == 0)
                gidx = gi * 4 + hp
                nt_ = 1 if gi == 0 else 4

                def q_group(qg):
                    if gi == 0:
                        return [(0, 4 * qg + j) for j in range(4)], (lambda T: T[:, 512 * qg:512 * qg + 512])
                    if gi == 1:
                        return [(qg, j) for j in range(4)], (lambda T: T[:, qg:S:4])
                    return ([(4 * qg + j, 0) for j in range(4)],
                            (lambda T: T[:, 0:S].rearrange("p (i r) -> p r i", r=16)[:, 4 * qg:4 * qg + 4, :]))

                def emit_S(qg, j, r_, c_, po, pob, pz, pzb):
                    hasp = c_ > 0
                    a, z, stp = blk_cols(r_, c_)
                    psA, psAb = self.bank("sS")
                    psB, psBb = self.bank("sS")

                    def fn_s(e):
                        ins = None
                        for hh, ps_ in ((0, psA), (1, psB)):
                            ph = slice(64 * hh, 64 * hh + 64)
                            ins = e.matmul(ps_[:, 0:128], lhsT=Kt[ph, a:z:stp], rhs=Qt[ph, a:z:stp], start=True, stop=True)
                            if hasp:
                                a2, z2, _ = blk_cols(r_, c_ - 1)
                                ins = e.matmul(ps_[:, 128:256], lhsT=Kt[ph, a2:z2:stp], rhs=Qt[ph, a:z:stp],
                                               start=True, stop=True)
                        return ins
                    P.op("pe", fn_s, r=list(Qtb[0:4]) + list(Ktb[0:4]), w=[psAb, psBb])
                    E_ = Eb[:, self.ei % 2, :]
                    Eb_ = Ebb[self.ei % 2]
                    self.ei += 1
                    P_ = Pt[:, self.pi % 4, :]
                    Pb_ = Ptb[self.pi % 4]
                    self.pi += 1
                    nv = 256 if hasp else 128
                    P.op("act", lambda e: e.activation(out=E_[:, 0:nv], in_=psA[:, 0:nv], func=AF.Exp), r=[psAb], w=[Eb_])
                    P.op("act", lambda e: e.activation(out=E_[:, 256:256 + nv], in_=psB[:, 0:nv], func=AF.Exp),
                         r=[psBb], w=[Eb_])
                    if hasp:
                        P.op("dve", lambda e: e.tensor_tensor(out=P_, in0=E_, in1=mk, op=ALU.mult), r=[Eb_, MASKS], w=[Pb_])
                    else:
                        v3 = lambda T: T.rearrange("p (a b) -> p a b", a=2)[:, :, 0:128]
                        P.op("dve", lambda e: e.tensor_tensor(out=v3(P_), in0=v3(E_), in1=v3(mk), op=ALU.mult),
                             r=[Eb_, MASKS], w=[Pb_])
                    return (qg, j, hasp, r_ * nblk + c_, P_, Pb_, po, pob, pz, pzb)

                def emit_PV(st):
                    qg, j, hasp, bcur, P_, Pb_, po, pob, pz, pzb = st

                    def fn_pv(e):
                        ins = None
                        for hh in range(2):
                            ph = slice(64 * hh, 64 * hh + 64)
                            oc = slice(128 * j, 128 * j + 128)
                            ins = e.matmul(po[ph, oc], lhsT=Vb[:, bcur, ph], rhs=P_[:, 256 * hh:256 * hh + 128],
                                           start=True, stop=not hasp)
                            if hasp:
                                ins = e.matmul(po[ph, oc], lhsT=Vb[:, bcur - 1, ph],
                                               rhs=P_[:, 256 * hh + 128:256 * hh + 256], start=False, stop=True)
                            ins = e.matmul(pz[ph, oc], lhsT=ones_bf[:, 0:64], rhs=P_[:, 256 * hh:256 * hh + 128],
                                           start=True, stop=not hasp)
                            if hasp:
                                ins = e.matmul(pz[ph, oc], lhsT=ones_bf[:, 0:64],
                                               rhs=P_[:, 256 * hh + 128:256 * hh + 256], start=False, stop=True)
                        return ins
                    P.op("pe", fn_pv, r=[Pb_, ONES] + Vbb, w=[pob, pzb])
                    if j == 3:
                        _, acc_sl = q_group(qg)
                        pv3 = (lambda T: T[:, :]) if gi < 2 else (lambda T: T[:, :].rearrange("p (r i) -> p r i", r=4))
                        if first:
                            P.op("dve", lambda e: e.tensor_copy(out=acc_sl(Oacc), in_=pv3(po)), r=[pob], w=Oaccb)
                            P.op("act", lambda e: e.activation(out=acc_sl(Zacc), in_=pv3(pz), func=AF.Copy), r=[pzb], w=Zaccb)
                        else:
                            P.op("dve", lambda e: e.tensor_tensor(out=acc_sl(Oacc), in0=pv3(po), in1=acc_sl(Oacc), op=ALU.add),
                                 r=[pob] + Oaccb, w=Oaccb)
                            P.op("dve", lambda e: e.tensor_tensor(out=acc_sl(Zacc), in0=pv3(pz), in1=acc_sl(Zacc), op=ALU.add),
                                 r=[pzb] + Zaccb, w=Zaccb)

                pso_full, psob = self.ps[7], self.psb[7]
                sstate = {}

                def s_T(s):
                    sl = (s % 2)
                    pt, ptb = self.bank("sS")

                    def fn_kt(e):
                        ins = None
                        for t in range(nt_):
                            ins = e.transpose(pt[:, 128 * t:128 * t + 128], sK32[:, s, t, :], ident[:])
                        return ins
                    P.op("pe", fn_kt, r=[sK32b[s], CONST], w=[ptb])
                    kt_ = sKt[:, sl, :]
                    ktb_ = sKtb[sl]
                    P.op("dve", lambda e: e.tensor_copy(out=kt_[:, 0:128 * nt_], in_=pt[:, 0:128 * nt_]), r=[ptb], w=[ktb_])
                    sstate[s] = (None, None, kt_, ktb_, sVb[:, s, :, :], sVbb[s])

                def s_st2(s):
                    kc_, kcb, kt_, ktb_, vb_, vbb_ = sstate[s]
                    pss0, pss0b = self.bank("sS")
                    pss1, pss1b = self.bank("sS")

                    def fn_ss(e):
                        ins = None
                        for hh, pss in ((0, pss0), (1, pss1)):
                            ph = slice(64 * hh, 64 * hh + 64)
                            for t in range(4):
                                tk = 0 if nt_ == 1 else t
                                ins = e.matmul(pss[:, t:t + 1], lhsT=kt_[ph, 128 * tk:128 * tk + 128],
                                               rhs=Qt[ph, S + 4 * s + t:S + 4 * s + t + 1], start=True, stop=True)
                            ins = e.matmul(pss[0:4, 16:20], lhsT=Kt[ph, S + 4 * s:S + 4 * s + 4],
                                           rhs=Qt[ph, S + 4 * s:S + 4 * s + 4], start=True, stop=True)
                        return ins
                    P.op("pe", fn_ss, r=[ktb_, Qtb[4], Ktb[4]], w=[pss0b, pss1b])
                    cs_ = slice(8 * s, 8 * s + 8)
                    for hh, pss, pssb in ((0, pss0, pss0b), (1, pss1, pss1b)):
                        c4 = slice(8 * s + 4 * hh, 8 * s + 4 * hh + 4)
                        P.op("act", lambda e, pss=pss, c4=c4: e.activation(out=sE[:, 0, c4], in_=pss[:, 0:4], func=AF.Exp),
                             r=[pssb], w=[sEb[0]])
                        P.op("act", lambda e, pss=pss, c4=c4: e.activation(out=sE[0:4, 1, c4], in_=pss[0:4, 16:20], func=AF.Exp),
                             r=[pssb], w=[sEb[1]])
                    P.op("dve", lambda e: e.tensor_tensor(out=sPp[:, cs_], in0=sE[:, 0, cs_], in1=smp[:, gidx, cs_], op=ALU.mult),
                         r=[sEb[0], SM], w=sPpb)
                    P.op("dve", lambda e: e.tensor_tensor(out=sPc[0:4, cs_], in0=sE[0:4, 1, cs_], in1=smc[0:4, gidx, cs_],
                                                          op=ALU.mult), r=[sEb[1], SM2], w=sPcb)

                def s_st3(s):
                    kc_, kcb, kt_, ktb_, vb_, vbb_ = sstate[s]

                    def fn_so(e):
                        ins = None
                        for hh in range(2):
                            ph = slice(64 * hh, 64 * hh + 64)
                            c0_ = s * 8 + hh * 4
                            o4 = slice(4 * s, 4 * s + 4)
                            z4 = slice(16 + 4 * s, 16 + 4 * s + 4)
                            if nt_ == 1:
                                ins = e.matmul(pso_full[ph, o4], lhsT=vb_[:, 0, ph], rhs=sPp[:, c0_:c0_ + 4], start=True, stop=False)
                            else:
                                for t in range(4):
                                    ins = e.matmul(pso_full[ph, 4 * s + t:4 * s + t + 1], lhsT=vb_[:, t, ph],
                                                   rhs=sPp[:, c0_ + t:c0_ + t + 1], start=(t == 0), stop=False,
                                                   skip_group_check=True)
                            ins = e.matmul(pso_full[ph, o4], lhsT=sNb[0:4, s, ph], rhs=sPc[0:4, c0_:c0_ + 4], start=False, stop=True,
                                           skip_group_check=True)
                            ins = e.matmul(pso_full[ph, z4], lhsT=ones_bf[:, 0:64], rhs=sPp[:, c0_:c0_ + 4], start=True, stop=False)
                            ins = e.matmul(pso_full[ph, z4], lhsT=ones_bf[0:4, 0:64], rhs=sPc[0:4, c0_:c0_ + 4], start=False, stop=True)
                        return ins
                    P.op("pe", fn_so, r=[vbb_, ONES] + sPpb + sPcb + sNbb, w=[psob])

                def s_acc(_):
                    if first:
                        P.op("dve", lambda e: e.tensor_copy(out=Oacc[:, S:NT], in_=pso_full[:, 0:16]), r=[psob], w=Oaccb)
                        P.op("dve", lambda e: e.tensor_copy(out=Zacc[:, S:NT], in_=pso_full[:, 16:32]), r=[psob], w=Zaccb)
                    else:
                        P.op("dve", lambda e: e.tensor_tensor(out=Oacc[:, S:NT], in0=pso_full[:, 0:16], in1=Oacc[:, S:NT],
                                                              op=ALU.add), r=[psob] + Oaccb, w=Oaccb)
                        P.op("dve", lambda e: e.tensor_tensor(out=Zacc[:, S:NT], in0=pso_full[:, 16:32], in1=Zacc[:, S:NT],
                                                              op=ALU.add), r=[psob] + Zaccb, w=Zaccb)

                ssched = {1: [(s_T, 0)], 2: [(s_T, 1)], 3: [(s_st2, 0)], 5: [(s_st3, 0), (s_st2, 1)], 6: [(s_T, 2)],
                          8: [(s_st3, 1), (s_T, 3)], 9: [(s_st2, 2)], 10: [(s_st2, 3)], 12: [(s_st3, 2)], 13: [(s_st3, 3)],
                          15: [(s_acc, 0)]}
                PIPE = 2
                pending = []
                qi = 0
                for qg in range(4):
                    qbl, _ = q_group(qg)
                    po, pob = self.bank("o")
                    pz, pzb = self.bank("z")
                    for j, (r_, c_) in enumerate(qbl):
                        pending.append(emit_S(qg, j, r_, c_, po, pob, pz, pzb))
                        if len(pending) > PIPE:
                            emit_PV(pending.pop(0))
                        for f_, a_ in ssched.get(qi, []):
                            f_(a_)
                        qi += 1
                while pending:
                    emit_PV(pending.pop(0))
            P.op("dve", lambda e: e.reciprocal(out=Zacc[:, :], in_=Zacc[:, :]), r=Zaccb, w=Zaccb)
            P.op("dve", lambda e, hp=hp: e.tensor_tensor(out=oatt[:, hp, :], in0=Oacc[:, :], in1=Zacc[:, :], op=ALU.mult),
                 r=Zaccb + Oaccb, w=oattb)

        self.stage(3 + 10 * l)
        self.new_phase(off_co)
        m, mb = self.carve("m", [128, 8, NT], BF16, 5, area=0)
        sgc, sgcb = self.carve("sgc", [128, 2, 512], F32, 2)
        sga, sgab = self.carve("sga", [128, 2, 512], F32, 2)
        t1, t1b = self.carve("t1", [128, 2, 512], F32, 2)
        t2, t2b = self.carve("t2", [128, 2, 512], F32, 2)
        mi = 0
        for og in range(2):
            for half in range(2):
                Wpc, Wpcb = self.wload(w_pc[l, :, 512 * og:512 * og + 512], 4, 512)
                Wpa, Wpab = self.wload(w_pa[l, :, 512 * og:512 * og + 512], 4, 512)
                cbase = 4 * og + 2 * half
                c0 = 1024 + 4608 + 128 * cbase
                Wgc, Wgcb = self.wload(w_in[l, :, c0:c0 + 256], 8, 256)
                Wga, Wgab = self.wload(w_in[l, :, c0 + 1024:c0 + 1024 + 256], 8, 256)
                for cc in range(2):
                    c = cbase + cc
                    for tt in range(5):
                        t0, n = TT[tt]
                        pyc, pycb = self.bank("all")
                        self.mm_acc(pyc[:, 0:n], pycb, [(Wpc[:, k, 128 * (c % 4):128 * (c % 4) + 128], cact[:, k, t0:t0 + n])
                                                         for k in range(4)], [Wpcb, cactb[tt]])
                        pgc, pgcb = self.bank("all")
                        gemm_u(pgc, pgcb, Wgc, Wgcb, 128 * cc, tt, n)
                        pya, pyab = self.bank("all")
                        self.mm_acc(pya[:, 0:n], pyab, [(Wpa[:, k, 128 * (c % 4):128 * (c % 4) + 128], oatt[:, k, t0:t0 + n])
                                                         for k in range(4)], [Wpab] + oattb)
                        pga, pgab = self.bank("all")
                        gemm_u(pga, pgab, Wga, Wgab, 128 * cc, tt, n)
                        i2 = mi % 2
                        mi += 1
                        P.op("act", lambda e, pgc=pgc, i2=i2, c=c, n=n: e.activation(
                            out=sgc[:, i2, 0:n], in_=pgc[:, 0:n], func=AF.Sigmoid, bias=vcol(V_BIN + 44 + c), scale=1.0),
                            r=[pgcb, VEC], w=[sgcb[i2]])
                        P.op("act", lambda e, pga=pga, i2=i2, c=c, n=n: e.activation(
                            out=sga[:, i2, 0:n], in_=pga[:, 0:n], func=AF.Sigmoid, bias=vcol(V_BIN + 52 + c), scale=1.0),
                            r=[pgab, VEC], w=[sgab[i2]])
                        P.op("dve", lambda e, pyc=pyc, i2=i2, c=c, n=n: e.scalar_tensor_tensor(
                            out=t1[:, i2, 0:n], in0=pyc[:, 0:n], scalar=vcol(V_BPC + c), in1=sgc[:, i2, 0:n],
                            op0=ALU.add, op1=ALU.mult), r=[pycb, sgcb[i2], VEC], w=[t1b[i2]])
                        P.op("dve", lambda e, pya=pya, i2=i2, c=c, n=n: e.scalar_tensor_tensor(
                            out=t2[:, i2, 0:n], in0=pya[:, 0:n], scalar=vcol(V_BPA + c), in1=sga[:, i2, 0:n],
                            op0=ALU.add, op1=ALU.mult), r=[pyab, sgab[i2], VEC], w=[t2b[i2]])
                        P.op("dve", lambda e, i2=i2, c=c, t0=t0, n=n: e.tensor_tensor(
                            out=m[:, c, t0:t0 + n], in0=t1[:, i2, 0:n], in1=t2[:, i2, 0:n], op=ALU.add),
                            r=[t1b[i2], t2b[i2]], w=[mb[tt]])
        self.stage(4 + 10 * l)
        self.new_phase(A0_END, A0_END, A0_END)
        self.ln_wfull = w_out[l]
        self.ln_block(l, 1 + 2 * l, lambda c: w_out[l, :, 128 * c:128 * c + 128], 8, m, mb, G1P, final=False, env=g,
                      tiles=range(5))

        self.stage(5 + 10 * l)
        for half, tiles in enumerate(([0, 1], [2, 3, 4])):
            self.new_phase(0, 0, 0)
            c0 = TT[tiles[0]][0]
            ncol = sum(TT[t][1] for t in tiles)
            h, hb = self.carve("h", [128, NF, ncol], BF16, len(tiles))
            sgt, sgtb = self.carve("sgt", [128, 2, 512], F32, 2)
            hi = 0
            for fg in range(11):
                Wg_, Wgb_ = self.wload(w_gate[l, :, 256 * fg:256 * fg + 256], 8, 256)
                Wu_, Wub_ = self.wload(w_up[l, :, 256 * fg:256 * fg + 256], 8, 256)
                for ff in range(2):
                    f = 2 * fg + ff
                    for ti, tt in enumerate(tiles):
                        t0, n = TT[tt]
                        pgt, pgtb = self.bank("all")
                        gemm_u(pgt, pgtb, Wg_, Wgb_, 128 * ff, tt, n)
                        pup, pupb = self.bank("all")
                        gemm_u(pup, pupb, Wu_, Wub_, 128 * ff, tt, n)
                        i2 = hi % 2
                        hi += 1
                        P.op("act", lambda e, pgt=pgt, i2=i2, n=n: e.activation(out=sgt[:, i2, 0:n], in_=pgt[:, 0:n],
                                                                               func=AF.Silu), r=[pgtb], w=[sgtb[i2]])
                        P.op("dve", lambda e, pup=pup, i2=i2, f=f, t0=t0, n=n: e.tensor_tensor(
                            out=h[:, f, t0 - c0:t0 - c0 + n], in0=pup[:, 0:n], in1=sgt[:, i2, 0:n], op=ALU.mult),
                            r=[pupb, sgtb[i2]], w=[hb[ti]])
            hview = lambda k, t0, n, h=h, c0=c0: h[:, k, t0 - c0:t0 - c0 + n]
            self.ln_block(l, 2 + 2 * l, lambda c: w_down[l, :, 128 * c:128 * c + 128], NF, None, hb, G2P,
                          final=(l == L - 1), env=g, tiles=tiles, hview=hview)

    def ln_stats(self, pm, pmb, pq, pqb, stt, sttb, n, eps_t, ONES):
        P = self.P
        mean = stt[:, 0, :]
        m2 = stt[:, 1, :]
        rstd = stt[:, 2, :]
        P.op("dve", lambda e: e.tensor_copy(out=mean[:, 0:n], in_=pm[:, 0:n]), r=[pmb], w=[sttb[0]])
        P.op("dve", lambda e: e.tensor_tensor(out=m2[:, 0:n], in0=mean[:, 0:n], in1=mean[:, 0:n], op=ALU.mult),
             r=[sttb[0]], w=[sttb[1]])
        P.op("dve", lambda e: e.tensor_tensor(out=m2[:, 0:n], in0=pq[:, 0:n], in1=m2[:, 0:n], op=ALU.subtract),
             r=[pqb, sttb[1]], w=[sttb[1]])
        P.op("act", lambda e: e.activation(out=rstd[:, 0:n], in_=m2[:, 0:n], func=AF.Sqrt, bias=eps_t[:, 0:1], scale=1.0),
             r=[sttb[1], ONES], w=[sttb[2]])
        P.op("dve", lambda e: e.reciprocal(out=rstd[:, 0:n], in_=rstd[:, 0:n]), r=[sttb[2]], w=[sttb[2]])
        return mean, rstd

    def ln_block(self, l, stg, wsrc, kch, xin, xinb, gofs, final, env, tiles, hview=None):
        P = self.P
        g = env
        U, Ub, ident, CONST, ONES = g["U"], g["Ub"], g["ident"], g["CONST"], g["ONES"]
        onesD, eps_t, modT, MOD, coef, COEF = g["onesD"], g["eps_t"], g["modT"], g["MOD"], g["coef"], g["COEF"]
        rscr, RS, yp, ys = g["rscr"], g["RS"], g["yp"], g["ys"]
        produce = g["produce"]
        tiles = list(tiles)
        pipelined = (kch <= 8)
        groups = [[t] for t in tiles] if pipelined else [tiles[i:i + 2] for i in range(0, len(tiles), 2)]
        zt, ztb = self.carve("zt", [128, 8, 1024], F32, 16)
        rt, rtb = self.carve("rt", [128, 4, 512], F32, 4)
        zbq, zbqb = self.carve("zbq", [128, 4, 512], BF16, 4)
        stt, sttb = self.carve("stt", [128, 3, 512], F32, 3)
        if final:
            yst, ystb = self.carve("yst", [128, 2, 1024], F32, 2)
            rst, rstb = None, None
        else:
            rst, rstb = self.carve("rst", [128, 4, 512], F32, 4)
        bufs = (zt, ztb, rt, rtb, zbq, zbqb, stt, sttb, rst, rstb, (yst, ystb) if final else None)
        self.ln_wpre = None
        if pipelined:
            self.ln_wpre = [self.wload(self.ln_wfull[:, 512 * i:512 * i + 512], 8, 512) for i in range(2)]
            for gi_, grp in enumerate(groups):
                self.ln_group(l, stg, wsrc, kch, xin, xinb, gofs, final, env, tiles, grp, hview, bufs, gi_ % 2, "proj")
                if gi_ > 0:
                    self.ln_group(l, stg, wsrc, kch, xin, xinb, gofs, final, env, tiles, groups[gi_ - 1], hview, bufs,
                                  (gi_ - 1) % 2, "norm")
            self.ln_group(l, stg, wsrc, kch, xin, xinb, gofs, final, env, tiles, groups[-1], hview, bufs,
                          (len(groups) - 1) % 2, "norm")
        else:
            for grp in groups:
                self.ln_group(l, stg, wsrc, kch, xin, xinb, gofs, final, env, tiles, grp, hview, bufs, 0, "proj")
                self.ln_group(l, stg, wsrc, kch, xin, xinb, gofs, final, env, tiles, grp, hview, bufs, 0, "norm")

    def ln_group(self, l, stg, wsrc, kch, xin, xinb, gofs, final, env, alltiles, tiles, hview, bufs, slot, part):
        P = self.P
        g = env
        U, Ub, ident, CONST, ONES = g["U"], g["Ub"], g["ident"], g["CONST"], g["ONES"]
        onesD, eps_t, modT, MOD, coef, COEF = g["onesD"], g["eps_t"], g["modT"], g["MOD"], g["coef"], g["COEF"]
        rscr, RS, yp, ys = g["rscr"], g["RS"], g["yp"], g["ys"]
        produce = g["produce"]
        zt, ztb, rt, rtb, zbq, zbqb, stt, sttb, rst, rstb, ysts = bufs
        if final:
            yst, ystb = ysts
        ri = self.ln_ri
        for c in (range(8) if part == "proj" else []):
            if self.ln_wpre is not None:
                Wfull, Wb = self.ln_wpre[c // 4]
                Wt = Wfull[:, :, 128 * (c % 4):128 * (c % 4) + 128]
            else:
                Wt, Wb = self.wload(wsrc(c), kch, 128)
            for ti, tt in enumerate(tiles):
                t0, n = TT[tt]
                zoff = (slot + ti) * 512
                pz_, pzb_ = self.bank("acc")
                if hview is None:
                    pairs = [(Wt[:, k, :], xin[:, k, t0:t0 + n]) for k in range(kch)]
                    rb = [Wb, xinb[tt]]
                else:
                    pairs = [(Wt[:, k, :], hview(k, t0, n)) for k in range(kch)]
                    rb = [Wb, xinb[alltiles.index(tt)]]
                self.mm_acc(pz_[:, 0:n], pzb_, pairs, rb)
                r_ = rt[:, ri % 4, :]
                rb_ = rtb[ri % 4]
                ri += 1
                P.op("sp", lambda e, r_=r_, c=c, t0=t0, n=n: e.dma_start(out=r_[:, 0:n], in_=rscr[:, c, t0:t0 + n]),
                     r=[RS[c][tt]], w=[rb_], dma=rb_)
                for (cs, s) in segs(tt):
                    P.op("dve", lambda e, pz_=pz_, r_=r_, c=c, cs=cs, s=s, t0=t0: e.scalar_tensor_tensor(
                        out=zt[:, c, zoff + cs.start:zoff + cs.stop], in0=pz_[:, cs], scalar=modT[:, l, gofs + c, s:s + 1],
                        in1=r_[:, cs], op0=ALU.mult, op1=ALU.add), r=[pzb_, rb_, MOD], w=[ztb[(slot + ti) * 8 + c]])
        self.ln_ri = ri
        zi = 0
        for ti, tt in enumerate(tiles if part == "norm" else []):
            t0, n = TT[tt]
            o0 = (slot + ti) * 512
            pm, pmb = self.bank("o")
            pq, pqb = self.bank("z")
            for c in range(8):
                zb_ = zbq[:, zi % 4, :]
                zbb = zbqb[zi % 4]
                zi += 1
                zs_ = zbq[:, zi % 4, :]
                zsb = zbqb[zi % 4]
                zi += 1
                P.op("act", lambda e, zb_=zb_, c=c: e.activation(out=zb_[:, 0:n], in_=zt[:, c, o0:o0 + n], func=AF.Copy),
                     r=[ztb[(slot + ti) * 8 + c]], w=[zbb])
                P.op("act", lambda e, zs_=zs_, c=c: e.activation(out=zs_[:, 0:n], in_=zt[:, c, o0:o0 + n], func=AF.Square),
                     r=[ztb[(slot + ti) * 8 + c]], w=[zsb])
                P.op("pe", lambda e, pm=pm, zb_=zb_, c=c: e.matmul(pm[:, 0:n], lhsT=onesD[:], rhs=zb_[:, 0:n],
                                                                  start=(c == 0), stop=(c == 7)), r=[zbb, ONES], w=[pmb])
                P.op("pe", lambda e, pq=pq, zs_=zs_, c=c: e.matmul(pq[:, 0:n], lhsT=onesD[:], rhs=zs_[:, 0:n],
                                                                  start=(c == 0), stop=(c == 7)), r=[zsb, ONES], w=[pqb])
            mean, rstd = self.ln_stats(pm, pmb, pq, pqb, stt, sttb, n, eps_t, ONES)
            for c in range(8):
                zc = zt[:, c, o0:o0 + n]
                P.op("dve", lambda e, zc=zc: e.tensor_tensor(out=zc, in0=zc, in1=mean[:, 0:n], op=ALU.subtract),
                     r=[sttb[0], ztb[(slot + ti) * 8 + c]], w=[ztb[(slot + ti) * 8 + c]])
                P.op("dve", lambda e, zc=zc: e.tensor_tensor(out=zc, in0=zc, in1=rstd[:, 0:n], op=ALU.mult),
                     r=[sttb[2], ztb[(slot + ti) * 8 + c]], w=[ztb[(slot + ti) * 8 + c]])
                if not final:
                    produce(zc, [ztb[(slot + ti) * 8 + c]], c, tt, stg, rst[:, ri % 4, :], rstb[ri % 4])
                    ri += 1
                    self.ln_ri = ri
                else:
                    P.op("act", lambda e, zc=zc, c=c: e.activation(out=zc, in_=zc, func=AF.Identity,
                                                                   scale=coef[:, stg, 0, c, 0:1], bias=coef[:, stg, 1, c, 0:1]),
                         r=[COEF, ztb[(slot + ti) * 8 + c]], w=[ztb[(slot + ti) * 8 + c]])
            if final:
                nb = (n + 127) // 128
                for tb in range(nb):
                    rows = min(128, n - 128 * tb)
                    ysl = yst[:, tb % 2, :]
                    yslb = ystb[tb % 2]
                    for hf in range(2):
                        pt, ptb = self.bank("acc")

                        def fn_yt(e, pt=pt, hf=hf, tb=tb, rows=rows):
                            ins = None
                            for cc in range(4):
                                c = 4 * hf + cc
                                ins = e.transpose(pt[0:rows, 128 * cc:128 * cc + 128],
                                                  zt[:, c, o0 + 128 * tb:o0 + 128 * tb + rows], ident[:])
                            return ins
                        P.op("pe", fn_yt, r=ztb[(slot + ti) * 8:(slot + ti) * 8 + 8] + [CONST], w=[ptb])
                        P.op("act", lambda e, pt=pt, hf=hf, ysl=ysl, rows=rows: e.activation(
                            out=ysl[0:rows, 512 * hf:512 * hf + 512], in_=pt[0:rows, :], func=AF.Copy), r=[ptb], w=[yslb])
                    if tt < 4:
                        dst = yp[t0 + 128 * tb:t0 + 128 * tb + 128, :]
                    else:
                        dst = ys[:, :]
                    P.op("sp", lambda e, dst=dst, ysl=ysl, rows=rows: e.dma_start(out=dst, in_=ysl[0:rows, :]),
                         r=[yslb], w=[Buf("x")], dma=yslb)


_CACHE = {}


def _consts():
    slopes = 2.0 ** (-8.0 * (np.arange(8) + 1) / 8.0)
    k = np.arange(128)[:, None].astype(np.float64)
    q = np.arange(128)[None, :].astype(np.float64)
    masks = np.zeros((128, 12, 512), np.float64)
    smp = np.zeros((128, 12, 32), np.float64)
    smc = np.zeros((4, 12, 32), np.float64)
    for g, d in enumerate(DIL):
        for hp in range(4):
            for hh in range(2):
                h = 2 * hp + hh
                cur = np.where(k <= q, np.exp(-slopes[h] * d * np.maximum(q - k, 0.0)), 0.0)
                prev = np.where(k >= q, np.exp(-slopes[h] * d * np.maximum(128 + q - k, 0.0)), 0.0)
                masks[:, g * 4 + hp, (2 * hh) * 128:(2 * hh + 1) * 128] = cur
                masks[:, g * 4 + hp, (2 * hh + 1) * 128:(2 * hh + 2) * 128] = prev
                for s in range(4):
                    for t in range(4):
                        col = s * 8 + hh * 4 + t
                        smp[:, g * 4 + hp, col] = prev[:, t if g == 0 else 0]
                        if g == 0:
                            smc[:, g * 4 + hp, col] = cur[0:4, t]
                        else:
                            smc[:, g * 4 + hp, col] = (np.arange(4) == t).astype(np.float64)
    return (np.eye(128, dtype=np.float32), masks.reshape(128, -1).astype(np.float32),
            smp.reshape(128, -1).astype(np.float32), smc.reshape(4, -1).astype(np.float32))


def _pack_vecs(inp):
    out = np.zeros((L, NVEC, 128), np.float32)
    for l in range(L):
        rows = [inp["b_in"][l].reshape(60, 128), inp["b_ada"][l].reshape(48, 128), inp["b_pc"][l].reshape(8, 128),
                inp["b_pa"][l].reshape(8, 128), inp["b_out"][l].reshape(8, 128), inp["ln1_g"][l].reshape(8, 128),
                inp["ln1_b"][l].reshape(8, 128), inp["ln2_g"][l].reshape(8, 128), inp["ln2_b"][l].reshape(8, 128),
                inp["b_dw"][l].reshape(4, 128), inp["conv_ln_g"][l].reshape(4, 128), inp["conv_ln_b"][l].reshape(4, 128),
                inp["w_dw"][l].reshape(124, 128)]
        cat = np.concatenate(rows, axis=0)
        out[l, :cat.shape[0]] = cat
    return out


def get_nc():
    global KSTAGE
    import os
    KSTAGE = int(os.environ.get("KSTAGE", "99"))
    if "nc" not in _CACHE:
        _CACHE["nc"] = Builder().build()
    return _CACHE["nc"]


def make_in_maps(inp, cores):
    f = lambda a: np.ascontiguousarray(np.asarray(a, dtype=np.float32))
    ident, masks, smp, smc = _consts()
    vecs = _pack_vecs({k: np.asarray(v) for k, v in inp.items()})
    shared = {k: f(inp[k]) for k in ("w_ada", "w_in", "w_pc", "w_pa", "w_out", "w_gate", "w_up", "w_down")}
    shared.update(vecs=vecs, ident=ident, masks=masks, smp=smp, smc=smc)
    maps = []
    for i in cores:
        m = dict(shared)
        m["xp"] = f(inp["x_prompt"][i])
        m["xs"] = f(np.asarray(inp["x_sample"][4 * i:4 * i + 4]).reshape(NS, D))
        m["c5"] = f(np.concatenate([np.asarray(inp["c_prompt"][i:i + 1]), np.asarray(inp["c_sample"][4 * i:4 * i + 4])], 0))
        caches = (inp["cache_kv_g0"], inp["cache_kv_g1"], inp["cache_kv_g2"])
        for g in range(3):
            m["ck%d" % g] = f(np.asarray(caches[g][:, 4 * i:4 * i + 4]).reshape(L, 4, WIN[g], 1024))
        m["stc"] = f(inp["state_conv"][:, 4 * i:4 * i + 4])
        maps.append(m)
    return maps


def kernel(**inputs):
    nc = get_nc()
    cores = list(range(NCORES))
    maps = make_in_maps(inputs, cores)
    res = run_bass_kernel_spmd(nc, maps, core_ids=cores)
    R = res.results
    y_p = np.stack([R[i]["yp"] for i in cores]).astype(np.float32)
    y_s = np.concatenate([R[i]["ys"].reshape(4, 4, D) for i in cores], 0).astype(np.float32)
    outs = [y_p, y_s]
    for g in range(3):
        outs.append(np.stack([R[i]["kvp%d" % g] for i in cores], 1).reshape(L, NCORES, WIN[g], 2, 8, 64).astype(np.float32))
    outs.append(np.stack([R[i]["cvp"] for i in cores], 1).astype(np.float32))
    for g in range(3):
        outs.append(np.concatenate([R[i]["kvs%d" % g] for i in cores], 1).reshape(L, 32, WIN[g], 2, 8, 64).astype(np.float32))
    outs.append(np.concatenate([R[i]["cvs"] for i in cores], 1).astype(np.float32))
    return tuple(outs)
```

```python
import contextlib
import numpy as np
import concourse.bass as bass
import concourse.mybir as mybir
from concourse.bass_utils import run_bass_kernel_spmd

F32 = mybir.dt.float32
BF16 = mybir.dt.bfloat16
AF = mybir.ActivationFunctionType
ALU = mybir.AluOpType

NCORES = 8
D = 1024
S = 2048
NS = 16
NT = S + NS
L = 2
DFF = 2816
NF = DFF // 128
ALPHA = float((2 * L) ** 0.25)
EPS = 1e-5
TT = [(0, 512), (512, 512), (1024, 512), (1536, 512), (2048, 16)]
DIL = (1, 4, 16)
WIN = (128, 512, 2048)
ENGS = ("sp", "act", "pool", "dve", "pe")
SAME_SYNC = True
NSLOT = 4
A0_END = 8 * NT * 2
RB_BYTES = 112 * 1024

V_BIN, V_BADA, V_BPC, V_BPA, V_BOUT, V_L1G, V_L1B, V_L2G, V_L2B, V_BDW, V_CLG, V_CLB, V_WDW = (
    0, 60, 108, 116, 124, 132, 140, 148, 156, 164, 168, 172, 176)
NVEC = 384


def segs(tt):
    if tt < 4:
        return [(slice(0, 512), 0)]
    return [(slice(4 * s, 4 * s + 4), 1 + s) for s in range(4)]


class DSem:
    def __init__(self, h):
        self.h = h
        self.count = 0
        self.last_buf = None
        self.last_op = None


class Buf:
    __slots__ = ("name", "w", "r", "dsem", "rng", "phase", "psum")

    def __init__(self, name, rng=None, phase=None):
        self.name = name
        self.psum = False
        self.w = {}
        self.r = {}
        self.dsem = None
        self.rng = rng
        self.phase = phase


class Op:
    __slots__ = ("eng", "fn", "deps", "sig", "sigidx", "dsem", "sigval", "seq", "calls")


class Rec:
    def __init__(self):
        self.calls = []

    def __getattr__(self, name):
        def f(*a, **k):
            self.calls.append((name, a, k))
            return self
        return f


class Prog:
    def __init__(self, nc, stack):
        self.nc = nc
        self.stack = stack
        self.ops = {e: [] for e in ENGS}
        self.dsems = []
        self.seq = 0
        self.esem = {e: stack.enter_context(nc.semaphore("es_" + e)) for e in ENGS}
        self.region_bufs = []
        self.phase = 0
        self.dsem_by_name = {}

    def dsem_of(self, buf):
        if buf.dsem is None:
            d = self.dsem_by_name.get(buf.name)
            if d is None:
                h = self.stack.enter_context(self.nc.semaphore("ds%d" % len(self.dsems)))
                d = DSem(h)
                self.dsems.append(d)
                self.dsem_by_name[buf.name] = d
            buf.dsem = d
        return buf.dsem

    @staticmethod
    def _key(o):
        return ("d", id(o.dsem)) if o.dsem is not None else ("e", o.eng)

    def op(self, eng, fn, r=(), w=(), dma=None):
        o = Op()
        o.eng = eng
        o.fn = fn
        rec = Rec()
        fn(rec)
        o.calls = rec.calls
        assert len(o.calls) > 0
        o.sig = False
        o.sigidx = 0
        o.dsem = None
        o.sigval = 0
        self.seq += 1
        o.seq = self.seq
        deps = {}
        strong = {}

        def add(p, st=True):
            k = self._key(p)
            q = deps.get(k)
            if q is None or q.seq < p.seq:
                deps[k] = p
            if st:
                q = strong.get(k)
                if q is None or q.seq < p.seq:
                    strong[k] = p

        for b in r:
            for p in b.w.values():
                add(p)
            if b.psum and eng in ("act", "dve"):
                for p in b.r.values():
                    if p.eng != eng and p.eng in ("act", "dve"):
                        add(p)
        for b in w:
            for p in b.w.values():
                add(p, p.eng != eng)
            for p in b.r.values():
                add(p, p.eng != eng)
        k_self = ("e", eng)
        if k_self in deps and dma is None:
            if k_self in strong:
                deps[k_self] = strong[k_self]
            else:
                del deps[k_self]
        if dma is not None:
            dsm = self.dsem_of(dma)
            if dsm.last_buf is not dma and dsm.last_op is not None:
                deps[("d", id(dsm))] = dsm.last_op
        o.deps = deps
        if dma is not None:
            o.dsem = self.dsem_of(dma)
            o.dsem.count += 1
            o.sigval = 16 * o.dsem.count
            o.dsem.last_buf = dma
            o.dsem.last_op = o
        k = self._key(o)
        for b in r:
            b.r[k] = o
        for b in w:
            b.w = {k: o}
            b.r = {}
        self.ops[eng].append(o)
        return o

    def rbuf(self, name, off, nbytes):
        b = Buf(name, (off, off + nbytes), self.phase)
        for ob in self.region_bufs:
            if ob.phase != self.phase and ob.rng[0] < b.rng[1] and b.rng[0] < ob.rng[1]:
                for p in list(ob.w.values()) + list(ob.r.values()):
                    k = self._key(p)
                    q = b.w.get(k)
                    if q is None or q.seq < p.seq:
                        b.w[k] = p
        self.region_bufs.append(b)
        return b

    def prune_region(self):
        pass

    def finalize(self):
        for e in ENGS:
            for o in self.ops[e]:
                for p in o.deps.values():
                    if p.dsem is None and not (p.eng == o.eng and (o.eng == "pe" or not SAME_SYNC)):
                        p.sig = True
        for e in ENGS:
            i = 0
            for o in self.ops[e]:
                if o.sig:
                    i += 1
                    o.sigidx = i

    def emit(self, eng, e):
        known = {}
        for o in self.ops[eng]:
            for p in sorted(o.deps.values(), key=lambda x: x.seq):
                if p.dsem is not None:
                    sem, val, kk = p.dsem.h, p.sigval, id(p.dsem)
                else:
                    if p.eng == eng and (eng == "pe" or not SAME_SYNC):
                        continue
                    sem, val, kk = self.esem[p.eng], p.sigidx, p.eng
                if known.get(kk, 0) >= val:
                    continue
                e.wait_ge(sem, val)
                known[kk] = val
            if o.fn is None:
                continue
            ins = None
            for (name, a, k) in o.calls:
                ins = getattr(e, name)(*a, **k)
            if o.dsem is not None:
                ins.then_inc(o.dsem.h, 16)
            elif o.sig:
                ins.then_inc(self.esem[eng], 1)


class StopBuild(Exception):
    pass


KSTAGE = 99


class Builder:
    def stage(self, k):
        if not hasattr(self, "marks"):
            self.marks = []
        self.marks.append((k, sum(len(o.calls) for o in self.P.ops["pe"])))
        if KSTAGE <= k:
            raise StopBuild()

    def __init__(self):
        self.nc = bass.Bass("TRN2", target_bir_lowering=False)
        self.stack = contextlib.ExitStack()

    def sb(self, name, shape, dt):
        return self.stack.enter_context(self.nc.sbuf_tensor("sb_" + name, list(shape), dt))

    def carve(self, name, shape, dt, nbufs=1, area=1):
        esz = 4 if dt == F32 else 2
        per = int(np.prod(shape[1:])) * esz
        per = (per + 63) // 64 * 64
        off = self.rb_offs[area]
        assert off % 4 == 0
        self.rb_offs[area] += per
        lim = self.rb_lims[area]
        assert self.rb_offs[area] <= lim, (name, area, self.rb_offs[area], lim)
        a = self.regB[:, off // 4:(off + per) // 4]
        if dt == BF16:
            a = a.bitcast(BF16)
        n = int(np.prod(shape[1:]))
        a = a[:, 0:n]
        if len(shape) > 2:
            names = ["d%d" % i for i in range(len(shape) - 1)]
            kw = {names[i]: shape[1 + i] for i in range(len(shape) - 2)}
            a = a.rearrange("p (%s) -> p %s" % (" ".join(names), " ".join(names)), **kw)
        bufs = [self.P.rbuf("%s_%d" % (name, i), off, per) for i in range(nbufs)]
        return a, bufs

    def new_phase(self, start1, start0=0, lim0=A0_END):
        self.P.phase += 1
        self.rb_offs = [start0, start1]
        self.rb_lims = [lim0, RB_BYTES]

    def wload(self, src2d, kch, ncols, slot=None):
        if slot == "free2":
            cand = [j for j in range(NSLOT) if j not in self.pinned]
            i = cand[self.free2_i % len(cand)]
            self.free2_i += 1
        else:
            i = self.ring_i % NSLOT
            self.ring_i += 1
        slot = self.ring[i]
        buf = self.ringb[i]
        view = slot[:, 0:kch * ncols].rearrange("p (k n) -> p k n", k=kch)
        src = src2d.rearrange("(k p) n -> p k n", p=128)
        self.P.op("pool", lambda e, view=view, src=src: e.dma_start(out=view, in_=src), w=[buf], dma=buf)
        return view, buf

    def bank(self, pool):
        lst = self.pools[pool]
        i = self.pool_i[pool] % len(lst)
        self.pool_i[pool] += 1
        b = lst[i]
        return self.ps[b], self.psb[b]

    def mm_acc(self, out, outbuf, pairs, rbufs):
        def fn(e, out=out, pairs=pairs):
            n = len(pairs)
            ins = None
            for i, (lt, rh) in enumerate(pairs):
                ins = e.matmul(out, lhsT=lt, rhs=rh, start=(i == 0), stop=(i == n - 1))
            return ins
        return self.P.op("pe", fn, r=rbufs, w=[outbuf])

    def build(self):
        nc = self.nc
        st = self.stack
        P = self.P = Prog(nc, st)

        def din(name, shape):
            return nc.dram_tensor(name, list(shape), F32, kind="ExternalInput").ap()

        def dout(name, shape):
            return nc.dram_tensor(name, list(shape), F32, kind="ExternalOutput").ap()

        xp = din("xp", [S, D])
        xs = din("xs", [NS, D])
        c5 = din("c5", [5, D])
        ck = [din("ck%d" % g, [L, 4, WIN[g], 2 * 512]) for g in range(3)]
        stc = din("stc", [L, 4, 30, 512])
        w_ada = din("w_ada", [L, D, 6 * D])
        w_in = din("w_in", [L, D, 7680])
        w_pc = din("w_pc", [L, 512, D])
        w_pa = din("w_pa", [L, 512, D])
        w_out = din("w_out", [L, D, D])
        w_gate = din("w_gate", [L, D, DFF])
        w_up = din("w_up", [L, D, DFF])
        w_down = din("w_down", [L, DFF, D])
        vecs = din("vecs", [L, NVEC, 128])
        ident_d = din("ident", [128, 128])
        masks_d = din("masks", [128, 12 * 512])
        smp_d = din("smp", [128, 12 * 32])
        smc_d = din("smc", [4, 12 * 32])

        yp = dout("yp", [S, D])
        ys = dout("ys", [NS, D])
        kvp = [dout("kvp%d" % g, [L, WIN[g], 2, 512]) for g in range(3)]
        cvp = dout("cvp", [L, 30, 512])
        kvs = [dout("kvs%d" % g, [L, 4, WIN[g], 2 * 512]) for g in range(3)]
        cvs = dout("cvs", [L, 4, 30, 512])
        rscr = nc.dram_tensor("rscr", [128, 8, NT], F32).ap()

        U = self.sb("U", [128, 8, NT], BF16)
        Ub = [[Buf("U%d_%d" % (c, t)) for t in range(5)] for c in range(8)]
        self.ring = [self.sb("ring%d" % i, [128, 4096], BF16) for i in range(NSLOT)]
        self.ringb = [Buf("ring%d" % i) for i in range(NSLOT)]
        self.ring_i = 0
        self.pinned = []
        self.free2_i = 0
        self.ln_ri = 0
        self.ln_zi = 0
        self.deferred_norm = []
        ident = self.sb("ident", [128, 128], F32)
        ones_bf = self.sb("ones_bf", [128, 128], BF16)
        onesD = self.sb("onesD", [128, 128], BF16)
        onesC = self.sb("onesC", [128, 128], BF16)
        ones_f = self.sb("ones_f", [128, 64], F32)
        eps_t = self.sb("eps_t", [128, 1], F32)
        alpha_t = self.sb("alpha_t", [128, 8, 5], F32)
        masks = self.sb("masks", [128, 12, 512], BF16)
        smp = self.sb("smp", [128, 12, 32], F32)
        smc = self.sb("smc", [4, 12, 32], F32)
        vecT = self.sb("vecT", [128, L, NVEC], F32)
        bq8 = self.sb("bq8", [128, L, 12], F32)
        modT = self.sb("modT", [128, L, 48, 5], F32)
        coef = self.sb("coef", [128, 5, 4, 8, 5], F32)
        scT = self.sb("scT", [128, 8, 5], BF16)
        ctmp = self.sb("ctmp", [128, 8, 5], F32)
        CONST = Buf("const")
        MASKS = Buf("masks")
        VEC = Buf("vec")
        MOD = Buf("mod")
        COEF = Buf("coef")
        SCT = Buf("sct")
        self.regB = self.sb("regB", [128, RB_BYTES // 4], F32)
        self.rb_offs = [0, 0]
        self.rb_lims = [0, RB_BYTES]

        self.ps = [st.enter_context(nc.psum_tensor("ps%d" % i, [128, 512], F32)) for i in range(8)]
        self.psb = [Buf("ps%d" % i) for i in range(8)]
        for b in self.psb:
            b.psum = True
        self.pools = {"acc": [0, 1, 2], "o": [3, 4], "z": [5, 6], "misc": [7], "sS": [0, 1, 2], "tr": [7, 0, 1, 2], "all": [0, 1, 2, 3, 4, 5, 6, 7]}
        self.pool_i = {k: 0 for k in self.pools}

        def vcol(l, row):
            return vecT[:, l, row:row + 1]

        self.new_phase(0, 0, 0)
        P.op("sp", lambda e: e.dma_start(out=ident[:], in_=ident_d), w=[CONST], dma=CONST)
        P.op("pool", lambda e: e.dma_start(out=masks[:], in_=masks_d.rearrange("p (a b) -> p a b", a=12)),
             w=[MASKS], dma=MASKS)
        SM = Buf("sm")
        P.op("sp", lambda e: e.dma_start(out=smp[:], in_=smp_d.rearrange("p (a b) -> p a b", a=12)), w=[SM], dma=SM)
        SM2 = Buf("sm2")
        P.op("sp", lambda e: e.dma_start(out=smc[:], in_=smc_d.rearrange("p (a b) -> p a b", a=12)), w=[SM2], dma=SM2)
        ONES = Buf("ones")
        P.op("dve", lambda e: e.memset(ones_bf[:], 1.0), w=[ONES])
        P.op("dve", lambda e: e.memset(onesD[:], 1.0 / 1024.0), w=[ONES])
        P.op("dve", lambda e: e.memset(onesC[:], 1.0 / 512.0), w=[ONES])
        P.op("dve", lambda e: e.memset(ones_f[:], 1.0), w=[ONES])
        P.op("dve", lambda e: e.memset(eps_t[:], EPS), w=[ONES])
        P.op("dve", lambda e: e.memset(alpha_t[:], ALPHA), w=[ONES])

        D2D = [Buf("d2d%d" % i) for i in range(4)]
        self.d2d_list = []
        self.d2d_i = 0
        for l in range(L):
            for s in range(4):
                self.d2d_list.append((stc[l, s, 4:30, :], cvs[l, s, 0:26, :]))
        nsmall = len(self.d2d_list)
        for l in range(L):
            for g in range(3):
                W = WIN[g]
                for s in range(4):
                    self.d2d_list.append((ck[g][l, s, 4:W, :], kvs[g][l, s, 0:W - 4, :]))

        def emit_d2d(n):
            for _ in range(n):
                if self.d2d_i >= len(self.d2d_list):
                    return
                src, dst = self.d2d_list[self.d2d_i]
                bq = D2D[self.d2d_i % 4]
                self.d2d_i += 1
                P.op("act", lambda e, src=src, dst=dst: e.dma_start(out=dst, in_=src), w=[bq], dma=bq)
        self.emit_d2d = emit_d2d
        emit_d2d(nsmall)

        vraw, vrawb = self.carve("vraw", [128, 2, 128], F32, 2)
        for l in range(L):
            for j in range(3):
                i = (l * 3 + j) % 2
                P.op("sp", lambda e, i=i, l=l, j=j: e.dma_start(out=vraw[:, i, :], in_=vecs[l, 128 * j:128 * j + 128, :]),
                     w=[vrawb[i]], dma=vrawb[i])
                pt, ptb = self.bank("misc")
                P.op("pe", lambda e, pt=pt, i=i: e.transpose(pt[:, 0:128], vraw[:, i, :], ident[:]),
                     r=[vrawb[i], CONST], w=[ptb])
                P.op("dve", lambda e, pt=pt, l=l, j=j: e.tensor_copy(out=vecT[:, l, 128 * j:128 * j + 128], in_=pt[:, 0:128]),
                     r=[ptb], w=[VEC])
        for l in range(L):
            P.op("dve", lambda e, l=l: e.tensor_scalar(out=bq8[:, l, :], in0=vecT[:, l, 8:20], scalar1=0.125,
                                                       scalar2=None, op0=ALU.mult), r=[VEC], w=[VEC])
        c5t, c5b = self.carve("c5t", [128, 1024], F32, 1)
        P.op("sp", lambda e: e.dma_start(out=c5t[0:5, :], in_=c5), w=c5b, dma=c5b[0])
        pt, ptb = self.bank("misc")

        def fn_ct(e, pt=pt):
            ins = None
            for k in range(8):
                ins = e.transpose(pt[:, 5 * k:5 * k + 5], c5t[0:5, 128 * k:128 * k + 128], ident[0:5, 0:5])
            return ins
        P.op("pe", fn_ct, r=[c5b[0], CONST], w=[ptb])
        P.op("act", lambda e, pt=pt: e.activation(out=scT[:], in_=pt[:, 0:40].rearrange("p (a b) -> p a b", a=8),
                                                  func=AF.Silu), r=[ptb], w=[SCT])
        def mod_group(l, grp, pm, pmb, slot=None):
            Wt, Wb = self.wload(w_ada[l, :, 512 * grp:512 * grp + 512], 8, 512, slot=slot)

            def fn_mod(e):
                ins = None
                for jj in range(4):
                    j = 4 * grp + jj
                    for k in range(8):
                        ins = e.matmul(pm[:, 5 * j:5 * j + 5], lhsT=Wt[:, k, 128 * jj:128 * jj + 128],
                                       rhs=scT[:, k, :], start=(k == 0), stop=(k == 7))
                return ins
            P.op("pe", fn_mod, r=[Wb, SCT], w=[pmb])

        def mod_finish(l, pm, pmb):
            P.op("dve", lambda e: e.tensor_tensor(
                out=modT[:, l], in0=pm[:, 0:240].rearrange("p (a b) -> p a b", a=48),
                in1=vecT[:, l, V_BADA:V_BADA + 48].unsqueeze(2).to_broadcast([128, 48, 5]), op=ALU.add),
                r=[pmb, VEC], w=[MOD])
            for (a, b) in ((8, 24), (32, 48)):
                P.op("dve", lambda e, a=a, b=b: e.tensor_scalar(
                    out=modT[:, l, a:b], in0=modT[:, l, a:b], scalar1=1.0, scalar2=None, op0=ALU.add),
                    r=[MOD], w=[MOD])

        pm0, pm0b = self.bank("acc")
        for grp in range(12):
            mod_group(0, grp, pm0, pm0b)
        mod_finish(0, pm0, pm0b)

        RS = [[Buf("rs%d_%d" % (c, t)) for t in range(5)] for c in range(8)]

        def vb(l, row):
            return vecT[:, l, row:row + 8].unsqueeze(2).to_broadcast([128, 8, 5])

        def cop(fn):
            P.op("dve", fn, r=[MOD, VEC, COEF, ONES], w=[COEF])

        SH1, SC1P, G1P, SH2, SC2P, G2P = 0, 8, 16, 24, 32, 40
        cop(lambda e: e.tensor_copy(out=coef[:, 0, 0], in_=modT[:, 0, SC1P:SC1P + 8]))
        cop(lambda e: e.tensor_copy(out=coef[:, 0, 1], in_=modT[:, 0, SH1:SH1 + 8]))
        cop(lambda e: e.tensor_copy(out=coef[:, 0, 2], in_=alpha_t[:]))
        cop(lambda e: e.tensor_tensor(out=coef[:, 0, 3], in0=modT[:, 0, G1P:G1P + 8], in1=vb(0, V_BOUT), op=ALU.mult))
        def coef_s1(l):
            s1 = 1 + 2 * l
            cop(lambda e, l=l, s1=s1: e.tensor_tensor(out=coef[:, s1, 0], in0=modT[:, l, SC2P:SC2P + 8],
                                                      in1=vb(l, V_L1G), op=ALU.mult))
            cop(lambda e, l=l, s1=s1: e.tensor_tensor(out=coef[:, s1, 1], in0=modT[:, l, SC2P:SC2P + 8],
                                                      in1=vb(l, V_L1B), op=ALU.mult))
            cop(lambda e, l=l, s1=s1: e.tensor_tensor(out=coef[:, s1, 1], in0=coef[:, s1, 1],
                                                      in1=modT[:, l, SH2:SH2 + 8], op=ALU.add))
            cop(lambda e, l=l, s1=s1: e.tensor_tensor(out=coef[:, s1, 2], in0=alpha_t[:], in1=vb(l, V_L1G), op=ALU.mult))
            cop(lambda e, l=l, s1=s1: e.tensor_tensor(out=coef[:, s1, 3], in0=alpha_t[:], in1=vb(l, V_L1B), op=ALU.mult))

        def coef_s2(l):
            s2 = 2 + 2 * l
            if l + 1 < L:
                n = l + 1
                cop(lambda e, l=l, s2=s2, n=n: e.tensor_tensor(out=coef[:, s2, 0], in0=modT[:, n, SC1P:SC1P + 8],
                                                               in1=vb(l, V_L2G), op=ALU.mult))
                cop(lambda e, l=l, s2=s2, n=n: e.tensor_tensor(out=coef[:, s2, 1], in0=modT[:, n, SC1P:SC1P + 8],
                                                               in1=vb(l, V_L2B), op=ALU.mult))
                cop(lambda e, l=l, s2=s2, n=n: e.tensor_tensor(out=coef[:, s2, 1], in0=coef[:, s2, 1],
                                                               in1=modT[:, n, SH1:SH1 + 8], op=ALU.add))
                cop(lambda e, l=l, s2=s2: e.tensor_tensor(out=coef[:, s2, 2], in0=alpha_t[:], in1=vb(l, V_L2G), op=ALU.mult))
                cop(lambda e, l=l, s2=s2: e.tensor_tensor(out=coef[:, s2, 3], in0=alpha_t[:], in1=vb(l, V_L2B), op=ALU.mult))
                cop(lambda e, n=n: e.tensor_tensor(out=ctmp[:], in0=modT[:, n, G1P:G1P + 8], in1=vb(n, V_BOUT), op=ALU.mult))
                cop(lambda e, s2=s2: e.tensor_tensor(out=coef[:, s2, 3], in0=coef[:, s2, 3], in1=ctmp[:], op=ALU.add))
            else:
                cop(lambda e, l=l, s2=s2: e.tensor_copy(out=coef[:, s2, 0], in_=vb(l, V_L2G)))
                cop(lambda e, l=l, s2=s2: e.tensor_copy(out=coef[:, s2, 1], in_=vb(l, V_L2B)))

        coef_s1(0)
        pm1, pm1b = self.ps[6], self.psb[6]
        self.deferred_mod = [(lambda grp=grp: mod_group(1, grp, pm1, pm1b, slot="free2")) for grp in range(12)]

        def mod1_finish():
            mod_finish(1, pm1, pm1b)
            coef_s2(0)
            coef_s1(1)
            coef_s2(1)
        self.deferred_mod.append(mod1_finish)

        def produce(src, srcbufs, c, tt, stg, rst, rstb, src_psum=False):
            t0, n = TT[tt]
            for (cs, s) in segs(tt):
                P.op("act", lambda e, cs=cs, s=s: e.activation(
                    out=U[:, c, t0 + cs.start:t0 + cs.stop], in_=src[:, cs], func=AF.Identity,
                    scale=coef[:, stg, 0, c, s:s + 1], bias=coef[:, stg, 1, c, s:s + 1]),
                    r=srcbufs + [COEF], w=[Ub[c][tt]])
                if src_psum:
                    P.op("dve", lambda e, cs=cs, s=s: e.tensor_scalar(
                        out=rst[:, cs], in0=src[:, cs], scalar1=coef[:, stg, 2, c, s:s + 1],
                        scalar2=coef[:, stg, 3, c, s:s + 1], op0=ALU.mult, op1=ALU.add),
                        r=srcbufs + [COEF], w=[rstb])
                else:
                    P.op("act", lambda e, cs=cs, s=s: e.activation(
                        out=rst[:, cs], in_=src[:, cs], func=AF.Identity, scale=coef[:, stg, 2, c, s:s + 1],
                        bias=coef[:, stg, 3, c, s:s + 1]), r=srcbufs + [COEF], w=[rstb])
            P.op("sp", lambda e: e.dma_start(out=rscr[:, c, t0:t0 + n], in_=rst[:, 0:n]), r=[rstb], w=[RS[c][tt]],
                 dma=rstb)

        xst, xstb = self.carve("xst", [128, 4, 1024], F32, 4)
        XIN = KSTAGE > 0
        rst, rstb = self.carve("rst", [128, 4, 512], F32, 4)
        ri = 0
        for tt in (range(5) if XIN else []):
            t0, n = TT[tt]
            if tt < 4:
                for tb in range(4):
                    P.op("sp", lambda e, tb=tb, t0=t0: e.dma_start(out=xst[:, tb, :], in_=xp[t0 + 128 * tb:t0 + 128 * tb + 128, :]),
                         w=[xstb[tb]], dma=xstb[tb])
            else:
                P.op("sp", lambda e: e.dma_start(out=xst[0:16, 0, :], in_=xs), w=[xstb[0]], dma=xstb[0])
            for c in range(8):
                pt, ptb = self.bank("acc")
                if tt < 4:
                    def fn_xt(e, pt=pt, c=c):
                        ins = None
                        for tb in range(4):
                            ins = e.transpose(pt[:, 128 * tb:128 * tb + 128], xst[:, tb, 128 * c:128 * c + 128], ident[:])
                        return ins
                    P.op("pe", fn_xt, r=xstb + [CONST], w=[ptb])
                else:
                    P.op("pe", lambda e, pt=pt, c=c: e.transpose(pt[:, 0:16], xst[0:16, 0, 128 * c:128 * c + 128],
                                                                 ident[0:16, 0:16]), r=[xstb[0], CONST], w=[ptb])
                produce(pt, [ptb], c, tt, 0, rst[:, ri % 4, :], rstb[ri % 4], src_psum=True)
                ri += 1

        try:
            self.stage(1)
            for l in range(L):
                self.layer(l, locals())
        except StopBuild:
            pass

        self.emit_d2d(1000)
        last = []
        allops = [o for e in ENGS for o in P.ops[e] if o.dsem is not None]
        lastd = {}
        for o in allops:
            q = lastd.get(id(o.dsem))
            if q is None or q.sigval < o.sigval:
                lastd[id(o.dsem)] = o
        fin = Op()
        fin.eng = "sp"
        fin.fn = None
        fin.sig = False
        fin.sigidx = 0
        fin.dsem = None
        fin.sigval = 0
        fin.seq = P.seq + 1
        fin.deps = {("d", k): o for k, o in lastd.items()}
        P.ops["sp"].append(fin)

        P.finalize()
        with nc.Block() as block:
            @block.sync
            def _(e):
                P.emit("sp", e)

            @block.scalar
            def _(e):
                P.emit("act", e)

            @block.gpsimd
            def _(e):
                P.emit("pool", e)

            @block.vector
            def _(e):
                P.emit("dve", e)

            @block.tensor
            def _(e):
                P.emit("pe", e)
        self.stack.close()
        return nc

    def layer(self, l, env):
        P = self.P
        g = env
        U, Ub, ident, CONST, ONES = g["U"], g["Ub"], g["ident"], g["CONST"], g["ONES"]
        ones_bf, onesD, onesC, ones_f, eps_t = g["ones_bf"], g["onesD"], g["onesC"], g["ones_f"], g["eps_t"]
        masks, MASKS, smp, smc, SM, SM2 = g["masks"], g["MASKS"], g["smp"], g["smc"], g["SM"], g["SM2"]
        vecT, VEC, bq8, modT, MOD, coef, COEF = g["vecT"], g["VEC"], g["bq8"], g["modT"], g["MOD"], g["coef"], g["COEF"]
        w_in, w_pc, w_pa, w_out, w_gate, w_up, w_down = (g["w_in"], g["w_pc"], g["w_pa"], g["w_out"], g["w_gate"],
                                                         g["w_up"], g["w_down"])
        ck, stc, kvp, cvp, kvs, cvs, yp, ys, rscr, RS = (g["ck"], g["stc"], g["kvp"], g["cvp"], g["kvs"], g["cvs"],
                                                         g["yp"], g["ys"], g["rscr"], g["RS"])
        produce = g["produce"]
        G1P, G2P = 16, 40

        def vcol(row):
            return vecT[:, l, row:row + 1]

        def ubufs(tt):
            return [Ub[k][tt] for k in range(8)]

        def gemm_u(ps, psb, Wt, Wb, col0, tt, n):
            t0 = TT[tt][0]
            self.mm_acc(ps[:, 0:n], psb, [(Wt[:, k, col0:col0 + 128], U[:, k, t0:t0 + n]) for k in range(8)],
                        [Wb] + ubufs(tt))

        self.new_phase(A0_END)
        cact, cactb = self.carve("cact", [128, 4, NT], BF16, 5)
        off_c = self.rb_offs[1]

        glu, glub = self.carve("glu", [128, 4, 30 + S], BF16, 4)
        sglu, sglub = self.carve("sglu", [128, 4, 4, 34], BF16, 1)
        gluf, glufb = self.carve("gluf", [128, 4, 48], F32, 1)
        ycv, ycvb = self.carve("ycv", [128, 4, NT], F32, 5, area=0)
        dwd, dwdb = self.carve("dwd", [128, 1, 31 * 128], BF16, 1)
        sig, sigb = self.carve("sig", [128, 2, 512], F32, 2)
        ybq, ybqb = self.carve("ybq", [128, 4, 512], BF16, 4)
        stt, sttb = self.carve("stt", [128, 4, 512], F32, 4)
        tmpa, tmpab = self.carve("tmpa", [128, 2, 512], F32, 2)
        sst, sstb = self.carve("sst", [128, 4, 512], F32, 1)
        cvo, cvob = self.carve("cvo", [128, 2, 512], F32, 2)

        for c in range(4):
            P.op("dve", lambda e, c=c: e.memset(glu[:, c, 0:30], 0.0), w=[glub[c]])
        for s in range(4):
            P.op("sp", lambda e, s=s: e.dma_start(out=sst[0:30, s, :], in_=stc[l, s, :, :]), w=sstb, dma=sstb[0])
        for c in range(4):
            pt, ptb = self.bank("misc")

            def fn_st(e, pt=pt, c=c):
                ins = None
                for s in range(4):
                    ins = e.transpose(pt[:, 30 * s:30 * s + 30], sst[0:30, s, 128 * c:128 * c + 128], ident[0:30, 0:30])
                return ins
            P.op("pe", fn_st, r=sstb + [CONST], w=[ptb])
            P.op("act", lambda e, pt=pt, c=c: e.activation(out=sglu[:, c, :, 0:30],
                                                           in_=pt[:, 0:120].rearrange("p (s t) -> p s t", s=4),
                                                           func=AF.Copy), r=[ptb], w=sglub)
        WA, WAb = self.wload(w_in[l, :, 0:512], 8, 512)
        WG, WGb = self.wload(w_in[l, :, 512:1024], 8, 512)
        self.pinned = [(self.ring_i - 2) % NSLOT, (self.ring_i - 1) % NSLOT]
        self.si = 0

        def drain_mod(n):
            for _ in range(n):
                if l == 0 and self.deferred_mod:
                    self.deferred_mod.pop(0)()

        def glu_chunk(c):
            for tt in range(5):
                t0, n = TT[tt]
                pa, pab = self.bank("acc")
                gemm_u(pa, pab, WA, WAb, 128 * c, tt, n)
                pg, pgb = self.bank("acc")
                gemm_u(pg, pgb, WG, WGb, 128 * c, tt, n)
                sg = sig[:, self.si % 2, :]
                sgb = sigb[self.si % 2]
                self.si += 1
                P.op("act", lambda e: e.activation(out=sg[:, 0:n], in_=pg[:, 0:n], func=AF.Sigmoid,
                                                   bias=vcol(V_BIN + 4 + c), scale=1.0), r=[pgb, VEC], w=[sgb])
                if tt < 4:
                    P.op("dve", lambda e: e.scalar_tensor_tensor(
                        out=glu[:, c, 30 + t0:30 + t0 + 512], in0=pa[:, 0:512], scalar=vcol(V_BIN + c), in1=sg[:, 0:512],
                        op0=ALU.add, op1=ALU.mult), r=[pab, sgb, VEC], w=[glub[c]])
                    if tt == 3:
                        P.op("dve", lambda e: e.scalar_tensor_tensor(
                            out=gluf[:, c, 0:32], in0=pa[:, 480:512], scalar=vcol(V_BIN + c), in1=sg[:, 480:512],
                            op0=ALU.add, op1=ALU.mult), r=[pab, sgb, VEC], w=glufb)
                else:
                    P.op("dve", lambda e: e.scalar_tensor_tensor(
                        out=gluf[:, c, 32:48], in0=pa[:, 0:16], scalar=vcol(V_BIN + c), in1=sg[:, 0:16],
                        op0=ALU.add, op1=ALU.mult), r=[pab, sgb, VEC], w=glufb)
                    P.op("dve", lambda e: e.tensor_copy(out=sglu[:, c, :, 30:34],
                                                        in_=gluf[:, c, 32:48].rearrange("p (s t) -> p s t", s=4)),
                         r=glufb, w=sglub)

        def conv_state_outputs():
            pt, ptb = self.bank("misc")

            def fn_cs(e, pt=pt):
                ins = None
                for c in range(4):
                    ins = e.transpose(pt[0:30, 128 * c:128 * c + 128], gluf[:, c, 2:32], ident[:])
                return ins
            P.op("pe", fn_cs, r=glufb + [CONST], w=[ptb])
            P.op("act", lambda e, pt=pt: e.activation(out=cvo[0:30, 0, :], in_=pt[0:30, :], func=AF.Copy), r=[ptb], w=[cvob[0]])
            P.op("sp", lambda e: e.dma_start(out=cvp[l, :, :], in_=cvo[0:30, 0, :]), r=[cvob[0]], w=[Buf("x")], dma=cvob[0])
            pt, ptb = self.bank("misc")

            def fn_cs2(e, pt=pt):
                ins = None
                for c in range(4):
                    ins = e.transpose(pt[0:16, 128 * c:128 * c + 128], gluf[:, c, 32:48], ident[:])
                return ins
            P.op("pe", fn_cs2, r=glufb + [CONST], w=[ptb])
            P.op("act", lambda e, pt=pt: e.activation(out=cvo[0:16, 1, :], in_=pt[0:16, :], func=AF.Copy), r=[ptb], w=[cvob[1]])
            for s in range(4):
                P.op("sp", lambda e, s=s: e.dma_start(out=cvs[l, s, 26:30, :], in_=cvo[4 * s:4 * s + 4, 1, :]),
                     r=[cvob[1]], w=[Buf("x")], dma=cvob[1])

        glu_chunk(0)
        drain_mod(2)
        glu_chunk(1)
        drain_mod(2)
        self.yi = 0

        def ln_tile(tt):
            t0, n = TT[tt]
            pm, pmb = self.bank("o")
            pq, pqb = self.bank("z")
            for c in range(4):
                yb_ = ybq[:, self.yi % 4, :]
                ybb = ybqb[self.yi % 4]
                self.yi += 1
                ys_ = ybq[:, self.yi % 4, :]
                ysb = ybqb[self.yi % 4]
                self.yi += 1
                P.op("dve", lambda e, yb_=yb_, c=c: e.tensor_copy(out=yb_[:, 0:n], in_=ycv[:, c, t0:t0 + n]),
                     r=[ycvb[tt]], w=[ybb])
                P.op("act", lambda e, ys_=ys_, c=c: e.activation(out=ys_[:, 0:n], in_=ycv[:, c, t0:t0 + n],
                                                                func=AF.Square), r=[ycvb[tt]], w=[ysb])
                P.op("pe", lambda e, yb_=yb_, c=c: e.matmul(pm[:, 0:n], lhsT=onesC[:], rhs=yb_[:, 0:n],
                                                           start=(c == 0), stop=(c == 3)), r=[ybb, ONES], w=[pmb])
                P.op("pe", lambda e, ys_=ys_, c=c: e.matmul(pq[:, 0:n], lhsT=onesC[:], rhs=ys_[:, 0:n],
                                                           start=(c == 0), stop=(c == 3)), r=[ysb, ONES], w=[pqb])
            mean, rstd = self.ln_stats(pm, pmb, pq, pqb, stt, sttb, n, eps_t, ONES)
            for c in range(4):
                ta = tmpa[:, c % 2, :]
                tab = tmpab[c % 2]
                P.op("dve", lambda e, ta=ta, c=c: e.tensor_tensor(out=ta[:, 0:n], in0=ycv[:, c, t0:t0 + n],
                                                                 in1=mean[:, 0:n], op=ALU.subtract),
                     r=[ycvb[tt], sttb[0]], w=[tab])
                P.op("dve", lambda e, ta=ta: e.tensor_tensor(out=ta[:, 0:n], in0=ta[:, 0:n], in1=rstd[:, 0:n],
                                                            op=ALU.mult), r=[tab, sttb[2]], w=[tab])
                P.op("act", lambda e, ta=ta, c=c: e.activation(out=cact[:, c, t0:t0 + n], in_=ta[:, 0:n],
                                                              func=AF.Silu, scale=vcol(V_CLG + c), bias=vcol(V_CLB + c)),
                     r=[tab, VEC], w=[cactb[tt]])

        def conv_chunk(c, with_ln):
            dw = dwd[:, 0, :].rearrange("p (k n) -> p k n", k=31)
            dwb = dwdb[0]
            P.op("dve", lambda e: e.tensor_tensor(
                out=dw, in0=ident[:].unsqueeze(1).to_broadcast([128, 31, 128]),
                in1=vecT[:, l, V_WDW + c:V_WDW + 124:4].unsqueeze(2).to_broadcast([128, 31, 128]), op=ALU.mult),
                r=[CONST, VEC], w=[dwb])
            for tt in range(5):
                t0, n = TT[tt]
                pc, pcb = self.bank("acc")
                if tt < 4:
                    pairs = [(dw[:, k, :], glu[:, c, t0 + k:t0 + k + 512]) for k in range(31)]
                    self.mm_acc(pc[:, 0:512], pcb, pairs, [dwb, glub[c]])
                else:
                    pairs = [(dw[:, k, :], sglu[:, c, :, k:k + 4]) for k in range(31)]
                    self.mm_acc(pc[:, 0:16], pcb, pairs, [dwb] + sglub)
                P.op("act", lambda e, pc=pc, t0=t0, n=n: e.activation(out=ycv[:, c, t0:t0 + n], in_=pc[:, 0:n],
                                                                     func=AF.Identity, bias=vcol(V_BDW + c), scale=1.0),
                     r=[pcb, VEC], w=[ycvb[tt]])
                if with_ln:
                    ln_tile(tt)

        conv_chunk(0, False)
        drain_mod(2)
        glu_chunk(2)
        drain_mod(2)
        conv_chunk(1, False)
        drain_mod(2)
        glu_chunk(3)
        drain_mod(2)
        conv_state_outputs()
        conv_chunk(2, False)
        drain_mod(100)
        conv_chunk(3, True)

        self.stage(2 + 10 * l)
        self.new_phase(off_c)
        oatt, oattb = self.carve("oatt", [128, 4, NT], BF16, 1)
        off_co = self.rb_offs[1]
        Qt, Qtb = self.carve("Qt", [128, NT], BF16, 5)
        Kt, Ktb = self.carve("Kt", [128, NT], BF16, 5)
        Ktf, Ktfb = self.carve("Ktf", [128, NT], F32, 5, area=0)
        Vtf, Vtfb = self.carve("Vtf", [128, NT], F32, 5, area=0)
        Vb, Vbb = self.carve("Vb", [128, 16, 128], BF16, 4)
        Oacc, Oaccb = self.carve("Oacc", [128, NT], F32, 1, area=0)
        Zacc, Zaccb = self.carve("Zacc", [128, NT], F32, 1, area=0)
        Eb, Ebb = self.carve("Eb", [128, 2, 512], BF16, 2)
        Pt, Ptb = self.carve("Pt", [128, 4, 512], BF16, 4)
        kvst, kvstb = self.carve("kvst", [128, 4, 512], F32, 4)
        sK32, sK32b = self.carve("sK32", [128, 4, 4, 128], F32, 4)
        sKt, sKtb = self.carve("sKt", [128, 2, 512], BF16, 2)
        sNew, sNewb = self.carve("sNew", [128, 4, 2, 128], F32, 1)
        sE, sEb = self.carve("sE", [128, 2, 32], F32, 2)
        sPp, sPpb = self.carve("sPp", [128, 32], BF16, 1)
        sPc, sPcb = self.carve("sPc", [128, 32], BF16, 1)
        sVb, sVbb = self.carve("sVb", [128, 4, 4, 128], BF16, 4)
        sNb, sNbb = self.carve("sNb", [128, 4, 128], BF16, 1)
        Qf, Qfb = self.carve("Qf", [128, 16], F32, 1)
        Ktfs, Ktfsb = self.carve("Ktfs", [128, 16], F32, 1)

        self.ei = 0
        self.pi = 0
        ki = 0
        import os
        for hp in range(int(os.environ.get('HPN', '4'))):
            for gi in range(int(os.environ.get('GIN', '3'))):
                d = DIL[gi]
                nblk = S // d // 128
                Wn = WIN[gi]
                qcol = 1024 + (0 + gi) * 512 + hp * 128
                kcol = 1024 + (3 + gi) * 512 + hp * 128
                vcol_ = 1024 + (6 + gi) * 512 + hp * 128
                self.emit_d2d(1)
                WQ, WQb = self.wload(w_in[l, :, qcol:qcol + 128], 8, 128)
                WK, WKb = self.wload(w_in[l, :, kcol:kcol + 128], 8, 128)
                WV, WVb = self.wload(w_in[l, :, vcol_:vcol_ + 128], 8, 128)
                nt_ = 1 if gi == 0 else 4
                for s_ in range(4):
                    if gi == 0:
                        srck = ck[gi][l, s_, :, :].rearrange("j (k f) -> j k f", k=2)[:, 0, hp * 128:hp * 128 + 128]
                        srcv = ck[gi][l, s_, :, :].rearrange("j (k f) -> j k f", k=2)[:, 1, hp * 128:hp * 128 + 128]
                        P.op("sp", lambda e, srck=srck, s_=s_: e.dma_start(out=sK32[:, s_, 0, :], in_=srck), w=[sK32b[s_]],
                             dma=sK32b[s_])
                        P.op("pool", lambda e, srcv=srcv, s_=s_: e.dma_start(out=sVb[:, s_, 0, :], in_=srcv), w=[sVbb[s_]],
                             dma=sVbb[s_])
                    else:
                        v4 = ck[gi][l, s_, :, :].rearrange("(j t) (k f) -> j t k f", t=d, k=2)
                        srck = v4[:, 0:4, 0, hp * 128:hp * 128 + 128]
                        srcv = v4[:, 0:4, 1, hp * 128:hp * 128 + 128]
                        P.op("sp", lambda e, srck=srck, s_=s_: e.dma_start(out=sK32[:, s_, :, :], in_=srck), w=[sK32b[s_]],
                             dma=sK32b[s_])
                        P.op("pool", lambda e, srcv=srcv, s_=s_: e.dma_start(out=sVb[:, s_, :, :], in_=srcv), w=[sVbb[s_]],
                             dma=sVbb[s_])
                for tt in range(5):
                    t0, n = TT[tt]
                    pq, pqb = self.bank("acc")
                    gemm_u(pq, pqb, WQ, WQb, 0, tt, n)
                    P.op("act", lambda e, pq=pq, t0=t0, n=n, gi=gi, hp=hp: e.activation(
                        out=Qt[:, t0:t0 + n], in_=pq[:, 0:n], func=AF.Identity, scale=0.125,
                        bias=bq8[:, l, gi * 4 + hp:gi * 4 + hp + 1]), r=[pqb, VEC], w=[Qtb[tt]])
                    pk, pkb = self.bank("acc")
                    gemm_u(pk, pkb, WK, WKb, 0, tt, n)
                    P.op("act", lambda e, pk=pk, t0=t0, n=n, kcol=kcol: e.activation(
                        out=Ktf[:, t0:t0 + n], in_=pk[:, 0:n], func=AF.Identity, scale=1.0, bias=vcol(kcol // 128)),
                        r=[pkb, VEC], w=[Ktfb[tt]])
                    P.op("dve", lambda e, pk=pk, t0=t0, n=n, kcol=kcol: e.tensor_scalar(
                        out=Kt[:, t0:t0 + n], in0=pk[:, 0:n], scalar1=vcol(kcol // 128), scalar2=None, op0=ALU.add),
                        r=[pkb, VEC], w=[Ktb[tt]])
                    pv, pvb = self.bank("acc")
                    gemm_u(pv, pvb, WV, WVb, 0, tt, n)
                    P.op("act", lambda e, pv=pv, t0=t0, n=n, vc=vcol_: e.activation(
                        out=Vtf[:, t0:t0 + n], in_=pv[:, 0:n], func=AF.Identity, scale=1.0, bias=vcol(vc // 128)),
                        r=[pvb, VEC], w=[Vtfb[tt]])

                def blk_cols(r, c):
                    a = r + d * 128 * c
                    return a, a + d * 127 + 1, d

                def blk_rc(b):
                    return b // nblk, b % nblk

                for bg in range(4):
                    for kv, src, srcb in ((1, Vtf, Vtfb), (0, Ktf, Ktfb)):
                        blks = [4 * bg + j for j in range(4)]
                        inwin = [(blk_rc(b)[0] + d * 128 * blk_rc(b)[1]) >= S - Wn for b in blks]
                        if kv == 0 and not any(inwin):
                            continue
                        pt, ptb = self.bank("tr")

                        def fn_tr(e, pt=pt, src=src, blks=blks):
                            ins = None
                            for j, b in enumerate(blks):
                                r_, c_ = blk_rc(b)
                                a, z, stp = blk_cols(r_, c_)
                                ins = e.transpose(pt[:, 128 * j:128 * j + 128], src[:, a:z:stp], ident[:])
                            return ins
                        P.op("pe", fn_tr, r=list(srcb[0:4]) + [CONST], w=[ptb])
                        if kv == 1:
                            P.op("dve", lambda e, pt=pt, bg=bg: e.tensor_copy(
                                out=Vb[:, 4 * bg:4 * bg + 4, :], in_=pt[:, :].rearrange("p (a b) -> p a b", a=4)),
                                r=[ptb], w=[Vbb[bg]])
                        if any(inwin):
                            kst = kvst[:, ki % 4, :]
                            kstb = kvstb[ki % 4]
                            ki += 1
                            P.op("act", lambda e, pt=pt, kst=kst: e.activation(out=kst, in_=pt[:, :], func=AF.Copy),
                                 r=[ptb], w=[kstb])
                            for j, b in enumerate(blks):
                                if not inwin[j]:
                                    continue
                                r_, c_ = blk_rc(b)
                                a = r_ + d * 128 * c_ - (S - Wn)
                                dst = kvp[gi][l, a:a + 127 * d + 1:d, kv, hp * 128:hp * 128 + 128]
                                P.op("sp", lambda e, dst=dst, kst=kst, j=j: e.dma_start(out=dst, in_=kst[:, 128 * j:128 * j + 128]),
                                     r=[kstb], w=[Buf("x")], dma=kstb)
                for half in range(2):
                    pt, ptb = self.bank("tr")

                    def fn_sn(e, pt=pt, half=half):
                        ins = None
                        for s in (2 * half, 2 * half + 1):
                            for kv, src in ((0, Ktf), (1, Vtf)):
                                o = ((s % 2) * 2 + kv) * 128
                                ins = e.transpose(pt[0:4, o:o + 128], src[:, S + 4 * s:S + 4 * s + 4], ident[:])
                        return ins
                    P.op("pe", fn_sn, r=[Ktfb[4], Vtfb[4], CONST], w=[ptb])
                    P.op("act", lambda e, pt=pt, half=half: e.activation(
                        out=sNew[0:4, 2 * half:2 * half + 2, :, :],
                        in_=pt[0:4, :].rearrange("p (s k f) -> p s k f", s=2, k=2), func=AF.Copy), r=[ptb], w=sNewb)
                P.op("act", lambda e: e.activation(out=sNb[0:4, :, :], in_=sNew[0:4, :, 1, :], func=AF.Copy), r=sNewb, w=sNbb)
                for s in range(4):
                    dst = kvs[gi][l, s, Wn - 4:Wn, :].rearrange("t (k f) -> t k f", k=2)[:, :, hp * 128:hp * 128 + 128]
                    P.op("sp", lambda e, dst=dst, s=s: e.dma_start(out=dst, in_=sNew[0:4, s, :, :]), r=sNewb, w=[Buf("x")],
                         dma=sNewb[0])

                mk = masks[:, gi * 4 + hp, :]
                first = (gi == 0)
                gidx = gi * 4 + hp
                nt_ = 1 if gi == 0 else 4

                def q_group(qg):
                    if gi == 0:
                        return [(0, 4 * qg + j) for j in range(4)], (lambda T: T[:, 512 * qg:512 * qg + 512])
                    if gi == 1:
                        return [(qg, j) for j in range(4)], (lambda T: T[:, qg:S:4])
                    return ([(4 * qg + j, 0) for j in range(4)],
                            (lambda T: T[:, 0:S].rearrange("p (i r) -> p r i", r=16)[:, 4 * qg:4 * qg + 4, :]))

                def emit_S(qg, j, r_, c_, po, pob, pz, pzb):
                    hasp = c_ > 0
                    a, z, stp = blk_cols(r_, c_)
                    psA, psAb = self.bank("sS")
                    psB, psBb = self.bank("sS")

                    def fn_s(e):
                        ins = None
                        for hh, ps_ in ((0, psA), (1, psB)):
                            ph = slice(64 * hh, 64 * hh + 64)
                            ins = e.matmul(ps_[:, 0:128], lhsT=Kt[ph, a:z:stp], rhs=Qt[ph, a:z:stp], start=True, stop=True)
                            if hasp:
                                a2, z2, _ = blk_cols(r_, c_ - 1)
                                ins = e.matmul(ps_[:, 128:256], lhsT=Kt[ph, a2:z2:stp], rhs=Qt[ph, a:z:stp],
                                               start=True, stop=True)
                        return ins
                    P.op("pe", fn_s, r=list(Qtb[0:4]) + list(Ktb[0:4]), w=[psAb, psBb])
                    E_ = Eb[:, self.ei % 2, :]
                    Eb_ = Ebb[self.ei % 2]
                    self.ei += 1
                    P_ = Pt[:, self.pi % 4, :]
                    Pb_ = Ptb[self.pi % 4]
                    self.pi += 1
                    nv = 256 if hasp else 128
                    P.op("act", lambda e: e.activation(out=E_[:, 0:nv], in_=psA[:, 0:nv], func=AF.Exp), r=[psAb], w=[Eb_])
                    P.op("act", lambda e: e.activation(out=E_[:, 256:256 + nv], in_=psB[:, 0:nv], func=AF.Exp),
                         r=[psBb], w=[Eb_])
                    if hasp:
                        P.op("dve", lambda e: e.tensor_tensor(out=P_, in0=E_, in1=mk, op=ALU.mult), r=[Eb_, MASKS], w=[Pb_])
                    else:
                        v3 = lambda T: T.rearrange("p (a b) -> p a b", a=2)[:, :, 0:128]
                        P.op("dve", lambda e: e.tensor_tensor(out=v3(P_), in0=v3(E_), in1=v3(mk), op=ALU.mult),
                             r=[Eb_, MASKS], w=[Pb_])
                    return (qg, j, hasp, r_ * nblk + c_, P_, Pb_, po, pob, pz, pzb)

                def emit_PV(st):
                    qg, j, hasp, bcur, P_, Pb_, po, pob, pz, pzb = st

                    def fn_pv(e):
                        ins = None
                        for hh in range(2):
                            ph = slice(64 * hh, 64 * hh + 64)
                            oc = slice(128 * j, 128 * j + 128)
                            ins = e.matmul(po[ph, oc], lhsT=Vb[:, bcur, ph], rhs=P_[:, 256 * hh:256 * hh + 128],
                                           start=True, stop=not hasp)
                            if hasp:
                                ins = e.matmul(po[ph, oc], lhsT=Vb[:, bcur - 1, ph],
                                               rhs=P_[:, 256 * hh + 128:256 * hh + 256], start=False, stop=True)
                            ins = e.matmul(pz[ph, oc], lhsT=ones_bf[:, 0:64], rhs=P_[:, 256 * hh:256 * hh + 128],
                                           start=True, stop=not hasp)
                            if hasp:
                                ins = e.matmul(pz[ph, oc], lhsT=ones_bf[:, 0:64],
                                               rhs=P_[:, 256 * hh + 128:256 * hh + 256], start=False, stop=True)
                        return ins
                    P.op("pe", fn_pv, r=[Pb_, ONES] + Vbb, w=[pob, pzb])
                    if j == 3:
                        _, acc_sl = q_group(qg)
                        pv3 = (lambda T: T[:, :]) if gi < 2 else (lambda T: T[:, :].rearrange("p (r i) -> p r i", r=4))
                        if first:
                            P.op("dve", lambda e: e.tensor_copy(out=acc_sl(Oacc), in_=pv3(po)), r=[pob], w=Oaccb)
                            P.op("act", lambda e: e.activation(out=acc_sl(Zacc), in_=pv3(pz), func=AF.Copy), r=[pzb], w=Zaccb)
                        else:
                            P.op("dve", lambda e: e.tensor_tensor(out=acc_sl(Oacc), in0=pv3(po), in1=acc_sl(Oacc), op=ALU.add),
                                 r=[pob] + Oaccb, w=Oaccb)
                            P.op("dve", lambda e: e.tensor_tensor(out=acc_sl(Zacc), in0=pv3(pz), in1=acc_sl(Zacc), op=ALU.add),
                                 r=[pzb] + Zaccb, w=Zaccb)

                pso_full, psob = self.ps[7], self.psb[7]
                sstate = {}

                def s_T(s):
                    sl = (s % 2)
                    pt, ptb = self.bank("sS")

                    def fn_kt(e):
                        ins = None
                        for t in range(nt_):
                            ins = e.transpose(pt[:, 128 * t:128 * t + 128], sK32[:, s, t, :], ident[:])
                        return ins
                    P.op("pe", fn_kt, r=[sK32b[s], CONST], w=[ptb])
                    kt_ = sKt[:, sl, :]
                    ktb_ = sKtb[sl]
                    P.op("dve", lambda e: e.tensor_copy(out=kt_[:, 0:128 * nt_], in_=pt[:, 0:128 * nt_]), r=[ptb], w=[ktb_])
                    sstate[s] = (None, None, kt_, ktb_, sVb[:, s, :, :], sVbb[s])

                def s_st2(s):
                    kc_, kcb, kt_, ktb_, vb_, vbb_ = sstate[s]
                    pss0, pss0b = self.bank("sS")
                    pss1, pss1b = self.bank("sS")

                    def fn_ss(e):
                        ins = None
                        for hh, pss in ((0, pss0), (1, pss1)):
                            ph = slice(64 * hh, 64 * hh + 64)
                            for t in range(4):
                                tk = 0 if nt_ == 1 else t
                                ins = e.matmul(pss[:, t:t + 1], lhsT=kt_[ph, 128 * tk:128 * tk + 128],
                                               rhs=Qt[ph, S + 4 * s + t:S + 4 * s + t + 1], start=True, stop=True)
                            ins = e.matmul(pss[0:4, 16:20], lhsT=Kt[ph, S + 4 * s:S + 4 * s + 4],
                                           rhs=Qt[ph, S + 4 * s:S + 4 * s + 4], start=True, stop=True)
                        return ins
                    P.op("pe", fn_ss, r=[ktb_, Qtb[4], Ktb[4]], w=[pss0b, pss1b])
                    cs_ = slice(8 * s, 8 * s + 8)
                    for hh, pss, pssb in ((0, pss0, pss0b), (1, pss1, pss1b)):
                        c4 = slice(8 * s + 4 * hh, 8 * s + 4 * hh + 4)
                        P.op("act", lambda e, pss=pss, c4=c4: e.activation(out=sE[:, 0, c4], in_=pss[:, 0:4], func=AF.Exp),
                             r=[pssb], w=[sEb[0]])
                        P.op("act", lambda e, pss=pss, c4=c4: e.activation(out=sE[0:4, 1, c4], in_=pss[0:4, 16:20], func=AF.Exp),
                             r=[pssb], w=[sEb[1]])
                    P.op("dve", lambda e: e.tensor_tensor(out=sPp[:, cs_], in0=sE[:, 0, cs_], in1=smp[:, gidx, cs_], op=ALU.mult),
                         r=[sEb[0], SM], w=sPpb)
                    P.op("dve", lambda e: e.tensor_tensor(out=sPc[0:4, cs_], in0=sE[0:4, 1, cs_], in1=smc[0:4, gidx, cs_],
                                                          op=ALU.mult), r=[sEb[1], SM2], w=sPcb)

                def s_st3(s):
                    kc_, kcb, kt_, ktb_, vb_, vbb_ = sstate[s]

                    def fn_so(e):
                        ins = None
                        for hh in range(2):
                            ph = slice(64 * hh, 64 * hh + 64)
                            c0_ = s * 8 + hh * 4
                            o4 = slice(4 * s, 4 * s + 4)
                            z4 = slice(16 + 4 * s, 16 + 4 * s + 4)
                            if nt_ == 1:
                                ins = e.matmul(pso_full[ph, o4], lhsT=vb_[:, 0, ph], rhs=sPp[:, c0_:c0_ + 4], start=True, stop=False)
                            else:
                                for t in range(4):
                                    ins = e.matmul(pso_full[ph, 4 * s + t:4 * s + t + 1], lhsT=vb_[:, t, ph],
                                                   rhs=sPp[:, c0_ + t:c0_ + t + 1], start=(t == 0), stop=False,
                                                   skip_group_check=True)
                            ins = e.matmul(pso_full[ph, o4], lhsT=sNb[0:4, s, ph], rhs=sPc[0:4, c0_:c0_ + 4], start=False, stop=True,
                                           skip_group_check=True)
                            ins = e.matmul(pso_full[ph, z4], lhsT=ones_bf[:, 0:64], rhs=sPp[:, c0_:c0_ + 4], start=True, stop=False)
                            ins = e.matmul(pso_full[ph, z4], lhsT=ones_bf[0:4, 0:64], rhs=sPc[0:4, c0_:c0_ + 4], start=False, stop=True)
                        return ins
                    P.op("pe", fn_so, r=[vbb_, ONES] + sPpb + sPcb + sNbb, w=[psob])

                def s_acc(_):
                    if first:
                        P.op("dve", lambda e: e.tensor_copy(out=Oacc[:, S:NT], in_=pso_full[:, 0:16]), r=[psob], w=Oaccb)
                        P.op("dve", lambda e: e.tensor_copy(out=Zacc[:, S:NT], in_=pso_full[:, 16:32]), r=[psob], w=Zaccb)
                    else:
                        P.op("dve", lambda e: e.tensor_tensor(out=Oacc[:, S:NT], in0=pso_full[:, 0:16], in1=Oacc[:, S:NT],
                                                              op=ALU.add), r=[psob] + Oaccb, w=Oaccb)
                        P.op("dve", lambda e: e.tensor_tensor(out=Zacc[:, S:NT], in0=pso_full[:, 16:32], in1=Zacc[:, S:NT],
                                                              op=ALU.add), r=[psob] + Zaccb, w=Zaccb)

                ssched = {1: [(s_T, 0)], 2: [(s_T, 1)], 3: [(s_st2, 0)], 5: [(s_st3, 0), (s_st2, 1)], 6: [(s_T, 2)],
                          8: [(s_st3, 1), (s_T, 3)], 9: [(s_st2, 2)], 10: [(s_st2, 3)], 12: [(s_st3, 2)], 13: [(s_st3, 3)],
                          15: [(s_acc, 0)]}
                PIPE = 2
                pending = []
                qi = 0
                for qg in range(4):
                    qbl, _ = q_group(qg)
                    po, pob = self.bank("o")
                    pz, pzb = self.bank("z")
                    for j, (r_, c_) in enumerate(qbl):
                        pending.append(emit_S(qg, j, r_, c_, po, pob, pz, pzb))
                        if len(pending) > PIPE:
                            emit_PV(pending.pop(0))
                        for f_, a_ in ssched.get(qi, []):
                            f_(a_)
                        qi += 1
                while pending:
                    emit_PV(pending.pop(0))
            P.op("dve", lambda e: e.reciprocal(out=Zacc[:, :], in_=Zacc[:, :]), r=Zaccb, w=Zaccb)
            P.op("dve", lambda e, hp=hp: e.tensor_tensor(out=oatt[:, hp, :], in0=Oacc[:, :], in1=Zacc[:, :], op=ALU.mult),
                 r=Zaccb + Oaccb, w=oattb)

        self.stage(3 + 10 * l)
        self.new_phase(off_co)
        m, mb = self.carve("m", [128, 8, NT], BF16, 5, area=0)
        sgc, sgcb = self.carve("sgc", [128, 2, 512], F32, 2)
        sga, sgab = self.carve("sga", [128, 2, 512], F32, 2)
        t1, t1b = self.carve("t1", [128, 2, 512], F32, 2)
        t2, t2b = self.carve("t2", [128, 2, 512], F32, 2)
        mi = 0
        for og in range(2):
            for half in range(2):
                Wpc, Wpcb = self.wload(w_pc[l, :, 512 * og:512 * og + 512], 4, 512)
                Wpa, Wpab = self.wload(w_pa[l, :, 512 * og:512 * og + 512], 4, 512)
                cbase = 4 * og + 2 * half
                c0 = 1024 + 4608 + 128 * cbase
                Wgc, Wgcb = self.wload(w_in[l, :, c0:c0 + 256], 8, 256)
                Wga, Wgab = self.wload(w_in[l, :, c0 + 1024:c0 + 1024 + 256], 8, 256)
                for cc in range(2):
                    c = cbase + cc
                    for tt in range(5):
                        t0, n = TT[tt]
                        pyc, pycb = self.bank("all")
                        self.mm_acc(pyc[:, 0:n], pycb, [(Wpc[:, k, 128 * (c % 4):128 * (c % 4) + 128], cact[:, k, t0:t0 + n])
                                                         for k in range(4)], [Wpcb, cactb[tt]])
                        pgc, pgcb = self.bank("all")
                        gemm_u(pgc, pgcb, Wgc, Wgcb, 128 * cc, tt, n)
                        pya, pyab = self.bank("all")
                        self.mm_acc(pya[:, 0:n], pyab, [(Wpa[:, k, 128 * (c % 4):128 * (c % 4) + 128], oatt[:, k, t0:t0 + n])
                                                         for k in range(4)], [Wpab] + oattb)
                        pga, pgab = self.bank("all")
                        gemm_u(pga, pgab, Wga, Wgab, 128 * cc, tt, n)
                        i2 = mi % 2
                        mi += 1
                        P.op("act", lambda e, pgc=pgc, i2=i2, c=c, n=n: e.activation(
                            out=sgc[:, i2, 0:n], in_=pgc[:, 0:n], func=AF.Sigmoid, bias=vcol(V_BIN + 44 + c), scale=1.0),
                            r=[pgcb, VEC], w=[sgcb[i2]])
                        P.op("act", lambda e, pga=pga, i2=i2, c=c, n=n: e.activation(
                            out=sga[:, i2, 0:n], in_=pga[:, 0:n], func=AF.Sigmoid, bias=vcol(V_BIN + 52 + c), scale=1.0),
                            r=[pgab, VEC], w=[sgab[i2]])
                        P.op("dve", lambda e, pyc=pyc, i2=i2, c=c, n=n: e.scalar_tensor_tensor(
                            out=t1[:, i2, 0:n], in0=pyc[:, 0:n], scalar=vcol(V_BPC + c), in1=sgc[:, i2, 0:n],
                            op0=ALU.add, op1=ALU.mult), r=[pycb, sgcb[i2], VEC], w=[t1b[i2]])
                        P.op("dve", lambda e, pya=pya, i2=i2, c=c, n=n: e.scalar_tensor_tensor(
                            out=t2[:, i2, 0:n], in0=pya[:, 0:n], scalar=vcol(V_BPA + c), in1=sga[:, i2, 0:n],
                            op0=ALU.add, op1=ALU.mult), r=[pyab, sgab[i2], VEC], w=[t2b[i2]])
                        P.op("dve", lambda e, i2=i2, c=c, t0=t0, n=n: e.tensor_tensor(
                            out=m[:, c, t0:t0 + n], in0=t1[:, i2, 0:n], in1=t2[:, i2, 0:n], op=ALU.add),
                            r=[t1b[i2], t2b[i2]], w=[mb[tt]])
        self.stage(4 + 10 * l)
        self.new_phase(A0_END, A0_END, A0_END)
        self.ln_wfull = w_out[l]
        self.ln_block(l, 1 + 2 * l, lambda c: w_out[l, :, 128 * c:128 * c + 128], 8, m, mb, G1P, final=False, env=g,
                      tiles=range(5))

        self.stage(5 + 10 * l)
        for half, tiles in enumerate(([0, 1], [2, 3, 4])):
            self.new_phase(0, 0, 0)
            c0 = TT[tiles[0]][0]
            ncol = sum(TT[t][1] for t in tiles)
            h, hb = self.carve("h", [128, NF, 1040], BF16, 3)
            sgt, sgtb = self.carve("sgt", [128, 2, 512], F32, 2)
            hi = 0
            ndrain = (len(self.deferred_norm) + 10) // 11
            for fg in range(11):
                for _ in range(ndrain):
                    if self.deferred_norm:
                        self.deferred_norm.pop(0)()
                Wg_, Wgb_ = self.wload(w_gate[l, :, 256 * fg:256 * fg + 256], 8, 256)
                Wu_, Wub_ = self.wload(w_up[l, :, 256 * fg:256 * fg + 256], 8, 256)
                for ff in range(2):
                    f = 2 * fg + ff
                    for ti, tt in enumerate(tiles):
                        t0, n = TT[tt]
                        pgt, pgtb = self.bank("all")
                        gemm_u(pgt, pgtb, Wg_, Wgb_, 128 * ff, tt, n)
                        pup, pupb = self.bank("all")
                        gemm_u(pup, pupb, Wu_, Wub_, 128 * ff, tt, n)
                        i2 = hi % 2
                        hi += 1
                        P.op("act", lambda e, pgt=pgt, i2=i2, n=n: e.activation(out=sgt[:, i2, 0:n], in_=pgt[:, 0:n],
                                                                               func=AF.Silu), r=[pgtb], w=[sgtb[i2]])
                        P.op("dve", lambda e, pup=pup, i2=i2, f=f, t0=t0, n=n: e.tensor_tensor(
                            out=h[:, f, t0 - c0:t0 - c0 + n], in0=pup[:, 0:n], in1=sgt[:, i2, 0:n], op=ALU.mult),
                            r=[pupb, sgtb[i2]], w=[hb[ti]])
            while self.deferred_norm:
                self.deferred_norm.pop(0)()
            hview = lambda k, t0, n, h=h, c0=c0: h[:, k, t0 - c0:t0 - c0 + n]
            self.ln_block(l, 2 + 2 * l, lambda c: w_down[l, :, 128 * c:128 * c + 128], NF, None, hb, G2P,
                          final=(l == L - 1), env=g, tiles=tiles, hview=hview, defer_last=(half == 0))

    def ln_stats(self, pm, pmb, pq, pqb, stt, sttb, n, eps_t, ONES):
        P = self.P
        mean = stt[:, 0, :]
        m2 = stt[:, 1, :]
        rstd = stt[:, 2, :]
        P.op("dve", lambda e: e.tensor_copy(out=mean[:, 0:n], in_=pm[:, 0:n]), r=[pmb], w=[sttb[0]])
        P.op("dve", lambda e: e.tensor_tensor(out=m2[:, 0:n], in0=mean[:, 0:n], in1=mean[:, 0:n], op=ALU.mult),
             r=[sttb[0]], w=[sttb[1]])
        P.op("dve", lambda e: e.tensor_tensor(out=m2[:, 0:n], in0=pq[:, 0:n], in1=m2[:, 0:n], op=ALU.subtract),
             r=[pqb, sttb[1]], w=[sttb[1]])
        P.op("act", lambda e: e.activation(out=rstd[:, 0:n], in_=m2[:, 0:n], func=AF.Sqrt, bias=eps_t[:, 0:1], scale=1.0),
             r=[sttb[1], ONES], w=[sttb[2]])
        P.op("dve", lambda e: e.reciprocal(out=rstd[:, 0:n], in_=rstd[:, 0:n]), r=[sttb[2]], w=[sttb[2]])
        return mean, rstd

    def ln_block(self, l, stg, wsrc, kch, xin, xinb, gofs, final, env, tiles, hview=None, defer_last=False):
        P = self.P
        g = env
        U, Ub, ident, CONST, ONES = g["U"], g["Ub"], g["ident"], g["CONST"], g["ONES"]
        onesD, eps_t, modT, MOD, coef, COEF = g["onesD"], g["eps_t"], g["modT"], g["MOD"], g["coef"], g["COEF"]
        rscr, RS, yp, ys = g["rscr"], g["RS"], g["yp"], g["ys"]
        produce = g["produce"]
        tiles = list(tiles)
        pipelined = (kch <= 8)
        groups = [[t] for t in tiles] if pipelined else [tiles[i:i + 2] for i in range(0, len(tiles), 2)]
        zt, ztb = self.carve("zt", [128, 8, 1024], F32, 16)
        rt, rtb = self.carve("rt", [128, 4, 512], F32, 4)
        zbq, zbqb = self.carve("zbq", [128, 4, 512], BF16, 4)
        stt, sttb = self.carve("stt", [128, 3, 512], F32, 3)
        if final:
            yst, ystb = self.carve("yst", [128, 2, 1024], F32, 2)
            rst, rstb = None, None
        else:
            rst, rstb = self.carve("rst", [128, 4, 512], F32, 4)
        bufs = (zt, ztb, rt, rtb, zbq, zbqb, stt, sttb, rst, rstb, (yst, ystb) if final else None)
        self.ln_wpre = None
        if pipelined:
            self.ln_wpre = [self.wload(self.ln_wfull[:, 512 * i:512 * i + 512], 8, 512) for i in range(2)]
            for gi_, grp in enumerate(groups):
                self.ln_group(l, stg, wsrc, kch, xin, xinb, gofs, final, env, tiles, grp, hview, bufs, gi_ % 2, "proj")
                if gi_ > 0:
                    self.ln_group(l, stg, wsrc, kch, xin, xinb, gofs, final, env, tiles, groups[gi_ - 1], hview, bufs,
                                  (gi_ - 1) % 2, "norm")
            self.ln_group(l, stg, wsrc, kch, xin, xinb, gofs, final, env, tiles, groups[-1], hview, bufs,
                          (len(groups) - 1) % 2, "norm")
        else:
            for gi_, grp in enumerate(groups):
                self.ln_group(l, stg, wsrc, kch, xin, xinb, gofs, final, env, tiles, grp, hview, bufs, 0, "proj")
                dfr = defer_last and gi_ == len(groups) - 1
                th = self.ln_group(l, stg, wsrc, kch, xin, xinb, gofs, final, env, tiles, grp, hview, bufs, 0, "norm",
                                   defer=dfr)
                if dfr:
                    self.deferred_norm = th

    def ln_group(self, l, stg, wsrc, kch, xin, xinb, gofs, final, env, alltiles, tiles, hview, bufs, slot, part, defer=False):
        P = self.P
        g = env
        U, Ub, ident, CONST, ONES = g["U"], g["Ub"], g["ident"], g["CONST"], g["ONES"]
        onesD, eps_t, modT, MOD, coef, COEF = g["onesD"], g["eps_t"], g["modT"], g["MOD"], g["coef"], g["COEF"]
        rscr, RS, yp, ys = g["rscr"], g["RS"], g["yp"], g["ys"]
        produce = g["produce"]
        zt, ztb, rt, rtb, zbq, zbqb, stt, sttb, rst, rstb, ysts = bufs
        if final:
            yst, ystb = ysts
        ri = self.ln_ri
        for c in (range(8) if part == "proj" else []):
            if self.ln_wpre is not None:
                Wfull, Wb = self.ln_wpre[c // 4]
                Wt = Wfull[:, :, 128 * (c % 4):128 * (c % 4) + 128]
            else:
                Wt, Wb = self.wload(wsrc(c), kch, 128)
            for ti, tt in enumerate(tiles):
                t0, n = TT[tt]
                zoff = (slot + ti) * 512
                pz_, pzb_ = self.bank("acc")
                if hview is None:
                    pairs = [(Wt[:, k, :], xin[:, k, t0:t0 + n]) for k in range(kch)]
                    rb = [Wb, xinb[tt]]
                else:
                    pairs = [(Wt[:, k, :], hview(k, t0, n)) for k in range(kch)]
                    rb = [Wb, xinb[alltiles.index(tt)]]
                self.mm_acc(pz_[:, 0:n], pzb_, pairs, rb)
                r_ = rt[:, ri % 4, :]
                rb_ = rtb[ri % 4]
                ri += 1
                P.op("sp", lambda e, r_=r_, c=c, t0=t0, n=n: e.dma_start(out=r_[:, 0:n], in_=rscr[:, c, t0:t0 + n]),
                     r=[RS[c][tt]], w=[rb_], dma=rb_)
                for (cs, s) in segs(tt):
                    P.op("dve", lambda e, pz_=pz_, r_=r_, c=c, cs=cs, s=s, t0=t0: e.scalar_tensor_tensor(
                        out=zt[:, c, zoff + cs.start:zoff + cs.stop], in0=pz_[:, cs], scalar=modT[:, l, gofs + c, s:s + 1],
                        in1=r_[:, cs], op0=ALU.mult, op1=ALU.add), r=[pzb_, rb_, MOD], w=[ztb[(slot + ti) * 8 + c]])
        self.ln_ri = ri
        thunks = []
        for ti, tt in enumerate(tiles if part == "norm" else []):
            thunks += self.norm_tile(l, stg, final, env, bufs, slot, ti, tt)
        if defer:
            return thunks
        for f in thunks:
            f()
        return []

    def norm_tile(self, l, stg, final, env, bufs, slot, ti, tt):
        P = self.P
        g = env
        ident, CONST, ONES = g["ident"], g["CONST"], g["ONES"]
        onesD, eps_t, coef, COEF = g["onesD"], g["eps_t"], g["coef"], g["COEF"]
        yp, ys = g["yp"], g["ys"]
        produce = g["produce"]
        zt, ztb, rt, rtb, zbq, zbqb, stt, sttb, rst, rstb, ysts = bufs
        if final:
            yst, ystb = ysts
        t0, n = TT[tt]
        o0 = (slot + ti) * 512
        zb0 = (slot + ti) * 8
        st = {}

        def stats():
            pm, pmb = self.bank("o")
            pq, pqb = self.bank("z")
            for c in range(8):
                zb_ = zbq[:, self.ln_zi % 4, :]
                zbb = zbqb[self.ln_zi % 4]
                self.ln_zi += 1
                zs_ = zbq[:, self.ln_zi % 4, :]
                zsb = zbqb[self.ln_zi % 4]
                self.ln_zi += 1
                P.op("act", lambda e: e.activation(out=zb_[:, 0:n], in_=zt[:, c, o0:o0 + n], func=AF.Copy),
                     r=[ztb[zb0 + c]], w=[zbb])
                P.op("act", lambda e: e.activation(out=zs_[:, 0:n], in_=zt[:, c, o0:o0 + n], func=AF.Square),
                     r=[ztb[zb0 + c]], w=[zsb])
                P.op("pe", lambda e: e.matmul(pm[:, 0:n], lhsT=onesD[:], rhs=zb_[:, 0:n], start=(c == 0), stop=(c == 7)),
                     r=[zbb, ONES], w=[pmb])
                P.op("pe", lambda e: e.matmul(pq[:, 0:n], lhsT=onesD[:], rhs=zs_[:, 0:n], start=(c == 0), stop=(c == 7)),
                     r=[zsb, ONES], w=[pqb])
            st["mean"], st["rstd"] = self.ln_stats(pm, pmb, pq, pqb, stt, sttb, n, eps_t, ONES)

        def norm_c(c):
            def f():
                mean, rstd = st["mean"], st["rstd"]
                zc = zt[:, c, o0:o0 + n]
                P.op("dve", lambda e: e.tensor_tensor(out=zc, in0=zc, in1=mean[:, 0:n], op=ALU.subtract),
                     r=[sttb[0], ztb[zb0 + c]], w=[ztb[zb0 + c]])
                P.op("dve", lambda e: e.tensor_tensor(out=zc, in0=zc, in1=rstd[:, 0:n], op=ALU.mult),
                     r=[sttb[2], ztb[zb0 + c]], w=[ztb[zb0 + c]])
                if not final:
                    ri = self.ln_ri
                    produce(zc, [ztb[zb0 + c]], c, tt, stg, rst[:, ri % 4, :], rstb[ri % 4])
                    self.ln_ri = ri + 1
                else:
                    P.op("act", lambda e: e.activation(out=zc, in_=zc, func=AF.Identity, scale=coef[:, stg, 0, c, 0:1],
                                                       bias=coef[:, stg, 1, c, 0:1]),
                         r=[COEF, ztb[zb0 + c]], w=[ztb[zb0 + c]])
            return f

        def out_tb(tb):
            def f():
                rows = min(128, n - 128 * tb)
                ysl = yst[:, tb % 2, :]
                yslb = ystb[tb % 2]
                for hf in range(2):
                    pt, ptb = self.bank("acc")

                    def fn_yt(e, pt=pt, hf=hf):
                        ins = None
                        for cc in range(4):
                            c = 4 * hf + cc
                            ins = e.transpose(pt[0:rows, 128 * cc:128 * cc + 128],
                                              zt[:, c, o0 + 128 * tb:o0 + 128 * tb + rows], ident[:])
                        return ins
                    P.op("pe", fn_yt, r=ztb[zb0:zb0 + 8] + [CONST], w=[ptb])
                    P.op("act", lambda e, pt=pt, hf=hf: e.activation(out=ysl[0:rows, 512 * hf:512 * hf + 512],
                                                                     in_=pt[0:rows, :], func=AF.Copy), r=[ptb], w=[yslb])
                dst = yp[t0 + 128 * tb:t0 + 128 * tb + 128, :] if tt < 4 else ys[:, :]
                P.op("sp", lambda e: e.dma_start(out=dst, in_=ysl[0:rows, :]), r=[yslb], w=[Buf("x")], dma=yslb)
            return f

        th = [stats] + [norm_c(c) for c in range(8)]
        if final:
            th += [out_tb(tb) for tb in range((n + 127) // 128)]
        return th


_CACHE = {}


def _consts():
    slopes = 2.0 ** (-8.0 * (np.arange(8) + 1) / 8.0)
    k = np.arange(128)[:, None].astype(np.float64)
    q = np.arange(128)[None, :].astype(np.float64)
    masks = np.zeros((128, 12, 512), np.float64)
    smp = np.zeros((128, 12, 32), np.float64)
    smc = np.zeros((4, 12, 32), np.float64)
    for g, d in enumerate(DIL):
        for hp in range(4):
            for hh in range(2):
                h = 2 * hp + hh
                cur = np.where(k <= q, np.exp(-slopes[h] * d * np.maximum(q - k, 0.0)), 0.0)
                prev = np.where(k >= q, np.exp(-slopes[h] * d * np.maximum(128 + q - k, 0.0)), 0.0)
                masks[:, g * 4 + hp, (2 * hh) * 128:(2 * hh + 1) * 128] = cur
                masks[:, g * 4 + hp, (2 * hh + 1) * 128:(2 * hh + 2) * 128] = prev
                for s in range(4):
                    for t in range(4):
                        col = s * 8 + hh * 4 + t
                        smp[:, g * 4 + hp, col] = prev[:, t if g == 0 else 0]
                        if g == 0:
                            smc[:, g * 4 + hp, col] = cur[0:4, t]
                        else:
                            smc[:, g * 4 + hp, col] = (np.arange(4) == t).astype(np.float64)
    return (np.eye(128, dtype=np.float32), masks.reshape(128, -1).astype(np.float32),
            smp.reshape(128, -1).astype(np.float32), smc.reshape(4, -1).astype(np.float32))


def _pack_vecs(inp):
    out = np.zeros((L, NVEC, 128), np.float32)
    for l in range(L):
        rows = [inp["b_in"][l].reshape(60, 128), inp["b_ada"][l].reshape(48, 128), inp["b_pc"][l].reshape(8, 128),
                inp["b_pa"][l].reshape(8, 128), inp["b_out"][l].reshape(8, 128), inp["ln1_g"][l].reshape(8, 128),
                inp["ln1_b"][l].reshape(8, 128), inp["ln2_g"][l].reshape(8, 128), inp["ln2_b"][l].reshape(8, 128),
                inp["b_dw"][l].reshape(4, 128), inp["conv_ln_g"][l].reshape(4, 128), inp["conv_ln_b"][l].reshape(4, 128),
                inp["w_dw"][l].reshape(124, 128)]
        cat = np.concatenate(rows, axis=0)
        out[l, :cat.shape[0]] = cat
    return out


def get_nc():
    global KSTAGE
    import os
    KSTAGE = int(os.environ.get("KSTAGE", "99"))
    if "nc" not in _CACHE:
        _CACHE["nc"] = Builder().build()
    return _CACHE["nc"]


def make_in_maps(inp, cores):
    f = lambda a: np.ascontiguousarray(np.asarray(a, dtype=np.float32))
    ident, masks, smp, smc = _consts()
    vecs = _pack_vecs({k: np.asarray(v) for k, v in inp.items()})
    shared = {k: f(inp[k]) for k in ("w_ada", "w_in", "w_pc", "w_pa", "w_out", "w_gate", "w_up", "w_down")}
    shared.update(vecs=vecs, ident=ident, masks=masks, smp=smp, smc=smc)
    maps = []
    for i in cores:
        m = dict(shared)
        m["xp"] = f(inp["x_prompt"][i])
        m["xs"] = f(np.asarray(inp["x_sample"][4 * i:4 * i + 4]).reshape(NS, D))
        m["c5"] = f(np.concatenate([np.asarray(inp["c_prompt"][i:i + 1]), np.asarray(inp["c_sample"][4 * i:4 * i + 4])], 0))
        caches = (inp["cache_kv_g0"], inp["cache_kv_g1"], inp["cache_kv_g2"])
        for g in range(3):
            m["ck%d" % g] = f(np.asarray(caches[g][:, 4 * i:4 * i + 4]).reshape(L, 4, WIN[g], 1024))
        m["stc"] = f(inp["state_conv"][:, 4 * i:4 * i + 4])
        maps.append(m)
    return maps


def kernel(**inputs):
    nc = get_nc()
    cores = list(range(NCORES))
    maps = make_in_maps(inputs, cores)
    res = run_bass_kernel_spmd(nc, maps, core_ids=cores)
    R = res.results
    y_p = np.stack([R[i]["yp"] for i in cores]).astype(np.float32)
    y_s = np.concatenate([R[i]["ys"].reshape(4, 4, D) for i in cores], 0).astype(np.float32)
    outs = [y_p, y_s]
    for g in range(3):
        outs.append(np.stack([R[i]["kvp%d" % g] for i in cores], 1).reshape(L, NCORES, WIN[g], 2, 8, 64).astype(np.float32))
    outs.append(np.stack([R[i]["cvp"] for i in cores], 1).astype(np.float32))
    for g in range(3):
        outs.append(np.concatenate([R[i]["kvs%d" % g] for i in cores], 1).reshape(L, 32, WIN[g], 2, 8, 64).astype(np.float32))
    outs.append(np.concatenate([R[i]["cvs"] for i in cores], 1).astype(np.float32))
    return tuple(outs)
```
